# Optimizing a Trainium2 kernel written in Bass

```python
import math
import jax
import jax.numpy as jnp
from jax import lax
import numpy as np

D_MODEL = 1024
BATCH = 32
SEQ = 256
DEPTH = 2
DEC_BATCH = 4
DEC_SEQ = 4096
PAST_LEN = 256

F32 = jnp.float32
GRID_W = 64
N_MOD = 9
D_FF = 2816
NORM_EPS = 1e-6
HEAD_NORM_EPS = 1e-5
RWKV_LN_EPS = 64e-5
ROPE_BASE = 10000.0

W_A = 512
H_A = 4
DK_A = W_A // H_A
DV_A = W_A // H_A
CHUNK_A = 64
W_B = 512
NB_B = 8
BS_B = W_B // NB_B
CONV_B = 4
CONV_PAD_LEFT = 2
LRU_C = 8.0
W_C = 512
GS_C = 16
G_C = W_C // GS_C
N_C = 64
W_D = 512
HD_D = 64
H_D = W_D // HD_D
R_W = 64
R_A = 64
R_G = 128
IN_EVEN = 4 * W_A + 2 * W_B
IN_ODD = W_C + 4 * W_D
MIX_OUT = W_A + W_B

kernel_name = 'bidir_hybrid_flow_step'


def flip(t):
    return jnp.flip(t, axis=1)


def rmsnorm(x, g):
    xf = x.astype(F32)
    y = xf * lax.rsqrt(jnp.mean(xf * xf, axis=-1, keepdims=True) + NORM_EPS)
    return (y * g.astype(F32)).astype(x.dtype)


def head_norm(o, eps):
    mu = jnp.mean(o, axis=-1, keepdims=True)
    oc = o - mu
    return oc * lax.rsqrt(jnp.mean(oc * oc, axis=-1, keepdims=True) + eps)


def swiglu(h, w13, w2):
    gate, up = jnp.split(h @ w13, 2, axis=-1)
    return (jax.nn.silu(gate) * up) @ w2


def centred_shift(z):
    zp = jnp.pad(z, ((0, 0), (1, 1), (0, 0)))
    return 0.5 * (zp[:, :-2] + zp[:, 2:])


def dwconv(x, w, b):
    L = x.shape[1]
    xp = jnp.pad(x, ((0, 0), (CONV_PAD_LEFT, CONV_B - 1 - CONV_PAD_LEFT), (0, 0)))
    out = b
    for j in range(CONV_B):
        out = out + xp[:, j:j + L] * w[j]
    return out


def rope_2d(x):
    Bn, L, H, dk = x.shape
    rows = L // GRID_W
    row = jnp.repeat(jnp.arange(rows, dtype=F32), GRID_W)
    col = jnp.tile(jnp.arange(GRID_W, dtype=F32), rows)
    nf = dk // 4
    freqs = ROPE_BASE ** (-jnp.arange(nf, dtype=F32) / nf)
    ang = jnp.stack([row, col], axis=-1)[:, :, None] * freqs
    cos = jnp.cos(ang)[None, :, None]
    sin = jnp.sin(ang)[None, :, None]
    xr = x.reshape(Bn, L, H, 2, 2, nf)
    x1, x2 = xr[..., 0, :], xr[..., 1, :]
    out = jnp.stack([x1 * cos - x2 * sin, x1 * sin + x2 * cos], axis=-2)
    return out.reshape(Bn, L, H, dk)


def linear_scan(a, b, h0):
    def comb(l, r):
        return (r[0] * l[0], r[0] * l[1] + r[1])
    a_cum, b_cum = lax.associative_scan(comb, (a, b), axis=1)
    h = b_cum + a_cum * h0[:, None]
    return h, h[:, -1]


def complex_scan(ar, ai, br, bi, h0r, h0i):
    def comb(l, r):
        lar, lai, lbr, lbi = l
        rar, rai, rbr, rbi = r
        return (rar * lar - rai * lai, rar * lai + rai * lar,
                rar * lbr - rai * lbi + rbr, rar * lbi + rai * lbr + rbi)
    car, cai, cbr, cbi = lax.associative_scan(comb, (ar, ai, br, bi), axis=1)
    hr = cbr + car * h0r[:, None] - cai * h0i[:, None]
    hi = cbi + car * h0i[:, None] + cai * h0r[:, None]
    return hr, hi


def retention_dir(q, k, v, lg, s0):
    Bn, L, H, dk = q.shape
    dv = v.shape[-1]
    n = L // CHUNK_A
    qc = q.reshape(Bn, n, CHUNK_A, H, dk).transpose(1, 0, 3, 2, 4)
    kc = k.reshape(Bn, n, CHUNK_A, H, dk).transpose(1, 0, 3, 2, 4)
    vc = v.reshape(Bn, n, CHUNK_A, H, dv).transpose(1, 0, 3, 2, 4)
    idx = jnp.arange(CHUNK_A, dtype=F32)
    rel = idx[:, None] - idx[None, :]
    decay_mask = jnp.where(rel >= 0, jnp.exp(lg[:, None, None] * jnp.maximum(rel, 0.0)), 0.0)
    q_decay = jnp.exp(lg[:, None] * (idx + 1.0))[..., None]
    k_decay = jnp.exp(lg[:, None] * (CHUNK_A - 1.0 - idx))[..., None]
    chunk_decay = jnp.exp(lg * CHUNK_A)[:, None, None]

    def step(S, inp):
        qi, ki, vi = inp
        scores = jnp.einsum('bhid,bhjd->bhij', qi, ki) * decay_mask
        inner = jnp.einsum('bhij,bhjv->bhiv', scores, vi)
        cross = jnp.einsum('bhid,bhdv->bhiv', qi * q_decay, S)
        S_new = S * chunk_decay + jnp.einsum('bhjd,bhjv->bhdv', ki * k_decay, vi)
        return S_new, inner + cross

    s_fin, out = lax.scan(step, s0, (qc, kc, vc))
    out = out.transpose(1, 0, 3, 2, 4).reshape(Bn, L, H, dv)
    return out, s_fin


def rglru_dir(x, lam, wa, ba, wx, bx, h0):
    Bn, L, W = x.shape
    xb = x.reshape(Bn, L, NB_B, BS_B)
    r = jax.nn.sigmoid(jnp.einsum('blnc,ncd->blnd', xb, wa).reshape(Bn, L, W) + ba)
    i = jax.nn.sigmoid(jnp.einsum('blnc,ncd->blnd', xb, wx).reshape(Bn, L, W) + bx)
    log_a = LRU_C * r * jax.nn.log_sigmoid(lam)
    b = jnp.sqrt(-jnp.expm1(2.0 * log_a)) * (i * x)
    return linear_scan(jnp.exp(log_a), b, h0)


def s5_dir(u, a_re, a_im, log_dt, b_re, b_im, c_re, c_im, h0r, h0i):
    dt = jnp.exp(log_dt)[:, None]
    mag = jnp.exp(a_re * dt)
    abr = mag * jnp.cos(a_im * dt)
    abi = mag * jnp.sin(a_im * dt)
    den = a_re * a_re + a_im * a_im
    fr = ((abr - 1.0) * a_re + abi * a_im) / den
    fi = (abi * a_re - (abr - 1.0) * a_im) / den
    bbr = fr[..., None] * b_re - fi[..., None] * b_im
    bbi = fr[..., None] * b_im + fi[..., None] * b_re
    bur = jnp.einsum('blgs,gns->blgn', u, bbr)
    bui = jnp.einsum('blgs,gns->blgn', u, bbi)
    hr, hi = complex_scan(jnp.broadcast_to(abr, bur.shape), jnp.broadcast_to(abi, bur.shape),
                          bur, bui, h0r, h0i)
    y = jnp.einsum('blgn,gsn->blgs', hr, c_re) - jnp.einsum('blgn,gsn->blgs', hi, c_im)
    return y, hr[:, -1], hi[:, -1]


def rwkv_scan(r, w, k, v, kk, a, s0):
    def step(S, inp):
        rt, wt, kt, vt, kkt, at = inp
        sk = jnp.einsum('bhvk,bhk->bhv', S, kkt)
        S = (S * wt[:, :, None, :] - sk[..., None] * (kkt * at)[:, :, None, :]
             + vt[..., None] * kt[:, :, None, :])
        return S, jnp.einsum('bhvk,bhk->bhv', S, rt)
    xs = tuple(jnp.moveaxis(t, 1, 0) for t in (r, w, k, v, kk, a))
    s_fin, ys = lax.scan(step, s0, xs)
    return jnp.moveaxis(ys, 0, 1), s_fin


def mix_even(h, s_ret, s_lru, is_latent, params):
    (w_in, w_out, ret_decay, conv_w, conv_b, lru_lam, lru_wa, lru_ba, lru_wx, lru_bx) = params
    Bn, L, _ = h.shape
    p = (h @ w_in).astype(F32)
    q, k, v, g, gb, xb = jnp.split(p, [W_A, 2 * W_A, 3 * W_A, 4 * W_A, 4 * W_A + W_B], axis=-1)
    q = q.reshape(Bn, L, H_A, DK_A)
    k = k.reshape(Bn, L, H_A, DK_A)
    v = v.reshape(Bn, L, H_A, DV_A)
    if is_latent:
        q = rope_2d(q)
        k = rope_2d(k)
    q = q * (DK_A ** -0.5)
    lg = jax.nn.log_sigmoid(ret_decay.astype(F32))
    s_ret = s_ret.astype(F32)
    o_f, sf = retention_dir(q, k, v, lg[0], s_ret[:, 0])
    o_b, sb = retention_dir(flip(q), flip(k), flip(v), lg[1], s_ret[:, 1])
    o = head_norm(o_f + flip(o_b), HEAD_NORM_EPS).reshape(Bn, L, W_A)
    y_a = o * jax.nn.silu(g)
    xc = dwconv(xb, conv_w.astype(F32), conv_b.astype(F32))
    s_lru = s_lru.astype(F32)
    hf, lf = rglru_dir(xc, lru_lam[0], lru_wa[0], lru_ba[0], lru_wx[0], lru_bx[0], s_lru[:, 0])
    hb, lb = rglru_dir(flip(xc), lru_lam[1], lru_wa[1], lru_ba[1], lru_wx[1], lru_bx[1], s_lru[:, 1])
    y_b = jax.nn.gelu(gb) * (hf + flip(hb))
    y = jnp.concatenate([y_a, y_b], axis=-1).astype(h.dtype) @ w_out
    return y, jnp.stack([sf, sb], axis=1), jnp.stack([lf, lb], axis=1)


def mix_odd(h, s_s5, s_rwkv, params):
    (w_in, w_out, a_re, a_im, log_dt, b_re, b_im, c_re, c_im, s5_d, glu_w, glu_b,
     rw_mu, rw_w0, rw_w1, rw_w2, rw_a0, rw_a1, rw_a2, rw_g1, rw_g2, rw_kk, rw_ka, rw_rk,
     ln_w, ln_b) = params
    Bn, L, _ = h.shape
    p = (h @ w_in).astype(F32)
    u, r, k, v, xd = jnp.split(p, [W_C, W_C + W_D, W_C + 2 * W_D, W_C + 3 * W_D], axis=-1)
    ub = u.reshape(Bn, L, G_C, GS_C)
    ss = s_s5.astype(F32)
    yf, hfr, hfi = s5_dir(ub, a_re[0], a_im[0], log_dt[0], b_re[0], b_im[0], c_re[0], c_im[0],
                          ss[:, 0, 0], ss[:, 0, 1])
    yb, hbr, hbi = s5_dir(flip(ub), a_re[1], a_im[1], log_dt[1], b_re[1], b_im[1], c_re[1], c_im[1],
                          ss[:, 1, 0], ss[:, 1, 1])
    ys = (yf + flip(yb)).reshape(Bn, L, W_C) + s5_d * u
    z = jax.nn.gelu(ys)
    y_c = z * jax.nn.sigmoid(z @ glu_w + glu_b)
    new_s5 = jnp.stack([jnp.stack([hfr, hfi], axis=1), jnp.stack([hbr, hbi], axis=1)], axis=1)
    mu = rw_mu.astype(F32)
    r = r + (centred_shift(r) - r) * mu[0]
    k = k + (centred_shift(k) - k) * mu[1]
    v = v + (centred_shift(v) - v) * mu[2]
    dxd = centred_shift(xd) - xd
    xw = xd + dxd * mu[3]
    xa = xd + dxd * mu[4]
    xg = xd + dxd * mu[5]
    g = jax.nn.sigmoid(xg @ rw_g1) @ rw_g2
    hs = (Bn, L, H_D, HD_D)
    kk = (k * rw_kk).reshape(hs)
    kk = kk / jnp.maximum(jnp.linalg.norm(kk, axis=-1, keepdims=True), 1e-12)
    rh = r.reshape(hs)
    vh = v.reshape(hs)
    rk = rw_rk.astype(F32).reshape(H_D, HD_D)
    sr = s_rwkv.astype(F32)
    y_sum = 0.0
    bonus = 0.0
    finals = []
    for d in range(2):
        wl = -jax.nn.softplus(-(rw_w0[d] + jnp.tanh(xw @ rw_w1[d]) @ rw_w2[d])) - 0.5
        decay = jnp.exp(-jnp.exp(wl)).reshape(hs)
        a = jax.nn.sigmoid(rw_a0[d] + (xa @ rw_a1[d]) @ rw_a2[d])
        kd = (k * (1.0 + (a - 1.0) * rw_ka)).reshape(hs)
        ah = a.reshape(hs)
        seqs = (rh, decay, kd, vh, kk, ah)
        if d == 1:
            seqs = tuple(flip(t) for t in seqs)
        y_d, s_fin = rwkv_scan(*seqs, sr[:, d])
        if d == 1:
            y_d = flip(y_d)
        y_sum = y_sum + y_d
        bonus = bonus + jnp.sum(rh * kd * rk, axis=-1, keepdims=True) * vh
        finals.append(s_fin)
    yd = head_norm(y_sum, RWKV_LN_EPS).reshape(Bn, L, W_D) * ln_w + ln_b
    y_dd = (yd + bonus.reshape(Bn, L, W_D)) * g
    y = jnp.concatenate([y_c, y_dd], axis=-1).astype(h.dtype) @ w_out
    return y, new_s5, jnp.stack(finals, axis=1)


def trunk(x, cond, init_states, is_latent, shared, even, odd):
    mod_w, mod_b, norm_g, f1a, f1b, f2a, f2b, final_norm = shared
    new_states = []
    for i in range(DEPTH):
        m = (jax.nn.silu(cond) @ mod_w[i] + mod_b[i]).reshape(cond.shape[0], N_MOD, 1, D_MODEL)
        h = rmsnorm(x, norm_g[i, 0]) * (1.0 + m[:, 1]) + m[:, 0]
        x = x + 0.5 * m[:, 2] * swiglu(h, f1a[i], f1b[i])
        h = rmsnorm(x, norm_g[i, 1]) * (1.0 + m[:, 4]) + m[:, 3]
        s_a, s_b = init_states[i]
        if i % 2 == 0:
            y, s_a, s_b = mix_even(h, s_a, s_b, is_latent, even)
        else:
            y, s_a, s_b = mix_odd(h, s_a, s_b, odd)
        new_states.append((s_a, s_b))
        x = x + m[:, 5] * y
        h = rmsnorm(x, norm_g[i, 2]) * (1.0 + m[:, 7]) + m[:, 6]
        x = x + 0.5 * m[:, 8] * swiglu(h, f2a[i], f2b[i])
    return rmsnorm(x, final_norm), new_states


def setup_inputs(seed: int = 0) -> dict:
    key = jax.random.key(seed)
    keys = iter(jax.random.split(key, 96))

    def nrm(shape, scale):
        return jax.random.normal(next(keys), shape, F32) * scale

    def uni(shape, lo, hi):
        return jax.random.uniform(next(keys), shape, F32, lo, hi)

    D = D_MODEL
    gam = 1.0 - 2.0 ** (-5.0 - jnp.arange(H_A, dtype=F32))
    ret_logit = jnp.log(gam / (1.0 - gam))
    lru_a = uni((2, W_B), 0.9, 0.999) ** (1.0 / LRU_C)
    lru_lam = jnp.log(lru_a / (1.0 - lru_a))
    a_im0 = math.pi * jnp.arange(N_C, dtype=F32)
    return {
        'x_prompt': nrm((BATCH, SEQ, D), 1.0),
        'x_sample': nrm((DEC_BATCH, DEC_SEQ, D), 1.0),
        'state_l0_ret': nrm((DEC_BATCH, 2, H_A, DK_A, DV_A), 2.0),
        'state_l0_lru': nrm((DEC_BATCH, 2, W_B), 0.5),
        'state_l1_s5': nrm((DEC_BATCH, 2, 2, G_C, N_C), 0.5),
        'state_l1_rwkv': nrm((DEC_BATCH, 2, H_D, HD_D, HD_D), 0.1),
        'c': nrm((DEC_BATCH, D), 1.0),
        'c_ctx': nrm((D,), 1.0),
        'mod_w': nrm((DEPTH, D, N_MOD * D), D ** -0.5),
        'mod_b': nrm((DEPTH, N_MOD * D), 0.02),
        'norm_g': 1.0 + nrm((DEPTH, 3, D), 0.02),
        'ffn1_w13': nrm((DEPTH, D, 2 * D_FF), D ** -0.5),
        'ffn1_w2': nrm((DEPTH, D_FF, D), D_FF ** -0.5),
        'ffn2_w13': nrm((DEPTH, D, 2 * D_FF), D ** -0.5),
        'ffn2_w2': nrm((DEPTH, D_FF, D), D_FF ** -0.5),
        'final_norm': 1.0 + nrm((D,), 0.02),
        'l0_w_in': nrm((D, IN_EVEN), D ** -0.5),
        'l0_w_out': nrm((MIX_OUT, D), MIX_OUT ** -0.5),
        'l0_ret_decay': ret_logit[None, :] + nrm((2, H_A), 0.05),
        'l0_conv_w': nrm((CONV_B, W_B), CONV_B ** -0.5),
        'l0_conv_b': nrm((W_B,), 0.02),
        'l0_lru_lam': lru_lam,
        'l0_lru_wa': nrm((2, NB_B, BS_B, BS_B), BS_B ** -0.5),
        'l0_lru_ba': nrm((2, W_B), 0.02),
        'l0_lru_wx': nrm((2, NB_B, BS_B, BS_B), BS_B ** -0.5),
        'l0_lru_bx': nrm((2, W_B), 0.02),
        'l1_w_in': nrm((D, IN_ODD), D ** -0.5),
        'l1_w_out': nrm((MIX_OUT, D), MIX_OUT ** -0.5),
        'l1_s5_a_re': -0.5 + nrm((2, G_C, N_C), 0.01),
        'l1_s5_a_im': a_im0 + nrm((2, G_C, N_C), 0.01),
        'l1_s5_log_dt': uni((2, G_C), math.log(0.001), math.log(0.1)),
        'l1_s5_b_re': nrm((2, G_C, N_C, GS_C), (2.0 * GS_C) ** -0.5),
        'l1_s5_b_im': nrm((2, G_C, N_C, GS_C), (2.0 * GS_C) ** -0.5),
        'l1_s5_c_re': nrm((2, G_C, GS_C, N_C), (2.0 * N_C) ** -0.5),
        'l1_s5_c_im': nrm((2, G_C, GS_C, N_C), (2.0 * N_C) ** -0.5),
        'l1_s5_d': nrm((W_C,), 0.5),
        'l1_glu_w': nrm((W_C, W_C), W_C ** -0.5),
        'l1_glu_b': nrm((W_C,), 0.02),
        'l1_rw_mu': uni((6, W_D), 0.1, 0.9),
        'l1_rw_w0': jnp.linspace(-6.5, -1.5, W_D, dtype=F32)[None, :] + nrm((2, W_D), 0.1),
        'l1_rw_w1': nrm((2, W_D, R_W), 0.1 * W_D ** -0.5),
        'l1_rw_w2': nrm((2, R_W, W_D), 0.1),
        'l1_rw_a0': nrm((2, W_D), 0.1),
        'l1_rw_a1': nrm((2, W_D, R_A), 0.1 * W_D ** -0.5),
        'l1_rw_a2': nrm((2, R_A, W_D), 0.1),
        'l1_rw_g1': nrm((W_D, R_G), W_D ** -0.5),
        'l1_rw_g2': nrm((R_G, W_D), R_G ** -0.5),
        'l1_rw_kk': 0.85 + nrm((W_D,), 0.02),
        'l1_rw_ka': 1.0 + nrm((W_D,), 0.02),
        'l1_rw_rk': nrm((W_D,), 0.1),
        'l1_ln_w': 1.0 + nrm((W_D,), 0.02),
        'l1_ln_b': nrm((W_D,), 0.02),
    }


def reference(x_prompt, x_sample, state_l0_ret, state_l0_lru, state_l1_s5, state_l1_rwkv, c, c_ctx,
              mod_w, mod_b, norm_g, ffn1_w13, ffn1_w2, ffn2_w13, ffn2_w2, final_norm,
              l0_w_in, l0_w_out, l0_ret_decay, l0_conv_w, l0_conv_b, l0_lru_lam, l0_lru_wa, l0_lru_ba,
              l0_lru_wx, l0_lru_bx,
              l1_w_in, l1_w_out, l1_s5_a_re, l1_s5_a_im, l1_s5_log_dt, l1_s5_b_re, l1_s5_b_im,
              l1_s5_c_re, l1_s5_c_im, l1_s5_d, l1_glu_w, l1_glu_b,
              l1_rw_mu, l1_rw_w0, l1_rw_w1, l1_rw_w2, l1_rw_a0, l1_rw_a1, l1_rw_a2, l1_rw_g1, l1_rw_g2,
              l1_rw_kk, l1_rw_ka, l1_rw_rk, l1_ln_w, l1_ln_b):
    shared = (mod_w, mod_b, norm_g, ffn1_w13, ffn1_w2, ffn2_w13, ffn2_w2, final_norm)
    even = (l0_w_in, l0_w_out, l0_ret_decay, l0_conv_w, l0_conv_b, l0_lru_lam, l0_lru_wa, l0_lru_ba,
            l0_lru_wx, l0_lru_bx)
    odd = (l1_w_in, l1_w_out, l1_s5_a_re, l1_s5_a_im, l1_s5_log_dt, l1_s5_b_re, l1_s5_b_im,
           l1_s5_c_re, l1_s5_c_im, l1_s5_d, l1_glu_w, l1_glu_b,
           l1_rw_mu, l1_rw_w0, l1_rw_w1, l1_rw_w2, l1_rw_a0, l1_rw_a1, l1_rw_a2, l1_rw_g1, l1_rw_g2,
           l1_rw_kk, l1_rw_ka, l1_rw_rk, l1_ln_w, l1_ln_b)
    bp = x_prompt.shape[0]
    ctx_init = [(jnp.zeros((bp, 2, H_A, DK_A, DV_A), F32), jnp.zeros((bp, 2, W_B), F32)),
                (jnp.zeros((bp, 2, 2, G_C, N_C), F32), jnp.zeros((bp, 2, H_D, HD_D, HD_D), F32))]
    y_prompt, ctx_states = trunk(x_prompt, c_ctx[None, :], ctx_init, False, shared, even, odd)
    lat_init = [(state_l0_ret, state_l0_lru), (state_l1_s5, state_l1_rwkv)]
    y_sample, _ = trunk(x_sample, c, lat_init, True, shared, even, odd)
    (new_l0_ret, new_l0_lru), (new_l1_s5, new_l1_rwkv) = ctx_states
    return (y_prompt, y_sample, new_l0_ret, new_l0_lru, new_l1_s5, new_l1_rwkv)
```

```python
import contextlib
import math
import numpy as np
import concourse.bass as bass
import concourse.mybir as mybir
from concourse.bass_utils import run_bass_kernel_spmd

F32 = mybir.dt.float32
BF16 = mybir.dt.bfloat16
I32 = mybir.dt.int32
ALU = mybir.AluOpType
AF = mybir.ActivationFunctionType

D = 1024
DFF = 2816
NJ = DFF // 128
NMOD = 9
LP = 256


class Ctx:
    def __init__(self, nc, es):
        self.nc = nc
        self.es = es
        self.eng = {'pe': nc.tensor, 'act': nc.scalar, 'dve': nc.vector, 'pool': nc.gpsimd, 'sp': nc.sync}
        self.sem = {}
        self.cnt = {}
        for n in ['pe', 'act', 'dve', 'pool']:
            self.sem[n] = es.enter_context(nc.semaphore('s_' + n))
            self.cnt[n] = 0
        self.slots = {}
        for q, n in (('sp', 8), ('pool', 6), ('act', 4)):
            lst = []
            for i in range(n):
                nm = 'd_%s%d' % (q, i)
                self.sem[nm] = es.enter_context(nc.semaphore(nm))
                self.cnt[nm] = 0
                lst.append(nm)
            self.slots[q] = lst
        self.slot_i = {'sp': 0, 'pool': 0, 'act': 0}
        self.waited = {e: {} for e in self.eng}
        self.lw = {}
        self.rd = {}
        self.nops = 0

    def _deps(self, r, w):
        d = {}

        def add(ev):
            if ev is None:
                return
            n, v = ev
            if d.get(n, 0) < v:
                d[n] = v
        for k in r:
            add(self.lw.get(k))
        for k in w:
            add(self.lw.get(k))
            for n, v in self.rd.get(k, {}).items():
                add((n, v))
        return d

    def _wait(self, e, d):
        for n, v in d.items():
            if e == 'pe' and n == 'pe':
                continue
            if self.waited[e].get(n, 0) < v:
                self.eng[e].wait_ge(self.sem[n], v)
                self.waited[e][n] = v

    def _rec(self, ev, r, w):
        for k in w:
            self.lw[k] = ev
            self.rd[k] = {}
        for k in r:
            dd = self.rd.setdefault(k, {})
            if dd.get(ev[0], 0) < ev[1]:
                dd[ev[0]] = ev[1]

    def op(self, e, fn, r=(), w=()):
        psr = [k for k in r if isinstance(k, tuple) and k[0] == 'ps' and k not in w]
        if psr:
            w = list(w) + psr
        self._wait(e, self._deps(r, w))
        inst = fn(self.eng[e])
        self.cnt[e] += 1
        inst.then_inc(self.sem[e], 1)
        self._rec((e, self.cnt[e]), r, w)
        self.nops += 1

    def dma(self, q, out, in_, r=(), w=(), **kw):
        d = self._deps(r, w)
        sl = self.slots[q][self.slot_i[q] % len(self.slots[q])]
        self.slot_i[q] += 1
        if self.cnt[sl] > 0:
            d[sl] = max(d.get(sl, 0), self.cnt[sl])
        self._wait(q, d)
        inst = self.eng[q].dma_start(out=out, in_=in_, **kw)
        self.cnt[sl] += 16
        inst.then_inc(self.sem[sl], 16)
        self._rec((sl, self.cnt[sl]), r, w)
        self.nops += 1

    def barrier(self, engines=('pe', 'act', 'dve', 'pool', 'sp')):
        d = {n: v for n, v in self.cnt.items() if v > 0}
        for e in engines:
            self._wait(e, d)
        self.lw = {}
        self.rd = {}

    def finish(self):
        d = {n: v for n, v in self.cnt.items() if v > 0}
        self._wait('sp', d)


def _col(ap, c):
    return ap[:, c:c + 1]


class Prog:
    def __init__(self, cfg):
        self.cfg = cfg
        self.LS = cfg['LS']
        self.NP = cfg['NP']
        self.NT = self.LS + self.NP * LP
        self.debug = cfg.get('debug', False)
        self.stages = cfg.get('stages', 'all')

    def build(self):
        nc = bass.Bass("TRN2", target_bir_lowering=False)
        self.nc = nc
        NT, LS, NP = self.NT, self.LS, self.NP

        def din(name, shape, dt=F32):
            return nc.dram_tensor(name, list(shape), dt, kind="ExternalInput").ap()

        def dout(name, shape, dt=F32):
            return nc.dram_tensor(name, list(shape), dt, kind="ExternalOutput").ap()

        def dscr(name, shape, dt=F32):
            kind = "ExternalOutput" if self.debug else "Internal"
            return nc.dram_tensor(name, list(shape), dt, kind=kind).ap()

        I = {}
        I['x_tok'] = din('x_tok', [NT, D])
        I['cond'] = din('cond', [2, D])
        I['st_ret'] = din('st_ret', [2, 4, 128, 128])
        I['st_lru'] = din('st_lru', [2, 512])
        I['st_s5'] = din('st_s5', [2, 2, 32, 64])
        I['st_rwkv'] = din('st_rwkv', [2, 8, 64, 64])
        I['mod_w'] = din('mod_w', [2, D, NMOD * D])
        I['mod_b'] = din('mod_b', [2, NMOD * D])
        I['norm_g'] = din('norm_g', [2, 3, D])
        for nm in ('ffn1', 'ffn2'):
            I[nm + '_w13'] = din(nm + '_w13', [2, D, 2 * DFF])
            I[nm + '_w2'] = din(nm + '_w2', [2, DFF, D])
        I['final_norm'] = din('final_norm', [D])
        I['l0_w_in'] = din('l0_w_in', [D, 3072])
        I['l0_w_out'] = din('l0_w_out', [D, D])
        I['l0_ret_decay'] = din('l0_ret_decay', [2, 4])
        I['l0_conv_w'] = din('l0_conv_w', [4, 512])
        I['l0_conv_b'] = din('l0_conv_b', [512])
        I['l0_lru_lam'] = din('l0_lru_lam', [2, 512])
        I['l0_lru_wa'] = din('l0_lru_wa', [2, 8, 64, 64])
        I['l0_lru_ba'] = din('l0_lru_ba', [2, 512])
        I['l0_lru_wx'] = din('l0_lru_wx', [2, 8, 64, 64])
        I['l0_lru_bx'] = din('l0_lru_bx', [2, 512])
        I['l1_w_in'] = din('l1_w_in', [D, 2560])
        I['l1_w_out'] = din('l1_w_out', [D, D])
        I['l1_s5_a_re'] = din('l1_s5_a_re', [2, 32, 64])
        I['l1_s5_a_im'] = din('l1_s5_a_im', [2, 32, 64])
        I['l1_s5_log_dt'] = din('l1_s5_log_dt', [2, 32])
        I['l1_s5_b_re'] = din('l1_s5_b_re', [2, 32, 64, 16])
        I['l1_s5_b_im'] = din('l1_s5_b_im', [2, 32, 64, 16])
        I['l1_s5_c_re'] = din('l1_s5_c_re', [2, 32, 16, 64])
        I['l1_s5_c_im'] = din('l1_s5_c_im', [2, 32, 16, 64])
        I['l1_s5_d'] = din('l1_s5_d', [512])
        I['l1_glu_w'] = din('l1_glu_w', [512, 512])
        I['l1_glu_b'] = din('l1_glu_b', [512])
        I['l1_rw_mu'] = din('l1_rw_mu', [6, 512])
        I['l1_rw_w0'] = din('l1_rw_w0', [2, 512])
        I['l1_rw_w1'] = din('l1_rw_w1', [2, 512, 64])
        I['l1_rw_w2'] = din('l1_rw_w2', [2, 64, 512])
        I['l1_rw_a0'] = din('l1_rw_a0', [2, 512])
        I['l1_rw_a1'] = din('l1_rw_a1', [2, 512, 64])
        I['l1_rw_a2'] = din('l1_rw_a2', [2, 64, 512])
        I['l1_rw_g1'] = din('l1_rw_g1', [512, 128])
        I['l1_rw_g2'] = din('l1_rw_g2', [128, 512])
        I['l1_rw_kk'] = din('l1_rw_kk', [512])
        I['l1_rw_ka'] = din('l1_rw_ka', [512])
        I['l1_rw_rk'] = din('l1_rw_rk', [512])
        I['l1_ln_w'] = din('l1_ln_w', [512])
        I['l1_ln_b'] = din('l1_ln_b', [512])
        I['c_ident'] = din('c_ident', [128, 128])
        I['c_rope_cos'] = din('c_rope_cos', [128, LS])
        I['c_rope_sin'] = din('c_rope_sin', [128, LS])
        I['c_rope_rot'] = din('c_rope_rot', [128, 128])
        I['c_misc'] = din('c_misc', [128, 1024])
        I['c_iota'] = din('c_iota', [128, 512])
        I['c_s5mask'] = din('c_s5mask', [128, 512])
        I['c_cmf'] = din('c_cmf', [128, 512])
        I['c_blk'] = din('c_blk', [128, 128])
        I['c_rwm'] = din('c_rwm', [128, 1024])
        self.I = I

        O = {}
        O['y_tok'] = dout('y_tok', [NT, D])
        O['new_ret'] = dout('new_ret', [NP, 2, 4, 128, 128])
        O['new_lru'] = dout('new_lru', [NP, 2, 512])
        O['new_s5'] = dout('new_s5', [NP, 2, 2, 32, 64])
        O['new_rwkv'] = dout('new_rwkv', [NP, 2, 8, 64, 64])
        self.O = O

        self.XS = dscr('XS', [D, NT])
        self.P = dscr('Pscr', [3072, NT])
        self.Y = dscr('Yscr', [D, NT], BF16)
        self.YF = dscr('YFscr', [512, NT])
        self.BFs = dscr('BFscr', [512, NT])

        with contextlib.ExitStack() as es:
            c = Ctx(nc, es)
            self.c = c
            self.es = es
            self.PS = [es.enter_context(nc.psum_tensor('ps%d' % i, [128, 512], F32)) for i in range(8)]
            self.ident = es.enter_context(nc.sbuf_tensor('ident', [128, 128], F32))
            self.ones_bf = es.enter_context(nc.sbuf_tensor('ones_bf', [128, 128], BF16))
            self.cst = es.enter_context(nc.sbuf_tensor('cst', [128, 8], F32))
            self.MODC = es.enter_context(nc.sbuf_tensor('MODC', [128, 2 * 3 * 3 * 2 + 2, 8], F32))
            c.dma('sp', self.ident[:], I['c_ident'][:, :], w=['ident'])
            c.op('dve', lambda e: e.memset(self.ones_bf[:], 1.0), w=['ones_bf'])
            for j, v in enumerate([1e-6, 1e-5, 64e-5, 0.0, 1.0]):
                c.op('dve', lambda e: e.memset(self.cst[:, j:j + 1], v), w=[('cst', j)])

            st = self.stages
            self.phase_transpose_in()
            self.phase_mod()
            for i in range(2):
                self.phase_ffn(i, 0)
                self.phase_inproj(i)
                if i == 0:
                    self.phase_mix_even()
                else:
                    self.phase_mix_odd()
                self.phase_outproj(i)
                self.phase_ffn(i, 2)
            self.phase_final()
            c.finish()
        return nc

    def sb(self, ph, name, shape, dt):
        self._uid = getattr(self, '_uid', 0) + 1
        return ph.enter_context(self.nc.sbuf_tensor('%s_u%d' % (name, self._uid), shape, dt))

    def modidx(self, i, s, kind, j):
        return ((i * 3 + s) * 3 + kind) * 2 + j

    def cond_of_tile(self, t0):
        return 0 if t0 < self.LS else 1

    def XSv(self):
        return self.XS.rearrange("(c p) t -> p c t", p=128)

    def phase_transpose_in(self):
        c, nc, I = self.c, self.nc, self.I
        with contextlib.ExitStack() as ph:
            xin = [self.sb(ph, 'p0_xin%d' % b, [128, 4, D], F32) for b in range(2)]
            xT = [self.sb(ph, 'p0_xT%d' % b, [128, 8, 512], F32) for b in range(2)]
            xv = I['x_tok'].rearrange("(n q p) d -> n p q d", p=128, q=4)
            ng = self.NT // 512
            for g in range(ng):
                b = g % 2
                c.dma('sp', xin[b][:], xv[g], w=[('xin', b)])
                for kc in range(8):
                    ps = self.PS[kc % 4]
                    for q in range(4):
                        c.op('pe', lambda e: e.transpose(out=ps[:, q * 128:(q + 1) * 128], in_=xin[b][:, q, kc * 128:(kc + 1) * 128], identity=self.ident[:]),
                             r=[('xin', b), 'ident'], w=[('ps', kc % 4)])
                    eng = 'act' if kc % 2 == 0 else 'dve'
                    if eng == 'act':
                        c.op('act', lambda e: e.activation(out=xT[b][:, kc, :], in_=ps[:], func=AF.Identity), r=[('ps', kc % 4)], w=[('xT', b, kc)])
                    else:
                        c.op('dve', lambda e: e.tensor_copy(out=xT[b][:, kc, :], in_=ps[:]), r=[('ps', kc % 4)], w=[('xT', b, kc)])
                c.dma('sp', self.XSv()[:, :, g * 512:(g + 1) * 512], xT[b][:], r=[('xT', b, kc) for kc in range(8)], w=[('XS', g * 512), ('XS', g * 512 + 256)])
            c.barrier()

    def phase_mod(self):
        c, nc, I = self.c, self.nc, self.I
        with contextlib.ExitStack() as ph:
            NB = 1152
            wb = [self.sb(ph, 'pm_w%d' % b, [128, 8, NB], F32) for b in range(2)]
            cT = self.sb(ph, 'pm_cT', [128, 8, 2], F32)
            sT = self.sb(ph, 'pm_sT', [128, 8, 2], F32)
            mb = self.sb(ph, 'pm_mb', [128, 2, 72], F32)
            ng = self.sb(ph, 'pm_ng', [128, 7, 8], F32)
            MT = self.sb(ph, 'pm_MT', [128, 2, 72, 2], F32)
            tmp = self.sb(ph, 'pm_tmp', [128, 8], F32)
            for j in range(2):
                c.dma('sp', cT[:, :, j], I['cond'][j].rearrange("(c p) -> p c", p=128), w=['cT'], allow_slow_non_contiguous=True)
            for i in range(2):
                c.dma('sp', mb[:, i, :], I['mod_b'][i].rearrange("(k p) -> p k", p=128), w=['mb'], allow_slow_non_contiguous=True)
                for s in range(3):
                    c.dma('sp', ng[:, i * 3 + s, :], I['norm_g'][i, s].rearrange("(c p) -> p c", p=128), w=['ng'], allow_slow_non_contiguous=True)
            c.dma('sp', ng[:, 6, :], I['final_norm'].rearrange("(c p) -> p c", p=128), w=['ng'], allow_slow_non_contiguous=True)
            c.op('act', lambda e: e.activation(out=sT[:], in_=cT[:], func=AF.Silu), r=['cT'], w=['sT'])
            ps = self.PS[0]
            psv = ps[:, 0:144].rearrange("p (f j) -> p f j", j=2)
            blk = 0
            for i in range(2):
                wv = I['mod_w'][i].rearrange("(c p) n -> p c n", p=128)
                for nb in range(8):
                    b = blk % 2
                    blk += 1
                    c.dma('sp', wb[b][:], wv[:, :, nb * NB:(nb + 1) * NB], w=[('wb', b)])
                    for fc in range(9):
                        f = nb * 9 + fc
                        for kc in range(8):
                            c.op('pe', lambda e: e.matmul(psv[:, f, :], lhsT=wb[b][:, kc, fc * 128:(fc + 1) * 128], rhs=sT[:, kc, :], start=(kc == 0), stop=(kc == 7)),
                                 r=[('wb', b), 'sT'], w=[('ps', 0)])
                c.op('dve', lambda e: e.tensor_tensor(out=MT[:, i, :, :], in0=psv[:, :, :], in1=mb[:, i, :].unsqueeze(2).broadcast_to([128, 72, 2]), op=ALU.add),
                     r=[('ps', 0), 'mb'], w=['MT'])
            MC = self.MODC
            for i in range(2):
                for s in range(3):
                    for j in range(2):
                        sh = MT[:, i, (3 * s) * 8:(3 * s + 1) * 8, j]
                        sc = MT[:, i, (3 * s + 1) * 8:(3 * s + 2) * 8, j]
                        ga = MT[:, i, (3 * s + 2) * 8:(3 * s + 3) * 8, j]
                        c.op('dve', lambda e: e.scalar_tensor_tensor(out=MC[:, self.modidx(i, s, 0, j), :], in0=sc, scalar=1.0, in1=ng[:, i * 3 + s, :], op0=ALU.add, op1=ALU.mult),
                             r=['MT', 'ng'], w=['MODC'])
                        c.op('dve', lambda e: e.tensor_copy(out=MC[:, self.modidx(i, s, 1, j), :], in_=sh), r=['MT'], w=['MODC'])
                        c.op('dve', lambda e: e.tensor_scalar(out=MC[:, self.modidx(i, s, 2, j), :], in0=ga, scalar1=(0.5 if s != 1 else 1.0), scalar2=None, op0=ALU.mult),
                             r=['MT'], w=['MODC'])
            c.op('dve', lambda e: e.tensor_copy(out=MC[:, 36, :], in_=ng[:, 6, :]), r=['ng'], w=['MODC'])
            c.op('dve', lambda e: e.memset(MC[:, 37, :], 0.0), w=['MODC'])
            c.barrier()

    def norm_mod(self, xt, xkey, TT, s1i, s2i, sq, rs, rstd, tmp, h, hkey):
        c = self.c
        MC = self.MODC
        for kc in range(8):
            c.op('act', lambda e: e.activation(out=sq[:, kc % 2, :TT], in_=xt[:, kc, :], func=AF.Square), r=[xkey], w=[('sq', kc % 2)])
            c.op('pe', lambda e: e.matmul(self.PS[0][:, :TT], lhsT=self.ones_bf[:], rhs=sq[:, kc % 2, :TT], start=(kc == 0), stop=(kc == 7)),
                 r=[('sq', kc % 2), 'ones_bf'], w=[('ps', 0)])
        c.op('act', lambda e: e.activation(out=rs[:, :TT], in_=self.PS[0][:, :TT], func=AF.Sqrt, scale=1.0 / D, bias=self.cst[:, 0:1]),
             r=[('ps', 0), ('cst', 0)], w=['rs'])
        rk = 'rs' if rstd is rs else 'rstd'
        c.op('dve', lambda e: e.reciprocal(out=rstd[:, :TT], in_=rs[:, :TT]), r=['rs'], w=[rk])
        for kc in range(8):
            t = tmp[kc % 2]
            c.op('dve', lambda e: e.tensor_tensor(out=t[:, :TT], in0=xt[:, kc, :], in1=rstd[:, :TT], op=ALU.mult), r=[xkey, rk], w=[('ntmp', kc % 2)])
            c.op('act', lambda e: e.activation(out=h[:, kc, :TT], in_=t[:, :TT], func=AF.Identity, scale=MC[:, s1i, kc:kc + 1], bias=MC[:, s2i, kc:kc + 1]),
                 r=[('ntmp', kc % 2), 'MODC'], w=[hkey])

    def phase_ffn(self, i, s):
        c, nc, I = self.c, self.nc, self.I
        nm = 'ffn1' if s == 0 else 'ffn2'
        TT = 512
        with contextlib.ExitStack() as ph:
            W13 = self.sb(ph, 'ff_w13', [128, 8, 2 * DFF], BF16)
            W2 = self.sb(ph, 'ff_w2', [128, NJ, D], BF16)
            xt = [self.sb(ph, 'ff_xt%d' % b, [128, 8, TT], F32) for b in range(2)]
            h = self.sb(ph, 'ff_h', [128, 8, TT], BF16)
            a = self.sb(ph, 'ff_a', [128, NJ, TT], BF16)
            sq = self.sb(ph, 'ff_sq', [128, 2, TT], BF16)
            rs = self.sb(ph, 'ff_rs', [128, TT], F32)
            rstd = rs
            tmp = [self.sb(ph, 'ff_tmp%d' % b, [128, TT], F32) for b in range(2)]
            sg = [self.sb(ph, 'ff_sg%d' % b, [128, TT], BF16) for b in range(2)]
            w13v = I[nm + '_w13'][i].rearrange("(c p) n -> p c n", p=128)
            w2v = I[nm + '_w2'][i].rearrange("(j p) n -> p j n", p=128)
            for q in range(4):
                c.dma('pool', W13[:, :, q * 1408:(q + 1) * 1408], w13v[:, :, q * 1408:(q + 1) * 1408], w=[('w13', q)])
            for q in range(2):
                c.dma('pool', W2[:, q * 11:(q + 1) * 11, :], w2v[:, q * 11:(q + 1) * 11, :], w=[('w2', q)])
            ntile = self.NT // TT
            XSv = self.XSv()

            def load(tt):
                c.dma('sp', xt[tt % 2][:], XSv[:, :, tt * TT:(tt + 1) * TT], r=[('XS', tt * TT)], w=[('xt', tt % 2)])

            def norm(tt):
                j = self.cond_of_tile(tt * TT)
                self.norm_mod(xt[tt % 2], ('xt', tt % 2), TT, self.modidx(i, s, 0, j), self.modidx(i, s, 1, j), sq, rs, rstd, tmp, h, 'h')

            load(0)
            norm(0)
            for tt in range(ntile):
                b = tt % 2
                j = self.cond_of_tile(tt * TT)
                if tt + 1 < ntile:
                    load(tt + 1)
                for jj in range(NJ):
                    gp = self.PS[1 + jj % 2]
                    up = self.PS[3 + jj % 2]
                    for kc in range(8):
                        c.op('pe', lambda e: e.matmul(gp[:, :TT], lhsT=W13[:, kc, jj * 128:(jj + 1) * 128], rhs=h[:, kc, :], start=(kc == 0), stop=(kc == 7)),
                             r=[('w13', jj // 11), 'h'], w=[('ps', 1 + jj % 2)])
                    for kc in range(8):
                        c.op('pe', lambda e: e.matmul(up[:, :TT], lhsT=W13[:, kc, DFF + jj * 128:DFF + (jj + 1) * 128], rhs=h[:, kc, :], start=(kc == 0), stop=(kc == 7)),
                             r=[('w13', 2 + jj // 11), 'h'], w=[('ps', 3 + jj % 2)])
                    c.op('act', lambda e: e.activation(out=sg[jj % 2][:], in_=gp[:, :TT], func=AF.Silu), r=[('ps', 1 + jj % 2)], w=[('sg', jj % 2)])
                    c.op('dve', lambda e: e.tensor_tensor(out=a[:, jj, :], in0=sg[jj % 2][:], in1=up[:, :TT], op=ALU.mult),
                         r=[('sg', jj % 2), ('ps', 3 + jj % 2)], w=[('a', jj)])
                if tt + 1 < ntile:
                    norm(tt + 1)
                gi = self.modidx(i, s, 2, j)
                for cc in range(8):
                    op_ = self.PS[5 + cc % 2]
                    for jj in range(NJ):
                        c.op('pe', lambda e: e.matmul(op_[:, :TT], lhsT=W2[:, jj, cc * 128:(cc + 1) * 128], rhs=a[:, jj, :], start=(jj == 0), stop=(jj == NJ - 1)),
                             r=[('w2', jj // 11), ('a', jj)], w=[('ps', 5 + cc % 2)])
                    c.op('dve', lambda e: e.scalar_tensor_tensor(out=xt[b][:, cc, :], in0=op_[:, :TT], scalar=self.MODC[:, gi, cc:cc + 1], in1=xt[b][:, cc, :], op0=ALU.mult, op1=ALU.add),
                         r=[('ps', 5 + cc % 2), 'MODC', ('xt', b)], w=[('xt', b)])
                c.dma('sp', XSv[:, :, tt * TT:(tt + 1) * TT], xt[b][:], r=[('xt', b)], w=[('XS', tt * TT)])
            c.barrier()

    def phase_inproj(self, i):
        c, nc, I = self.c, self.nc, self.I
        NIN = 3072 if i == 0 else 2560
        TT = 256
        nm_ = NIN // 128
        with contextlib.ExitStack() as ph:
            W = self.sb(ph, 'ip_w', [128, 8, NIN], BF16)
            xt = [self.sb(ph, 'ip_xt%d' % b, [128, 8, TT], F32) for b in range(2)]
            h = self.sb(ph, 'ip_h', [128, 8, TT], BF16)
            sq = self.sb(ph, 'ip_sq', [128, 8, TT], BF16)
            rs = self.sb(ph, 'ip_rs', [128, TT], F32)
            rstd = self.sb(ph, 'ip_rstd', [128, TT], F32)
            tmp = [self.sb(ph, 'ip_tmp%d' % b, [128, TT], F32) for b in range(2)]
            og = [self.sb(ph, 'ip_og%d' % b, [128, 4, TT], F32) for b in range(2)]
            wv = I['l%d_w_in' % i].rearrange("(c p) n -> p c n", p=128)
            for q in range(2):
                hh = NIN // 2
                c.dma('pool', W[:, :, q * hh:(q + 1) * hh], wv[:, :, q * hh:(q + 1) * hh], w=[('w', q)])
            ntile = self.NT // TT
            XSv = self.XSv()
            Pv = self.P.rearrange("(m p) t -> p m t", p=128)

            def load(tt):
                c.dma('sp', xt[tt % 2][:], XSv[:, :, tt * TT:(tt + 1) * TT], r=[('XS', tt * TT)], w=[('xt', tt % 2)])
            load(0)
            gcount = 0
            for tt in range(ntile):
                j = self.cond_of_tile(tt * TT)
                if tt + 1 < ntile:
                    load(tt + 1)
                self.norm_mod(xt[tt % 2], ('xt', tt % 2), TT, self.modidx(i, 1, 0, j), self.modidx(i, 1, 1, j), sq, rs, rstd, tmp, h, 'h')
                for m in range(nm_):
                    ps = self.PS[1 + m % 4]
                    g = gcount % 2
                    for kc in range(8):
                        c.op('pe', lambda e: e.matmul(ps[:, :TT], lhsT=W[:, kc, m * 128:(m + 1) * 128], rhs=h[:, kc, :], start=(kc == 0), stop=(kc == 7)),
                             r=[('w', (m * 128) // (NIN // 2)), 'h'], w=[('ps', 1 + m % 4)])
                    scale = (128.0 ** -0.5) if (i == 0 and m < 4) else 1.0
                    if m % 2 == 0:
                        c.op('act', lambda e: e.activation(out=og[g][:, m % 4, :], in_=ps[:, :TT], func=AF.Identity, scale=scale), r=[('ps', 1 + m % 4)], w=[('og', g, m % 4)])
                    else:
                        c.op('dve', lambda e: e.tensor_scalar(out=og[g][:, m % 4, :], in0=ps[:, :TT], scalar1=scale, scalar2=None, op0=ALU.mult), r=[('ps', 1 + m % 4)], w=[('og', g, m % 4)])
                    if m % 4 == 3:
                        m0 = m - 3
                        c.dma('sp', Pv[:, m0:m0 + 4, tt * TT:(tt + 1) * TT], og[g][:], r=[('og', g, q) for q in range(4)], w=[('P', tt * TT)])
                        gcount += 1
            c.barrier()

    def phase_outproj(self, i):
        c, nc, I = self.c, self.nc, self.I
        TT = 256
        with contextlib.ExitStack() as ph:
            W = self.sb(ph, 'op_w', [128, 8, D], BF16)
            xt = [self.sb(ph, 'op_xt%d' % b, [128, 8, TT], F32) for b in range(2)]
            yt = [self.sb(ph, 'op_yt%d' % b, [128, 8, TT], BF16) for b in range(2)]
            wv = I['l%d_w_out' % i].rearrange("(c p) n -> p c n", p=128)
            c.dma('pool', W[:], wv, w=['w'])
            ntile = self.NT // TT
            XSv = self.XSv()
            Yv = self.Y.rearrange("(c p) t -> p c t", p=128)

            def load(tt):
                c.dma('sp', xt[tt % 2][:], XSv[:, :, tt * TT:(tt + 1) * TT], r=[('XS', tt * TT)], w=[('xt', tt % 2)])
                c.dma('sp', yt[tt % 2][:], Yv[:, :, tt * TT:(tt + 1) * TT], r=[('Y', tt * TT)], w=[('yt', tt % 2)])
            load(0)
            for tt in range(ntile):
                b = tt % 2
                j = self.cond_of_tile(tt * TT)
                if tt + 1 < ntile:
                    load(tt + 1)
                gi = self.modidx(i, 1, 2, j)
                for cc in range(8):
                    ps = self.PS[1 + cc % 4]
                    for kc in range(8):
                        c.op('pe', lambda e: e.matmul(ps[:, :TT], lhsT=W[:, kc, cc * 128:(cc + 1) * 128], rhs=yt[b][:, kc, :], start=(kc == 0), stop=(kc == 7)),
                             r=['w', ('yt', b)], w=[('ps', 1 + cc % 4)])
                    c.op('dve', lambda e: e.scalar_tensor_tensor(out=xt[b][:, cc, :], in0=ps[:, :TT], scalar=self.MODC[:, gi, cc:cc + 1], in1=xt[b][:, cc, :], op0=ALU.mult, op1=ALU.add),
                         r=[('ps', 1 + cc % 4), 'MODC', ('xt', b)], w=[('xt', b)])
                c.dma('sp', XSv[:, :, tt * TT:(tt + 1) * TT], xt[b][:], r=[('xt', b)], w=[('XS', tt * TT)])
            c.barrier()

    def phase_final(self):
        c, nc, I, O = self.c, self.nc, self.I, self.O
        TT = 256
        with contextlib.ExitStack() as ph:
            xt = [self.sb(ph, 'fn_xt%d' % b, [128, 8, TT], F32) for b in range(2)]
            h = self.sb(ph, 'fn_h', [128, 8, TT], F32)
            sq = self.sb(ph, 'fn_sq', [128, 8, TT], BF16)
            rs = self.sb(ph, 'fn_rs', [128, TT], F32)
            rstd = self.sb(ph, 'fn_rstd', [128, TT], F32)
            tmp = [self.sb(ph, 'fn_tmp%d' % b, [128, TT], F32) for b in range(2)]
            yo = [self.sb(ph, 'fn_yo%d' % b, [128, 2, D], F32) for b in range(2)]
            ntile = self.NT // TT
            XSv = self.XSv()
            yv = O['y_tok'].rearrange("(n q p) d -> n p q d", p=128, q=2)

            def load(tt):
                c.dma('sp', xt[tt % 2][:], XSv[:, :, tt * TT:(tt + 1) * TT], r=[('XS', tt * TT)], w=[('xt', tt % 2)])
            load(0)
            for tt in range(ntile):
                b = tt % 2
                if tt + 1 < ntile:
                    load(tt + 1)
                self.norm_mod(xt[b], ('xt', b), TT, 36, 37, sq, rs, rstd, tmp, h, 'h')
                for q in range(2):
                    for kc in range(8):
                        ps = self.PS[1 + (kc // 4) % 2 + 2 * q]
                        c.op('pe', lambda e: e.transpose(out=ps[:, (kc % 4) * 128:(kc % 4 + 1) * 128], in_=h[:, kc, q * 128:(q + 1) * 128], identity=self.ident[:]),
                             r=['h', 'ident'], w=[('ps', 1 + (kc // 4) % 2 + 2 * q)])
                        if kc % 4 == 3:
                            half = kc // 4
                            if half == 0:
                                c.op('act', lambda e: e.activation(out=yo[b][:, q, 0:512], in_=ps[:], func=AF.Identity), r=[('ps', 1 + 2 * q)], w=[('yo', b, q, 0)])
                            else:
                                c.op('dve', lambda e: e.tensor_copy(out=yo[b][:, q, 512:1024], in_=ps[:]), r=[('ps', 2 + 2 * q)], w=[('yo', b, q, 1)])
                c.dma('sp', yv[tt], yo[b][:], r=[('yo', b, q, hh) for q in range(2) for hh in range(2)], w=[('yout', tt)])
            c.barrier()

    def seqs(self):
        lst = [dict(off=0, L=self.LS, latent=True, pidx=None)]
        for i in range(self.NP):
            lst.append(dict(off=self.LS + i * LP, L=LP, latent=False, pidx=i))
        return lst

    def phase_mix_even(self):
        import os
        which = os.environ.get('EVENPARTS', 'rl')
        if which != 'rl':
            self.mix_stub()
        if 'r' in which:
            self.mix_even_ret()
        if 'l' in which:
            self.mix_even_lru()

    def mix_even_ret(self):
        c, nc, I, O = self.c, self.nc, self.I, self.O
        LS = self.LS
        NCH = LS // 128
        Pv = self.P.rearrange("(m p) t -> p m t", p=128)
        Yv = self.Y.rearrange("(c p) t -> p c t", p=128)
        with contextlib.ExitStack() as ph:
            sb = lambda n, sh, dt: self.sb(ph, 're_' + n, sh, dt)
            misc = sb('misc', [128, 1024], F32)
            ROT = sb('rot', [128, 128], F32)
            COS = sb('cos', [128, LS], F32)
            SIN = sb('sin', [128, LS], F32)
            RD = sb('rd', [128, 8], F32)
            LG = sb('lg', [128, 8], F32)
            MT = sb('MT', [128, 4, 128], F32)
            QD = sb('QD', [128, 8, 128], F32)
            KD = sb('KD', [128, 8], F32)
            GC = sb('GC', [128, 8], F32)
            onesd = sb('onesd', [128, 128], F32)
            tA = sb('tA', [128, 128], F32)
            tB = sb('tB', [128, 128], F32)
            q_bf = sb('q_bf', [128, LS], BF16)
            k_bf = sb('k_bf', [128, LS], BF16)
            qf_ = sb('qf', [128, LS], BF16)
            qb_ = sb('qb', [128, LS], BF16)
            Vtok = sb('Vtok', [128, NCH, 128], BF16)
            Kf = sb('Kf', [128, NCH, 128], BF16)
            Kb = sb('Kb', [128, NCH, 128], BF16)
            o_a = sb('o_a', [128, LS], F32)
            o_b = sb('o_b', [128, LS], F32)
            qs = sb('qs', [128, 512], F32)
            ks = sb('ks', [128, 512], F32)
            vs = sb('vs', [128, 512], F32)
            gs = sb('gs', [128, 512], F32)
            qr = sb('qr', [128, 512], F32)
            kr = sb('kr', [128, 512], F32)
            t1 = sb('t1', [128, 512], F32)
            t2 = sb('t2', [128, 512], F32)
            t3 = sb('t3', [128, 512], F32)
            ya = sb('ya', [128, 512], BF16)
            sTm = [sb('sTm%d' % b, [128, 128], BF16) for b in range(2)]
            Sst = [sb('S%d' % d, [128, 128], F32) for d in range(2)]
            Sbf = [sb('Sbf%d' % d, [128, 128], BF16) for d in range(2)]
            PS = self.PS

            c.dma('sp', misc[:], I['c_misc'][:, :], w=['misc'])
            c.dma('sp', ROT[:], I['c_rope_rot'][:, :], w=['ROT'])
            c.dma('sp', COS[:], I['c_rope_cos'][:, :], w=['COS'])
            c.dma('sp', SIN[:], I['c_rope_sin'][:, :], w=['SIN'])
            c.dma('sp', RD[:], I['l0_ret_decay'].rearrange("d h -> (d h)").partition_broadcast(128), w=['RD'])
            c.op('dve', lambda e: e.memset(onesd[:], 1.0 / 128.0), w=['onesd'])
            c.op('act', lambda e: e.activation(out=LG[:], in_=RD[:], func=AF.Sigmoid), r=['RD'], w=['LG'])
            c.op('act', lambda e: e.activation(out=LG[:], in_=LG[:], func=AF.Ln), r=['LG'], w=['LG'])
            absrel = misc[:, 0:128]
            m_le = misc[:, 128:256]
            m_ge = misc[:, 256:384]
            iota0 = misc[:, 640:768]
            iota1 = misc[:, 768:896]
            for h in range(4):
                c.op('act', lambda e: e.activation(out=tA[:], in_=absrel, func=AF.Exp, scale=LG[:, h:h + 1]), r=['misc', 'LG'], w=['tA'])
                c.op('dve', lambda e: e.tensor_tensor(out=tA[:], in0=tA[:], in1=m_le, op=ALU.mult), r=['tA', 'misc'], w=['tA'])
                c.op('act', lambda e: e.activation(out=tB[:], in_=absrel, func=AF.Exp, scale=LG[:, 4 + h:5 + h]), r=['misc', 'LG'], w=['tB'])
                c.op('dve', lambda e: e.tensor_tensor(out=tB[:], in0=tB[:], in1=m_ge, op=ALU.mult), r=['tB', 'misc'], w=['tB'])
                c.op('dve', lambda e: e.tensor_tensor(out=MT[:, h, :], in0=tA[:], in1=tB[:], op=ALU.add), r=['tA', 'tB'], w=['MT'])
                c.op('act', lambda e: e.activation(out=QD[:, h, :], in_=iota1, func=AF.Exp, scale=LG[:, h:h + 1]), r=['misc', 'LG'], w=['QD'])
                c.op('dve', lambda e: e.tensor_scalar(out=tA[:], in0=iota0, scalar1=-1.0, scalar2=128.0, op0=ALU.mult, op1=ALU.add), r=['misc', 'MT'], w=['tA'])
                c.op('act', lambda e: e.activation(out=QD[:, 4 + h, :], in_=tA[:], func=AF.Exp, scale=LG[:, 4 + h:5 + h]), r=['tA', 'LG'], w=['QD'])
                c.op('act', lambda e: e.activation(out=KD[:, h:h + 1], in_=misc[:, 897:898], func=AF.Exp, scale=LG[:, h:h + 1]), r=['misc', 'LG'], w=['KD'])
                c.op('act', lambda e: e.activation(out=KD[:, 4 + h:5 + h], in_=misc[:, 896:897], func=AF.Exp, scale=LG[:, 4 + h:5 + h]), r=['misc', 'LG'], w=['KD'])
                for d in range(2):
                    c.op('act', lambda e: e.activation(out=GC[:, d * 4 + h:d * 4 + h + 1], in_=misc[:, 898:899], func=AF.Exp, scale=LG[:, d * 4 + h:d * 4 + h + 1]), r=['misc', 'LG'], w=['GC'])

            import os
            RETSTOP = int(os.environ.get('RETSTOP', '9'))
            for sq_ in (self.seqs() if RETSTOP > 0 else []):
                off, L, latent, pidx = sq_['off'], sq_['L'], sq_['latent'], sq_['pidx']
                TS = min(512, L)
                nt = L // TS
                nch = L // 128
                cpt = TS // 128
                for h in range(4):
                    for ti in range(nt):
                        t0 = ti * TS
                        g0 = off + t0
                        c.dma('sp', qs[:, :TS], Pv[:, h, g0:g0 + TS], w=['qs'])
                        c.dma('sp', ks[:, :TS], Pv[:, 4 + h, g0:g0 + TS], w=['ks'])
                        c.dma('sp', vs[:, :TS], Pv[:, 8 + h, g0:g0 + TS], w=['vs'])
                        if latent:
                            for (src, skey, dst, dkey) in ((qs, 'qs', qr, 'qr'), (ks, 'ks', kr, 'kr')):
                                c.op('pe', lambda e: e.matmul(PS[1][:, :TS], lhsT=ROT[:], rhs=src[:, :TS], start=True, stop=True), r=['ROT', skey], w=[('ps', 1)])
                                c.op('dve', lambda e: e.tensor_tensor(out=t1[:, :TS], in0=src[:, :TS], in1=COS[:, t0:t0 + TS], op=ALU.mult), r=[skey, 'COS'], w=['t1'])
                                c.op('dve', lambda e: e.tensor_tensor(out=t2[:, :TS], in0=PS[1][:, :TS], in1=SIN[:, t0:t0 + TS], op=ALU.mult), r=[('ps', 1), 'SIN'], w=['t2'])
                                c.op('dve', lambda e: e.tensor_tensor(out=dst[:, :TS], in0=t1[:, :TS], in1=t2[:, :TS], op=ALU.add), r=['t1', 't2'], w=[dkey])
                            qsrc, qk_, ksrc, kk_ = qr, 'qr', kr, 'kr'
                        else:
                            qsrc, qk_, ksrc, kk_ = qs, 'qs', ks, 'ks'
                        c.op('act', lambda e: e.activation(out=q_bf[:, t0:t0 + TS], in_=qsrc[:, :TS], func=AF.Identity), r=[qk_], w=['q_bf'])
                        c.op('act', lambda e: e.activation(out=k_bf[:, t0:t0 + TS], in_=ksrc[:, :TS], func=AF.Identity), r=[kk_], w=['k_bf'])
                        qv = qsrc[:, :TS].rearrange("p (c j) -> p c j", j=128)
                        c.op('dve', lambda e: e.tensor_tensor(out=qf_[:, t0:t0 + TS].rearrange("p (c j) -> p c j", j=128), in0=qv, in1=QD[:, h, :].unsqueeze(1).broadcast_to([128, cpt, 128]), op=ALU.mult),
                             r=[qk_, 'QD'], w=['qf'])
                        c.op('dve', lambda e: e.tensor_tensor(out=qb_[:, t0:t0 + TS].rearrange("p (c j) -> p c j", j=128), in0=qv, in1=QD[:, 4 + h, :].unsqueeze(1).broadcast_to([128, cpt, 128]), op=ALU.mult),
                             r=[qk_, 'QD'], w=['qb'])
                        for cc in range(cpt):
                            ch = ti * cpt + cc
                            c.op('pe', lambda e: e.transpose(out=PS[2][:, cc * 128:(cc + 1) * 128], in_=vs[:, cc * 128:(cc + 1) * 128], identity=self.ident[:]), r=['vs', 'ident'], w=[('ps', 2)])
                            c.op('pe', lambda e: e.transpose(out=PS[3][:, cc * 128:(cc + 1) * 128], in_=ksrc[:, cc * 128:(cc + 1) * 128], identity=self.ident[:]), r=[kk_, 'ident'], w=[('ps', 3)])
                        ch0 = ti * cpt
                        c.op('act', lambda e: e.activation(out=Vtok[:, ch0:ch0 + cpt, :], in_=PS[2][:, :TS].rearrange("p (c j) -> p c j", j=128), func=AF.Identity), r=[('ps', 2)], w=['Vtok'])
                        c.op('act', lambda e: e.activation(out=Kf[:, ch0:ch0 + cpt, :], in_=PS[3][:, :TS].rearrange("p (c j) -> p c j", j=128), func=AF.Identity, scale=KD[:, h:h + 1]), r=[('ps', 3), 'KD'], w=['Kf'])
                        c.op('dve', lambda e: e.tensor_scalar(out=Kb[:, ch0:ch0 + cpt, :], in0=PS[3][:, :TS].rearrange("p (c j) -> p c j", j=128), scalar1=KD[:, 4 + h:5 + h], scalar2=None, op0=ALU.mult), r=[('ps', 3), 'KD'], w=['Kb'])
                    if RETSTOP < 2:
                        continue
                    for d in range(2):
                        if latent:
                            c.dma('sp', Sst[d][:], I['st_ret'][d, h], w=[('S', d)])
                        else:
                            c.op('dve', lambda e: e.memset(Sst[d][:], 0.0), w=[('S', d)])
                        c.op('act', lambda e: e.activation(out=Sbf[d][:], in_=Sst[d][:], func=AF.Identity), r=[('S', d)], w=[('Sbf', d)])
                    for idx in range(nch):
                        cf = idx
                        cb = nch - 1 - idx
                        p2 = idx % 2
                        bS = 1 if p2 == 0 else 6
                        bO = 2 if p2 == 0 else 7
                        fsl = slice(cf * 128, cf * 128 + 128)
                        bsl = slice(cb * 128, cb * 128 + 128)
                        sl = slice(0, 128)
                        c.op('pe', lambda e: e.matmul(PS[bS][:, sl], lhsT=k_bf[:, fsl], rhs=q_bf[:, fsl], start=True, stop=True), r=['k_bf', 'q_bf'], w=[('ps', bS)])
                        c.op('dve', lambda e: e.tensor_tensor(out=sTm[p2][:], in0=PS[bS][:, sl], in1=MT[:, h, :], op=ALU.mult), r=[('ps', bS), 'MT'], w=[('sTm', p2)])
                        c.op('pe', lambda e: e.matmul(PS[bO][:, sl], lhsT=Vtok[:, cf, :], rhs=sTm[p2][:], start=True, stop=False), r=['Vtok', ('sTm', p2)], w=[('ps', bO)])
                        c.op('pe', lambda e: e.matmul(PS[bO][:, sl], lhsT=Sbf[0][:], rhs=qf_[:, fsl], start=False, stop=True), r=[('Sbf', 0), 'qf'], w=[('ps', bO)])
                        c.op('act', lambda e: e.activation(out=o_a[:, fsl], in_=PS[bO][:, sl], func=AF.Identity), r=[('ps', bO)], w=['o_a'])
                        c.op('pe', lambda e: e.matmul(PS[3][:, sl], lhsT=Kf[:, cf, :], rhs=Vtok[:, cf, :], start=True, stop=True), r=['Kf', 'Vtok'], w=[('ps', 3)])
                        c.op('dve', lambda e: e.scalar_tensor_tensor(out=Sst[0][:], in0=Sst[0][:], scalar=GC[:, h:h + 1], in1=PS[3][:, sl], op0=ALU.mult, op1=ALU.add), r=[('S', 0), 'GC', ('ps', 3)], w=[('S', 0)])
                        c.op('act', lambda e: e.activation(out=Sbf[0][:], in_=Sst[0][:], func=AF.Identity), r=[('S', 0)], w=[('Sbf', 0)])
                        c.op('pe', lambda e: e.matmul(PS[4][:, sl], lhsT=Sbf[1][:], rhs=qb_[:, bsl], start=True, stop=True), r=[('Sbf', 1), 'qb'], w=[('ps', 4)])
                        c.op('act', lambda e: e.activation(out=o_b[:, bsl], in_=PS[4][:, sl], func=AF.Identity), r=[('ps', 4)], w=['o_b'])
                        c.op('pe', lambda e: e.matmul(PS[5][:, sl], lhsT=Kb[:, cb, :], rhs=Vtok[:, cb, :], start=True, stop=True), r=['Kb', 'Vtok'], w=[('ps', 5)])
                        c.op('dve', lambda e: e.scalar_tensor_tensor(out=Sst[1][:], in0=Sst[1][:], scalar=GC[:, 4 + h:5 + h], in1=PS[5][:, sl], op0=ALU.mult, op1=ALU.add), r=[('S', 1), 'GC', ('ps', 5)], w=[('S', 1)])
                        c.op('act', lambda e: e.activation(out=Sbf[1][:], in_=Sst[1][:], func=AF.Identity), r=[('S', 1)], w=[('Sbf', 1)])
                    if pidx is not None:
                        for d in range(2):
                            c.dma('sp', O['new_ret'][pidx, d, h], Sst[d][:], r=[('S', d)], w=[('new_ret', pidx, d, h)])
                    if RETSTOP < 3:
                        continue
                    for ti in range(nt):
                        t0 = ti * TS
                        g0 = off + t0
                        tsl = slice(t0, t0 + TS)
                        c.dma('sp', gs[:, :TS], Pv[:, 12 + h, g0:g0 + TS], w=['gs'])
                        c.op('dve', lambda e: e.tensor_tensor(out=t1[:, :TS], in0=o_a[:, tsl], in1=o_b[:, tsl], op=ALU.add), r=['o_a', 'o_b'], w=['t1'])
                        c.op('pe', lambda e: e.matmul(PS[6][:, :TS], lhsT=onesd[:], rhs=t1[:, :TS], start=True, stop=True), r=['onesd', 't1'], w=[('ps', 6)])
                        c.op('dve', lambda e: e.tensor_tensor(out=t2[:, :TS], in0=t1[:, :TS], in1=PS[6][:, :TS], op=ALU.subtract), r=['t1', ('ps', 6)], w=['t2'])
                        c.op('act', lambda e: e.activation(out=t3[:, :TS], in_=t2[:, :TS], func=AF.Square), r=['t2'], w=['t3'])
                        c.op('pe', lambda e: e.matmul(PS[7][:, :TS], lhsT=onesd[:], rhs=t3[:, :TS], start=True, stop=True), r=['onesd', 't3'], w=[('ps', 7)])
                        c.op('act', lambda e: e.activation(out=t3[:, :TS], in_=PS[7][:, :TS], func=AF.Sqrt, bias=self.cst[:, 1:2]), r=[('ps', 7), ('cst', 1)], w=['t3'])
                        c.op('dve', lambda e: e.reciprocal(out=t3[:, :TS], in_=t3[:, :TS]), r=['t3'], w=['t3'])
                        c.op('dve', lambda e: e.tensor_tensor(out=t2[:, :TS], in0=t2[:, :TS], in1=t3[:, :TS], op=ALU.mult), r=['t2', 't3'], w=['t2'])
                        c.op('act', lambda e: e.activation(out=gs[:, :TS], in_=gs[:, :TS], func=AF.Silu), r=['gs'], w=['gs'])
                        c.op('dve', lambda e: e.tensor_tensor(out=ya[:, :TS], in0=t2[:, :TS], in1=gs[:, :TS], op=ALU.mult), r=['t2', 'gs'], w=['ya'])
                        c.dma('sp', Yv[:, h, g0:g0 + TS], ya[:, :TS], r=['ya'], w=[('Y', h, g0)])
            c.barrier()

    def mix_even_lru(self):
        c, nc, I, O = self.c, self.nc, self.I, self.O
        LS = self.LS
        Pv = self.P.rearrange("(m p) t -> p m t", p=128)
        Yv = self.Y.rearrange("(c p) t -> p c t", p=128)
        PS = self.PS
        with contextlib.ExitStack() as ph:
            sb = lambda n, sh, dt: self.sb(ph, 'lr_' + n, sh, dt)
            CW = sb('CW', [128, 4, 4], F32)
            CB = sb('CB', [128, 4], F32)
            LAM = sb('LAM', [128, 2, 4], F32)
            C8 = sb('C8', [128, 2, 4], F32)
            C16 = sb('C16', [128, 2, 4], F32)
            BA = sb('BA', [128, 2, 4], F32)
            BX = sb('BX', [128, 2, 4], F32)
            WA = sb('WA', [128, 8, 128], F32)
            WX = sb('WX', [128, 8, 128], F32)
            H0 = sb('H0', [128, 2, 4], F32)
            xpad = sb('xpad', [128, LS + 3], F32)
            xc = sb('xc', [128, LS], F32)
            a_ = sb('a', [128, LS], F32)
            b_ = sb('b', [128, LS], F32)
            hf = sb('hf', [128, LS], F32)
            hb = sb('hb', [128, LS], F32)
            gb = sb('gb', [128, 512], F32)
            r_ = sb('r', [128, 512], F32)
            i_ = sb('i', [128, 512], F32)
            a2 = sb('a2', [128, 512], F32)
            yb = sb('yb', [128, 512], BF16)
            for j in range(4):
                c.dma('sp', CW[:, :, j], I['l0_conv_w'][j].rearrange("(u p) -> p u", p=128), w=['CW'], allow_slow_non_contiguous=True)
            c.dma('sp', CB[:], I['l0_conv_b'].rearrange("(u p) -> p u", p=128), w=['CB'], allow_slow_non_contiguous=True)
            for d in range(2):
                c.dma('sp', LAM[:, d, :], I['l0_lru_lam'][d].rearrange("(u p) -> p u", p=128), w=['LAM'], allow_slow_non_contiguous=True)
                c.dma('sp', BA[:, d, :], I['l0_lru_ba'][d].rearrange("(u p) -> p u", p=128), w=['BA'], allow_slow_non_contiguous=True)
                c.dma('sp', BX[:, d, :], I['l0_lru_bx'][d].rearrange("(u p) -> p u", p=128), w=['BX'], allow_slow_non_contiguous=True)
                c.dma('sp', H0[:, d, :], I['st_lru'][d].rearrange("(u p) -> p u", p=128), w=['H0'], allow_slow_non_contiguous=True)
            c.op('dve', lambda e: e.memset(WA[:], 0.0), w=['WA'])
            c.op('dve', lambda e: e.memset(WX[:], 0.0), w=['WX'])
            for d in range(2):
                for U in range(4):
                    for g2 in range(2):
                        c.dma('sp', WA[g2 * 64:(g2 + 1) * 64, d * 4 + U, g2 * 64:(g2 + 1) * 64], I['l0_lru_wa'][d, 2 * U + g2], w=['WA'])
                        c.dma('sp', WX[g2 * 64:(g2 + 1) * 64, d * 4 + U, g2 * 64:(g2 + 1) * 64], I['l0_lru_wx'][d, 2 * U + g2], w=['WX'])
            c.op('act', lambda e: e.activation(out=C8[:], in_=LAM[:], func=AF.Sigmoid), r=['LAM'], w=['C8'])
            c.op('act', lambda e: e.activation(out=C8[:], in_=C8[:], func=AF.Ln), r=['C8'], w=['C8'])
            c.op('dve', lambda e: e.tensor_scalar(out=C16[:], in0=C8[:], scalar1=16.0, scalar2=None, op0=ALU.mult), r=['C8'], w=['C16'])
            c.op('dve', lambda e: e.tensor_scalar(out=C8[:], in0=C8[:], scalar1=8.0, scalar2=None, op0=ALU.mult), r=['C8', 'C16'], w=['C8'])
            for sq_ in self.seqs():
                off, L, latent, pidx = sq_['off'], sq_['L'], sq_['latent'], sq_['pidx']
                TS = min(512, L)
                nt = L // TS
                for U in range(4):
                    c.op('dve', lambda e: e.memset(xpad[:, 0:2], 0.0), w=['xpad'])
                    c.op('dve', lambda e: e.memset(xpad[:, L + 2:L + 3], 0.0), w=['xpad'])
                    c.dma('sp', xpad[:, 2:L + 2], Pv[:, 20 + U, off:off + L], w=['xpad'])
                    c.op('dve', lambda e: e.tensor_scalar(out=xc[:, :L], in0=xpad[:, 0:L], scalar1=CW[:, U, 0:1], scalar2=CB[:, U:U + 1], op0=ALU.mult, op1=ALU.add), r=['xpad', 'CW', 'CB'], w=['xc'])
                    for j in range(1, 4):
                        c.op('dve', lambda e: e.scalar_tensor_tensor(out=xc[:, :L], in0=xpad[:, j:j + L], scalar=CW[:, U, j:j + 1], in1=xc[:, :L], op0=ALU.mult, op1=ALU.add), r=['xpad', 'CW', 'xc'], w=['xc'])
                    for d in range(2):
                        for ti in range(nt):
                            tsl = slice(ti * TS, (ti + 1) * TS)
                            c.op('pe', lambda e: e.matmul(PS[1][:, :TS], lhsT=WA[:, d * 4 + U, :], rhs=xc[:, tsl], start=True, stop=True), r=['WA', 'xc'], w=[('ps', 1)])
                            c.op('pe', lambda e: e.matmul(PS[2][:, :TS], lhsT=WX[:, d * 4 + U, :], rhs=xc[:, tsl], start=True, stop=True), r=['WX', 'xc'], w=[('ps', 2)])
                            c.op('act', lambda e: e.activation(out=r_[:, :TS], in_=PS[1][:, :TS], func=AF.Sigmoid, bias=BA[:, d, U:U + 1]), r=[('ps', 1), 'BA'], w=['r'])
                            c.op('act', lambda e: e.activation(out=i_[:, :TS], in_=PS[2][:, :TS], func=AF.Sigmoid, bias=BX[:, d, U:U + 1]), r=[('ps', 2), 'BX'], w=['i'])
                            c.op('act', lambda e: e.activation(out=a_[:, tsl], in_=r_[:, :TS], func=AF.Exp, scale=C8[:, d, U:U + 1]), r=['r', 'C8'], w=['a'])
                            c.op('act', lambda e: e.activation(out=a2[:, :TS], in_=r_[:, :TS], func=AF.Exp, scale=C16[:, d, U:U + 1]), r=['r', 'C16'], w=['a2'])
                            c.op('dve', lambda e: e.tensor_scalar(out=a2[:, :TS], in0=a2[:, :TS], scalar1=-1.0, scalar2=1.0, op0=ALU.mult, op1=ALU.add), r=['a2'], w=['a2'])
                            c.op('act', lambda e: e.activation(out=a2[:, :TS], in_=a2[:, :TS], func=AF.Sqrt), r=['a2'], w=['a2'])
                            c.op('dve', lambda e: e.tensor_tensor(out=i_[:, :TS], in0=i_[:, :TS], in1=a2[:, :TS], op=ALU.mult), r=['i', 'a2'], w=['i'])
                            c.op('dve', lambda e: e.tensor_tensor(out=b_[:, tsl], in0=i_[:, :TS], in1=xc[:, tsl], op=ALU.mult), r=['i', 'xc'], w=['b'])
                        init = H0[:, d, U:U + 1] if latent else 0.0
                        if d == 0:
                            c.op('dve', lambda e: e.tensor_tensor_scan(out=hf[:, 0:L], data0=a_[:, 0:L], data1=b_[:, 0:L], initial=init, op0=ALU.mult, op1=ALU.add), r=['a', 'b', 'H0'], w=['hf'])
                        else:
                            c.op('dve', lambda e: e.tensor_tensor_scan(out=hb[:, 0:L][:, ::-1], data0=a_[:, 0:L][:, ::-1], data1=b_[:, 0:L][:, ::-1], initial=init, op0=ALU.mult, op1=ALU.add), r=['a', 'b', 'H0'], w=['hb'])
                    if pidx is not None:
                        c.dma('sp', O['new_lru'][pidx, 0, U * 128:(U + 1) * 128].rearrange("(p o) -> p o", o=1), hf[:, L - 1:L], r=['hf'], w=[('new_lru', pidx, 0, U)], allow_slow_non_contiguous=True)
                        c.dma('sp', O['new_lru'][pidx, 1, U * 128:(U + 1) * 128].rearrange("(p o) -> p o", o=1), hb[:, 0:1], r=['hb'], w=[('new_lru', pidx, 1, U)], allow_slow_non_contiguous=True)
                    for ti in range(nt):
                        tsl = slice(ti * TS, (ti + 1) * TS)
                        g0 = off + ti * TS
                        c.dma('sp', gb[:, :TS], Pv[:, 16 + U, g0:g0 + TS], w=['gb'])
                        c.op('act', lambda e: e.activation(out=gb[:, :TS], in_=gb[:, :TS], func=AF.Gelu_apprx_tanh), r=['gb'], w=['gb'])
                        c.op('dve', lambda e: e.tensor_tensor(out=r_[:, :TS], in0=hf[:, tsl], in1=hb[:, tsl], op=ALU.add), r=['hf', 'hb'], w=['r'])
                        c.op('dve', lambda e: e.tensor_tensor(out=yb[:, :TS], in0=r_[:, :TS], in1=gb[:, :TS], op=ALU.mult), r=['r', 'gb'], w=['yb'])
                        c.dma('sp', Yv[:, 4 + U, g0:g0 + TS], yb[:, :TS], r=['yb'], w=[('Y', 4 + U, g0)])
            c.barrier()

    def phase_mix_odd(self):
        import os
        which = os.environ.get('ODDPARTS', 'sr')
        if which != 'sr':
            self.mix_stub()
        if 's' in which:
            self.mix_odd_s5()
        if 'r' in which:
            self.mix_odd_rwkv()

    def sincos(self, sb_, turns, shape, out_s, out_c, keys_r, key_s, key_c, tmpf, tmpi, tkey='sc_tmpf'):
        c = self.c
        TWO_PI = 6.283184
        for (dst, dkey, shift) in ((out_s, key_s, 0.0), (out_c, key_c, 0.25)):
            c.op('dve', lambda e: e.tensor_scalar(out=tmpf, in0=turns, scalar1=shift, scalar2=None, op0=ALU.add), r=keys_r, w=[tkey])
            c.op('dve', lambda e: e.tensor_copy(out=tmpi, in_=tmpf), r=[tkey], w=['sc_tmpi'])
            c.op('dve', lambda e: e.tensor_copy(out=dst, in_=tmpi), r=['sc_tmpi'], w=[dkey])
            c.op('dve', lambda e: e.tensor_tensor(out=dst, in0=tmpf, in1=dst, op=ALU.subtract), r=[tkey, dkey], w=[dkey])
            c.op('act', lambda e: e.activation(out=dst, in_=dst, func=AF.Sin, scale=TWO_PI), r=[dkey], w=[dkey])

    def mix_odd_s5(self):
        c, nc, I, O = self.c, self.nc, self.I, self.O
        NT, NP = self.NT, self.NP
        Pv = self.P.rearrange("(m p) t -> p m t", p=128)
        Yv = self.Y.rearrange("(c p) t -> p c t", p=128)
        PS = self.PS
        CBM = 256
        with contextlib.ExitStack() as ph:
            sb = lambda n, sh, dt: self.sb(ph, 's5_' + n, sh, dt)
            ARE = sb('are', [128, 2, 16], F32)
            AIM = sb('aim', [128, 2, 16], F32)
            DT = sb('dt', [128, 2, 16], F32)
            MAG = sb('mag', [128, 2, 16], F32)
            THT = sb('tht', [128, 2, 16], F32)
            CS = sb('cs', [128, 2, 16], F32)
            SN = sb('sn', [128, 2, 16], F32)
            ABR = sb('abr', [128, 2, 16], F32)
            ABI = sb('abi', [128, 2, 16], F32)
            DEN = sb('den', [128, 2, 16], F32)
            FR = sb('fr', [128, 2, 16], F32)
            FI = sb('fi', [128, 2, 16], F32)
            p1 = sb('p1', [128, 2, 16], F32)
            p2 = sb('p2', [128, 2, 16], F32)
            pI = sb('pI', [128, 2, 16], I32)
            BRE = sb('bre', [128, 2, 16, 16], F32)
            BIM = sb('bim', [128, 2, 16, 16], F32)
            BBR = sb('bbr', [128, 2, 16, 16], F32)
            BBI = sb('bbi', [128, 2, 16, 16], F32)
            bt1 = sb('bt1', [128, 16, 16], F32)
            CRE = sb('cre', [128, 2, 4, 64], F32)
            CIM = sb('cim', [128, 2, 4, 64], F32)
            Bx = [sb('Bx%d' % q, [128, 128], F32) for q in range(4)]
            Cx = [sb('Cx%d' % q, [128, 128], F32) for q in range(4)]
            LB = sb('LB', [128, 4, 2, 128], BF16)
            LC = sb('LC', [128, 4, 2, 128], BF16)
            ST0 = sb('st0', [128, 2, 2, 16], F32)
            CMASK = sb('cmask', [128, 4, 128], F32)
            FIN = sb('fin', [128, max(NP, 2) * 64], F32)
            FINT = sb('fint', [128, 128], F32)
            IOTA = sb('iota', [128, 512], F32)
            COST = sb('cost', [128, 4, CBM], F32)
            SINT = sb('sint', [128, 4, CBM], F32)
            RHOT = sb('rhot', [128, 4, CBM], F32)
            ti_ = sb('ti', [128, CBM], I32)
            u_bf = sb('u_bf', [128, 4, NT], BF16)
            y_acc = sb('y_acc', [128, 4, NT], F32)
            w_ = {n: sb('w_' + n, [128, CBM], F32) for n in ['br', 'bi', 'hr', 'hi', 't1', 't2', 't3', 't4', 'or', 'oi', 'p3', 'p4']}
            hb_ = {n: sb('hb_' + n, [128, CBM], BF16) for n in ['r', 'i']}
            CAR = sb('car', [128, 4, 2], F32)
            tf, tfs, uf, sg = w_['t3'], w_['t4'], w_['or'], w_['oi']
            SD = sb('sd', [128, 4], F32)
            GB = sb('gb', [128, 4], F32)
            GW = sb('gw', [128, 4, 512], BF16)
            yc = sb('yc', [128, CBM], BF16)

            for d in range(2):
                c.dma('sp', ARE[:, d, :], I['l1_s5_a_re'][d].rearrange("(T g) n -> (g n) T", g=2), w=['ARE'], allow_slow_non_contiguous=True)
                c.dma('sp', AIM[:, d, :], I['l1_s5_a_im'][d].rearrange("(T g) n -> (g n) T", g=2), w=['AIM'], allow_slow_non_contiguous=True)
                ldt = I['l1_s5_log_dt'][d].rearrange("(T g) -> g T", g=2)
                for g2 in range(2):
                    c.dma('sp', DT[g2 * 64:(g2 + 1) * 64, d, :], ldt[g2].partition_broadcast(64), w=['DT'], allow_slow_non_contiguous=True)
                c.dma('sp', BRE[:, d, :, :], I['l1_s5_b_re'][d].rearrange("(T g) n s -> (g n) T s", g=2), w=['BRE'])
                c.dma('sp', BIM[:, d, :, :], I['l1_s5_b_im'][d].rearrange("(T g) n s -> (g n) T s", g=2), w=['BIM'])
                c.dma('sp', CRE[:, d, :, :], I['l1_s5_c_re'][d].rearrange("(U g) s n -> (g s) U n", g=8), w=['CRE'])
                c.dma('sp', CIM[:, d, :, :], I['l1_s5_c_im'][d].rearrange("(U g) s n -> (g s) U n", g=8), w=['CIM'])
                for ri in range(2):
                    c.dma('sp', ST0[:, d, ri, :], I['st_s5'][d, ri].rearrange("(T g) n -> (g n) T", g=2), w=['ST0'], allow_slow_non_contiguous=True)
            c.dma('sp', IOTA[:], I['c_iota'][:, :], w=['IOTA'])
            c.dma('sp', CMASK[:], I['c_s5mask'].rearrange("p (q n) -> p q n", q=4), w=['CMASK'])
            c.dma('sp', SD[:], I['l1_s5_d'].rearrange("(u p) -> p u", p=128), w=['SD'], allow_slow_non_contiguous=True)
            c.dma('sp', GB[:], I['l1_glu_b'].rearrange("(u p) -> p u", p=128), w=['GB'], allow_slow_non_contiguous=True)
            c.dma('pool', GW[:], I['l1_glu_w'].rearrange("(k p) n -> p k n", p=128), w=['GW'])
            for U in range(4):
                c.dma('pool', u_bf[:, U, :], Pv[:, U, :], w=['u_bf'])
            for q in range(4):
                c.op('dve', lambda e: e.memset(Bx[q][:], 0.0), w=[('Bx', q)])
                c.op('dve', lambda e: e.memset(Cx[q][:], 0.0), w=[('Cx', q)])
            c.op('dve', lambda e: e.memset(FIN[:], 0.0), w=['FIN'])
            c.op('act', lambda e: e.activation(out=DT[:], in_=DT[:], func=AF.Exp), r=['DT'], w=['DT'])
            c.op('dve', lambda e: e.tensor_tensor(out=p1[:], in0=ARE[:], in1=DT[:], op=ALU.mult), r=['ARE', 'DT'], w=['p1'])
            c.op('act', lambda e: e.activation(out=MAG[:], in_=p1[:], func=AF.Exp), r=['p1'], w=['MAG'])
            c.op('dve', lambda e: e.scalar_tensor_tensor(out=THT[:], in0=AIM[:], scalar=1.0 / (2.0 * math.pi), in1=DT[:], op0=ALU.mult, op1=ALU.mult), r=['AIM', 'DT'], w=['THT'])
            self.sincos(sb, THT[:], None, SN[:], CS[:], ['THT'], 'SN', 'CS', p2[:], pI[:])
            c.op('dve', lambda e: e.tensor_tensor(out=ABR[:], in0=MAG[:], in1=CS[:], op=ALU.mult), r=['MAG', 'CS'], w=['ABR'])
            c.op('dve', lambda e: e.tensor_tensor(out=ABI[:], in0=MAG[:], in1=SN[:], op=ALU.mult), r=['MAG', 'SN'], w=['ABI'])
            c.op('dve', lambda e: e.tensor_tensor(out=DEN[:], in0=ARE[:], in1=ARE[:], op=ALU.mult), r=['ARE'], w=['DEN'])
            c.op('dve', lambda e: e.tensor_tensor(out=p1[:], in0=AIM[:], in1=AIM[:], op=ALU.mult), r=['AIM', 'MAG'], w=['p1'])
            c.op('dve', lambda e: e.tensor_tensor(out=DEN[:], in0=DEN[:], in1=p1[:], op=ALU.add), r=['DEN', 'p1'], w=['DEN'])
            c.op('dve', lambda e: e.reciprocal(out=DEN[:], in_=DEN[:]), r=['DEN'], w=['DEN'])
            c.op('dve', lambda e: e.tensor_scalar(out=p1[:], in0=ABR[:], scalar1=-1.0, scalar2=None, op0=ALU.add), r=['ABR', 'DEN'], w=['p1'])
            c.op('dve', lambda e: e.tensor_tensor(out=FR[:], in0=p1[:], in1=ARE[:], op=ALU.mult), r=['p1', 'ARE'], w=['FR'])
            c.op('dve', lambda e: e.tensor_tensor(out=p2[:], in0=ABI[:], in1=AIM[:], op=ALU.mult), r=['ABI', 'AIM', 'SN', 'CS'], w=['p2'])
            c.op('dve', lambda e: e.tensor_tensor(out=FR[:], in0=FR[:], in1=p2[:], op=ALU.add), r=['FR', 'p2'], w=['FR'])
            c.op('dve', lambda e: e.tensor_tensor(out=FR[:], in0=FR[:], in1=DEN[:], op=ALU.mult), r=['FR', 'DEN'], w=['FR'])
            c.op('dve', lambda e: e.tensor_tensor(out=FI[:], in0=ABI[:], in1=ARE[:], op=ALU.mult), r=['ABI', 'ARE'], w=['FI'])
            c.op('dve', lambda e: e.tensor_tensor(out=p2[:], in0=p1[:], in1=AIM[:], op=ALU.mult), r=['p1', 'AIM', 'FR'], w=['p2'])
            c.op('dve', lambda e: e.tensor_tensor(out=FI[:], in0=FI[:], in1=p2[:], op=ALU.subtract), r=['FI', 'p2'], w=['FI'])
            c.op('dve', lambda e: e.tensor_tensor(out=FI[:], in0=FI[:], in1=DEN[:], op=ALU.mult), r=['FI', 'DEN'], w=['FI'])
            for d in range(2):
                frb = FR[:, d, :].unsqueeze(2).broadcast_to([128, 16, 16])
                fib = FI[:, d, :].unsqueeze(2).broadcast_to([128, 16, 16])
                c.op('dve', lambda e: e.tensor_tensor(out=BBR[:, d], in0=BRE[:, d], in1=frb, op=ALU.mult), r=['BRE', 'FR'], w=['BBR'])
                c.op('dve', lambda e: e.tensor_tensor(out=bt1[:], in0=BIM[:, d], in1=fib, op=ALU.mult), r=['BIM', 'FI'], w=['bt1'])
                c.op('dve', lambda e: e.tensor_tensor(out=BBR[:, d], in0=BBR[:, d], in1=bt1[:], op=ALU.subtract), r=['BBR', 'bt1'], w=['BBR'])
                c.op('dve', lambda e: e.tensor_tensor(out=BBI[:, d], in0=BIM[:, d], in1=frb, op=ALU.mult), r=['BIM', 'FR'], w=['BBI'])
                c.op('dve', lambda e: e.tensor_tensor(out=bt1[:], in0=BRE[:, d], in1=fib, op=ALU.mult), r=['BRE', 'FI', 'BBR'], w=['bt1'])
                c.op('dve', lambda e: e.tensor_tensor(out=BBI[:, d], in0=BBI[:, d], in1=bt1[:], op=ALU.add), r=['BBI', 'bt1'], w=['BBI'])

            seqs = self.seqs()
            first_contrib = {}
            ucnt = [0]
            w_sets = [w_, {n: sb('w2_' + n, [128, CBM], F32) for n in ['br', 'bi', 'hr', 'hi', 't1', 't2', 't3', 't4', 'or', 'oi', 'p3', 'p4']}]
            hb_sets = [hb_, {n: sb('hb2_' + n, [128, CBM], BF16) for n in ['r', 'i']}]
            pend_y = []

            def flush_y():
                while pend_y:
                    ykey, isfirst, U_, b0_, CB_ = pend_y.pop(0)
                    if isfirst:
                        c.op('act', lambda e: e.activation(out=y_acc[:, U_, b0_:b0_ + CB_], in_=PS[5][:, :CB_], func=AF.Identity), r=[('ps', 5)], w=[ykey])
                    else:
                        c.op('dve', lambda e: e.tensor_tensor(out=y_acc[:, U_, b0_:b0_ + CB_], in0=y_acc[:, U_, b0_:b0_ + CB_], in1=PS[5][:, :CB_], op=ALU.add), r=[('ps', 5), ykey], w=[ykey])
            for d in range(2):
                for U in range(4):
                    for q in range(4):
                        T = U * 4 + q
                        for ri, BB in ((0, BBR), (1, BBI)):
                            for g2 in range(2):
                                ps_ = slice(g2 * 64, (g2 + 1) * 64)
                                cs_ = slice((2 * q + g2) * 16, (2 * q + g2) * 16 + 16)
                                c.op('dve', lambda e: e.tensor_copy(out=Bx[q][ps_, cs_], in_=BB[ps_, d, T, :]), r=['BBR', 'BBI'], w=[('Bx', q)])
                            c.op('pe', lambda e: e.transpose(out=PS[1][:, 0:128], in_=Bx[q][:], identity=self.ident[:]), r=[('Bx', q), 'ident'], w=[('ps', 1)])
                            c.op('act', lambda e: e.activation(out=LB[:, q, ri, :], in_=PS[1][:, 0:128], func=AF.Identity), r=[('ps', 1)], w=['LB'])
                        for ri, CC, sgn in ((0, CRE, 1.0), (1, CIM, -1.0)):
                            c.op('dve', lambda e: e.scalar_tensor_tensor(out=Cx[q][:].rearrange("p (g n) -> p g n", g=2), in0=CC[:, d, U, :].unsqueeze(1).broadcast_to([128, 2, 64]), scalar=sgn,
                                                                         in1=CMASK[:, q, :].rearrange("p (g n) -> p g n", g=2), op0=ALU.mult, op1=ALU.mult), r=['CRE', 'CIM', 'CMASK'], w=[('Cx', q)])
                            c.op('pe', lambda e: e.transpose(out=PS[2][:, 0:128], in_=Cx[q][:], identity=self.ident[:]), r=[('Cx', q), 'ident'], w=[('ps', 2)])
                            c.op('act', lambda e: e.activation(out=LC[:, q, ri, :], in_=PS[2][:, 0:128], func=AF.Identity), r=[('ps', 2)], w=['LC'])
                        c.op('dve', lambda e: e.tensor_scalar(out=tf[:], in0=IOTA[:, :CBM], scalar1=THT[:, d, T:T + 1], scalar2=None, op0=ALU.mult), r=['IOTA', 'THT'], w=[('t3', 0)])
                        self.sincos(sb, tf[:], None, SINT[:, q, :], COST[:, q, :], [('t3', 0)], ('SINT', q), ('COST', q), tfs[:], ti_[:], tkey=('t4', 0))
                        c.op('dve', lambda e: e.tensor_scalar(out=RHOT[:, q, :], in0=IOTA[:, :CBM], scalar1=0.0, scalar2=MAG[:, d, T:T + 1], op0=ALU.mult, op1=ALU.add), r=['IOTA', 'MAG'], w=[('RHOT', q)])
                    units = []
                    for sq_ in seqs:
                        CB_ = min(CBM, sq_['L'])
                        nb_ = sq_['L'] // CB_
                        for bi_ in range(nb_):
                            for q in range(4):
                                units.append((sq_, bi_, q, ucnt[0] % 2))
                                ucnt[0] += 1

                    def unpack(u):
                        sq_, bi_, q, up = u
                        off, L, latent, pidx = sq_['off'], sq_['L'], sq_['latent'], sq_['pidx']
                        CB = min(CBM, L)
                        nb = L // CB
                        blk = bi_ if d == 0 else nb - 1 - bi_
                        b0 = off + blk * CB
                        T = U * 4 + q
                        pB0, pB1 = (3, 4) if up == 0 else (6, 7)
                        return off, L, latent, pidx, CB, nb, bi_, b0, q, T, up, pB0, pB1

                    def emit_B(u):
                        off, L, latent, pidx, CB, nb, bi_, b0, q, T, up, pB0, pB1 = unpack(u)
                        c.op('pe', lambda e: e.matmul(PS[pB0][:, :CB], lhsT=LB[:, q, 0, :], rhs=u_bf[:, U, b0:b0 + CB], start=True, stop=True), r=['LB', 'u_bf'], w=[('ps', pB0)])
                        c.op('pe', lambda e: e.matmul(PS[pB1][:, :CB], lhsT=LB[:, q, 1, :], rhs=u_bf[:, U, b0:b0 + CB], start=True, stop=True), r=['LB', 'u_bf'], w=[('ps', pB1)])

                    def emit_rest(u, nxt):
                        off, L, latent, pidx, CB, nb, bi_, b0, q, T, up, pB0, pB1 = unpack(u)
                        w_ = w_sets[up]
                        hb_ = hb_sets[up]
                        Dv = (lambda x: x[:, 0:CB]) if d == 0 else (lambda x: x[:, 0:CB][:, ::-1])
                        cosv = COST[:, q, :CB]
                        sinv = SINT[:, q, :CB]
                        c.op('dve', lambda e: e.tensor_tensor(out=w_['t1'][:, :CB], in0=Dv(PS[pB0]), in1=cosv, op=ALU.mult), r=[('ps', pB0), ('COST', q)], w=[('t1', up)])
                        c.op('dve', lambda e: e.tensor_tensor(out=w_['t2'][:, :CB], in0=Dv(PS[pB1]), in1=sinv, op=ALU.mult), r=[('ps', pB1), ('SINT', q)], w=[('t2', up)])
                        c.op('dve', lambda e: e.tensor_tensor(out=w_['br'][:, :CB], in0=w_['t1'][:, :CB], in1=w_['t2'][:, :CB], op=ALU.add), r=[('t1', up), ('t2', up)], w=[('br', up)])
                        c.op('dve', lambda e: e.tensor_tensor(out=w_['t3'][:, :CB], in0=Dv(PS[pB1]), in1=cosv, op=ALU.mult), r=[('ps', pB1), ('COST', q)], w=[('t3', up)])
                        c.op('dve', lambda e: e.tensor_tensor(out=w_['t4'][:, :CB], in0=Dv(PS[pB0]), in1=sinv, op=ALU.mult), r=[('ps', pB0), ('SINT', q)], w=[('t4', up)])
                        c.op('dve', lambda e: e.tensor_tensor(out=w_['bi'][:, :CB], in0=w_['t3'][:, :CB], in1=w_['t4'][:, :CB], op=ALU.subtract), r=[('t3', up), ('t4', up)], w=[('bi', up)])
                        flush_y()
                        if nxt is not None:
                            emit_B(nxt)
                        if bi_ == 0:
                            if latent:
                                ir, ii = ST0[:, d, 0, T:T + 1], ST0[:, d, 1, T:T + 1]
                            else:
                                ir, ii = 0.0, 0.0
                        else:
                            ir, ii = CAR[:, q, 0:1], CAR[:, q, 1:2]
                        c.op('dve', lambda e: e.tensor_tensor_scan(out=w_['hr'][:, :CB], data0=RHOT[:, q, :CB], data1=w_['br'][:, :CB], initial=ir, op0=ALU.mult, op1=ALU.add), r=[('RHOT', q), ('br', up), ('CAR', q), 'ST0'], w=[('hr', up)])
                        c.op('dve', lambda e: e.tensor_tensor_scan(out=w_['hi'][:, :CB], data0=RHOT[:, q, :CB], data1=w_['bi'][:, :CB], initial=ii, op0=ALU.mult, op1=ALU.add), r=[('RHOT', q), ('bi', up), ('CAR', q), 'ST0'], w=[('hi', up)])
                        c.op('dve', lambda e: e.tensor_tensor(out=w_['t1'][:, :CB], in0=w_['hr'][:, :CB], in1=cosv, op=ALU.mult), r=[('hr', up), ('COST', q)], w=[('t1', up)])
                        c.op('dve', lambda e: e.tensor_tensor(out=w_['t2'][:, :CB], in0=w_['hi'][:, :CB], in1=sinv, op=ALU.mult), r=[('hi', up), ('SINT', q)], w=[('t2', up)])
                        c.op('dve', lambda e: e.tensor_tensor(out=w_['or'][:, :CB], in0=w_['t1'][:, :CB], in1=w_['t2'][:, :CB], op=ALU.subtract), r=[('t1', up), ('t2', up)], w=[('or', up)])
                        c.op('pool', lambda e: e.tensor_tensor(out=w_['p3'][:, :CB], in0=w_['hi'][:, :CB], in1=cosv, op=ALU.mult), r=[('hi', up), ('COST', q)], w=[('p3', up)])
                        c.op('pool', lambda e: e.tensor_tensor(out=w_['p4'][:, :CB], in0=w_['hr'][:, :CB], in1=sinv, op=ALU.mult), r=[('hr', up), ('SINT', q)], w=[('p4', up)])
                        c.op('pool', lambda e: e.tensor_tensor(out=w_['oi'][:, :CB], in0=w_['p3'][:, :CB], in1=w_['p4'][:, :CB], op=ALU.add), r=[('p3', up), ('p4', up)], w=[('oi', up)])
                        c.op('dve', lambda e: e.tensor_copy(out=CAR[:, q, 0:1], in_=w_['or'][:, CB - 1:CB]), r=[('or', up)], w=[('CAR', q)])
                        c.op('pool', lambda e: e.tensor_copy(out=CAR[:, q, 1:2], in_=w_['oi'][:, CB - 1:CB]), r=[('oi', up)], w=[('CAR', q)])
                        c.op('act', lambda e: e.activation(out=Dv(hb_['r']), in_=w_['or'][:, :CB], func=AF.Identity), r=[('or', up)], w=[('hb_r', up)])
                        c.op('act', lambda e: e.activation(out=Dv(hb_['i']), in_=w_['oi'][:, :CB], func=AF.Identity), r=[('oi', up)], w=[('hb_i', up)])
                        c.op('pe', lambda e: e.matmul(PS[5][:, :CB], lhsT=LC[:, q, 0, :], rhs=hb_['r'][:, :CB], start=(q == 0), stop=False), r=['LC', ('hb_r', up)], w=[('ps', 5)])
                        c.op('pe', lambda e: e.matmul(PS[5][:, :CB], lhsT=LC[:, q, 1, :], rhs=hb_['i'][:, :CB], start=False, stop=(q == 3)), r=['LC', ('hb_i', up)], w=[('ps', 5)])
                        if pidx is not None and bi_ == nb - 1:
                            for ri in range(2):
                                col = ((pidx * 2 + d) * 2 + ri) * 16 + T
                                c.op('dve', lambda e: e.tensor_copy(out=FIN[:, col:col + 1], in_=CAR[:, q, ri:ri + 1]), r=[('CAR', q)], w=['FIN'])
                        if q == 3:
                            ykey = ('y_acc', U, b0)
                            isfirst = ykey not in first_contrib
                            first_contrib[ykey] = True
                            pend_y.append((ykey, isfirst, U, b0, CB))


                    emit_B(units[0])
                    for ui, u in enumerate(units):
                        emit_rest(u, units[ui + 1] if ui + 1 < len(units) else None)
            flush_y()
            ns5 = O['new_s5'].rearrange("b d r (T g) n -> (b d r T) (g n)", g=2)
            for blk in range((NP * 64 + 127) // 128):
                ncol = min(128, NP * 64 - blk * 128)
                c.op('pe', lambda e: e.transpose(out=PS[1][:ncol, 0:128], in_=FIN[:, blk * 128:blk * 128 + ncol], identity=self.ident[:]), r=['FIN', 'ident'], w=[('ps', 1)])
                c.op('act', lambda e: e.activation(out=FINT[:ncol, :], in_=PS[1][:ncol, 0:128], func=AF.Identity), r=[('ps', 1)], w=['FINT'])
                c.dma('sp', ns5[blk * 128:blk * 128 + ncol, :], FINT[:ncol, :], r=['FINT'], w=[('ns5', blk)])
            z_bf = u_bf
            for tt in range(NT // CBM):
                t0 = tt * CBM
                tsl = slice(t0, t0 + CBM)
                for U in range(4):
                    c.dma('sp', uf[:], Pv[:, U, tsl], w=[('or', 0)])
                    c.op('dve', lambda e: e.scalar_tensor_tensor(out=y_acc[:, U, tsl], in0=uf[:], scalar=SD[:, U:U + 1], in1=y_acc[:, U, tsl], op0=ALU.mult, op1=ALU.add), r=[('or', 0), 'SD'] + [k for k in first_contrib if k[1] == U], w=[('z', U, tt)])
                    c.op('act', lambda e: e.activation(out=y_acc[:, U, tsl], in_=y_acc[:, U, tsl], func=AF.Gelu_apprx_tanh), r=[('z', U, tt)], w=[('z', U, tt)])
                    c.op('act', lambda e: e.activation(out=z_bf[:, U, tsl], in_=y_acc[:, U, tsl], func=AF.Identity), r=[('z', U, tt), 'u_bf'], w=[('zb', U, tt)])
                for fo in range(4):
                    for k in range(4):
                        c.op('pe', lambda e: e.matmul(PS[6][:, :CBM], lhsT=GW[:, k, fo * 128:(fo + 1) * 128], rhs=z_bf[:, k, tsl], start=(k == 0), stop=(k == 3)), r=['GW', ('zb', k, tt)], w=[('ps', 6)])
                    c.op('act', lambda e: e.activation(out=sg[:], in_=PS[6][:, :CBM], func=AF.Sigmoid, bias=GB[:, fo:fo + 1]), r=[('ps', 6), 'GB'], w=[('oi', 0)])
                    c.op('dve', lambda e: e.tensor_tensor(out=yc[:], in0=sg[:], in1=y_acc[:, fo, tsl], op=ALU.mult), r=[('oi', 0), ('z', fo, tt)], w=['yc'])
                    c.dma('sp', Yv[:, fo, tsl], yc[:], r=['yc'], w=[('Y', fo, tt)])
            c.barrier()

    def mix_odd_rwkv(self):
        c, nc, I, O = self.c, self.nc, self.I, self.O
        NT, NP = self.NT, self.NP
        Pv = self.P.rearrange("(m p) t -> p m t", p=128)
        Yv = self.Y.rearrange("(c p) t -> p c t", p=128)
        YFv = self.YF.rearrange("(c p) t -> p c t", p=128)
        BFv = self.BFs.rearrange("(c p) t -> p c t", p=128)
        PS = self.PS
        TBM = 256
        RW = F32
        EM05 = math.exp(-0.5)
        with contextlib.ExitStack() as ph:
            sb = lambda n, sh, dt: self.sb(ph, 'rw_' + n, sh, dt)
            misc = sb('misc', [128, 1024], F32)
            CMF = sb('cmf', [128, 512], F32)
            BLK = sb('blk', [128, 128], F32)
            MU = sb('mu', [128, 6, 4], F32)
            MUH = sb('muh', [128, 6, 4], F32)
            OMU = sb('omu', [128, 6, 4], F32)
            W0 = sb('w0', [128, 2, 4], F32)
            A0 = sb('a0', [128, 2, 4], F32)
            KKc = sb('kkc', [128, 4], F32)
            KAc = sb('kac', [128, 4], F32)
            OMKA = sb('omka', [128, 4], F32)
            RKc = sb('rkc', [128, 4], F32)
            LNW = sb('lnw', [128, 4], F32)
            LNB = sb('lnb', [128, 4], F32)
            W1 = sb('w1', [128, 2, 4, 64], BF16)
            A1 = sb('a1', [128, 2, 4, 64], BF16)
            W2 = sb('w2', [64, 2, 512], BF16)
            A2 = sb('a2', [64, 2, 512], BF16)
            G1 = sb('g1', [128, 4, 128], BF16)
            G2 = sb('g2', [128, 512], BF16)
            xdp = sb('xdp', [128, TBM + 2], F32)
            zp = {n: sb('zp_' + n, [128, TBM + 2], F32) for n in 'rkv'}
            cs = sb('cs', [128, TBM], F32)
            xw = sb('xw', [128, 4, TBM], BF16)
            xa = sb('xa', [128, 4, TBM], BF16)
            xg = sb('xg', [128, 4, TBM], BF16)
            hw = sb('hw', [64, TBM], BF16)
            ha = sb('ha', [64, TBM], BF16)
            hg = sb('hg', [128, TBM], BF16)
            names = ['rp', 'kp', 'vp', 'LW', 'a', 'kk', 'kd', 'akk', 'G', 'EG', 'EGN', 'EGm', 'AT', 'KAT', 'KT', 'RT', 'y', 't1', 't2', 'gg', 'bon']
            A_ = {n: sb('A_' + n, [128, TBM], F32) for n in names}
            yb = sb('yb', [128, TBM], BF16)
            NCN = 4
            Qb = [[sb('Q%d%d' % (n, g), [128, 128], RW) for g in range(2)] for n in range(NCN)]
            QTb = [[sb('QT%d%d' % (n, g), [128, 128], RW) for g in range(2)] for n in range(NCN)]
            Zb = [sb('Z%d' % n, [128, 128], RW) for n in range(NCN)]
            ZTb = [sb('ZT%d' % n, [128, 128], RW) for n in range(NCN)]
            Eu = [[sb('Eu%d%d' % (n, lv), [128, 128], RW) for lv in range(3)] for n in range(NCN)]
            El = [[sb('El%d%d' % (n, lv), [128, 128], RW) for lv in range(3)] for n in range(NCN)]
            Fb = [sb('F%d' % n, [128, 128], RW) for n in range(NCN)]
            Fpb = [sb('Fp%d' % n, [128, 128], RW) for n in range(NCN)]
            RWM = sb('rwm', [128, 8, 128], F32)
            A2T = [sb('A2T%d' % n, [128, 128], RW) for n in range(NCN)]
            B1T = [[sb('B1T%d_%d' % (pp, n), [128, 128], RW) for n in range(NCN)] for pp in range(2)]
            B2T = [[sb('B2T%d_%d' % (pp, n), [128, 128], RW) for n in range(NCN)] for pp in range(2)]
            Vx = [[sb('Vx%d_%d' % (pp, n), [128, 128], RW) for n in range(NCN)] for pp in range(2)]
            KAx = [sb('KAx%d' % n, [128, 128], RW) for n in range(NCN)]
            Ax = [[sb('Ax%d_%d' % (pp, n), [128, 128], RW) for n in range(NCN)] for pp in range(2)]
            Kx = [[sb('Kx%d_%d' % (pp, n), [128, 128], RW) for n in range(NCN)] for pp in range(2)]
            Ux = [sb('Ux%d' % hh, [128, 128], RW) for hh in range(2)]
            Xs = [sb('Xs%d' % n, [128, 64], RW) for n in range(NCN)]
            Uvs = [[sb('Uvs%d_%d' % (pp, n), [128, 64], RW) for n in range(NCN)] for pp in range(2)]
            WTs = [[sb('WTs%d_%d' % (pp, n), [128, 128], RW) for n in range(NCN)] for pp in range(2)]
            Ptmp = [sb('Ptmp%d' % hh, [128, 64], RW) for hh in range(2)]
            P2 = [sb('P2_%d' % U, [128, 128], RW) for U in range(4)]
            Sld = sb('Sld', [128, 128], F32)
            Sout = sb('Sout', [128, 128], F32)

            c.dma('sp', misc[:], I['c_misc'][:, :], w=['misc'])
            c.dma('sp', CMF[:], I['c_cmf'][:, :], w=['CMF'])
            c.dma('sp', BLK[:], I['c_blk'][:, :], w=['BLK'])
            c.dma('sp', RWM[:], I['c_rwm'].rearrange("p (q n) -> p q n", q=8), w=['RWM'])
            for i6 in range(6):
                c.dma('sp', MU[:, i6, :], I['l1_rw_mu'][i6].rearrange("(u p) -> p u", p=128), w=['MU'], allow_slow_non_contiguous=True)
            for d in range(2):
                c.dma('sp', W0[:, d, :], I['l1_rw_w0'][d].rearrange("(u p) -> p u", p=128), w=['W0'], allow_slow_non_contiguous=True)
                c.dma('sp', A0[:, d, :], I['l1_rw_a0'][d].rearrange("(u p) -> p u", p=128), w=['A0'], allow_slow_non_contiguous=True)
                c.dma('pool', W1[:, d, :, :], I['l1_rw_w1'][d].rearrange("(u p) r -> p u r", p=128), w=['W1'])
                c.dma('pool', A1[:, d, :, :], I['l1_rw_a1'][d].rearrange("(u p) r -> p u r", p=128), w=['A1'])
                c.dma('pool', W2[:, d, :], I['l1_rw_w2'][d], w=['W2'])
                c.dma('pool', A2[:, d, :], I['l1_rw_a2'][d], w=['A2'])
            c.dma('pool', G1[:], I['l1_rw_g1'].rearrange("(u p) r -> p u r", p=128), w=['G1'])
            c.dma('pool', G2[:], I['l1_rw_g2'][:, :], w=['G2'])
            for (t_, nm) in ((KKc, 'l1_rw_kk'), (KAc, 'l1_rw_ka'), (RKc, 'l1_rw_rk'), (LNW, 'l1_ln_w'), (LNB, 'l1_ln_b')):
                c.dma('sp', t_[:], I[nm].rearrange("(u p) -> p u", p=128), w=[nm], allow_slow_non_contiguous=True)
            c.op('dve', lambda e: e.tensor_scalar(out=MUH[:], in0=MU[:], scalar1=0.5, scalar2=None, op0=ALU.mult), r=['MU'], w=['MUH'])
            c.op('dve', lambda e: e.tensor_scalar(out=OMU[:], in0=MU[:], scalar1=-1.0, scalar2=1.0, op0=ALU.mult, op1=ALU.add), r=['MU'], w=['OMU'])
            c.op('dve', lambda e: e.tensor_scalar(out=OMKA[:], in0=KAc[:], scalar1=-1.0, scalar2=1.0, op0=ALU.mult, op1=ALU.add), r=['l1_rw_ka'], w=['OMKA'])
            for n in range(NCN):
                c.op('dve', lambda e: e.memset(KAx[n][:], 0.0), w=[('KAx', n)])
                for pp in range(2):
                    for t_, k_ in ((Vx, 'Vx'), (Ax, 'Ax'), (Kx, 'Kx')):
                        c.op('dve', lambda e: e.memset(t_[pp][n][:], 0.0), w=[(k_, pp, n)])
            for hh in range(2):
                c.op('dve', lambda e: e.memset(Ux[hh][:], 0.0), w=[('Ux', hh)])
            PK = ['MU', 'MUH', 'OMU', 'W0', 'A0', 'l1_rw_kk', 'l1_rw_ka', 'l1_rw_rk', 'l1_ln_w', 'l1_ln_b', 'OMKA', 'misc', 'CMF', 'BLK']
            MSf, MSb = misc[:, 384:512], misc[:, 512:640]
            MIf, MIb = misc[:, 128:256], misc[:, 256:384]

            DBN = ['AT', 'KAT', 'KT', 'RT', 'vp', 'EG', 'y', 'bon']
            A2_ = dict(A_)
            for n in DBN:
                A2_[n] = sb('B_' + n, [128, TBM], F32)
            A_sets = [A_, A2_]
            TBN = ['RT', 'EG', 'y', 'bon']
            A3_ = [dict((n, (A_[n] if t3 == 0 else (A2_[n] if t3 == 1 else sb('C_' + n, [128, TBM], F32)))) for n in TBN) for t3 in range(3)]
            o1 = sb('o1', [128, TBM], F32)
            o2 = sb('o2', [128, TBM], F32)
            hgs = [hg, sb('hg2', [128, TBM], BF16)]

            def mk_item(d, sq_, bi_, U, kidx):
                off, L = sq_['off'], sq_['L']
                TB = min(TBM, L)
                nb = L // TB
                blk = bi_ if d == 0 else nb - 1 - bi_
                return dict(d=d, sq=sq_, bi=bi_, U=U, TB=TB, nb=nb, ncc=TB // 128, b0=off + blk * TB, par=kidx % 2, bpar=(kidx // 4) % 2, p3=kidx % 3)

            def prep_gen(it):
                d, U, TB, b0, par, bpar = it['d'], it['U'], it['TB'], it['b0'], it['par'], it['bpar']
                off, L = it['sq']['off'], it['sq']['L']
                p3 = it['p3']
                A_ = dict(A_sets[par])
                A_.update(A3_[p3])
                K = lambda n: (n, 't', p3) if n in TBN else ((n, par) if n in DBN else n)
                lo_t = max(b0 - 1, off)
                hi_t = min(b0 + TB + 1, off + L)
                dlo = lo_t - (b0 - 1)
                dhi = dlo + (hi_t - lo_t)

                def load_pad(buf, key, m):
                    if dlo > 0:
                        c.op('dve', lambda e: e.memset(buf[:, 0:1], 0.0), w=[key])
                    if dhi < TB + 2:
                        c.op('dve', lambda e: e.memset(buf[:, TB + 1:TB + 2], 0.0), w=[key])
                    c.dma('sp', buf[:, dlo:dhi], Pv[:, m, lo_t:hi_t], w=[key])
                if U == 0:
                    Ucur = U
                    for U in range(4):
                        load_pad(xdp, 'xdp', 16 + U)
                        yield
                        c.op('dve', lambda e: e.tensor_tensor(out=cs[:, :TB], in0=xdp[:, 0:TB], in1=xdp[:, 2:TB + 2], op=ALU.add), r=['xdp'], w=['cs'])
                        yield
                        for (i6, dst, dk) in ((3, xw, 'xw'), (4, xa, 'xa'), (5, xg, 'xg')):
                            c.op('dve', lambda e: e.tensor_scalar(out=A_['t1'][:, :TB], in0=cs[:, :TB], scalar1=MUH[:, i6, U:U + 1], scalar2=None, op0=ALU.mult), r=['cs', 'MUH'], w=[K('t1')])
                            yield
                            c.op('dve', lambda e: e.scalar_tensor_tensor(out=dst[:, U, :TB], in0=xdp[:, 1:TB + 1], scalar=OMU[:, i6, U:U + 1], in1=A_['t1'][:, :TB], op0=ALU.mult, op1=ALU.add), r=['xdp', 'OMU', K('t1')], w=[dk])
                            yield
                    for U in range(4):
                        c.op('pe', lambda e: e.matmul(PS[0][:64, :TB], lhsT=W1[:, d, U, :], rhs=xw[:, U, :TB], start=(U == 0), stop=(U == 3)), r=['W1', 'xw'], w=[('ps', 0)])
                        yield
                    c.op('act', lambda e: e.activation(out=hw[:, :TB], in_=PS[0][:64, :TB], func=AF.Tanh), r=[('ps', 0)], w=['hw'])
                    yield
                    for U in range(4):
                        c.op('pe', lambda e: e.matmul(PS[0][:64, :TB], lhsT=A1[:, d, U, :], rhs=xa[:, U, :TB], start=(U == 0), stop=(U == 3)), r=['A1', 'xa'], w=[('ps', 0)])
                        yield
                    c.op('act', lambda e: e.activation(out=ha[:, :TB], in_=PS[0][:64, :TB], func=AF.Identity), r=[('ps', 0)], w=['ha'])
                    yield
                    if d == 1:
                        for U in range(4):
                            c.op('pe', lambda e: e.matmul(PS[0][:, :TB], lhsT=G1[:, U, :], rhs=xg[:, U, :TB], start=(U == 0), stop=(U == 3)), r=['G1', 'xg'], w=[('ps', 0)])
                            yield
                        c.op('act', lambda e: e.activation(out=hgs[bpar][:, :TB], in_=PS[0][:, :TB], func=AF.Sigmoid), r=[('ps', 0)], w=[('hg', bpar)])
                        yield
                    U = Ucur
                if True:
                    for (i6, n) in ((0, 'r'), (1, 'k'), (2, 'v')):
                        load_pad(zp[n], 'zp_' + n, 4 + 4 * i6 + U)
                        yield
                        c.op('dve', lambda e: e.tensor_tensor(out=cs[:, :TB], in0=zp[n][:, 0:TB], in1=zp[n][:, 2:TB + 2], op=ALU.add), r=['zp_' + n], w=['cs'])
                        yield
                        c.op('dve', lambda e: e.tensor_scalar(out=A_['t1'][:, :TB], in0=cs[:, :TB], scalar1=MUH[:, i6, U:U + 1], scalar2=None, op0=ALU.mult), r=['cs', 'MUH'], w=[K('t1')])
                        yield
                        c.op('dve', lambda e: e.scalar_tensor_tensor(out=A_[n + 'p'][:, :TB], in0=zp[n][:, 1:TB + 1], scalar=OMU[:, i6, U:U + 1], in1=A_['t1'][:, :TB], op0=ALU.mult, op1=ALU.add), r=['zp_' + n, 'OMU', K('t1')], w=[K(n + 'p')])
                        yield
                    rp, kp, vp = A_['rp'], A_['kp'], A_['vp']
                    c.op('pe', lambda e: e.matmul(PS[0][:, :TB], lhsT=W2[:, d, U * 128:(U + 1) * 128], rhs=hw[:, :TB], start=True, stop=True), r=['W2', 'hw'], w=[('ps', 0)])
                    yield
                    c.op('act', lambda e: e.activation(out=A_['LW'][:, :TB], in_=PS[0][:, :TB], func=AF.Sigmoid, bias=W0[:, d, U:U + 1]), r=[('ps', 0), 'W0'], w=[K('LW')])
                    yield
                    c.op('dve', lambda e: e.tensor_scalar(out=A_['LW'][:, :TB], in0=A_['LW'][:, :TB], scalar1=-EM05, scalar2=None, op0=ALU.mult), r=[K('LW')], w=[K('LW')])
                    yield
                    c.op('pe', lambda e: e.matmul(PS[0][:, :TB], lhsT=A2[:, d, U * 128:(U + 1) * 128], rhs=ha[:, :TB], start=True, stop=True), r=['A2', 'ha'], w=[('ps', 0)])
                    yield
                    c.op('act', lambda e: e.activation(out=A_['a'][:, :TB], in_=PS[0][:, :TB], func=AF.Sigmoid, bias=A0[:, d, U:U + 1]), r=[('ps', 0), 'A0'], w=[K('a')])
                    yield
                    c.op('dve', lambda e: e.tensor_scalar(out=A_['kk'][:, :TB], in0=kp[:, :TB], scalar1=KKc[:, U:U + 1], scalar2=None, op0=ALU.mult), r=[K('kp'), 'l1_rw_kk'], w=[K('kk')])
                    yield
                    c.op('act', lambda e: e.activation(out=A_['t1'][:, :TB], in_=A_['kk'][:, :TB], func=AF.Square), r=[K('kk')], w=[K('t1')])
                    yield
                    c.op('pe', lambda e: e.matmul(PS[0][:, :TB], lhsT=BLK[:], rhs=A_['t1'][:, :TB], start=True, stop=True), r=['BLK', K('t1')], w=[('ps', 0)])
                    yield
                    c.op('act', lambda e: e.activation(out=A_['t2'][:, :TB], in_=PS[0][:, :TB], func=AF.Sqrt), r=[('ps', 0)], w=[K('t2')])
                    yield
                    c.op('dve', lambda e: e.tensor_scalar(out=A_['t2'][:, :TB], in0=A_['t2'][:, :TB], scalar1=1e-12, scalar2=None, op0=ALU.max), r=[K('t2')], w=[K('t2')])
                    yield
                    c.op('dve', lambda e: e.reciprocal(out=A_['t2'][:, :TB], in_=A_['t2'][:, :TB]), r=[K('t2')], w=[K('t2')])
                    yield
                    c.op('dve', lambda e: e.tensor_tensor(out=A_['kk'][:, :TB], in0=A_['kk'][:, :TB], in1=A_['t2'][:, :TB], op=ALU.mult), r=[K('kk'), K('t2')], w=[K('kk')])
                    yield
                    c.op('dve', lambda e: e.tensor_scalar(out=A_['t1'][:, :TB], in0=A_['a'][:, :TB], scalar1=KAc[:, U:U + 1], scalar2=OMKA[:, U:U + 1], op0=ALU.mult, op1=ALU.add), r=[K('a'), 'l1_rw_ka', 'OMKA'], w=[K('t1')])
                    yield
                    c.op('dve', lambda e: e.tensor_tensor(out=A_['kd'][:, :TB], in0=kp[:, :TB], in1=A_['t1'][:, :TB], op=ALU.mult), r=[K('kp'), K('t1')], w=[K('kd')])
                    yield
                    c.op('dve', lambda e: e.tensor_tensor(out=A_['akk'][:, :TB], in0=A_['a'][:, :TB], in1=A_['kk'][:, :TB], op=ALU.mult), r=[K('a'), K('kk')], w=[K('akk')])
                    yield
                    c.op('dve', lambda e: e.scalar_tensor_tensor(out=A_['t1'][:, :TB], in0=rp[:, :TB], scalar=RKc[:, U:U + 1], in1=A_['kd'][:, :TB], op0=ALU.mult, op1=ALU.mult), r=[K('rp'), 'l1_rw_rk', K('kd')], w=[K('t1')])
                    yield
                    c.op('pe', lambda e: e.matmul(PS[0][:, :TB], lhsT=BLK[:], rhs=A_['t1'][:, :TB], start=True, stop=True), r=['BLK', K('t1')], w=[('ps', 0)])
                    yield
                    c.op('dve', lambda e: e.tensor_tensor(out=A_['bon'][:, :TB], in0=PS[0][:, :TB], in1=vp[:, :TB], op=ALU.mult), r=[('ps', 0), K('vp')], w=[K('bon')])
                    yield
                    Dv = (lambda x: x[:, 0:TB]) if d == 0 else (lambda x: x[:, 0:TB][:, ::-1])
                    c.op('dve', lambda e: e.tensor_tensor_scan(out=Dv(A_['G']), data0=CMF[:, :TB], data1=Dv(A_['LW']), initial=0.0, op0=ALU.mult, op1=ALU.add), r=['CMF', K('LW')], w=[K('G')])
                    yield
                    c.op('act', lambda e: e.activation(out=A_['EG'][:, :TB], in_=A_['G'][:, :TB], func=AF.Exp), r=[K('G')], w=[K('EG')])
                    yield
                    c.op('act', lambda e: e.activation(out=A_['EGN'][:, :TB], in_=A_['G'][:, :TB], func=AF.Exp, scale=-1.0), r=[K('G')], w=[K('EGN')])
                    yield
                    c.op('dve', lambda e: e.tensor_tensor(out=A_['t1'][:, :TB], in0=A_['G'][:, :TB], in1=A_['LW'][:, :TB], op=ALU.subtract), r=[K('G'), K('LW')], w=[K('t1')])
                    yield
                    c.op('act', lambda e: e.activation(out=A_['EGm'][:, :TB], in_=A_['t1'][:, :TB], func=AF.Exp), r=[K('t1')], w=[K('EGm')])
                    yield
                    c.op('dve', lambda e: e.tensor_tensor(out=A_['AT'][:, :TB], in0=A_['akk'][:, :TB], in1=A_['EGN'][:, :TB], op=ALU.mult), r=[K('akk'), K('EGN')], w=[K('AT')])
                    yield
                    c.op('dve', lambda e: e.tensor_tensor(out=A_['KAT'][:, :TB], in0=A_['kk'][:, :TB], in1=A_['EGm'][:, :TB], op=ALU.mult), r=[K('kk'), K('EGm')], w=[K('KAT')])
                    yield
                    c.op('dve', lambda e: e.tensor_tensor(out=A_['KT'][:, :TB], in0=A_['kd'][:, :TB], in1=A_['EGN'][:, :TB], op=ALU.mult), r=[K('kd'), K('EGN')], w=[K('KT')])
                    yield
                    c.op('dve', lambda e: e.tensor_tensor(out=A_['RT'][:, :TB], in0=rp[:, :TB], in1=A_['EG'][:, :TB], op=ALU.mult), r=[K('rp'), K('EG')], w=[K('RT')])
                    yield
                    AT, KAT, KT, RT = A_['AT'], A_['KAT'], A_['KT'], A_['RT']

            def run_item(it, nxt, pend):
                d, U, TB, b0, par, bpar, ncc = it['d'], it['U'], it['TB'], it['b0'], it['par'], it['bpar'], it['ncc']
                sq_ = it['sq']
                off, L, latent, pidx = sq_['off'], sq_['L'], sq_['latent'], sq_['pidx']
                p3 = it['p3']
                A_ = dict(A_sets[par])
                A_.update(A3_[p3])
                K = lambda n: (n, 't', p3) if n in TBN else ((n, par) if n in DBN else n)
                mS = MSf if d == 0 else MSb
                mI = MIf if d == 0 else MIb
                mo_S = 0 if d == 0 else 4
                mo_T = 4 if d == 0 else 0
                AT, KAT, KT, RT = A_['AT'], A_['KAT'], A_['KT'], A_['RT']
                if it['bi'] == 0 and U == 0:
                    for Ui in range(4):
                        if latent:
                            c.op('dve', lambda e: e.memset(Sld[:], 0.0), w=['Sld'])
                            for hh in range(2):
                                lo = 64 * hh
                                c.dma('sp', Sld[lo:lo + 64, lo:lo + 64], I['st_rwkv'][d, 2 * Ui + hh], w=['Sld'])
                            c.op('pe', lambda e: e.transpose(out=PS[0][:, 0:128], in_=Sld[:], identity=self.ident[:]), r=['Sld', 'ident'], w=[('ps', 0)])
                            c.op('act', lambda e: e.activation(out=P2[Ui][:], in_=PS[0][:, 0:128], func=AF.Identity), r=[('ps', 0)], w=[('P2', Ui)])
                        else:
                            c.op('dve', lambda e: e.memset(P2[Ui][:], 0.0), w=[('P2', Ui)])
                if True:
                    def X(arr, hh, csl):
                        return arr[64 * hh:64 * hh + 64, csl]

                    def chain(n, cc, hh):
                        csl = slice(cc * 128, (cc + 1) * 128)
                        lo = 64 * hh
                        bnk = 1 + n % 7
                        pk = ('ps', bnk)
                        c.op('pe', lambda e: e.matmul(PS[bnk][:, 0:128], lhsT=X(AT, hh, csl), rhs=X(KAT, hh, csl), start=True, stop=True), r=[K('AT'), K('KAT')], w=[pk])
                        c.op('dve', lambda e: e.scalar_tensor_tensor(out=Qb[n][0][:], in0=PS[bnk][:, 0:128], scalar=-1.0, in1=RWM[:, mo_S, :], op0=ALU.mult, op1=ALU.mult), r=[pk, 'RWM'], w=[('Q', n, 0)])
                        yield
                        c.op('pe', lambda e: e.matmul(PS[bnk][:, 0:128], lhsT=X(KAT, hh, csl), rhs=X(AT, hh, csl), start=True, stop=True), r=[K('AT'), K('KAT')], w=[pk])
                        c.op('dve', lambda e: e.scalar_tensor_tensor(out=QTb[n][0][:], in0=PS[bnk][:, 0:128], scalar=-1.0, in1=RWM[:, mo_T, :], op0=ALU.mult, op1=ALU.mult), r=[pk, 'RWM'], w=[('QT', n, 0)])
                        for lv in range(3):
                            c.op('dve', lambda e: e.tensor_tensor(out=El[n][lv][:], in0=PS[bnk][:, 0:128], in1=RWM[:, mo_T + 1 + lv, :], op=ALU.mult), r=[pk, 'RWM'], w=[('El', n, lv)])
                        yield
                        for (nm, la, ra, msk, dst) in (('A2T', KT, KAT, mS, A2T[n]), ('B1T', AT, RT, mI, B1T[par][n]), ('B2T', KT, RT, mI, B2T[par][n])):
                            c.op('pe', lambda e: e.matmul(PS[bnk][:, 0:128], lhsT=X(la, hh, csl), rhs=X(ra, hh, csl), start=True, stop=True), r=[K('AT'), K('KAT'), K('KT'), K('RT')], w=[pk])
                            c.op('dve', lambda e: e.tensor_tensor(out=dst[:], in0=PS[bnk][:, 0:128], in1=msk, op=ALU.mult), r=[pk, 'misc'], w=[((nm, par, n) if nm != 'A2T' else (nm, n))])
                            yield
                        c.op('dve', lambda e: e.tensor_tensor(out=Zb[n][:], in0=Qb[n][0][:], in1=self.ident[:], op=ALU.add), r=[('Q', n, 0), 'ident'], w=[('Z', n)])
                        for (src, skey, dstl, dk) in ((A_['vp'], 'vp', Vx[par], ('Vx', par)), (KAT, 'KAT', KAx, ('KAx',)), (AT, 'AT', Ax[par], ('Ax', par)), (KT, 'KT', Kx[par], ('Kx', par))):
                            c.op('pe', lambda e: e.transpose(out=PS[bnk][:, 0:64], in_=src[lo:lo + 64, csl], identity=self.ident[lo:lo + 64, lo:lo + 64]), r=[K(skey), 'ident'], w=[pk])
                            c.op('act', lambda e: e.activation(out=dstl[n][:, lo:lo + 64], in_=PS[bnk][:, 0:64], func=AF.Identity), r=[pk], w=[dk + (n,)])
                            yield
                        for i in range(1, 4):
                            g0, g1 = (i - 1) % 2, i % 2
                            if i < 3:
                                c.op('pe', lambda e: e.matmul(PS[bnk][:, 0:128], lhsT=QTb[n][g0][:], rhs=Qb[n][g0][:], start=True, stop=True), r=[('QT', n, g0), ('Q', n, g0)], w=[pk])
                                c.op('act', lambda e: e.activation(out=Qb[n][g1][:], in_=PS[bnk][:, 0:128], func=AF.Identity), r=[pk], w=[('Q', n, g1)])
                                yield
                            c.op('pe', lambda e: e.matmul(PS[bnk][:, 0:128], lhsT=Qb[n][g0][:], rhs=QTb[n][g0][:], start=True, stop=True), r=[('QT', n, g0), ('Q', n, g0)], w=[pk])
                            c.op('act', lambda e: e.activation(out=QTb[n][g1][:], in_=PS[bnk][:, 0:128], func=AF.Identity), r=[pk], w=[('QT', n, g1)])
                            yield
                            c.op('pe', lambda e: e.matmul(PS[bnk][:, 0:128], lhsT=QTb[n][g1][:], rhs=Zb[n][:], start=True, stop=True), r=[('QT', n, g1), ('Z', n)], w=[pk])
                            c.op('dve', lambda e: e.tensor_tensor(out=Zb[n][:], in0=Zb[n][:], in1=PS[bnk][:, 0:128], op=ALU.add), r=[pk, ('Z', n)], w=[('Z', n)])
                            yield
                        for lv in range(3):
                            c.op('pe', lambda e: e.transpose(out=PS[bnk][:, 0:128], in_=Zb[n][:], identity=self.ident[:]), r=[('Z', n), 'ident'], w=[pk])
                            c.op('act', lambda e: e.activation(out=ZTb[n][:], in_=PS[bnk][:, 0:128], func=AF.Identity), r=[pk], w=[('ZT', n)])
                            yield
                            c.op('pe', lambda e: e.matmul(PS[bnk][:, 0:128], lhsT=El[n][lv][:], rhs=Zb[n][:], start=True, stop=True), r=[('El', n, lv), ('Z', n)], w=[pk])
                            c.op('act', lambda e: e.activation(out=Fb[n][:], in_=PS[bnk][:, 0:128], func=AF.Identity), r=[pk], w=[('F', n)])
                            yield
                            c.op('pe', lambda e: e.matmul(PS[bnk][:, 0:128], lhsT=ZTb[n][:], rhs=Fb[n][:], start=True, stop=True), r=[('ZT', n), ('F', n)], w=[pk])
                            c.op('dve', lambda e: e.tensor_tensor(out=Zb[n][:], in0=Zb[n][:], in1=PS[bnk][:, 0:128], op=ALU.subtract), r=[pk, ('Z', n)], w=[('Z', n)])
                            yield
                        c.op('pe', lambda e: e.matmul(PS[bnk][:, 0:64], lhsT=A2T[n][:], rhs=Vx[par][n][:, lo:lo + 64], start=True, stop=True), r=[('A2T', n), ('Vx', par, n)], w=[pk])
                        c.op('act', lambda e: e.activation(out=Xs[n][:], in_=PS[bnk][:, 0:64], func=AF.Identity), r=[pk], w=[('Xs', n)])
                        yield
                        c.op('pe', lambda e: e.matmul(PS[bnk][:, 0:64], lhsT=Zb[n][:], rhs=Xs[n][:], start=True, stop=True), r=[('Z', n), ('Xs', n)], w=[pk])
                        c.op('act', lambda e: e.activation(out=Uvs[par][n][:], in_=PS[bnk][:, 0:64], func=AF.Identity), r=[pk], w=[('Uvs', par, n)])
                        yield
                        c.op('pe', lambda e: e.matmul(PS[bnk][:, 0:128], lhsT=KAx[n][:], rhs=Zb[n][:], start=True, stop=True), r=[('KAx', n), ('Z', n)], w=[pk])
                        c.op('dve', lambda e: e.tensor_copy(out=WTs[par][n][lo:lo + 64, :], in_=PS[bnk][lo:lo + 64, 0:128]), r=[pk], w=[('WTs', par, n)])
                        yield
                chains = []
                for ci in range(ncc):
                    cc = ci if d == 0 else ncc - 1 - ci
                    for hh in range(2):
                        chains.append(chain(ci * 2 + hh, cc, hh))
                if nxt is not None:
                    chains.append(prep_gen(nxt))
                if pend is not None:
                    chains.append(pend)
                while chains:
                    for g_ in list(chains):
                        try:
                            next(g_)
                        except StopIteration:
                            chains.remove(g_)
            def so_gen(it):
                d, U, TB, b0, par, bpar, ncc = it['d'], it['U'], it['TB'], it['b0'], it['par'], it['bpar'], it['ncc']
                sq_ = it['sq']
                off, L, latent, pidx = sq_['off'], sq_['L'], sq_['latent'], sq_['pidx']
                p3 = it['p3']
                A_ = dict(A_sets[par])
                A_.update(A3_[p3])
                K = lambda n: (n, 't', p3) if n in TBN else ((n, par) if n in DBN else n)
                mS = MSf if d == 0 else MSb
                mI = MIf if d == 0 else MIb
                mo_S = 0 if d == 0 else 4
                mo_T = 4 if d == 0 else 0
                AT, KAT, KT, RT = A_['AT'], A_['KAT'], A_['KT'], A_['RT']
                if True:
                    for ci in range(ncc):
                        cc = ci if d == 0 else ncc - 1 - ci
                        csl = slice(cc * 128, (cc + 1) * 128)
                        glast = (cc * 128 + 127) if d == 0 else cc * 128
                        for hh in range(2):
                            n = ci * 2 + hh
                            lo = 64 * hh
                            bnk = 5 + hh
                            c.op('pe', lambda e: e.matmul(PS[bnk][:, 0:64], lhsT=WTs[par][n][lo:lo + 64, :], rhs=P2[U][lo:lo + 64, lo:lo + 64], start=True, stop=True), r=[('WTs', par, n), ('P2', U)], w=[('ps', bnk)])
                            yield
                            c.op('dve', lambda e: e.scalar_tensor_tensor(out=Ux[hh][:, lo:lo + 64], in0=PS[bnk][:, 0:64], scalar=-1.0, in1=Uvs[par][n][:], op0=ALU.mult, op1=ALU.subtract), r=[('ps', bnk), ('Uvs', par, n)], w=[('Ux', hh)])
                            yield
                        for hh in range(2):
                            n = ci * 2 + hh
                            lo = 64 * hh
                            c.op('pe', lambda e: e.matmul(PS[7][:, 0:128], lhsT=P2[U][lo:lo + 64, :], rhs=RT[lo:lo + 64, csl], start=(hh == 0), stop=False), r=[('P2', U), K('RT')], w=[('ps', 7)])
                            yield
                            c.op('pe', lambda e: e.matmul(PS[7][:, 0:128], lhsT=Ux[hh][:], rhs=B1T[par][n][:], start=False, stop=False), r=[('Ux', hh), ('B1T', par, n)], w=[('ps', 7)])
                            yield
                            c.op('pe', lambda e: e.matmul(PS[7][:, 0:128], lhsT=Vx[par][n][:], rhs=B2T[par][n][:], start=False, stop=(hh == 1)), r=[('Vx', par, n), ('B2T', par, n)], w=[('ps', 7)])
                            yield
                        c.op('act', lambda e: e.activation(out=A_['y'][:, csl], in_=PS[7][:, 0:128], func=AF.Identity), r=[('ps', 7)], w=[K('y')])
                        yield
                        for hh in range(2):
                            n = ci * 2 + hh
                            lo = 64 * hh
                            bnk = 5 + hh
                            c.op('pe', lambda e: e.matmul(PS[bnk][:, 0:64], lhsT=Ax[par][n][:], rhs=Ux[hh][:, lo:lo + 64], start=True, stop=False), r=[('Ax', par, n), ('Ux', hh)], w=[('ps', bnk)])
                            yield
                            c.op('pe', lambda e: e.matmul(PS[bnk][:, 0:64], lhsT=Kx[par][n][:], rhs=Vx[par][n][:, lo:lo + 64], start=False, stop=True), r=[('Kx', par, n), ('Vx', par, n)], w=[('ps', bnk)])
                            yield
                            c.op('dve', lambda e: e.tensor_tensor(out=Ptmp[hh][lo:lo + 64, :], in0=P2[U][lo:lo + 64, lo:lo + 64], in1=PS[bnk][lo:lo + 64, 0:64], op=ALU.add), r=[('ps', bnk), ('P2', U)], w=[('Ptmp', hh)])
                            yield
                            c.op('act', lambda e: e.activation(out=P2[U][lo:lo + 64, lo:lo + 64], in_=Ptmp[hh][lo:lo + 64, :], func=AF.Identity, scale=A_['EG'][lo:lo + 64, glast:glast + 1]), r=[('Ptmp', hh), K('EG')], w=[('P2', U)])
                            yield
                if True:
                    g0_ = b0
                    if d == 0:
                        c.dma('sp', YFv[:, U, g0_:g0_ + TB], A_['y'][:, :TB], r=[K('y')], w=[('YF', U, g0_)])
                        yield
                        c.dma('sp', BFv[:, U, g0_:g0_ + TB], A_['bon'][:, :TB], r=[K('bon')], w=[('BF', U, g0_)])
                        yield
                    else:
                        c.dma('sp', o1[:, :TB], YFv[:, U, g0_:g0_ + TB], r=[('YF', U, g0_)], w=['o1'])
                        yield
                        c.dma('sp', o2[:, :TB], BFv[:, U, g0_:g0_ + TB], r=[('BF', U, g0_)], w=['o2'])
                        yield
                        c.op('dve', lambda e: e.tensor_tensor(out=A_['y'][:, :TB], in0=A_['y'][:, :TB], in1=o1[:, :TB], op=ALU.add), r=[K('y'), 'o1'], w=[K('y')])
                        yield
                        c.op('dve', lambda e: e.tensor_tensor(out=A_['bon'][:, :TB], in0=A_['bon'][:, :TB], in1=o2[:, :TB], op=ALU.add), r=[K('bon'), 'o2'], w=[K('bon')])
                        yield
                        c.op('pe', lambda e: e.matmul(PS[5][:, :TB], lhsT=BLK[:], rhs=A_['y'][:, :TB], start=True, stop=True), r=['BLK', K('y')], w=[('ps', 5)])
                        yield
                        c.op('dve', lambda e: e.scalar_tensor_tensor(out=o1[:, :TB], in0=PS[5][:, :TB], scalar=-1.0 / 64.0, in1=A_['y'][:, :TB], op0=ALU.mult, op1=ALU.add), r=[('ps', 5), K('y')], w=['o1'])
                        yield
                        c.op('act', lambda e: e.activation(out=o2[:, :TB], in_=o1[:, :TB], func=AF.Square), r=['o1'], w=['o2'])
                        yield
                        c.op('pe', lambda e: e.matmul(PS[5][:, :TB], lhsT=BLK[:], rhs=o2[:, :TB], start=True, stop=True), r=['BLK', 'o2'], w=[('ps', 5)])
                        yield
                        c.op('act', lambda e: e.activation(out=o2[:, :TB], in_=PS[5][:, :TB], func=AF.Sqrt, scale=1.0 / 64.0, bias=self.cst[:, 2:3]), r=[('ps', 5), ('cst', 2)], w=['o2'])
                        yield
                        c.op('dve', lambda e: e.reciprocal(out=o2[:, :TB], in_=o2[:, :TB]), r=['o2'], w=['o2'])
                        yield
                        c.op('dve', lambda e: e.tensor_tensor(out=o1[:, :TB], in0=o1[:, :TB], in1=o2[:, :TB], op=ALU.mult), r=['o1', 'o2'], w=['o1'])
                        yield
                        c.op('dve', lambda e: e.tensor_scalar(out=o1[:, :TB], in0=o1[:, :TB], scalar1=LNW[:, U:U + 1], scalar2=LNB[:, U:U + 1], op0=ALU.mult, op1=ALU.add), r=['o1', 'l1_ln_w', 'l1_ln_b'], w=['o1'])
                        yield
                        c.op('dve', lambda e: e.tensor_tensor(out=o1[:, :TB], in0=o1[:, :TB], in1=A_['bon'][:, :TB], op=ALU.add), r=['o1', K('bon')], w=['o1'])
                        yield
                        c.op('pe', lambda e: e.matmul(PS[5][:, :TB], lhsT=G2[:, U * 128:(U + 1) * 128], rhs=hgs[bpar][:, :TB], start=True, stop=True), r=['G2', ('hg', bpar)], w=[('ps', 5)])
                        yield
                        c.op('dve', lambda e: e.tensor_tensor(out=yb[:, :TB], in0=o1[:, :TB], in1=PS[5][:, :TB], op=ALU.mult), r=['o1', ('ps', 5)], w=['yb'])
                        yield
                        c.dma('sp', Yv[:, 4 + U, g0_:g0_ + TB], yb[:, :TB], r=['yb'], w=[('Y', 4 + U, g0_)])
                        yield
                if it['bi'] == it['nb'] - 1 and U == 3:
                    if pidx is not None:
                        for Ui in range(4):
                            c.op('pe', lambda e: e.transpose(out=PS[5][:, 0:128], in_=P2[Ui][:], identity=self.ident[:]), r=[('P2', Ui), 'ident'], w=[('ps', 5)])
                            yield
                            c.op('act', lambda e: e.activation(out=Sout[:], in_=PS[5][:, 0:128], func=AF.Identity), r=[('ps', 5)], w=['Sout'])
                            yield
                            for hh in range(2):
                                lo = 64 * hh
                                c.dma('sp', O['new_rwkv'][pidx, d, 2 * Ui + hh], Sout[lo:lo + 64, lo:lo + 64], r=['Sout'], w=[('nrw', pidx, d, Ui, hh)])
                                yield

            for d in range(2):
                items = []
                for sq_ in self.seqs():
                    TB_ = min(TBM, sq_['L'])
                    for bi_ in range(sq_['L'] // TB_):
                        for U in range(4):
                            items.append(mk_item(d, sq_, bi_, U, len(items)))
                for _ in prep_gen(items[0]):
                    pass
                pend = None
                for k, it in enumerate(items):
                    if it['bi'] == 0 and it['U'] == 0 and pend is not None:
                        for _ in pend:
                            pass
                        pend = None
                    run_item(it, items[k + 1] if k + 1 < len(items) else None, pend)
                    pend = so_gen(it)
                for _ in pend:
                    pass
            c.barrier()


def host_consts(LS):
    ident = np.eye(128, dtype=np.float32)
    GRID_W = 64
    nf = 32
    t = np.arange(LS)
    row = (t // GRID_W).astype(np.float32)
    col = (t % GRID_W).astype(np.float32)
    freqs = (10000.0 ** (-np.arange(nf, dtype=np.float32) / nf)).astype(np.float32)
    cos = np.zeros((128, LS), np.float32)
    sin = np.zeros((128, LS), np.float32)
    rot = np.zeros((128, 128), np.float32)
    for a in range(2):
        pos = row if a == 0 else col
        ang = (pos[None, :] * freqs[:, None]).astype(np.float32)
        for b in range(2):
            p0 = a * 64 + b * 32
            cos[p0:p0 + 32] = np.cos(ang)
            sin[p0:p0 + 32] = np.sin(ang)
            for f in range(nf):
                m = p0 + f
                partner = a * 64 + (1 - b) * 32 + f
                rot[partner, m] = -1.0 if b == 0 else 1.0
    misc = np.zeros((128, 1024), np.float32)
    idx = np.arange(128)
    rel = idx[None, :] - idx[:, None]
    misc[:, 0:128] = np.abs(rel)
    misc[:, 128:256] = (rel >= 0)
    misc[:, 256:384] = (rel <= 0)
    misc[:, 384:512] = (rel > 0)
    misc[:, 512:640] = (rel < 0)
    misc[:, 640:768] = idx[None, :]
    misc[:, 768:896] = idx[None, :] + 1
    misc[:, 896] = idx
    misc[:, 897] = 127 - idx
    misc[:, 898] = 128.0
    iota = np.tile(np.arange(1, 513, dtype=np.float32)[None, :], (128, 1))
    s5m = np.zeros((128, 4, 128), np.float32)
    for p in range(128):
        for q in range(4):
            for col in range(128):
                if p // 16 == 2 * q + col // 64:
                    s5m[p, q, col] = 1.0
    cmf = np.ones((128, 512), np.float32)
    cmf[:, 0::128] = 0.0
    blk = np.zeros((128, 128), np.float32)
    blk[:64, :64] = 1.0
    blk[64:, 64:] = 1.0
    rwm = np.zeros((128, 8, 128), np.float32)
    pp = idx[:, None]
    ff = idx[None, :]
    up = pp < ff
    pats = [up & ((pp // 16) == (ff // 16))]
    for k in (16, 32, 64):
        pats.append(up & ((pp // (2 * k)) == (ff // (2 * k))) & ((pp // k) != (ff // k)))
    for i, pt in enumerate(pats):
        rwm[:, i, :] = pt
        rwm[:, 4 + i, :] = pt.T
    return dict(c_ident=ident, c_rope_cos=cos, c_rope_sin=sin, c_rope_rot=rot, c_misc=misc, c_iota=iota, c_s5mask=s5m.reshape(128, 512), c_cmf=cmf, c_blk=blk,
                c_rwm=rwm.reshape(128, 1024))


WEIGHT_KEYS = ['mod_w', 'mod_b', 'norm_g', 'ffn1_w13', 'ffn1_w2', 'ffn2_w13', 'ffn2_w2', 'final_norm',
               'l0_w_in', 'l0_w_out', 'l0_ret_decay', 'l0_conv_w', 'l0_conv_b', 'l0_lru_lam', 'l0_lru_wa', 'l0_lru_ba',
               'l0_lru_wx', 'l0_lru_bx', 'l1_w_in', 'l1_w_out', 'l1_s5_a_re', 'l1_s5_a_im', 'l1_s5_log_dt',
               'l1_s5_b_re', 'l1_s5_b_im', 'l1_s5_c_re', 'l1_s5_c_im', 'l1_s5_d', 'l1_glu_w', 'l1_glu_b',
               'l1_rw_mu', 'l1_rw_w0', 'l1_rw_w1', 'l1_rw_w2', 'l1_rw_a0', 'l1_rw_a1', 'l1_rw_a2', 'l1_rw_g1', 'l1_rw_g2',
               'l1_rw_kk', 'l1_rw_ka', 'l1_rw_rk', 'l1_ln_w', 'l1_ln_b']


def core_inputs(inp, b, pj, NP, consts):
    f = lambda a: np.ascontiguousarray(np.asarray(a, dtype=np.float32))
    xs = f(inp['x_sample'][b])
    xp = f(inp['x_prompt'][pj * NP:(pj + 1) * NP]).reshape(NP * LP, D)
    m = {'x_tok': np.concatenate([xs, xp], axis=0),
         'cond': np.stack([f(inp['c'][b]), f(inp['c_ctx'])], axis=0),
         'st_ret': f(inp['state_l0_ret'][b]), 'st_lru': f(inp['state_l0_lru'][b]),
         'st_s5': f(inp['state_l1_s5'][b]), 'st_rwkv': f(inp['state_l1_rwkv'][b])}
    for k in WEIGHT_KEYS:
        m[k] = f(inp[k])
    m.update(consts)
    return m


_PROG_CACHE = {}


def get_prog(cfg):
    key = tuple(sorted(cfg.items()))
    if key not in _PROG_CACHE:
        p = Prog(cfg)
        p.build()
        _PROG_CACHE[key] = p
    return _PROG_CACHE[key]


def kernel(**inputs):
    LS = 4096
    NP = 4
    cfg = dict(LS=LS, NP=NP)
    prog = get_prog(cfg)
    consts = host_consts(LS)
    in_maps = [core_inputs(inputs, cid % 4, cid, NP, consts) for cid in range(8)]
    res = run_bass_kernel_spmd(prog.nc, in_maps, core_ids=list(range(8)))
    R = res.results
    y_sample = np.stack([R[b]['y_tok'][:LS] for b in range(4)], axis=0)
    y_prompt = np.concatenate([R[cid]['y_tok'][LS:].reshape(NP, LP, D) for cid in range(8)], axis=0)
    new_ret = np.concatenate([R[cid]['new_ret'] for cid in range(8)], axis=0)
    new_lru = np.concatenate([R[cid]['new_lru'] for cid in range(8)], axis=0)
    new_s5 = np.concatenate([R[cid]['new_s5'] for cid in range(8)], axis=0)
    new_rwkv = np.concatenate([R[cid]['new_rwkv'] for cid in range(8)], axis=0)
    return (y_prompt.astype(np.float32), y_sample.astype(np.float32), new_ret.astype(np.float32),
            new_lru.astype(np.float32), new_s5.astype(np.float32), new_rwkv.astype(np.float32))
```

```python
import contextlib
import math
import numpy as np
import concourse.bass as bass
import concourse.mybir as mybir
from concourse.bass_utils import run_bass_kernel_spmd

F32 = mybir.dt.float32
BF16 = mybir.dt.bfloat16
I32 = mybir.dt.int32
ALU = mybir.AluOpType
AF = mybir.ActivationFunctionType

D = 1024
DFF = 2816
NJ = DFF // 128
NMOD = 9
LP = 256


class Ctx:
    def __init__(self, nc, es):
        self.nc = nc
        self.es = es
        self.eng = {'pe': nc.tensor, 'act': nc.scalar, 'dve': nc.vector, 'pool': nc.gpsimd, 'sp': nc.sync}
        self.sem = {}
        self.cnt = {}
        for n in ['pe', 'act', 'dve', 'pool']:
            self.sem[n] = es.enter_context(nc.semaphore('s_' + n))
            self.cnt[n] = 0
        self.slots = {}
        for q, n in (('sp', 8), ('pool', 6), ('act', 4)):
            lst = []
            for i in range(n):
                nm = 'd_%s%d' % (q, i)
                self.sem[nm] = es.enter_context(nc.semaphore(nm))
                self.cnt[nm] = 0
                lst.append(nm)
            self.slots[q] = lst
        self.slot_i = {'sp': 0, 'pool': 0, 'act': 0}
        self.waited = {e: {} for e in self.eng}
        self.lw = {}
        self.rd = {}
        self.nops = 0

    def _deps(self, r, w):
        d = {}

        def add(ev):
            if ev is None:
                return
            n, v = ev
            if d.get(n, 0) < v:
                d[n] = v
        for k in r:
            add(self.lw.get(k))
        for k in w:
            add(self.lw.get(k))
            for n, v in self.rd.get(k, {}).items():
                add((n, v))
        return d

    def _wait(self, e, d):
        for n, v in d.items():
            if e == 'pe' and n == 'pe':
                continue
            if self.waited[e].get(n, 0) < v:
                self.eng[e].wait_ge(self.sem[n], v)
                self.waited[e][n] = v

    def _rec(self, ev, r, w):
        for k in w:
            self.lw[k] = ev
            self.rd[k] = {}
        for k in r:
            dd = self.rd.setdefault(k, {})
            if dd.get(ev[0], 0) < ev[1]:
                dd[ev[0]] = ev[1]

    def op(self, e, fn, r=(), w=()):
        psr = [k for k in r if isinstance(k, tuple) and k[0] == 'ps' and k not in w]
        if psr:
            w = list(w) + psr
        self._wait(e, self._deps(r, w))
        inst = fn(self.eng[e])
        self.cnt[e] += 1
        inst.then_inc(self.sem[e], 1)
        self._rec((e, self.cnt[e]), r, w)
        self.nops += 1

    def dma(self, q, out, in_, r=(), w=(), **kw):
        d = self._deps(r, w)
        sl = self.slots[q][self.slot_i[q] % len(self.slots[q])]
        self.slot_i[q] += 1
        if self.cnt[sl] > 0:
            d[sl] = max(d.get(sl, 0), self.cnt[sl])
        self._wait(q, d)
        inst = self.eng[q].dma_start(out=out, in_=in_, **kw)
        self.cnt[sl] += 16
        inst.then_inc(self.sem[sl], 16)
        self._rec((sl, self.cnt[sl]), r, w)
        self.nops += 1

    def barrier(self, engines=('pe', 'act', 'dve', 'pool', 'sp')):
        d = {n: v for n, v in self.cnt.items() if v > 0}
        for e in engines:
            self._wait(e, d)
        self.lw = {}
        self.rd = {}

    def finish(self):
        d = {n: v for n, v in self.cnt.items() if v > 0}
        self._wait('sp', d)


def _col(ap, c):
    return ap[:, c:c + 1]


class Prog:
    def __init__(self, cfg):
        self.cfg = cfg
        self.LS = cfg['LS']
        self.NP = cfg['NP']
        self.NT = self.LS + self.NP * LP
        self.debug = cfg.get('debug', False)
        self.stages = cfg.get('stages', 'all')

    def build(self):
        nc = bass.Bass("TRN2", target_bir_lowering=False)
        self.nc = nc
        NT, LS, NP = self.NT, self.LS, self.NP

        def din(name, shape, dt=F32):
            return nc.dram_tensor(name, list(shape), dt, kind="ExternalInput").ap()

        def dout(name, shape, dt=F32):
            return nc.dram_tensor(name, list(shape), dt, kind="ExternalOutput").ap()

        def dscr(name, shape, dt=F32):
            kind = "ExternalOutput" if self.debug else "Internal"
            return nc.dram_tensor(name, list(shape), dt, kind=kind).ap()

        I = {}
        I['x_tok'] = din('x_tok', [NT, D])
        I['cond'] = din('cond', [2, D])
        I['st_ret'] = din('st_ret', [2, 4, 128, 128])
        I['st_lru'] = din('st_lru', [2, 512])
        I['st_s5'] = din('st_s5', [2, 2, 32, 64])
        I['st_rwkv'] = din('st_rwkv', [2, 8, 64, 64])
        I['mod_w'] = din('mod_w', [2, D, NMOD * D])
        I['mod_b'] = din('mod_b', [2, NMOD * D])
        I['norm_g'] = din('norm_g', [2, 3, D])
        for nm in ('ffn1', 'ffn2'):
            I[nm + '_w13'] = din(nm + '_w13', [2, D, 2 * DFF])
            I[nm + '_w2'] = din(nm + '_w2', [2, DFF, D])
        I['final_norm'] = din('final_norm', [D])
        I['l0_w_in'] = din('l0_w_in', [D, 3072])
        I['l0_w_out'] = din('l0_w_out', [D, D])
        I['l0_ret_decay'] = din('l0_ret_decay', [2, 4])
        I['l0_conv_w'] = din('l0_conv_w', [4, 512])
        I['l0_conv_b'] = din('l0_conv_b', [512])
        I['l0_lru_lam'] = din('l0_lru_lam', [2, 512])
        I['l0_lru_wa'] = din('l0_lru_wa', [2, 8, 64, 64])
        I['l0_lru_ba'] = din('l0_lru_ba', [2, 512])
        I['l0_lru_wx'] = din('l0_lru_wx', [2, 8, 64, 64])
        I['l0_lru_bx'] = din('l0_lru_bx', [2, 512])
        I['l1_w_in'] = din('l1_w_in', [D, 2560])
        I['l1_w_out'] = din('l1_w_out', [D, D])
        I['l1_s5_a_re'] = din('l1_s5_a_re', [2, 32, 64])
        I['l1_s5_a_im'] = din('l1_s5_a_im', [2, 32, 64])
        I['l1_s5_log_dt'] = din('l1_s5_log_dt', [2, 32])
        I['l1_s5_b_re'] = din('l1_s5_b_re', [2, 32, 64, 16])
        I['l1_s5_b_im'] = din('l1_s5_b_im', [2, 32, 64, 16])
        I['l1_s5_c_re'] = din('l1_s5_c_re', [2, 32, 16, 64])
        I['l1_s5_c_im'] = din('l1_s5_c_im', [2, 32, 16, 64])
        I['l1_s5_d'] = din('l1_s5_d', [512])
        I['l1_glu_w'] = din('l1_glu_w', [512, 512])
        I['l1_glu_b'] = din('l1_glu_b', [512])
        I['l1_rw_mu'] = din('l1_rw_mu', [6, 512])
        I['l1_rw_w0'] = din('l1_rw_w0', [2, 512])
        I['l1_rw_w1'] = din('l1_rw_w1', [2, 512, 64])
        I['l1_rw_w2'] = din('l1_rw_w2', [2, 64, 512])
        I['l1_rw_a0'] = din('l1_rw_a0', [2, 512])
        I['l1_rw_a1'] = din('l1_rw_a1', [2, 512, 64])
        I['l1_rw_a2'] = din('l1_rw_a2', [2, 64, 512])
        I['l1_rw_g1'] = din('l1_rw_g1', [512, 128])
        I['l1_rw_g2'] = din('l1_rw_g2', [128, 512])
        I['l1_rw_kk'] = din('l1_rw_kk', [512])
        I['l1_rw_ka'] = din('l1_rw_ka', [512])
        I['l1_rw_rk'] = din('l1_rw_rk', [512])
        I['l1_ln_w'] = din('l1_ln_w', [512])
        I['l1_ln_b'] = din('l1_ln_b', [512])
        I['c_ident'] = din('c_ident', [128, 128])
        I['c_rope_cos'] = din('c_rope_cos', [128, LS])
        I['c_rope_sin'] = din('c_rope_sin', [128, LS])
        I['c_rope_rot'] = din('c_rope_rot', [128, 128])
        I['c_misc'] = din('c_misc', [128, 1024])
        I['c_iota'] = din('c_iota', [128, 512])
        I['c_s5mask'] = din('c_s5mask', [128, 512])
        I['c_cmf'] = din('c_cmf', [128, 512])
        I['c_blk'] = din('c_blk', [128, 128])
        I['c_rwm'] = din('c_rwm', [128, 1024])
        self.I = I

        O = {}
        O['y_tok'] = dout('y_tok', [NT, D])
        O['new_ret'] = dout('new_ret', [NP, 2, 4, 128, 128])
        O['new_lru'] = dout('new_lru', [NP, 2, 512])
        O['new_s5'] = dout('new_s5', [NP, 2, 2, 32, 64])
        O['new_rwkv'] = dout('new_rwkv', [NP, 2, 8, 64, 64])
        self.O = O

        self.XS = dscr('XS', [D, NT])
        self.P = dscr('Pscr', [3072, NT])
        self.Y = dscr('Yscr', [D, NT], BF16)
        self.YF = dscr('YFscr', [512, NT])
        self.BFs = dscr('BFscr', [512, NT])

        with contextlib.ExitStack() as es:
            c = Ctx(nc, es)
            self.c = c
            self.es = es
            self.PS = [es.enter_context(nc.psum_tensor('ps%d' % i, [128, 512], F32)) for i in range(8)]
            self.ident = es.enter_context(nc.sbuf_tensor('ident', [128, 128], F32))
            self.ones_bf = es.enter_context(nc.sbuf_tensor('ones_bf', [128, 128], BF16))
            self.cst = es.enter_context(nc.sbuf_tensor('cst', [128, 8], F32))
            self.MODC = es.enter_context(nc.sbuf_tensor('MODC', [128, 2 * 3 * 3 * 2 + 2, 8], F32))
            c.dma('sp', self.ident[:], I['c_ident'][:, :], w=['ident'])
            c.op('dve', lambda e: e.memset(self.ones_bf[:], 1.0), w=['ones_bf'])
            for j, v in enumerate([1e-6, 1e-5, 64e-5, 0.0, 1.0]):
                c.op('dve', lambda e: e.memset(self.cst[:, j:j + 1], v), w=[('cst', j)])

            st = self.stages
            self.phase_transpose_in()
            self.phase_mod()
            for i in range(2):
                self.phase_ffn(i, 0)
                self.phase_inproj(i)
                if i == 0:
                    self.phase_mix_even()
                else:
                    self.phase_mix_odd()
                self.phase_outproj(i)
                self.phase_ffn(i, 2)
            self.phase_final()
            c.finish()
        return nc

    def sb(self, ph, name, shape, dt):
        self._uid = getattr(self, '_uid', 0) + 1
        return ph.enter_context(self.nc.sbuf_tensor('%s_u%d' % (name, self._uid), shape, dt))

    def modidx(self, i, s, kind, j):
        return ((i * 3 + s) * 3 + kind) * 2 + j

    def cond_of_tile(self, t0):
        return 0 if t0 < self.LS else 1

    def XSv(self):
        return self.XS.rearrange("(c p) t -> p c t", p=128)

    def phase_transpose_in(self):
        c, nc, I = self.c, self.nc, self.I
        with contextlib.ExitStack() as ph:
            xin = [self.sb(ph, 'p0_xin%d' % b, [128, 4, D], F32) for b in range(2)]
            xT = [self.sb(ph, 'p0_xT%d' % b, [128, 8, 512], F32) for b in range(2)]
            xv = I['x_tok'].rearrange("(n q p) d -> n p q d", p=128, q=4)
            ng = self.NT // 512
            for g in range(ng):
                b = g % 2
                c.dma('sp', xin[b][:], xv[g], w=[('xin', b)])
                for kc in range(8):
                    ps = self.PS[kc % 4]
                    for q in range(4):
                        c.op('pe', lambda e: e.transpose(out=ps[:, q * 128:(q + 1) * 128], in_=xin[b][:, q, kc * 128:(kc + 1) * 128], identity=self.ident[:]),
                             r=[('xin', b), 'ident'], w=[('ps', kc % 4)])
                    eng = 'act' if kc % 2 == 0 else 'dve'
                    if eng == 'act':
                        c.op('act', lambda e: e.activation(out=xT[b][:, kc, :], in_=ps[:], func=AF.Identity), r=[('ps', kc % 4)], w=[('xT', b, kc)])
                    else:
                        c.op('dve', lambda e: e.tensor_copy(out=xT[b][:, kc, :], in_=ps[:]), r=[('ps', kc % 4)], w=[('xT', b, kc)])
                c.dma('sp', self.XSv()[:, :, g * 512:(g + 1) * 512], xT[b][:], r=[('xT', b, kc) for kc in range(8)], w=[('XS', g * 512), ('XS', g * 512 + 256)])
            c.barrier()

    def phase_mod(self):
        c, nc, I = self.c, self.nc, self.I
        with contextlib.ExitStack() as ph:
            NB = 1152
            wb = [self.sb(ph, 'pm_w%d' % b, [128, 8, NB], F32) for b in range(2)]
            cT = self.sb(ph, 'pm_cT', [128, 8, 2], F32)
            sT = self.sb(ph, 'pm_sT', [128, 8, 2], F32)
            mb = self.sb(ph, 'pm_mb', [128, 2, 72], F32)
            ng = self.sb(ph, 'pm_ng', [128, 7, 8], F32)
            MT = self.sb(ph, 'pm_MT', [128, 2, 72, 2], F32)
            tmp = self.sb(ph, 'pm_tmp', [128, 8], F32)
            for j in range(2):
                c.dma('sp', cT[:, :, j], I['cond'][j].rearrange("(c p) -> p c", p=128), w=['cT'], allow_slow_non_contiguous=True)
            for i in range(2):
                c.dma('sp', mb[:, i, :], I['mod_b'][i].rearrange("(k p) -> p k", p=128), w=['mb'], allow_slow_non_contiguous=True)
                for s in range(3):
                    c.dma('sp', ng[:, i * 3 + s, :], I['norm_g'][i, s].rearrange("(c p) -> p c", p=128), w=['ng'], allow_slow_non_contiguous=True)
            c.dma('sp', ng[:, 6, :], I['final_norm'].rearrange("(c p) -> p c", p=128), w=['ng'], allow_slow_non_contiguous=True)
            c.op('act', lambda e: e.activation(out=sT[:], in_=cT[:], func=AF.Silu), r=['cT'], w=['sT'])
            ps = self.PS[0]
            psv = ps[:, 0:144].rearrange("p (f j) -> p f j", j=2)
            blk = 0
            for i in range(2):
                wv = I['mod_w'][i].rearrange("(c p) n -> p c n", p=128)
                for nb in range(8):
                    b = blk % 2
                    blk += 1
                    c.dma('sp', wb[b][:], wv[:, :, nb * NB:(nb + 1) * NB], w=[('wb', b)])
                    for fc in range(9):
                        f = nb * 9 + fc
                        for kc in range(8):
                            c.op('pe', lambda e: e.matmul(psv[:, f, :], lhsT=wb[b][:, kc, fc * 128:(fc + 1) * 128], rhs=sT[:, kc, :], start=(kc == 0), stop=(kc == 7)),
                                 r=[('wb', b), 'sT'], w=[('ps', 0)])
                c.op('dve', lambda e: e.tensor_tensor(out=MT[:, i, :, :], in0=psv[:, :, :], in1=mb[:, i, :].unsqueeze(2).broadcast_to([128, 72, 2]), op=ALU.add),
                     r=[('ps', 0), 'mb'], w=['MT'])
            MC = self.MODC
            for i in range(2):
                for s in range(3):
                    for j in range(2):
                        sh = MT[:, i, (3 * s) * 8:(3 * s + 1) * 8, j]
                        sc = MT[:, i, (3 * s + 1) * 8:(3 * s + 2) * 8, j]
                        ga = MT[:, i, (3 * s + 2) * 8:(3 * s + 3) * 8, j]
                        c.op('dve', lambda e: e.scalar_tensor_tensor(out=MC[:, self.modidx(i, s, 0, j), :], in0=sc, scalar=1.0, in1=ng[:, i * 3 + s, :], op0=ALU.add, op1=ALU.mult),
                             r=['MT', 'ng'], w=['MODC'])
                        c.op('dve', lambda e: e.tensor_copy(out=MC[:, self.modidx(i, s, 1, j), :], in_=sh), r=['MT'], w=['MODC'])
                        c.op('dve', lambda e: e.tensor_scalar(out=MC[:, self.modidx(i, s, 2, j), :], in0=ga, scalar1=(0.5 if s != 1 else 1.0), scalar2=None, op0=ALU.mult),
                             r=['MT'], w=['MODC'])
            c.op('dve', lambda e: e.tensor_copy(out=MC[:, 36, :], in_=ng[:, 6, :]), r=['ng'], w=['MODC'])
            c.op('dve', lambda e: e.memset(MC[:, 37, :], 0.0), w=['MODC'])
            c.barrier()

    def norm_mod(self, xt, xkey, TT, s1i, s2i, sq, rs, rstd, tmp, h, hkey):
        c = self.c
        MC = self.MODC
        for kc in range(8):
            c.op('act', lambda e: e.activation(out=sq[:, kc % 2, :TT], in_=xt[:, kc, :], func=AF.Square), r=[xkey], w=[('sq', kc % 2)])
            c.op('pe', lambda e: e.matmul(self.PS[0][:, :TT], lhsT=self.ones_bf[:], rhs=sq[:, kc % 2, :TT], start=(kc == 0), stop=(kc == 7)),
                 r=[('sq', kc % 2), 'ones_bf'], w=[('ps', 0)])
        c.op('act', lambda e: e.activation(out=rs[:, :TT], in_=self.PS[0][:, :TT], func=AF.Sqrt, scale=1.0 / D, bias=self.cst[:, 0:1]),
             r=[('ps', 0), ('cst', 0)], w=['rs'])
        rk = 'rs' if rstd is rs else 'rstd'
        c.op('dve', lambda e: e.reciprocal(out=rstd[:, :TT], in_=rs[:, :TT]), r=['rs'], w=[rk])
        for kc in range(8):
            t = tmp[kc % 2]
            c.op('dve', lambda e: e.tensor_tensor(out=t[:, :TT], in0=xt[:, kc, :], in1=rstd[:, :TT], op=ALU.mult), r=[xkey, rk], w=[('ntmp', kc % 2)])
            c.op('act', lambda e: e.activation(out=h[:, kc, :TT], in_=t[:, :TT], func=AF.Identity, scale=MC[:, s1i, kc:kc + 1], bias=MC[:, s2i, kc:kc + 1]),
                 r=[('ntmp', kc % 2), 'MODC'], w=[hkey])

    def phase_ffn(self, i, s):
        c, nc, I = self.c, self.nc, self.I
        nm = 'ffn1' if s == 0 else 'ffn2'
        TT = 512
        with contextlib.ExitStack() as ph:
            W13 = self.sb(ph, 'ff_w13', [128, 8, 2 * DFF], BF16)
            W2 = self.sb(ph, 'ff_w2', [128, NJ, D], BF16)
            xt = [self.sb(ph, 'ff_xt%d' % b, [128, 8, TT], F32) for b in range(2)]
            h = self.sb(ph, 'ff_h', [128, 8, TT], BF16)
            a = self.sb(ph, 'ff_a', [128, NJ, TT], BF16)
            sq = self.sb(ph, 'ff_sq', [128, 2, TT], BF16)
            rs = self.sb(ph, 'ff_rs', [128, TT], F32)
            rstd = rs
            tmp = [self.sb(ph, 'ff_tmp%d' % b, [128, TT], F32) for b in range(2)]
            sg = [self.sb(ph, 'ff_sg%d' % b, [128, TT], BF16) for b in range(2)]
            w13v = I[nm + '_w13'][i].rearrange("(c p) n -> p c n", p=128)
            w2v = I[nm + '_w2'][i].rearrange("(j p) n -> p j n", p=128)
            for q in range(4):
                c.dma('pool', W13[:, :, q * 1408:(q + 1) * 1408], w13v[:, :, q * 1408:(q + 1) * 1408], w=[('w13', q)])
            for q in range(2):
                c.dma('pool', W2[:, q * 11:(q + 1) * 11, :], w2v[:, q * 11:(q + 1) * 11, :], w=[('w2', q)])
            ntile = self.NT // TT
            XSv = self.XSv()

            def load(tt):
                c.dma('sp', xt[tt % 2][:], XSv[:, :, tt * TT:(tt + 1) * TT], r=[('XS', tt * TT)], w=[('xt', tt % 2)])

            def norm(tt):
                j = self.cond_of_tile(tt * TT)
                self.norm_mod(xt[tt % 2], ('xt', tt % 2), TT, self.modidx(i, s, 0, j), self.modidx(i, s, 1, j), sq, rs, rstd, tmp, h, 'h')

            load(0)
            norm(0)
            for tt in range(ntile):
                b = tt % 2
                j = self.cond_of_tile(tt * TT)
                if tt + 1 < ntile:
                    load(tt + 1)
                for jj in range(NJ):
                    gp = self.PS[1 + jj % 2]
                    up = self.PS[3 + jj % 2]
                    for kc in range(8):
                        c.op('pe', lambda e: e.matmul(gp[:, :TT], lhsT=W13[:, kc, jj * 128:(jj + 1) * 128], rhs=h[:, kc, :], start=(kc == 0), stop=(kc == 7)),
                             r=[('w13', jj // 11), 'h'], w=[('ps', 1 + jj % 2)])
                    for kc in range(8):
                        c.op('pe', lambda e: e.matmul(up[:, :TT], lhsT=W13[:, kc, DFF + jj * 128:DFF + (jj + 1) * 128], rhs=h[:, kc, :], start=(kc == 0), stop=(kc == 7)),
                             r=[('w13', 2 + jj // 11), 'h'], w=[('ps', 3 + jj % 2)])
                    c.op('act', lambda e: e.activation(out=sg[jj % 2][:], in_=gp[:, :TT], func=AF.Silu), r=[('ps', 1 + jj % 2)], w=[('sg', jj % 2)])
                    c.op('dve', lambda e: e.tensor_tensor(out=a[:, jj, :], in0=sg[jj % 2][:], in1=up[:, :TT], op=ALU.mult),
                         r=[('sg', jj % 2), ('ps', 3 + jj % 2)], w=[('a', jj)])
                if tt + 1 < ntile:
                    norm(tt + 1)
                gi = self.modidx(i, s, 2, j)
                for cc in range(8):
                    op_ = self.PS[5 + cc % 2]
                    for jj in range(NJ):
                        c.op('pe', lambda e: e.matmul(op_[:, :TT], lhsT=W2[:, jj, cc * 128:(cc + 1) * 128], rhs=a[:, jj, :], start=(jj == 0), stop=(jj == NJ - 1)),
                             r=[('w2', jj // 11), ('a', jj)], w=[('ps', 5 + cc % 2)])
                    c.op('dve', lambda e: e.scalar_tensor_tensor(out=xt[b][:, cc, :], in0=op_[:, :TT], scalar=self.MODC[:, gi, cc:cc + 1], in1=xt[b][:, cc, :], op0=ALU.mult, op1=ALU.add),
                         r=[('ps', 5 + cc % 2), 'MODC', ('xt', b)], w=[('xt', b)])
                c.dma('sp', XSv[:, :, tt * TT:(tt + 1) * TT], xt[b][:], r=[('xt', b)], w=[('XS', tt * TT)])
            c.barrier()

    def phase_inproj(self, i):
        c, nc, I = self.c, self.nc, self.I
        NIN = 3072 if i == 0 else 2560
        TT = 512
        nm_ = NIN // 128
        with contextlib.ExitStack() as ph:
            W = self.sb(ph, 'ip_w', [128, 8, NIN], BF16)
            xt = [self.sb(ph, 'ip_xt%d' % b, [128, 8, TT], F32) for b in range(2)]
            h = self.sb(ph, 'ip_h', [128, 8, TT], BF16)
            sq = self.sb(ph, 'ip_sq', [128, 8, TT], BF16)
            rs = self.sb(ph, 'ip_rs', [128, TT], F32)
            rstd = self.sb(ph, 'ip_rstd', [128, TT], F32)
            tmp = [self.sb(ph, 'ip_tmp%d' % b, [128, TT], F32) for b in range(2)]
            og = [self.sb(ph, 'ip_og%d' % b, [128, 4, TT], F32) for b in range(2)]
            wv = I['l%d_w_in' % i].rearrange("(c p) n -> p c n", p=128)
            for q in range(2):
                hh = NIN // 2
                c.dma('pool', W[:, :, q * hh:(q + 1) * hh], wv[:, :, q * hh:(q + 1) * hh], w=[('w', q)])
            ntile = self.NT // TT
            XSv = self.XSv()
            Pv = self.P.rearrange("(m p) t -> p m t", p=128)

            def load(tt):
                c.dma('sp', xt[tt % 2][:], XSv[:, :, tt * TT:(tt + 1) * TT], r=[('XS', tt * TT)], w=[('xt', tt % 2)])
            load(0)
            gcount = 0
            for tt in range(ntile):
                j = self.cond_of_tile(tt * TT)
                if tt + 1 < ntile:
                    load(tt + 1)
                self.norm_mod(xt[tt % 2], ('xt', tt % 2), TT, self.modidx(i, 1, 0, j), self.modidx(i, 1, 1, j), sq, rs, rstd, tmp, h, 'h')
                for m in range(nm_):
                    ps = self.PS[1 + m % 4]
                    g = gcount % 2
                    for kc in range(8):
                        c.op('pe', lambda e: e.matmul(ps[:, :TT], lhsT=W[:, kc, m * 128:(m + 1) * 128], rhs=h[:, kc, :], start=(kc == 0), stop=(kc == 7)),
                             r=[('w', (m * 128) // (NIN // 2)), 'h'], w=[('ps', 1 + m % 4)])
                    scale = (128.0 ** -0.5) if (i == 0 and m < 4) else 1.0
                    if m % 2 == 0:
                        c.op('act', lambda e: e.activation(out=og[g][:, m % 4, :], in_=ps[:, :TT], func=AF.Identity, scale=scale), r=[('ps', 1 + m % 4)], w=[('og', g, m % 4)])
                    else:
                        c.op('dve', lambda e: e.tensor_scalar(out=og[g][:, m % 4, :], in0=ps[:, :TT], scalar1=scale, scalar2=None, op0=ALU.mult), r=[('ps', 1 + m % 4)], w=[('og', g, m % 4)])
                    if m % 4 == 3:
                        m0 = m - 3
                        c.dma('sp', Pv[:, m0:m0 + 4, tt * TT:(tt + 1) * TT], og[g][:], r=[('og', g, q) for q in range(4)], w=[('P', tt * TT)])
                        gcount += 1
            c.barrier()

    def phase_outproj(self, i):
        c, nc, I = self.c, self.nc, self.I
        TT = 512
        with contextlib.ExitStack() as ph:
            W = self.sb(ph, 'op_w', [128, 8, D], BF16)
            xt = [self.sb(ph, 'op_xt%d' % b, [128, 8, TT], F32) for b in range(2)]
            yt = [self.sb(ph, 'op_yt%d' % b, [128, 8, TT], BF16) for b in range(2)]
            wv = I['l%d_w_out' % i].rearrange("(c p) n -> p c n", p=128)
            c.dma('pool', W[:], wv, w=['w'])
            ntile = self.NT // TT
            XSv = self.XSv()
            Yv = self.Y.rearrange("(c p) t -> p c t", p=128)

            def load(tt):
                c.dma('sp', xt[tt % 2][:], XSv[:, :, tt * TT:(tt + 1) * TT], r=[('XS', tt * TT)], w=[('xt', tt % 2)])
                c.dma('sp', yt[tt % 2][:], Yv[:, :, tt * TT:(tt + 1) * TT], r=[('Y', tt * TT)], w=[('yt', tt % 2)])
            load(0)
            for tt in range(ntile):
                b = tt % 2
                j = self.cond_of_tile(tt * TT)
                if tt + 1 < ntile:
                    load(tt + 1)
                gi = self.modidx(i, 1, 2, j)
                for cc in range(8):
                    ps = self.PS[1 + cc % 4]
                    for kc in range(8):
                        c.op('pe', lambda e: e.matmul(ps[:, :TT], lhsT=W[:, kc, cc * 128:(cc + 1) * 128], rhs=yt[b][:, kc, :], start=(kc == 0), stop=(kc == 7)),
                             r=['w', ('yt', b)], w=[('ps', 1 + cc % 4)])
                    c.op('dve', lambda e: e.scalar_tensor_tensor(out=xt[b][:, cc, :], in0=ps[:, :TT], scalar=self.MODC[:, gi, cc:cc + 1], in1=xt[b][:, cc, :], op0=ALU.mult, op1=ALU.add),
                         r=[('ps', 1 + cc % 4), 'MODC', ('xt', b)], w=[('xt', b)])
                c.dma('sp', XSv[:, :, tt * TT:(tt + 1) * TT], xt[b][:], r=[('xt', b)], w=[('XS', tt * TT)])
            c.barrier()

    def phase_final(self):
        c, nc, I, O = self.c, self.nc, self.I, self.O
        TT = 256
        with contextlib.ExitStack() as ph:
            xt = [self.sb(ph, 'fn_xt%d' % b, [128, 8, TT], F32) for b in range(2)]
            h = self.sb(ph, 'fn_h', [128, 8, TT], F32)
            sq = self.sb(ph, 'fn_sq', [128, 8, TT], BF16)
            rs = self.sb(ph, 'fn_rs', [128, TT], F32)
            rstd = self.sb(ph, 'fn_rstd', [128, TT], F32)
            tmp = [self.sb(ph, 'fn_tmp%d' % b, [128, TT], F32) for b in range(2)]
            yo = [self.sb(ph, 'fn_yo%d' % b, [128, 2, D], F32) for b in range(2)]
            ntile = self.NT // TT
            XSv = self.XSv()
            yv = O['y_tok'].rearrange("(n q p) d -> n p q d", p=128, q=2)

            def load(tt):
                c.dma('sp', xt[tt % 2][:], XSv[:, :, tt * TT:(tt + 1) * TT], r=[('XS', tt * TT)], w=[('xt', tt % 2)])
            load(0)
            for tt in range(ntile):
                b = tt % 2
                if tt + 1 < ntile:
                    load(tt + 1)
                self.norm_mod(xt[b], ('xt', b), TT, 36, 37, sq, rs, rstd, tmp, h, 'h')
                for q in range(2):
                    for kc in range(8):
                        ps = self.PS[1 + (kc // 4) % 2 + 2 * q]
                        c.op('pe', lambda e: e.transpose(out=ps[:, (kc % 4) * 128:(kc % 4 + 1) * 128], in_=h[:, kc, q * 128:(q + 1) * 128], identity=self.ident[:]),
                             r=['h', 'ident'], w=[('ps', 1 + (kc // 4) % 2 + 2 * q)])
                        if kc % 4 == 3:
                            half = kc // 4
                            if half == 0:
                                c.op('act', lambda e: e.activation(out=yo[b][:, q, 0:512], in_=ps[:], func=AF.Identity), r=[('ps', 1 + 2 * q)], w=[('yo', b, q, 0)])
                            else:
                                c.op('dve', lambda e: e.tensor_copy(out=yo[b][:, q, 512:1024], in_=ps[:]), r=[('ps', 2 + 2 * q)], w=[('yo', b, q, 1)])
                c.dma('sp', yv[tt], yo[b][:], r=[('yo', b, q, hh) for q in range(2) for hh in range(2)], w=[('yout', tt)])
            c.barrier()

    def seqs(self):
        lst = [dict(off=0, L=self.LS, latent=True, pidx=None)]
        for i in range(self.NP):
            lst.append(dict(off=self.LS + i * LP, L=LP, latent=False, pidx=i))
        return lst

    def phase_mix_even(self):
        import os
        which = os.environ.get('EVENPARTS', 'rl')
        if which != 'rl':
            self.mix_stub()
        if 'r' in which:
            self.mix_even_ret()
        if 'l' in which:
            self.mix_even_lru()

    def mix_even_ret(self):
        c, nc, I, O = self.c, self.nc, self.I, self.O
        LS = self.LS
        NCH = LS // 128
        Pv = self.P.rearrange("(m p) t -> p m t", p=128)
        Yv = self.Y.rearrange("(c p) t -> p c t", p=128)
        with contextlib.ExitStack() as ph:
            sb = lambda n, sh, dt: self.sb(ph, 're_' + n, sh, dt)
            misc = sb('misc', [128, 1024], F32)
            ROT = sb('rot', [128, 128], F32)
            COS = sb('cos', [128, LS], F32)
            SIN = sb('sin', [128, LS], F32)
            RD = sb('rd', [128, 8], F32)
            LG = sb('lg', [128, 8], F32)
            MT = sb('MT', [128, 4, 128], F32)
            QD = sb('QD', [128, 8, 128], F32)
            KD = sb('KD', [128, 8], F32)
            GC = sb('GC', [128, 8], F32)
            onesd = sb('onesd', [128, 128], F32)
            tA = sb('tA', [128, 128], F32)
            tB = sb('tB', [128, 128], F32)
            q_bf = sb('q_bf', [128, LS], BF16)
            k_bf = sb('k_bf', [128, LS], BF16)
            qf_ = sb('qf', [128, LS], BF16)
            qb_ = sb('qb', [128, LS], BF16)
            Vtok = sb('Vtok', [128, NCH, 128], BF16)
            Kf = sb('Kf', [128, NCH, 128], BF16)
            Kb = sb('Kb', [128, NCH, 128], BF16)
            o_a = sb('o_a', [128, LS], F32)
            o_b = sb('o_b', [128, LS], F32)
            qs = sb('qs', [128, 512], F32)
            ks = sb('ks', [128, 512], F32)
            vs = sb('vs', [128, 512], F32)
            gs = sb('gs', [128, 512], F32)
            qr = sb('qr', [128, 512], F32)
            kr = sb('kr', [128, 512], F32)
            t1 = sb('t1', [128, 512], F32)
            t2 = sb('t2', [128, 512], F32)
            t3 = sb('t3', [128, 512], F32)
            ya = sb('ya', [128, 512], BF16)
            sTm = [sb('sTm%d' % b, [128, 128], BF16) for b in range(2)]
            Sst = [sb('S%d' % d, [128, 128], F32) for d in range(2)]
            Sbf = [sb('Sbf%d' % d, [128, 128], BF16) for d in range(2)]
            PS = self.PS

            c.dma('sp', misc[:], I['c_misc'][:, :], w=['misc'])
            c.dma('sp', ROT[:], I['c_rope_rot'][:, :], w=['ROT'])
            c.dma('sp', COS[:], I['c_rope_cos'][:, :], w=['COS'])
            c.dma('sp', SIN[:], I['c_rope_sin'][:, :], w=['SIN'])
            c.dma('sp', RD[:], I['l0_ret_decay'].rearrange("d h -> (d h)").partition_broadcast(128), w=['RD'])
            c.op('dve', lambda e: e.memset(onesd[:], 1.0 / 128.0), w=['onesd'])
            c.op('act', lambda e: e.activation(out=LG[:], in_=RD[:], func=AF.Sigmoid), r=['RD'], w=['LG'])
            c.op('act', lambda e: e.activation(out=LG[:], in_=LG[:], func=AF.Ln), r=['LG'], w=['LG'])
            absrel = misc[:, 0:128]
            m_le = misc[:, 128:256]
            m_ge = misc[:, 256:384]
            iota0 = misc[:, 640:768]
            iota1 = misc[:, 768:896]
            for h in range(4):
                c.op('act', lambda e: e.activation(out=tA[:], in_=absrel, func=AF.Exp, scale=LG[:, h:h + 1]), r=['misc', 'LG'], w=['tA'])
                c.op('dve', lambda e: e.tensor_tensor(out=tA[:], in0=tA[:], in1=m_le, op=ALU.mult), r=['tA', 'misc'], w=['tA'])
                c.op('act', lambda e: e.activation(out=tB[:], in_=absrel, func=AF.Exp, scale=LG[:, 4 + h:5 + h]), r=['misc', 'LG'], w=['tB'])
                c.op('dve', lambda e: e.tensor_tensor(out=tB[:], in0=tB[:], in1=m_ge, op=ALU.mult), r=['tB', 'misc'], w=['tB'])
                c.op('dve', lambda e: e.tensor_tensor(out=MT[:, h, :], in0=tA[:], in1=tB[:], op=ALU.add), r=['tA', 'tB'], w=['MT'])
                c.op('act', lambda e: e.activation(out=QD[:, h, :], in_=iota1, func=AF.Exp, scale=LG[:, h:h + 1]), r=['misc', 'LG'], w=['QD'])
                c.op('dve', lambda e: e.tensor_scalar(out=tA[:], in0=iota0, scalar1=-1.0, scalar2=128.0, op0=ALU.mult, op1=ALU.add), r=['misc', 'MT'], w=['tA'])
                c.op('act', lambda e: e.activation(out=QD[:, 4 + h, :], in_=tA[:], func=AF.Exp, scale=LG[:, 4 + h:5 + h]), r=['tA', 'LG'], w=['QD'])
                c.op('act', lambda e: e.activation(out=KD[:, h:h + 1], in_=misc[:, 897:898], func=AF.Exp, scale=LG[:, h:h + 1]), r=['misc', 'LG'], w=['KD'])
                c.op('act', lambda e: e.activation(out=KD[:, 4 + h:5 + h], in_=misc[:, 896:897], func=AF.Exp, scale=LG[:, 4 + h:5 + h]), r=['misc', 'LG'], w=['KD'])
                for d in range(2):
                    c.op('act', lambda e: e.activation(out=GC[:, d * 4 + h:d * 4 + h + 1], in_=misc[:, 898:899], func=AF.Exp, scale=LG[:, d * 4 + h:d * 4 + h + 1]), r=['misc', 'LG'], w=['GC'])

            import os
            RETSTOP = int(os.environ.get('RETSTOP', '9'))
            for sq_ in (self.seqs() if RETSTOP > 0 else []):
                off, L, latent, pidx = sq_['off'], sq_['L'], sq_['latent'], sq_['pidx']
                TS = min(512, L)
                nt = L // TS
                nch = L // 128
                cpt = TS // 128
                for h in range(4):
                    for ti in range(nt):
                        t0 = ti * TS
                        g0 = off + t0
                        c.dma('sp', qs[:, :TS], Pv[:, h, g0:g0 + TS], w=['qs'])
                        c.dma('sp', ks[:, :TS], Pv[:, 4 + h, g0:g0 + TS], w=['ks'])
                        c.dma('sp', vs[:, :TS], Pv[:, 8 + h, g0:g0 + TS], w=['vs'])
                        if latent:
                            for (src, skey, dst, dkey) in ((qs, 'qs', qr, 'qr'), (ks, 'ks', kr, 'kr')):
                                c.op('pe', lambda e: e.matmul(PS[1][:, :TS], lhsT=ROT[:], rhs=src[:, :TS], start=True, stop=True), r=['ROT', skey], w=[('ps', 1)])
                                c.op('dve', lambda e: e.tensor_tensor(out=t1[:, :TS], in0=src[:, :TS], in1=COS[:, t0:t0 + TS], op=ALU.mult), r=[skey, 'COS'], w=['t1'])
                                c.op('dve', lambda e: e.tensor_tensor(out=t2[:, :TS], in0=PS[1][:, :TS], in1=SIN[:, t0:t0 + TS], op=ALU.mult), r=[('ps', 1), 'SIN'], w=['t2'])
                                c.op('dve', lambda e: e.tensor_tensor(out=dst[:, :TS], in0=t1[:, :TS], in1=t2[:, :TS], op=ALU.add), r=['t1', 't2'], w=[dkey])
                            qsrc, qk_, ksrc, kk_ = qr, 'qr', kr, 'kr'
                        else:
                            qsrc, qk_, ksrc, kk_ = qs, 'qs', ks, 'ks'
                        c.op('act', lambda e: e.activation(out=q_bf[:, t0:t0 + TS], in_=qsrc[:, :TS], func=AF.Identity), r=[qk_], w=['q_bf'])
                        c.op('act', lambda e: e.activation(out=k_bf[:, t0:t0 + TS], in_=ksrc[:, :TS], func=AF.Identity), r=[kk_], w=['k_bf'])
                        qv = qsrc[:, :TS].rearrange("p (c j) -> p c j", j=128)
                        c.op('dve', lambda e: e.tensor_tensor(out=qf_[:, t0:t0 + TS].rearrange("p (c j) -> p c j", j=128), in0=qv, in1=QD[:, h, :].unsqueeze(1).broadcast_to([128, cpt, 128]), op=ALU.mult),
                             r=[qk_, 'QD'], w=['qf'])
                        c.op('dve', lambda e: e.tensor_tensor(out=qb_[:, t0:t0 + TS].rearrange("p (c j) -> p c j", j=128), in0=qv, in1=QD[:, 4 + h, :].unsqueeze(1).broadcast_to([128, cpt, 128]), op=ALU.mult),
                             r=[qk_, 'QD'], w=['qb'])
                        for cc in range(cpt):
                            ch = ti * cpt + cc
                            c.op('pe', lambda e: e.transpose(out=PS[2][:, cc * 128:(cc + 1) * 128], in_=vs[:, cc * 128:(cc + 1) * 128], identity=self.ident[:]), r=['vs', 'ident'], w=[('ps', 2)])
                            c.op('pe', lambda e: e.transpose(out=PS[3][:, cc * 128:(cc + 1) * 128], in_=ksrc[:, cc * 128:(cc + 1) * 128], identity=self.ident[:]), r=[kk_, 'ident'], w=[('ps', 3)])
                        ch0 = ti * cpt
                        c.op('act', lambda e: e.activation(out=Vtok[:, ch0:ch0 + cpt, :], in_=PS[2][:, :TS].rearrange("p (c j) -> p c j", j=128), func=AF.Identity), r=[('ps', 2)], w=['Vtok'])
                        c.op('act', lambda e: e.activation(out=Kf[:, ch0:ch0 + cpt, :], in_=PS[3][:, :TS].rearrange("p (c j) -> p c j", j=128), func=AF.Identity, scale=KD[:, h:h + 1]), r=[('ps', 3), 'KD'], w=['Kf'])
                        c.op('dve', lambda e: e.tensor_scalar(out=Kb[:, ch0:ch0 + cpt, :], in0=PS[3][:, :TS].rearrange("p (c j) -> p c j", j=128), scalar1=KD[:, 4 + h:5 + h], scalar2=None, op0=ALU.mult), r=[('ps', 3), 'KD'], w=['Kb'])
                    if RETSTOP < 2:
                        continue
                    for d in range(2):
                        if latent:
                            c.dma('sp', Sst[d][:], I['st_ret'][d, h], w=[('S', d)])
                        else:
                            c.op('dve', lambda e: e.memset(Sst[d][:], 0.0), w=[('S', d)])
                        c.op('act', lambda e: e.activation(out=Sbf[d][:], in_=Sst[d][:], func=AF.Identity), r=[('S', d)], w=[('Sbf', d)])
                    for idx in range(nch):
                        cf = idx
                        cb = nch - 1 - idx
                        p2 = idx % 2
                        bS = 1 if p2 == 0 else 6
                        bO = 2 if p2 == 0 else 7
                        fsl = slice(cf * 128, cf * 128 + 128)
                        bsl = slice(cb * 128, cb * 128 + 128)
                        sl = slice(0, 128)
                        c.op('pe', lambda e: e.matmul(PS[bS][:, sl], lhsT=k_bf[:, fsl], rhs=q_bf[:, fsl], start=True, stop=True), r=['k_bf', 'q_bf'], w=[('ps', bS)])
                        c.op('dve', lambda e: e.tensor_tensor(out=sTm[p2][:], in0=PS[bS][:, sl], in1=MT[:, h, :], op=ALU.mult), r=[('ps', bS), 'MT'], w=[('sTm', p2)])
                        c.op('pe', lambda e: e.matmul(PS[bO][:, sl], lhsT=Vtok[:, cf, :], rhs=sTm[p2][:], start=True, stop=False), r=['Vtok', ('sTm', p2)], w=[('ps', bO)])
                        c.op('pe', lambda e: e.matmul(PS[bO][:, sl], lhsT=Sbf[0][:], rhs=qf_[:, fsl], start=False, stop=True), r=[('Sbf', 0), 'qf'], w=[('ps', bO)])
                        c.op('act', lambda e: e.activation(out=o_a[:, fsl], in_=PS[bO][:, sl], func=AF.Identity), r=[('ps', bO)], w=['o_a'])
                        c.op('pe', lambda e: e.matmul(PS[3][:, sl], lhsT=Kf[:, cf, :], rhs=Vtok[:, cf, :], start=True, stop=True), r=['Kf', 'Vtok'], w=[('ps', 3)])
                        c.op('dve', lambda e: e.scalar_tensor_tensor(out=Sst[0][:], in0=Sst[0][:], scalar=GC[:, h:h + 1], in1=PS[3][:, sl], op0=ALU.mult, op1=ALU.add), r=[('S', 0), 'GC', ('ps', 3)], w=[('S', 0)])
                        c.op('act', lambda e: e.activation(out=Sbf[0][:], in_=Sst[0][:], func=AF.Identity), r=[('S', 0)], w=[('Sbf', 0)])
                        c.op('pe', lambda e: e.matmul(PS[4][:, sl], lhsT=Sbf[1][:], rhs=qb_[:, bsl], start=True, stop=True), r=[('Sbf', 1), 'qb'], w=[('ps', 4)])
                        c.op('act', lambda e: e.activation(out=o_b[:, bsl], in_=PS[4][:, sl], func=AF.Identity), r=[('ps', 4)], w=['o_b'])
                        c.op('pe', lambda e: e.matmul(PS[5][:, sl], lhsT=Kb[:, cb, :], rhs=Vtok[:, cb, :], start=True, stop=True), r=['Kb', 'Vtok'], w=[('ps', 5)])
                        c.op('dve', lambda e: e.scalar_tensor_tensor(out=Sst[1][:], in0=Sst[1][:], scalar=GC[:, 4 + h:5 + h], in1=PS[5][:, sl], op0=ALU.mult, op1=ALU.add), r=[('S', 1), 'GC', ('ps', 5)], w=[('S', 1)])
                        c.op('act', lambda e: e.activation(out=Sbf[1][:], in_=Sst[1][:], func=AF.Identity), r=[('S', 1)], w=[('Sbf', 1)])
                    if pidx is not None:
                        for d in range(2):
                            c.dma('sp', O['new_ret'][pidx, d, h], Sst[d][:], r=[('S', d)], w=[('new_ret', pidx, d, h)])
                    if RETSTOP < 3:
                        continue
                    for ti in range(nt):
                        t0 = ti * TS
                        g0 = off + t0
                        tsl = slice(t0, t0 + TS)
                        c.dma('sp', gs[:, :TS], Pv[:, 12 + h, g0:g0 + TS], w=['gs'])
                        c.op('dve', lambda e: e.tensor_tensor(out=t1[:, :TS], in0=o_a[:, tsl], in1=o_b[:, tsl], op=ALU.add), r=['o_a', 'o_b'], w=['t1'])
                        c.op('pe', lambda e: e.matmul(PS[6][:, :TS], lhsT=onesd[:], rhs=t1[:, :TS], start=True, stop=True), r=['onesd', 't1'], w=[('ps', 6)])
                        c.op('dve', lambda e: e.tensor_tensor(out=t2[:, :TS], in0=t1[:, :TS], in1=PS[6][:, :TS], op=ALU.subtract), r=['t1', ('ps', 6)], w=['t2'])
                        c.op('act', lambda e: e.activation(out=t3[:, :TS], in_=t2[:, :TS], func=AF.Square), r=['t2'], w=['t3'])
                        c.op('pe', lambda e: e.matmul(PS[7][:, :TS], lhsT=onesd[:], rhs=t3[:, :TS], start=True, stop=True), r=['onesd', 't3'], w=[('ps', 7)])
                        c.op('act', lambda e: e.activation(out=t3[:, :TS], in_=PS[7][:, :TS], func=AF.Sqrt, bias=self.cst[:, 1:2]), r=[('ps', 7), ('cst', 1)], w=['t3'])
                        c.op('dve', lambda e: e.reciprocal(out=t3[:, :TS], in_=t3[:, :TS]), r=['t3'], w=['t3'])
                        c.op('dve', lambda e: e.tensor_tensor(out=t2[:, :TS], in0=t2[:, :TS], in1=t3[:, :TS], op=ALU.mult), r=['t2', 't3'], w=['t2'])
                        c.op('act', lambda e: e.activation(out=gs[:, :TS], in_=gs[:, :TS], func=AF.Silu), r=['gs'], w=['gs'])
                        c.op('dve', lambda e: e.tensor_tensor(out=ya[:, :TS], in0=t2[:, :TS], in1=gs[:, :TS], op=ALU.mult), r=['t2', 'gs'], w=['ya'])
                        c.dma('sp', Yv[:, h, g0:g0 + TS], ya[:, :TS], r=['ya'], w=[('Y', h, g0)])
            c.barrier()

    def mix_even_lru(self):
        c, nc, I, O = self.c, self.nc, self.I, self.O
        LS = self.LS
        Pv = self.P.rearrange("(m p) t -> p m t", p=128)
        Yv = self.Y.rearrange("(c p) t -> p c t", p=128)
        PS = self.PS
        with contextlib.ExitStack() as ph:
            sb = lambda n, sh, dt: self.sb(ph, 'lr_' + n, sh, dt)
            CW = sb('CW', [128, 4, 4], F32)
            CB = sb('CB', [128, 4], F32)
            LAM = sb('LAM', [128, 2, 4], F32)
            C8 = sb('C8', [128, 2, 4], F32)
            C16 = sb('C16', [128, 2, 4], F32)
            BA = sb('BA', [128, 2, 4], F32)
            BX = sb('BX', [128, 2, 4], F32)
            WA = sb('WA', [128, 8, 128], F32)
            WX = sb('WX', [128, 8, 128], F32)
            H0 = sb('H0', [128, 2, 4], F32)
            xpad = sb('xpad', [128, LS + 3], F32)
            xc = sb('xc', [128, LS], F32)
            a_ = sb('a', [128, LS], F32)
            b_ = sb('b', [128, LS], F32)
            hf = sb('hf', [128, LS], F32)
            hb = sb('hb', [128, LS], F32)
            gb = sb('gb', [128, 512], F32)
            r_ = sb('r', [128, 512], F32)
            i_ = sb('i', [128, 512], F32)
            a2 = sb('a2', [128, 512], F32)
            yb = sb('yb', [128, 512], BF16)
            for j in range(4):
                c.dma('sp', CW[:, :, j], I['l0_conv_w'][j].rearrange("(u p) -> p u", p=128), w=['CW'], allow_slow_non_contiguous=True)
            c.dma('sp', CB[:], I['l0_conv_b'].rearrange("(u p) -> p u", p=128), w=['CB'], allow_slow_non_contiguous=True)
            for d in range(2):
                c.dma('sp', LAM[:, d, :], I['l0_lru_lam'][d].rearrange("(u p) -> p u", p=128), w=['LAM'], allow_slow_non_contiguous=True)
                c.dma('sp', BA[:, d, :], I['l0_lru_ba'][d].rearrange("(u p) -> p u", p=128), w=['BA'], allow_slow_non_contiguous=True)
                c.dma('sp', BX[:, d, :], I['l0_lru_bx'][d].rearrange("(u p) -> p u", p=128), w=['BX'], allow_slow_non_contiguous=True)
                c.dma('sp', H0[:, d, :], I['st_lru'][d].rearrange("(u p) -> p u", p=128), w=['H0'], allow_slow_non_contiguous=True)
            c.op('dve', lambda e: e.memset(WA[:], 0.0), w=['WA'])
            c.op('dve', lambda e: e.memset(WX[:], 0.0), w=['WX'])
            for d in range(2):
                for U in range(4):
                    for g2 in range(2):
                        c.dma('sp', WA[g2 * 64:(g2 + 1) * 64, d * 4 + U, g2 * 64:(g2 + 1) * 64], I['l0_lru_wa'][d, 2 * U + g2], w=['WA'])
                        c.dma('sp', WX[g2 * 64:(g2 + 1) * 64, d * 4 + U, g2 * 64:(g2 + 1) * 64], I['l0_lru_wx'][d, 2 * U + g2], w=['WX'])
            c.op('act', lambda e: e.activation(out=C8[:], in_=LAM[:], func=AF.Sigmoid), r=['LAM'], w=['C8'])
            c.op('act', lambda e: e.activation(out=C8[:], in_=C8[:], func=AF.Ln), r=['C8'], w=['C8'])
            c.op('dve', lambda e: e.tensor_scalar(out=C16[:], in0=C8[:], scalar1=16.0, scalar2=None, op0=ALU.mult), r=['C8'], w=['C16'])
            c.op('dve', lambda e: e.tensor_scalar(out=C8[:], in0=C8[:], scalar1=8.0, scalar2=None, op0=ALU.mult), r=['C8', 'C16'], w=['C8'])
            for sq_ in self.seqs():
                off, L, latent, pidx = sq_['off'], sq_['L'], sq_['latent'], sq_['pidx']
                TS = min(512, L)
                nt = L // TS
                for U in range(4):
                    c.op('dve', lambda e: e.memset(xpad[:, 0:2], 0.0), w=['xpad'])
                    c.op('dve', lambda e: e.memset(xpad[:, L + 2:L + 3], 0.0), w=['xpad'])
                    c.dma('sp', xpad[:, 2:L + 2], Pv[:, 20 + U, off:off + L], w=['xpad'])
                    c.op('dve', lambda e: e.tensor_scalar(out=xc[:, :L], in0=xpad[:, 0:L], scalar1=CW[:, U, 0:1], scalar2=CB[:, U:U + 1], op0=ALU.mult, op1=ALU.add), r=['xpad', 'CW', 'CB'], w=['xc'])
                    for j in range(1, 4):
                        c.op('dve', lambda e: e.scalar_tensor_tensor(out=xc[:, :L], in0=xpad[:, j:j + L], scalar=CW[:, U, j:j + 1], in1=xc[:, :L], op0=ALU.mult, op1=ALU.add), r=['xpad', 'CW', 'xc'], w=['xc'])
                    for d in range(2):
                        for ti in range(nt):
                            tsl = slice(ti * TS, (ti + 1) * TS)
                            c.op('pe', lambda e: e.matmul(PS[1][:, :TS], lhsT=WA[:, d * 4 + U, :], rhs=xc[:, tsl], start=True, stop=True), r=['WA', 'xc'], w=[('ps', 1)])
                            c.op('pe', lambda e: e.matmul(PS[2][:, :TS], lhsT=WX[:, d * 4 + U, :], rhs=xc[:, tsl], start=True, stop=True), r=['WX', 'xc'], w=[('ps', 2)])
                            c.op('act', lambda e: e.activation(out=r_[:, :TS], in_=PS[1][:, :TS], func=AF.Sigmoid, bias=BA[:, d, U:U + 1]), r=[('ps', 1), 'BA'], w=['r'])
                            c.op('act', lambda e: e.activation(out=i_[:, :TS], in_=PS[2][:, :TS], func=AF.Sigmoid, bias=BX[:, d, U:U + 1]), r=[('ps', 2), 'BX'], w=['i'])
                            c.op('act', lambda e: e.activation(out=a_[:, tsl], in_=r_[:, :TS], func=AF.Exp, scale=C8[:, d, U:U + 1]), r=['r', 'C8'], w=['a'])
                            c.op('act', lambda e: e.activation(out=a2[:, :TS], in_=r_[:, :TS], func=AF.Exp, scale=C16[:, d, U:U + 1]), r=['r', 'C16'], w=['a2'])
                            c.op('dve', lambda e: e.tensor_scalar(out=a2[:, :TS], in0=a2[:, :TS], scalar1=-1.0, scalar2=1.0, op0=ALU.mult, op1=ALU.add), r=['a2'], w=['a2'])
                            c.op('act', lambda e: e.activation(out=a2[:, :TS], in_=a2[:, :TS], func=AF.Sqrt), r=['a2'], w=['a2'])
                            c.op('dve', lambda e: e.tensor_tensor(out=i_[:, :TS], in0=i_[:, :TS], in1=a2[:, :TS], op=ALU.mult), r=['i', 'a2'], w=['i'])
                            c.op('dve', lambda e: e.tensor_tensor(out=b_[:, tsl], in0=i_[:, :TS], in1=xc[:, tsl], op=ALU.mult), r=['i', 'xc'], w=['b'])
                        init = H0[:, d, U:U + 1] if latent else 0.0
                        if d == 0:
                            c.op('dve', lambda e: e.tensor_tensor_scan(out=hf[:, 0:L], data0=a_[:, 0:L], data1=b_[:, 0:L], initial=init, op0=ALU.mult, op1=ALU.add), r=['a', 'b', 'H0'], w=['hf'])
                        else:
                            c.op('dve', lambda e: e.tensor_tensor_scan(out=hb[:, 0:L][:, ::-1], data0=a_[:, 0:L][:, ::-1], data1=b_[:, 0:L][:, ::-1], initial=init, op0=ALU.mult, op1=ALU.add), r=['a', 'b', 'H0'], w=['hb'])
                    if pidx is not None:
                        c.dma('sp', O['new_lru'][pidx, 0, U * 128:(U + 1) * 128].rearrange("(p o) -> p o", o=1), hf[:, L - 1:L], r=['hf'], w=[('new_lru', pidx, 0, U)], allow_slow_non_contiguous=True)
                        c.dma('sp', O['new_lru'][pidx, 1, U * 128:(U + 1) * 128].rearrange("(p o) -> p o", o=1), hb[:, 0:1], r=['hb'], w=[('new_lru', pidx, 1, U)], allow_slow_non_contiguous=True)
                    for ti in range(nt):
                        tsl = slice(ti * TS, (ti + 1) * TS)
                        g0 = off + ti * TS
                        c.dma('sp', gb[:, :TS], Pv[:, 16 + U, g0:g0 + TS], w=['gb'])
                        c.op('act', lambda e: e.activation(out=gb[:, :TS], in_=gb[:, :TS], func=AF.Gelu_apprx_tanh), r=['gb'], w=['gb'])
                        c.op('dve', lambda e: e.tensor_tensor(out=r_[:, :TS], in0=hf[:, tsl], in1=hb[:, tsl], op=ALU.add), r=['hf', 'hb'], w=['r'])
                        c.op('dve', lambda e: e.tensor_tensor(out=yb[:, :TS], in0=r_[:, :TS], in1=gb[:, :TS], op=ALU.mult), r=['r', 'gb'], w=['yb'])
                        c.dma('sp', Yv[:, 4 + U, g0:g0 + TS], yb[:, :TS], r=['yb'], w=[('Y', 4 + U, g0)])
            c.barrier()

    def phase_mix_odd(self):
        import os
        which = os.environ.get('ODDPARTS', 'sr')
        if which != 'sr':
            self.mix_stub()
        if 's' in which:
            self.mix_odd_s5()
        if 'r' in which:
            self.mix_odd_rwkv()

    def sincos(self, sb_, turns, shape, out_s, out_c, keys_r, key_s, key_c, tmpf, tmpi, tkey='sc_tmpf'):
        c = self.c
        TWO_PI = 6.283184
        for (dst, dkey, shift) in ((out_s, key_s, 0.0), (out_c, key_c, 0.25)):
            c.op('dve', lambda e: e.tensor_scalar(out=tmpf, in0=turns, scalar1=shift, scalar2=None, op0=ALU.add), r=keys_r, w=[tkey])
            c.op('dve', lambda e: e.tensor_copy(out=tmpi, in_=tmpf), r=[tkey], w=['sc_tmpi'])
            c.op('dve', lambda e: e.tensor_copy(out=dst, in_=tmpi), r=['sc_tmpi'], w=[dkey])
            c.op('dve', lambda e: e.tensor_tensor(out=dst, in0=tmpf, in1=dst, op=ALU.subtract), r=[tkey, dkey], w=[dkey])
            c.op('act', lambda e: e.activation(out=dst, in_=dst, func=AF.Sin, scale=TWO_PI), r=[dkey], w=[dkey])

    def mix_odd_s5(self):
        c, nc, I, O = self.c, self.nc, self.I, self.O
        NT, NP = self.NT, self.NP
        Pv = self.P.rearrange("(m p) t -> p m t", p=128)
        Yv = self.Y.rearrange("(c p) t -> p c t", p=128)
        PS = self.PS
        CBM = 256
        with contextlib.ExitStack() as ph:
            sb = lambda n, sh, dt: self.sb(ph, 's5_' + n, sh, dt)
            ARE = sb('are', [128, 2, 16], F32)
            AIM = sb('aim', [128, 2, 16], F32)
            DT = sb('dt', [128, 2, 16], F32)
            MAG = sb('mag', [128, 2, 16], F32)
            THT = sb('tht', [128, 2, 16], F32)
            CS = sb('cs', [128, 2, 16], F32)
            SN = sb('sn', [128, 2, 16], F32)
            ABR = sb('abr', [128, 2, 16], F32)
            ABI = sb('abi', [128, 2, 16], F32)
            DEN = sb('den', [128, 2, 16], F32)
            FR = sb('fr', [128, 2, 16], F32)
            FI = sb('fi', [128, 2, 16], F32)
            p1 = sb('p1', [128, 2, 16], F32)
            p2 = sb('p2', [128, 2, 16], F32)
            pI = sb('pI', [128, 2, 16], I32)
            BRE = sb('bre', [128, 2, 16, 16], F32)
            BIM = sb('bim', [128, 2, 16, 16], F32)
            BBR = sb('bbr', [128, 2, 16, 16], F32)
            BBI = sb('bbi', [128, 2, 16, 16], F32)
            bt1 = sb('bt1', [128, 16, 16], F32)
            CRE = sb('cre', [128, 2, 4, 64], F32)
            CIM = sb('cim', [128, 2, 4, 64], F32)
            Bx = [sb('Bx%d' % q, [128, 128], F32) for q in range(4)]
            Cx = [sb('Cx%d' % q, [128, 128], F32) for q in range(4)]
            LB = sb('LB', [128, 4, 2, 128], BF16)
            LC = sb('LC', [128, 4, 2, 128], BF16)
            ST0 = sb('st0', [128, 2, 2, 16], F32)
            CMASK = sb('cmask', [128, 4, 128], F32)
            FIN = sb('fin', [128, max(NP, 2) * 64], F32)
            FINT = sb('fint', [128, 128], F32)
            IOTA = sb('iota', [128, 512], F32)
            COST = sb('cost', [128, 4, CBM], F32)
            SINT = sb('sint', [128, 4, CBM], F32)
            RHOT = sb('rhot', [128, 4, CBM], F32)
            ti_ = sb('ti', [128, CBM], I32)
            u_bf = sb('u_bf', [128, 4, NT], BF16)
            y_acc = sb('y_acc', [128, 4, NT], F32)
            w_ = {n: sb('w_' + n, [128, CBM], F32) for n in ['br', 'bi', 'hr', 'hi', 't1', 't2', 't3', 't4', 'or', 'oi', 'p3', 'p4']}
            hb_ = {n: sb('hb_' + n, [128, CBM], BF16) for n in ['r', 'i']}
            CAR = sb('car', [128, 4, 2], F32)
            tf, tfs, uf, sg = w_['t3'], w_['t4'], w_['or'], w_['oi']
            SD = sb('sd', [128, 4], F32)
            GB = sb('gb', [128, 4], F32)
            GW = sb('gw', [128, 4, 512], BF16)
            yc = sb('yc', [128, CBM], BF16)

            for d in range(2):
                c.dma('sp', ARE[:, d, :], I['l1_s5_a_re'][d].rearrange("(T g) n -> (g n) T", g=2), w=['ARE'], allow_slow_non_contiguous=True)
                c.dma('sp', AIM[:, d, :], I['l1_s5_a_im'][d].rearrange("(T g) n -> (g n) T", g=2), w=['AIM'], allow_slow_non_contiguous=True)
                ldt = I['l1_s5_log_dt'][d].rearrange("(T g) -> g T", g=2)
                for g2 in range(2):
                    c.dma('sp', DT[g2 * 64:(g2 + 1) * 64, d, :], ldt[g2].partition_broadcast(64), w=['DT'], allow_slow_non_contiguous=True)
                c.dma('sp', BRE[:, d, :, :], I['l1_s5_b_re'][d].rearrange("(T g) n s -> (g n) T s", g=2), w=['BRE'])
                c.dma('sp', BIM[:, d, :, :], I['l1_s5_b_im'][d].rearrange("(T g) n s -> (g n) T s", g=2), w=['BIM'])
                c.dma('sp', CRE[:, d, :, :], I['l1_s5_c_re'][d].rearrange("(U g) s n -> (g s) U n", g=8), w=['CRE'])
                c.dma('sp', CIM[:, d, :, :], I['l1_s5_c_im'][d].rearrange("(U g) s n -> (g s) U n", g=8), w=['CIM'])
                for ri in range(2):
                    c.dma('sp', ST0[:, d, ri, :], I['st_s5'][d, ri].rearrange("(T g) n -> (g n) T", g=2), w=['ST0'], allow_slow_non_contiguous=True)
            c.dma('sp', IOTA[:], I['c_iota'][:, :], w=['IOTA'])
            c.dma('sp', CMASK[:], I['c_s5mask'].rearrange("p (q n) -> p q n", q=4), w=['CMASK'])
            c.dma('sp', SD[:], I['l1_s5_d'].rearrange("(u p) -> p u", p=128), w=['SD'], allow_slow_non_contiguous=True)
            c.dma('sp', GB[:], I['l1_glu_b'].rearrange("(u p) -> p u", p=128), w=['GB'], allow_slow_non_contiguous=True)
            c.dma('pool', GW[:], I['l1_glu_w'].rearrange("(k p) n -> p k n", p=128), w=['GW'])
            for U in range(4):
                c.dma('pool', u_bf[:, U, :], Pv[:, U, :], w=['u_bf'])
            for q in range(4):
                c.op('dve', lambda e: e.memset(Bx[q][:], 0.0), w=[('Bx', q)])
                c.op('dve', lambda e: e.memset(Cx[q][:], 0.0), w=[('Cx', q)])
            c.op('dve', lambda e: e.memset(FIN[:], 0.0), w=['FIN'])
            c.op('act', lambda e: e.activation(out=DT[:], in_=DT[:], func=AF.Exp), r=['DT'], w=['DT'])
            c.op('dve', lambda e: e.tensor_tensor(out=p1[:], in0=ARE[:], in1=DT[:], op=ALU.mult), r=['ARE', 'DT'], w=['p1'])
            c.op('act', lambda e: e.activation(out=MAG[:], in_=p1[:], func=AF.Exp), r=['p1'], w=['MAG'])
            c.op('dve', lambda e: e.scalar_tensor_tensor(out=THT[:], in0=AIM[:], scalar=1.0 / (2.0 * math.pi), in1=DT[:], op0=ALU.mult, op1=ALU.mult), r=['AIM', 'DT'], w=['THT'])
            self.sincos(sb, THT[:], None, SN[:], CS[:], ['THT'], 'SN', 'CS', p2[:], pI[:])
            c.op('dve', lambda e: e.tensor_tensor(out=ABR[:], in0=MAG[:], in1=CS[:], op=ALU.mult), r=['MAG', 'CS'], w=['ABR'])
            c.op('dve', lambda e: e.tensor_tensor(out=ABI[:], in0=MAG[:], in1=SN[:], op=ALU.mult), r=['MAG', 'SN'], w=['ABI'])
            c.op('dve', lambda e: e.tensor_tensor(out=DEN[:], in0=ARE[:], in1=ARE[:], op=ALU.mult), r=['ARE'], w=['DEN'])
            c.op('dve', lambda e: e.tensor_tensor(out=p1[:], in0=AIM[:], in1=AIM[:], op=ALU.mult), r=['AIM', 'MAG'], w=['p1'])
            c.op('dve', lambda e: e.tensor_tensor(out=DEN[:], in0=DEN[:], in1=p1[:], op=ALU.add), r=['DEN', 'p1'], w=['DEN'])
            c.op('dve', lambda e: e.reciprocal(out=DEN[:], in_=DEN[:]), r=['DEN'], w=['DEN'])
            c.op('dve', lambda e: e.tensor_scalar(out=p1[:], in0=ABR[:], scalar1=-1.0, scalar2=None, op0=ALU.add), r=['ABR', 'DEN'], w=['p1'])
            c.op('dve', lambda e: e.tensor_tensor(out=FR[:], in0=p1[:], in1=ARE[:], op=ALU.mult), r=['p1', 'ARE'], w=['FR'])
            c.op('dve', lambda e: e.tensor_tensor(out=p2[:], in0=ABI[:], in1=AIM[:], op=ALU.mult), r=['ABI', 'AIM', 'SN', 'CS'], w=['p2'])
            c.op('dve', lambda e: e.tensor_tensor(out=FR[:], in0=FR[:], in1=p2[:], op=ALU.add), r=['FR', 'p2'], w=['FR'])
            c.op('dve', lambda e: e.tensor_tensor(out=FR[:], in0=FR[:], in1=DEN[:], op=ALU.mult), r=['FR', 'DEN'], w=['FR'])
            c.op('dve', lambda e: e.tensor_tensor(out=FI[:], in0=ABI[:], in1=ARE[:], op=ALU.mult), r=['ABI', 'ARE'], w=['FI'])
            c.op('dve', lambda e: e.tensor_tensor(out=p2[:], in0=p1[:], in1=AIM[:], op=ALU.mult), r=['p1', 'AIM', 'FR'], w=['p2'])
            c.op('dve', lambda e: e.tensor_tensor(out=FI[:], in0=FI[:], in1=p2[:], op=ALU.subtract), r=['FI', 'p2'], w=['FI'])
            c.op('dve', lambda e: e.tensor_tensor(out=FI[:], in0=FI[:], in1=DEN[:], op=ALU.mult), r=['FI', 'DEN'], w=['FI'])
            for d in range(2):
                frb = FR[:, d, :].unsqueeze(2).broadcast_to([128, 16, 16])
                fib = FI[:, d, :].unsqueeze(2).broadcast_to([128, 16, 16])
                c.op('dve', lambda e: e.tensor_tensor(out=BBR[:, d], in0=BRE[:, d], in1=frb, op=ALU.mult), r=['BRE', 'FR'], w=['BBR'])
                c.op('dve', lambda e: e.tensor_tensor(out=bt1[:], in0=BIM[:, d], in1=fib, op=ALU.mult), r=['BIM', 'FI'], w=['bt1'])
                c.op('dve', lambda e: e.tensor_tensor(out=BBR[:, d], in0=BBR[:, d], in1=bt1[:], op=ALU.subtract), r=['BBR', 'bt1'], w=['BBR'])
                c.op('dve', lambda e: e.tensor_tensor(out=BBI[:, d], in0=BIM[:, d], in1=frb, op=ALU.mult), r=['BIM', 'FR'], w=['BBI'])
                c.op('dve', lambda e: e.tensor_tensor(out=bt1[:], in0=BRE[:, d], in1=fib, op=ALU.mult), r=['BRE', 'FI', 'BBR'], w=['bt1'])
                c.op('dve', lambda e: e.tensor_tensor(out=BBI[:, d], in0=BBI[:, d], in1=bt1[:], op=ALU.add), r=['BBI', 'bt1'], w=['BBI'])

            seqs = self.seqs()
            first_contrib = {}
            ucnt = [0]
            w_sets = [w_, {n: sb('w2_' + n, [128, CBM], F32) for n in ['br', 'bi', 'hr', 'hi', 't1', 't2', 't3', 't4', 'or', 'oi', 'p3', 'p4']}]
            hb_sets = [hb_, {n: sb('hb2_' + n, [128, CBM], BF16) for n in ['r', 'i']}]
            pend_y = []

            def flush_y():
                while pend_y:
                    ykey, isfirst, U_, b0_, CB_ = pend_y.pop(0)
                    if isfirst:
                        c.op('act', lambda e: e.activation(out=y_acc[:, U_, b0_:b0_ + CB_], in_=PS[5][:, :CB_], func=AF.Identity), r=[('ps', 5)], w=[ykey])
                    else:
                        c.op('dve', lambda e: e.tensor_tensor(out=y_acc[:, U_, b0_:b0_ + CB_], in0=y_acc[:, U_, b0_:b0_ + CB_], in1=PS[5][:, :CB_], op=ALU.add), r=[('ps', 5), ykey], w=[ykey])
            for d in range(2):
                for U in range(4):
                    for q in range(4):
                        T = U * 4 + q
                        for ri, BB in ((0, BBR), (1, BBI)):
                            for g2 in range(2):
                                ps_ = slice(g2 * 64, (g2 + 1) * 64)
                                cs_ = slice((2 * q + g2) * 16, (2 * q + g2) * 16 + 16)
                                c.op('dve', lambda e: e.tensor_copy(out=Bx[q][ps_, cs_], in_=BB[ps_, d, T, :]), r=['BBR', 'BBI'], w=[('Bx', q)])
                            c.op('pe', lambda e: e.transpose(out=PS[1][:, 0:128], in_=Bx[q][:], identity=self.ident[:]), r=[('Bx', q), 'ident'], w=[('ps', 1)])
                            c.op('act', lambda e: e.activation(out=LB[:, q, ri, :], in_=PS[1][:, 0:128], func=AF.Identity), r=[('ps', 1)], w=['LB'])
                        for ri, CC, sgn in ((0, CRE, 1.0), (1, CIM, -1.0)):
                            c.op('dve', lambda e: e.scalar_tensor_tensor(out=Cx[q][:].rearrange("p (g n) -> p g n", g=2), in0=CC[:, d, U, :].unsqueeze(1).broadcast_to([128, 2, 64]), scalar=sgn,
                                                                         in1=CMASK[:, q, :].rearrange("p (g n) -> p g n", g=2), op0=ALU.mult, op1=ALU.mult), r=['CRE', 'CIM', 'CMASK'], w=[('Cx', q)])
                            c.op('pe', lambda e: e.transpose(out=PS[2][:, 0:128], in_=Cx[q][:], identity=self.ident[:]), r=[('Cx', q), 'ident'], w=[('ps', 2)])
                            c.op('act', lambda e: e.activation(out=LC[:, q, ri, :], in_=PS[2][:, 0:128], func=AF.Identity), r=[('ps', 2)], w=['LC'])
                        c.op('dve', lambda e: e.tensor_scalar(out=tf[:], in0=IOTA[:, :CBM], scalar1=THT[:, d, T:T + 1], scalar2=None, op0=ALU.mult), r=['IOTA', 'THT'], w=[('t3', 0)])
                        self.sincos(sb, tf[:], None, SINT[:, q, :], COST[:, q, :], [('t3', 0)], ('SINT', q), ('COST', q), tfs[:], ti_[:], tkey=('t4', 0))
                        c.op('dve', lambda e: e.tensor_scalar(out=RHOT[:, q, :], in0=IOTA[:, :CBM], scalar1=0.0, scalar2=MAG[:, d, T:T + 1], op0=ALU.mult, op1=ALU.add), r=['IOTA', 'MAG'], w=[('RHOT', q)])
                    units = []
                    for sq_ in seqs:
                        CB_ = min(CBM, sq_['L'])
                        nb_ = sq_['L'] // CB_
                        for bi_ in range(nb_):
                            for q in range(4):
                                units.append((sq_, bi_, q, ucnt[0] % 2))
                                ucnt[0] += 1

                    def unpack(u):
                        sq_, bi_, q, up = u
                        off, L, latent, pidx = sq_['off'], sq_['L'], sq_['latent'], sq_['pidx']
                        CB = min(CBM, L)
                        nb = L // CB
                        blk = bi_ if d == 0 else nb - 1 - bi_
                        b0 = off + blk * CB
                        T = U * 4 + q
                        pB0, pB1 = (3, 4) if up == 0 else (6, 7)
                        return off, L, latent, pidx, CB, nb, bi_, b0, q, T, up, pB0, pB1

                    def emit_B(u):
                        off, L, latent, pidx, CB, nb, bi_, b0, q, T, up, pB0, pB1 = unpack(u)
                        c.op('pe', lambda e: e.matmul(PS[pB0][:, :CB], lhsT=LB[:, q, 0, :], rhs=u_bf[:, U, b0:b0 + CB], start=True, stop=True), r=['LB', 'u_bf'], w=[('ps', pB0)])
                        c.op('pe', lambda e: e.matmul(PS[pB1][:, :CB], lhsT=LB[:, q, 1, :], rhs=u_bf[:, U, b0:b0 + CB], start=True, stop=True), r=['LB', 'u_bf'], w=[('ps', pB1)])

                    def emit_rest(u, nxt):
                        off, L, latent, pidx, CB, nb, bi_, b0, q, T, up, pB0, pB1 = unpack(u)
                        w_ = w_sets[up]
                        hb_ = hb_sets[up]
                        Dv = (lambda x: x[:, 0:CB]) if d == 0 else (lambda x: x[:, 0:CB][:, ::-1])
                        cosv = COST[:, q, :CB]
                        sinv = SINT[:, q, :CB]
                        c.op('dve', lambda e: e.tensor_tensor(out=w_['t1'][:, :CB], in0=Dv(PS[pB0]), in1=cosv, op=ALU.mult), r=[('ps', pB0), ('COST', q)], w=[('t1', up)])
                        c.op('dve', lambda e: e.tensor_tensor(out=w_['t2'][:, :CB], in0=Dv(PS[pB1]), in1=sinv, op=ALU.mult), r=[('ps', pB1), ('SINT', q)], w=[('t2', up)])
                        c.op('dve', lambda e: e.tensor_tensor(out=w_['br'][:, :CB], in0=w_['t1'][:, :CB], in1=w_['t2'][:, :CB], op=ALU.add), r=[('t1', up), ('t2', up)], w=[('br', up)])
                        c.op('dve', lambda e: e.tensor_tensor(out=w_['t3'][:, :CB], in0=Dv(PS[pB1]), in1=cosv, op=ALU.mult), r=[('ps', pB1), ('COST', q)], w=[('t3', up)])
                        c.op('dve', lambda e: e.tensor_tensor(out=w_['t4'][:, :CB], in0=Dv(PS[pB0]), in1=sinv, op=ALU.mult), r=[('ps', pB0), ('SINT', q)], w=[('t4', up)])
                        c.op('dve', lambda e: e.tensor_tensor(out=w_['bi'][:, :CB], in0=w_['t3'][:, :CB], in1=w_['t4'][:, :CB], op=ALU.subtract), r=[('t3', up), ('t4', up)], w=[('bi', up)])
                        flush_y()
                        if nxt is not None:
                            emit_B(nxt)
                        if bi_ == 0:
                            if latent:
                                ir, ii = ST0[:, d, 0, T:T + 1], ST0[:, d, 1, T:T + 1]
                            else:
                                ir, ii = 0.0, 0.0
                        else:
                            ir, ii = CAR[:, q, 0:1], CAR[:, q, 1:2]
                        c.op('dve', lambda e: e.tensor_tensor_scan(out=w_['hr'][:, :CB], data0=RHOT[:, q, :CB], data1=w_['br'][:, :CB], initial=ir, op0=ALU.mult, op1=ALU.add), r=[('RHOT', q), ('br', up), ('CAR', q), 'ST0'], w=[('hr', up)])
                        c.op('dve', lambda e: e.tensor_tensor_scan(out=w_['hi'][:, :CB], data0=RHOT[:, q, :CB], data1=w_['bi'][:, :CB], initial=ii, op0=ALU.mult, op1=ALU.add), r=[('RHOT', q), ('bi', up), ('CAR', q), 'ST0'], w=[('hi', up)])
                        c.op('dve', lambda e: e.tensor_tensor(out=w_['t1'][:, :CB], in0=w_['hr'][:, :CB], in1=cosv, op=ALU.mult), r=[('hr', up), ('COST', q)], w=[('t1', up)])
                        c.op('dve', lambda e: e.tensor_tensor(out=w_['t2'][:, :CB], in0=w_['hi'][:, :CB], in1=sinv, op=ALU.mult), r=[('hi', up), ('SINT', q)], w=[('t2', up)])
                        c.op('dve', lambda e: e.tensor_tensor(out=w_['or'][:, :CB], in0=w_['t1'][:, :CB], in1=w_['t2'][:, :CB], op=ALU.subtract), r=[('t1', up), ('t2', up)], w=[('or', up)])
                        c.op('pool', lambda e: e.tensor_tensor(out=w_['p3'][:, :CB], in0=w_['hi'][:, :CB], in1=cosv, op=ALU.mult), r=[('hi', up), ('COST', q)], w=[('p3', up)])
                        c.op('pool', lambda e: e.tensor_tensor(out=w_['p4'][:, :CB], in0=w_['hr'][:, :CB], in1=sinv, op=ALU.mult), r=[('hr', up), ('SINT', q)], w=[('p4', up)])
                        c.op('pool', lambda e: e.tensor_tensor(out=w_['oi'][:, :CB], in0=w_['p3'][:, :CB], in1=w_['p4'][:, :CB], op=ALU.add), r=[('p3', up), ('p4', up)], w=[('oi', up)])
                        c.op('dve', lambda e: e.tensor_copy(out=CAR[:, q, 0:1], in_=w_['or'][:, CB - 1:CB]), r=[('or', up)], w=[('CAR', q)])
                        c.op('pool', lambda e: e.tensor_copy(out=CAR[:, q, 1:2], in_=w_['oi'][:, CB - 1:CB]), r=[('oi', up)], w=[('CAR', q)])
                        c.op('act', lambda e: e.activation(out=Dv(hb_['r']), in_=w_['or'][:, :CB], func=AF.Identity), r=[('or', up)], w=[('hb_r', up)])
                        c.op('act', lambda e: e.activation(out=Dv(hb_['i']), in_=w_['oi'][:, :CB], func=AF.Identity), r=[('oi', up)], w=[('hb_i', up)])
                        c.op('pe', lambda e: e.matmul(PS[5][:, :CB], lhsT=LC[:, q, 0, :], rhs=hb_['r'][:, :CB], start=(q == 0), stop=False), r=['LC', ('hb_r', up)], w=[('ps', 5)])
                        c.op('pe', lambda e: e.matmul(PS[5][:, :CB], lhsT=LC[:, q, 1, :], rhs=hb_['i'][:, :CB], start=False, stop=(q == 3)), r=['LC', ('hb_i', up)], w=[('ps', 5)])
                        if pidx is not None and bi_ == nb - 1:
                            for ri in range(2):
                                col = ((pidx * 2 + d) * 2 + ri) * 16 + T
                                c.op('dve', lambda e: e.tensor_copy(out=FIN[:, col:col + 1], in_=CAR[:, q, ri:ri + 1]), r=[('CAR', q)], w=['FIN'])
                        if q == 3:
                            ykey = ('y_acc', U, b0)
                            isfirst = ykey not in first_contrib
                            first_contrib[ykey] = True
                            pend_y.append((ykey, isfirst, U, b0, CB))


                    emit_B(units[0])
                    for ui, u in enumerate(units):
                        emit_rest(u, units[ui + 1] if ui + 1 < len(units) else None)
            flush_y()
            ns5 = O['new_s5'].rearrange("b d r (T g) n -> (b d r T) (g n)", g=2)
            for blk in range((NP * 64 + 127) // 128):
                ncol = min(128, NP * 64 - blk * 128)
                c.op('pe', lambda e: e.transpose(out=PS[1][:ncol, 0:128], in_=FIN[:, blk * 128:blk * 128 + ncol], identity=self.ident[:]), r=['FIN', 'ident'], w=[('ps', 1)])
                c.op('act', lambda e: e.activation(out=FINT[:ncol, :], in_=PS[1][:ncol, 0:128], func=AF.Identity), r=[('ps', 1)], w=['FINT'])
                c.dma('sp', ns5[blk * 128:blk * 128 + ncol, :], FINT[:ncol, :], r=['FINT'], w=[('ns5', blk)])
            z_bf = u_bf
            for tt in range(NT // CBM):
                t0 = tt * CBM
                tsl = slice(t0, t0 + CBM)
                for U in range(4):
                    c.dma('sp', uf[:], Pv[:, U, tsl], w=[('or', 0)])
                    c.op('dve', lambda e: e.scalar_tensor_tensor(out=y_acc[:, U, tsl], in0=uf[:], scalar=SD[:, U:U + 1], in1=y_acc[:, U, tsl], op0=ALU.mult, op1=ALU.add), r=[('or', 0), 'SD'] + [k for k in first_contrib if k[1] == U], w=[('z', U, tt)])
                    c.op('act', lambda e: e.activation(out=y_acc[:, U, tsl], in_=y_acc[:, U, tsl], func=AF.Gelu_apprx_tanh), r=[('z', U, tt)], w=[('z', U, tt)])
                    c.op('act', lambda e: e.activation(out=z_bf[:, U, tsl], in_=y_acc[:, U, tsl], func=AF.Identity), r=[('z', U, tt), 'u_bf'], w=[('zb', U, tt)])
                for fo in range(4):
                    for k in range(4):
                        c.op('pe', lambda e: e.matmul(PS[6][:, :CBM], lhsT=GW[:, k, fo * 128:(fo + 1) * 128], rhs=z_bf[:, k, tsl], start=(k == 0), stop=(k == 3)), r=['GW', ('zb', k, tt)], w=[('ps', 6)])
                    c.op('act', lambda e: e.activation(out=sg[:], in_=PS[6][:, :CBM], func=AF.Sigmoid, bias=GB[:, fo:fo + 1]), r=[('ps', 6), 'GB'], w=[('oi', 0)])
                    c.op('dve', lambda e: e.tensor_tensor(out=yc[:], in0=sg[:], in1=y_acc[:, fo, tsl], op=ALU.mult), r=[('oi', 0), ('z', fo, tt)], w=['yc'])
                    c.dma('sp', Yv[:, fo, tsl], yc[:], r=['yc'], w=[('Y', fo, tt)])
            c.barrier()

    def mix_odd_rwkv(self):
        c, nc, I, O = self.c, self.nc, self.I, self.O
        NT, NP = self.NT, self.NP
        Pv = self.P.rearrange("(m p) t -> p m t", p=128)
        Yv = self.Y.rearrange("(c p) t -> p c t", p=128)
        YFv = self.YF.rearrange("(c p) t -> p c t", p=128)
        BFv = self.BFs.rearrange("(c p) t -> p c t", p=128)
        PS = self.PS
        TBM = 256
        RW = F32
        EM05 = math.exp(-0.5)
        with contextlib.ExitStack() as ph:
            sb = lambda n, sh, dt: self.sb(ph, 'rw_' + n, sh, dt)
            misc = sb('misc', [128, 1024], F32)
            CMF = sb('cmf', [128, 512], F32)
            BLK = sb('blk', [128, 128], F32)
            MU = sb('mu', [128, 6, 4], F32)
            MUH = sb('muh', [128, 6, 4], F32)
            OMU = sb('omu', [128, 6, 4], F32)
            W0 = sb('w0', [128, 2, 4], F32)
            A0 = sb('a0', [128, 2, 4], F32)
            KKc = sb('kkc', [128, 4], F32)
            KAc = sb('kac', [128, 4], F32)
            OMKA = sb('omka', [128, 4], F32)
            RKc = sb('rkc', [128, 4], F32)
            LNW = sb('lnw', [128, 4], F32)
            LNB = sb('lnb', [128, 4], F32)
            W1 = sb('w1', [128, 2, 4, 64], BF16)
            A1 = sb('a1', [128, 2, 4, 64], BF16)
            W2 = sb('w2', [64, 2, 512], BF16)
            A2 = sb('a2', [64, 2, 512], BF16)
            G1 = sb('g1', [128, 4, 128], BF16)
            G2 = sb('g2', [128, 512], BF16)
            xdp = sb('xdp', [128, TBM + 2], F32)
            zp = {n: sb('zp_' + n, [128, TBM + 2], F32) for n in 'rkv'}
            cs = sb('cs', [128, TBM], F32)
            xw = sb('xw', [128, 4, TBM], BF16)
            xa = sb('xa', [128, 4, TBM], BF16)
            xg = sb('xg', [128, 4, TBM], BF16)
            hw = sb('hw', [64, TBM], BF16)
            ha = sb('ha', [64, TBM], BF16)
            hg = sb('hg', [128, TBM], BF16)
            names = ['rp', 'kp', 'vp', 'LW', 'a', 'kk', 'kd', 'akk', 'G', 'EG', 'EGN', 'EGm', 'AT', 'KAT', 'KT', 'RT', 'y', 't1', 't2', 'gg', 'bon']
            A_ = {n: sb('A_' + n, [128, TBM], F32) for n in names}
            yb = sb('yb', [128, TBM], BF16)
            NCN = 4
            Qb = [[sb('Q%d%d' % (n, g), [128, 128], RW) for g in range(2)] for n in range(NCN)]
            QTb = [[sb('QT%d%d' % (n, g), [128, 128], RW) for g in range(2)] for n in range(NCN)]
            Zb = [sb('Z%d' % n, [128, 128], RW) for n in range(NCN)]
            ZTb = [sb('ZT%d' % n, [128, 128], RW) for n in range(NCN)]
            Eu = [[sb('Eu%d%d' % (n, lv), [128, 128], RW) for lv in range(3)] for n in range(NCN)]
            El = [[sb('El%d%d' % (n, lv), [128, 128], RW) for lv in range(3)] for n in range(NCN)]
            Fb = [sb('F%d' % n, [128, 128], RW) for n in range(NCN)]
            Fpb = [sb('Fp%d' % n, [128, 128], RW) for n in range(NCN)]
            RWM = sb('rwm', [128, 8, 128], F32)
            A2T = [sb('A2T%d' % n, [128, 128], RW) for n in range(NCN)]
            B1T = [[sb('B1T%d_%d' % (pp, n), [128, 128], RW) for n in range(NCN)] for pp in range(2)]
            B2T = [[sb('B2T%d_%d' % (pp, n), [128, 128], RW) for n in range(NCN)] for pp in range(2)]
            Vx = [[sb('Vx%d_%d' % (pp, n), [128, 128], RW) for n in range(NCN)] for pp in range(2)]
            KAx = [sb('KAx%d' % n, [128, 128], RW) for n in range(NCN)]
            Ax = [[sb('Ax%d_%d' % (pp, n), [128, 128], RW) for n in range(NCN)] for pp in range(2)]
            Kx = [[sb('Kx%d_%d' % (pp, n), [128, 128], RW) for n in range(NCN)] for pp in range(2)]
            Ux = [sb('Ux%d' % hh, [128, 128], RW) for hh in range(2)]
            Xs = [sb('Xs%d' % n, [128, 64], RW) for n in range(NCN)]
            Uvs = [[sb('Uvs%d_%d' % (pp, n), [128, 64], RW) for n in range(NCN)] for pp in range(2)]
            WTs = [[sb('WTs%d_%d' % (pp, n), [128, 128], RW) for n in range(NCN)] for pp in range(2)]
            Ptmp = [sb('Ptmp%d' % hh, [128, 64], RW) for hh in range(2)]
            P2 = [sb('P2_%d' % U, [128, 128], RW) for U in range(4)]
            Sld = sb('Sld', [128, 128], F32)
            Sout = sb('Sout', [128, 128], F32)

            c.dma('sp', misc[:], I['c_misc'][:, :], w=['misc'])
            c.dma('sp', CMF[:], I['c_cmf'][:, :], w=['CMF'])
            c.dma('sp', BLK[:], I['c_blk'][:, :], w=['BLK'])
            c.dma('sp', RWM[:], I['c_rwm'].rearrange("p (q n) -> p q n", q=8), w=['RWM'])
            for i6 in range(6):
                c.dma('sp', MU[:, i6, :], I['l1_rw_mu'][i6].rearrange("(u p) -> p u", p=128), w=['MU'], allow_slow_non_contiguous=True)
            for d in range(2):
                c.dma('sp', W0[:, d, :], I['l1_rw_w0'][d].rearrange("(u p) -> p u", p=128), w=['W0'], allow_slow_non_contiguous=True)
                c.dma('sp', A0[:, d, :], I['l1_rw_a0'][d].rearrange("(u p) -> p u", p=128), w=['A0'], allow_slow_non_contiguous=True)
                c.dma('pool', W1[:, d, :, :], I['l1_rw_w1'][d].rearrange("(u p) r -> p u r", p=128), w=['W1'])
                c.dma('pool', A1[:, d, :, :], I['l1_rw_a1'][d].rearrange("(u p) r -> p u r", p=128), w=['A1'])
                c.dma('pool', W2[:, d, :], I['l1_rw_w2'][d], w=['W2'])
                c.dma('pool', A2[:, d, :], I['l1_rw_a2'][d], w=['A2'])
            c.dma('pool', G1[:], I['l1_rw_g1'].rearrange("(u p) r -> p u r", p=128), w=['G1'])
            c.dma('pool', G2[:], I['l1_rw_g2'][:, :], w=['G2'])
            for (t_, nm) in ((KKc, 'l1_rw_kk'), (KAc, 'l1_rw_ka'), (RKc, 'l1_rw_rk'), (LNW, 'l1_ln_w'), (LNB, 'l1_ln_b')):
                c.dma('sp', t_[:], I[nm].rearrange("(u p) -> p u", p=128), w=[nm], allow_slow_non_contiguous=True)
            c.op('dve', lambda e: e.tensor_scalar(out=MUH[:], in0=MU[:], scalar1=0.5, scalar2=None, op0=ALU.mult), r=['MU'], w=['MUH'])
            c.op('dve', lambda e: e.tensor_scalar(out=OMU[:], in0=MU[:], scalar1=-1.0, scalar2=1.0, op0=ALU.mult, op1=ALU.add), r=['MU'], w=['OMU'])
            c.op('dve', lambda e: e.tensor_scalar(out=OMKA[:], in0=KAc[:], scalar1=-1.0, scalar2=1.0, op0=ALU.mult, op1=ALU.add), r=['l1_rw_ka'], w=['OMKA'])
            for n in range(NCN):
                c.op('dve', lambda e: e.memset(KAx[n][:], 0.0), w=[('KAx', n)])
                for pp in range(2):
                    for t_, k_ in ((Vx, 'Vx'), (Ax, 'Ax'), (Kx, 'Kx')):
                        c.op('dve', lambda e: e.memset(t_[pp][n][:], 0.0), w=[(k_, pp, n)])
            for hh in range(2):
                c.op('dve', lambda e: e.memset(Ux[hh][:], 0.0), w=[('Ux', hh)])
            PK = ['MU', 'MUH', 'OMU', 'W0', 'A0', 'l1_rw_kk', 'l1_rw_ka', 'l1_rw_rk', 'l1_ln_w', 'l1_ln_b', 'OMKA', 'misc', 'CMF', 'BLK']
            MSf, MSb = misc[:, 384:512], misc[:, 512:640]
            MIf, MIb = misc[:, 128:256], misc[:, 256:384]

            DBN = ['AT', 'KAT', 'KT', 'RT', 'vp', 'EG', 'y', 'bon']
            A2_ = dict(A_)
            for n in DBN:
                A2_[n] = sb('B_' + n, [128, TBM], F32)
            A_sets = [A_, A2_]
            TBN = ['RT', 'EG', 'y', 'bon']
            A3_ = [dict((n, (A_[n] if t3 == 0 else (A2_[n] if t3 == 1 else sb('C_' + n, [128, TBM], F32)))) for n in TBN) for t3 in range(3)]
            o1 = sb('o1', [128, TBM], F32)
            o2 = sb('o2', [128, TBM], F32)
            hgs = [hg, sb('hg2', [128, TBM], BF16)]

            def mk_item(d, sq_, bi_, U, kidx):
                off, L = sq_['off'], sq_['L']
                TB = min(TBM, L)
                nb = L // TB
                blk = bi_ if d == 0 else nb - 1 - bi_
                return dict(d=d, sq=sq_, bi=bi_, U=U, TB=TB, nb=nb, ncc=TB // 128, b0=off + blk * TB, par=kidx % 2, bpar=(kidx // 4) % 2, p3=kidx % 3)

            def prep_gen(it):
                d, U, TB, b0, par, bpar = it['d'], it['U'], it['TB'], it['b0'], it['par'], it['bpar']
                off, L = it['sq']['off'], it['sq']['L']
                p3 = it['p3']
                A_ = dict(A_sets[par])
                A_.update(A3_[p3])
                K = lambda n: (n, 't', p3) if n in TBN else ((n, par) if n in DBN else n)
                lo_t = max(b0 - 1, off)
                hi_t = min(b0 + TB + 1, off + L)
                dlo = lo_t - (b0 - 1)
                dhi = dlo + (hi_t - lo_t)

                def load_pad(buf, key, m):
                    if dlo > 0:
                        c.op('dve', lambda e: e.memset(buf[:, 0:1], 0.0), w=[key])
                    if dhi < TB + 2:
                        c.op('dve', lambda e: e.memset(buf[:, TB + 1:TB + 2], 0.0), w=[key])
                    c.dma('sp', buf[:, dlo:dhi], Pv[:, m, lo_t:hi_t], w=[key])
                if U == 0:
                    Ucur = U
                    for U in range(4):
                        load_pad(xdp, 'xdp', 16 + U)
                        yield
                        c.op('dve', lambda e: e.tensor_tensor(out=cs[:, :TB], in0=xdp[:, 0:TB], in1=xdp[:, 2:TB + 2], op=ALU.add), r=['xdp'], w=['cs'])
                        yield
                        for (i6, dst, dk) in ((3, xw, 'xw'), (4, xa, 'xa'), (5, xg, 'xg')):
                            c.op('dve', lambda e: e.tensor_scalar(out=A_['t1'][:, :TB], in0=cs[:, :TB], scalar1=MUH[:, i6, U:U + 1], scalar2=None, op0=ALU.mult), r=['cs', 'MUH'], w=[K('t1')])
                            yield
                            c.op('dve', lambda e: e.scalar_tensor_tensor(out=dst[:, U, :TB], in0=xdp[:, 1:TB + 1], scalar=OMU[:, i6, U:U + 1], in1=A_['t1'][:, :TB], op0=ALU.mult, op1=ALU.add), r=['xdp', 'OMU', K('t1')], w=[dk])
                            yield
                    for U in range(4):
                        c.op('pe', lambda e: e.matmul(PS[0][:64, :TB], lhsT=W1[:, d, U, :], rhs=xw[:, U, :TB], start=(U == 0), stop=(U == 3)), r=['W1', 'xw'], w=[('ps', 0)])
                        yield
                    c.op('act', lambda e: e.activation(out=hw[:, :TB], in_=PS[0][:64, :TB], func=AF.Tanh), r=[('ps', 0)], w=['hw'])
                    yield
                    for U in range(4):
                        c.op('pe', lambda e: e.matmul(PS[0][:64, :TB], lhsT=A1[:, d, U, :], rhs=xa[:, U, :TB], start=(U == 0), stop=(U == 3)), r=['A1', 'xa'], w=[('ps', 0)])
                        yield
                    c.op('act', lambda e: e.activation(out=ha[:, :TB], in_=PS[0][:64, :TB], func=AF.Identity), r=[('ps', 0)], w=['ha'])
                    yield
                    if d == 1:
                        for U in range(4):
                            c.op('pe', lambda e: e.matmul(PS[0][:, :TB], lhsT=G1[:, U, :], rhs=xg[:, U, :TB], start=(U == 0), stop=(U == 3)), r=['G1', 'xg'], w=[('ps', 0)])
                            yield
                        c.op('act', lambda e: e.activation(out=hgs[bpar][:, :TB], in_=PS[0][:, :TB], func=AF.Sigmoid), r=[('ps', 0)], w=[('hg', bpar)])
                        yield
                    U = Ucur
                if True:
                    for (i6, n) in ((0, 'r'), (1, 'k'), (2, 'v')):
                        load_pad(zp[n], 'zp_' + n, 4 + 4 * i6 + U)
                        yield
                        c.op('dve', lambda e: e.tensor_tensor(out=cs[:, :TB], in0=zp[n][:, 0:TB], in1=zp[n][:, 2:TB + 2], op=ALU.add), r=['zp_' + n], w=['cs'])
                        yield
                        c.op('dve', lambda e: e.tensor_scalar(out=A_['t1'][:, :TB], in0=cs[:, :TB], scalar1=MUH[:, i6, U:U + 1], scalar2=None, op0=ALU.mult), r=['cs', 'MUH'], w=[K('t1')])
                        yield
                        c.op('dve', lambda e: e.scalar_tensor_tensor(out=A_[n + 'p'][:, :TB], in0=zp[n][:, 1:TB + 1], scalar=OMU[:, i6, U:U + 1], in1=A_['t1'][:, :TB], op0=ALU.mult, op1=ALU.add), r=['zp_' + n, 'OMU', K('t1')], w=[K(n + 'p')])
                        yield
                    rp, kp, vp = A_['rp'], A_['kp'], A_['vp']
                    c.op('pe', lambda e: e.matmul(PS[0][:, :TB], lhsT=W2[:, d, U * 128:(U + 1) * 128], rhs=hw[:, :TB], start=True, stop=True), r=['W2', 'hw'], w=[('ps', 0)])
                    yield
                    c.op('act', lambda e: e.activation(out=A_['LW'][:, :TB], in_=PS[0][:, :TB], func=AF.Sigmoid, bias=W0[:, d, U:U + 1]), r=[('ps', 0), 'W0'], w=[K('LW')])
                    yield
                    c.op('dve', lambda e: e.tensor_scalar(out=A_['LW'][:, :TB], in0=A_['LW'][:, :TB], scalar1=-EM05, scalar2=None, op0=ALU.mult), r=[K('LW')], w=[K('LW')])
                    yield
                    c.op('pe', lambda e: e.matmul(PS[0][:, :TB], lhsT=A2[:, d, U * 128:(U + 1) * 128], rhs=ha[:, :TB], start=True, stop=True), r=['A2', 'ha'], w=[('ps', 0)])
                    yield
                    c.op('act', lambda e: e.activation(out=A_['a'][:, :TB], in_=PS[0][:, :TB], func=AF.Sigmoid, bias=A0[:, d, U:U + 1]), r=[('ps', 0), 'A0'], w=[K('a')])
                    yield
                    c.op('dve', lambda e: e.tensor_scalar(out=A_['kk'][:, :TB], in0=kp[:, :TB], scalar1=KKc[:, U:U + 1], scalar2=None, op0=ALU.mult), r=[K('kp'), 'l1_rw_kk'], w=[K('kk')])
                    yield
                    c.op('act', lambda e: e.activation(out=A_['t1'][:, :TB], in_=A_['kk'][:, :TB], func=AF.Square), r=[K('kk')], w=[K('t1')])
                    yield
                    c.op('pe', lambda e: e.matmul(PS[0][:, :TB], lhsT=BLK[:], rhs=A_['t1'][:, :TB], start=True, stop=True), r=['BLK', K('t1')], w=[('ps', 0)])
                    yield
                    c.op('act', lambda e: e.activation(out=A_['t2'][:, :TB], in_=PS[0][:, :TB], func=AF.Sqrt), r=[('ps', 0)], w=[K('t2')])
                    yield
                    c.op('dve', lambda e: e.tensor_scalar(out=A_['t2'][:, :TB], in0=A_['t2'][:, :TB], scalar1=1e-12, scalar2=None, op0=ALU.max), r=[K('t2')], w=[K('t2')])
                    yield
                    c.op('dve', lambda e: e.reciprocal(out=A_['t2'][:, :TB], in_=A_['t2'][:, :TB]), r=[K('t2')], w=[K('t2')])
                    yield
                    c.op('dve', lambda e: e.tensor_tensor(out=A_['kk'][:, :TB], in0=A_['kk'][:, :TB], in1=A_['t2'][:, :TB], op=ALU.mult), r=[K('kk'), K('t2')], w=[K('kk')])
                    yield
                    c.op('dve', lambda e: e.tensor_scalar(out=A_['t1'][:, :TB], in0=A_['a'][:, :TB], scalar1=KAc[:, U:U + 1], scalar2=OMKA[:, U:U + 1], op0=ALU.mult, op1=ALU.add), r=[K('a'), 'l1_rw_ka', 'OMKA'], w=[K('t1')])
                    yield
                    c.op('dve', lambda e: e.tensor_tensor(out=A_['kd'][:, :TB], in0=kp[:, :TB], in1=A_['t1'][:, :TB], op=ALU.mult), r=[K('kp'), K('t1')], w=[K('kd')])
                    yield
                    c.op('dve', lambda e: e.tensor_tensor(out=A_['akk'][:, :TB], in0=A_['a'][:, :TB], in1=A_['kk'][:, :TB], op=ALU.mult), r=[K('a'), K('kk')], w=[K('akk')])
                    yield
                    c.op('dve', lambda e: e.scalar_tensor_tensor(out=A_['t1'][:, :TB], in0=rp[:, :TB], scalar=RKc[:, U:U + 1], in1=A_['kd'][:, :TB], op0=ALU.mult, op1=ALU.mult), r=[K('rp'), 'l1_rw_rk', K('kd')], w=[K('t1')])
                    yield
                    c.op('pe', lambda e: e.matmul(PS[0][:, :TB], lhsT=BLK[:], rhs=A_['t1'][:, :TB], start=True, stop=True), r=['BLK', K('t1')], w=[('ps', 0)])
                    yield
                    c.op('dve', lambda e: e.tensor_tensor(out=A_['bon'][:, :TB], in0=PS[0][:, :TB], in1=vp[:, :TB], op=ALU.mult), r=[('ps', 0), K('vp')], w=[K('bon')])
                    yield
                    Dv = (lambda x: x[:, 0:TB]) if d == 0 else (lambda x: x[:, 0:TB][:, ::-1])
                    c.op('dve', lambda e: e.tensor_tensor_scan(out=Dv(A_['G']), data0=CMF[:, :TB], data1=Dv(A_['LW']), initial=0.0, op0=ALU.mult, op1=ALU.add), r=['CMF', K('LW')], w=[K('G')])
                    yield
                    c.op('act', lambda e: e.activation(out=A_['EG'][:, :TB], in_=A_['G'][:, :TB], func=AF.Exp), r=[K('G')], w=[K('EG')])
                    yield
                    c.op('act', lambda e: e.activation(out=A_['EGN'][:, :TB], in_=A_['G'][:, :TB], func=AF.Exp, scale=-1.0), r=[K('G')], w=[K('EGN')])
                    yield
                    c.op('dve', lambda e: e.tensor_tensor(out=A_['t1'][:, :TB], in0=A_['G'][:, :TB], in1=A_['LW'][:, :TB], op=ALU.subtract), r=[K('G'), K('LW')], w=[K('t1')])
                    yield
                    c.op('act', lambda e: e.activation(out=A_['EGm'][:, :TB], in_=A_['t1'][:, :TB], func=AF.Exp), r=[K('t1')], w=[K('EGm')])
                    yield
                    c.op('dve', lambda e: e.tensor_tensor(out=A_['AT'][:, :TB], in0=A_['akk'][:, :TB], in1=A_['EGN'][:, :TB], op=ALU.mult), r=[K('akk'), K('EGN')], w=[K('AT')])
                    yield
                    c.op('dve', lambda e: e.tensor_tensor(out=A_['KAT'][:, :TB], in0=A_['kk'][:, :TB], in1=A_['EGm'][:, :TB], op=ALU.mult), r=[K('kk'), K('EGm')], w=[K('KAT')])
                    yield
                    c.op('dve', lambda e: e.tensor_tensor(out=A_['KT'][:, :TB], in0=A_['kd'][:, :TB], in1=A_['EGN'][:, :TB], op=ALU.mult), r=[K('kd'), K('EGN')], w=[K('KT')])
                    yield
                    c.op('dve', lambda e: e.tensor_tensor(out=A_['RT'][:, :TB], in0=rp[:, :TB], in1=A_['EG'][:, :TB], op=ALU.mult), r=[K('rp'), K('EG')], w=[K('RT')])
                    yield
                    AT, KAT, KT, RT = A_['AT'], A_['KAT'], A_['KT'], A_['RT']

            def run_item(it, nxt, pend):
                d, U, TB, b0, par, bpar, ncc = it['d'], it['U'], it['TB'], it['b0'], it['par'], it['bpar'], it['ncc']
                sq_ = it['sq']
                off, L, latent, pidx = sq_['off'], sq_['L'], sq_['latent'], sq_['pidx']
                p3 = it['p3']
                A_ = dict(A_sets[par])
                A_.update(A3_[p3])
                K = lambda n: (n, 't', p3) if n in TBN else ((n, par) if n in DBN else n)
                mS = MSf if d == 0 else MSb
                mI = MIf if d == 0 else MIb
                mo_S = 0 if d == 0 else 4
                mo_T = 4 if d == 0 else 0
                AT, KAT, KT, RT = A_['AT'], A_['KAT'], A_['KT'], A_['RT']
                if it['bi'] == 0 and U == 0:
                    for Ui in range(4):
                        if latent:
                            c.op('dve', lambda e: e.memset(Sld[:], 0.0), w=['Sld'])
                            for hh in range(2):
                                lo = 64 * hh
                                c.dma('sp', Sld[lo:lo + 64, lo:lo + 64], I['st_rwkv'][d, 2 * Ui + hh], w=['Sld'])
                            c.op('pe', lambda e: e.transpose(out=PS[0][:, 0:128], in_=Sld[:], identity=self.ident[:]), r=['Sld', 'ident'], w=[('ps', 0)])
                            c.op('act', lambda e: e.activation(out=P2[Ui][:], in_=PS[0][:, 0:128], func=AF.Identity), r=[('ps', 0)], w=[('P2', Ui)])
                        else:
                            c.op('dve', lambda e: e.memset(P2[Ui][:], 0.0), w=[('P2', Ui)])
                if True:
                    def X(arr, hh, csl):
                        return arr[64 * hh:64 * hh + 64, csl]

                    def chain(n, cc, hh):
                        csl = slice(cc * 128, (cc + 1) * 128)
                        lo = 64 * hh
                        bnk = 1 + n % 7
                        pk = ('ps', bnk)
                        c.op('pe', lambda e: e.matmul(PS[bnk][:, 0:128], lhsT=X(AT, hh, csl), rhs=X(KAT, hh, csl), start=True, stop=True), r=[K('AT'), K('KAT')], w=[pk])
                        c.op('dve', lambda e: e.scalar_tensor_tensor(out=Qb[n][0][:], in0=PS[bnk][:, 0:128], scalar=-1.0, in1=RWM[:, mo_S, :], op0=ALU.mult, op1=ALU.mult), r=[pk, 'RWM'], w=[('Q', n, 0)])
                        yield
                        c.op('pe', lambda e: e.matmul(PS[bnk][:, 0:128], lhsT=X(KAT, hh, csl), rhs=X(AT, hh, csl), start=True, stop=True), r=[K('AT'), K('KAT')], w=[pk])
                        c.op('dve', lambda e: e.scalar_tensor_tensor(out=QTb[n][0][:], in0=PS[bnk][:, 0:128], scalar=-1.0, in1=RWM[:, mo_T, :], op0=ALU.mult, op1=ALU.mult), r=[pk, 'RWM'], w=[('QT', n, 0)])
                        for lv in range(3):
                            c.op('dve', lambda e: e.tensor_tensor(out=El[n][lv][:], in0=PS[bnk][:, 0:128], in1=RWM[:, mo_T + 1 + lv, :], op=ALU.mult), r=[pk, 'RWM'], w=[('El', n, lv)])
                        yield
                        for (nm, la, ra, msk, dst) in (('A2T', KT, KAT, mS, A2T[n]), ('B1T', AT, RT, mI, B1T[par][n]), ('B2T', KT, RT, mI, B2T[par][n])):
                            c.op('pe', lambda e: e.matmul(PS[bnk][:, 0:128], lhsT=X(la, hh, csl), rhs=X(ra, hh, csl), start=True, stop=True), r=[K('AT'), K('KAT'), K('KT'), K('RT')], w=[pk])
                            c.op('dve', lambda e: e.tensor_tensor(out=dst[:], in0=PS[bnk][:, 0:128], in1=msk, op=ALU.mult), r=[pk, 'misc'], w=[((nm, par, n) if nm != 'A2T' else (nm, n))])
                            yield
                        c.op('dve', lambda e: e.tensor_tensor(out=Zb[n][:], in0=Qb[n][0][:], in1=self.ident[:], op=ALU.add), r=[('Q', n, 0), 'ident'], w=[('Z', n)])
                        for (src, skey, dstl, dk) in ((A_['vp'], 'vp', Vx[par], ('Vx', par)), (KAT, 'KAT', KAx, ('KAx',)), (AT, 'AT', Ax[par], ('Ax', par)), (KT, 'KT', Kx[par], ('Kx', par))):
                            c.op('pe', lambda e: e.transpose(out=PS[bnk][:, 0:64], in_=src[lo:lo + 64, csl], identity=self.ident[lo:lo + 64, lo:lo + 64]), r=[K(skey), 'ident'], w=[pk])
                            c.op('act', lambda e: e.activation(out=dstl[n][:, lo:lo + 64], in_=PS[bnk][:, 0:64], func=AF.Identity), r=[pk], w=[dk + (n,)])
                            yield
                        for i in range(1, 4):
                            g0, g1 = (i - 1) % 2, i % 2
                            if i < 3:
                                c.op('pe', lambda e: e.matmul(PS[bnk][:, 0:128], lhsT=QTb[n][g0][:], rhs=Qb[n][g0][:], start=True, stop=True), r=[('QT', n, g0), ('Q', n, g0)], w=[pk])
                                c.op('act', lambda e: e.activation(out=Qb[n][g1][:], in_=PS[bnk][:, 0:128], func=AF.Identity), r=[pk], w=[('Q', n, g1)])
                                yield
                            c.op('pe', lambda e: e.matmul(PS[bnk][:, 0:128], lhsT=Qb[n][g0][:], rhs=QTb[n][g0][:], start=True, stop=True), r=[('QT', n, g0), ('Q', n, g0)], w=[pk])
                            c.op('act', lambda e: e.activation(out=QTb[n][g1][:], in_=PS[bnk][:, 0:128], func=AF.Identity), r=[pk], w=[('QT', n, g1)])
                            yield
                            c.op('pe', lambda e: e.matmul(PS[bnk][:, 0:128], lhsT=QTb[n][g1][:], rhs=Zb[n][:], start=True, stop=True), r=[('QT', n, g1), ('Z', n)], w=[pk])
                            c.op('dve', lambda e: e.tensor_tensor(out=Zb[n][:], in0=Zb[n][:], in1=PS[bnk][:, 0:128], op=ALU.add), r=[pk, ('Z', n)], w=[('Z', n)])
                            yield
                        for lv in range(3):
                            c.op('pe', lambda e: e.transpose(out=PS[bnk][:, 0:128], in_=Zb[n][:], identity=self.ident[:]), r=[('Z', n), 'ident'], w=[pk])
                            c.op('act', lambda e: e.activation(out=ZTb[n][:], in_=PS[bnk][:, 0:128], func=AF.Identity), r=[pk], w=[('ZT', n)])
                            yield
                            c.op('pe', lambda e: e.matmul(PS[bnk][:, 0:128], lhsT=El[n][lv][:], rhs=Zb[n][:], start=True, stop=True), r=[('El', n, lv), ('Z', n)], w=[pk])
                            c.op('act', lambda e: e.activation(out=Fb[n][:], in_=PS[bnk][:, 0:128], func=AF.Identity), r=[pk], w=[('F', n)])
                            yield
                            c.op('pe', lambda e: e.matmul(PS[bnk][:, 0:128], lhsT=ZTb[n][:], rhs=Fb[n][:], start=True, stop=True), r=[('ZT', n), ('F', n)], w=[pk])
                            c.op('dve', lambda e: e.tensor_tensor(out=Zb[n][:], in0=Zb[n][:], in1=PS[bnk][:, 0:128], op=ALU.subtract), r=[pk, ('Z', n)], w=[('Z', n)])
                            yield
                        c.op('pe', lambda e: e.matmul(PS[bnk][:, 0:64], lhsT=A2T[n][:], rhs=Vx[par][n][:, lo:lo + 64], start=True, stop=True), r=[('A2T', n), ('Vx', par, n)], w=[pk])
                        c.op('act', lambda e: e.activation(out=Xs[n][:], in_=PS[bnk][:, 0:64], func=AF.Identity), r=[pk], w=[('Xs', n)])
                        yield
                        c.op('pe', lambda e: e.matmul(PS[bnk][:, 0:64], lhsT=Zb[n][:], rhs=Xs[n][:], start=True, stop=True), r=[('Z', n), ('Xs', n)], w=[pk])
                        c.op('act', lambda e: e.activation(out=Uvs[par][n][:], in_=PS[bnk][:, 0:64], func=AF.Identity), r=[pk], w=[('Uvs', par, n)])
                        yield
                        c.op('pe', lambda e: e.matmul(PS[bnk][:, 0:128], lhsT=KAx[n][:], rhs=Zb[n][:], start=True, stop=True), r=[('KAx', n), ('Z', n)], w=[pk])
                        c.op('dve', lambda e: e.tensor_copy(out=WTs[par][n][lo:lo + 64, :], in_=PS[bnk][lo:lo + 64, 0:128]), r=[pk], w=[('WTs', par, n)])
                        yield
                chains = []
                for ci in range(ncc):
                    cc = ci if d == 0 else ncc - 1 - ci
                    for hh in range(2):
                        chains.append(chain(ci * 2 + hh, cc, hh))
                if nxt is not None:
                    chains.append(prep_gen(nxt))
                if pend is not None:
                    chains.append(pend)
                while chains:
                    for g_ in list(chains):
                        try:
                            next(g_)
                        except StopIteration:
                            chains.remove(g_)
            def so_gen(it):
                d, U, TB, b0, par, bpar, ncc = it['d'], it['U'], it['TB'], it['b0'], it['par'], it['bpar'], it['ncc']
                sq_ = it['sq']
                off, L, latent, pidx = sq_['off'], sq_['L'], sq_['latent'], sq_['pidx']
                p3 = it['p3']
                A_ = dict(A_sets[par])
                A_.update(A3_[p3])
                K = lambda n: (n, 't', p3) if n in TBN else ((n, par) if n in DBN else n)
                mS = MSf if d == 0 else MSb
                mI = MIf if d == 0 else MIb
                mo_S = 0 if d == 0 else 4
                mo_T = 4 if d == 0 else 0
                AT, KAT, KT, RT = A_['AT'], A_['KAT'], A_['KT'], A_['RT']
                if True:
                    for ci in range(ncc):
                        cc = ci if d == 0 else ncc - 1 - ci
                        csl = slice(cc * 128, (cc + 1) * 128)
                        glast = (cc * 128 + 127) if d == 0 else cc * 128
                        for hh in range(2):
                            n = ci * 2 + hh
                            lo = 64 * hh
                            bnk = 5 + hh
                            c.op('pe', lambda e: e.matmul(PS[bnk][:, 0:64], lhsT=WTs[par][n][lo:lo + 64, :], rhs=P2[U][lo:lo + 64, lo:lo + 64], start=True, stop=True), r=[('WTs', par, n), ('P2', U)], w=[('ps', bnk)])
                            yield
                            c.op('dve', lambda e: e.scalar_tensor_tensor(out=Ux[hh][:, lo:lo + 64], in0=PS[bnk][:, 0:64], scalar=-1.0, in1=Uvs[par][n][:], op0=ALU.mult, op1=ALU.subtract), r=[('ps', bnk), ('Uvs', par, n)], w=[('Ux', hh)])
                            yield
                        for hh in range(2):
                            n = ci * 2 + hh
                            lo = 64 * hh
                            c.op('pe', lambda e: e.matmul(PS[7][:, 0:128], lhsT=P2[U][lo:lo + 64, :], rhs=RT[lo:lo + 64, csl], start=(hh == 0), stop=False), r=[('P2', U), K('RT')], w=[('ps', 7)])
                            yield
                            c.op('pe', lambda e: e.matmul(PS[7][:, 0:128], lhsT=Ux[hh][:], rhs=B1T[par][n][:], start=False, stop=False), r=[('Ux', hh), ('B1T', par, n)], w=[('ps', 7)])
                            yield
                            c.op('pe', lambda e: e.matmul(PS[7][:, 0:128], lhsT=Vx[par][n][:], rhs=B2T[par][n][:], start=False, stop=(hh == 1)), r=[('Vx', par, n), ('B2T', par, n)], w=[('ps', 7)])
                            yield
                        c.op('act', lambda e: e.activation(out=A_['y'][:, csl], in_=PS[7][:, 0:128], func=AF.Identity), r=[('ps', 7)], w=[K('y')])
                        yield
                        for hh in range(2):
                            n = ci * 2 + hh
                            lo = 64 * hh
                            bnk = 5 + hh
                            c.op('pe', lambda e: e.matmul(PS[bnk][:, 0:64], lhsT=Ax[par][n][:], rhs=Ux[hh][:, lo:lo + 64], start=True, stop=False), r=[('Ax', par, n), ('Ux', hh)], w=[('ps', bnk)])
                            yield
                            c.op('pe', lambda e: e.matmul(PS[bnk][:, 0:64], lhsT=Kx[par][n][:], rhs=Vx[par][n][:, lo:lo + 64], start=False, stop=True), r=[('Kx', par, n), ('Vx', par, n)], w=[('ps', bnk)])
                            yield
                            c.op('dve', lambda e: e.tensor_tensor(out=Ptmp[hh][lo:lo + 64, :], in0=P2[U][lo:lo + 64, lo:lo + 64], in1=PS[bnk][lo:lo + 64, 0:64], op=ALU.add), r=[('ps', bnk), ('P2', U)], w=[('Ptmp', hh)])
                            yield
                            c.op('act', lambda e: e.activation(out=P2[U][lo:lo + 64, lo:lo + 64], in_=Ptmp[hh][lo:lo + 64, :], func=AF.Identity, scale=A_['EG'][lo:lo + 64, glast:glast + 1]), r=[('Ptmp', hh), K('EG')], w=[('P2', U)])
                            yield
                if True:
                    g0_ = b0
                    if d == 0:
                        c.dma('sp', YFv[:, U, g0_:g0_ + TB], A_['y'][:, :TB], r=[K('y')], w=[('YF', U, g0_)])
                        yield
                        c.dma('sp', BFv[:, U, g0_:g0_ + TB], A_['bon'][:, :TB], r=[K('bon')], w=[('BF', U, g0_)])
                        yield
                    else:
                        c.dma('sp', o1[:, :TB], YFv[:, U, g0_:g0_ + TB], r=[('YF', U, g0_)], w=['o1'])
                        yield
                        c.dma('sp', o2[:, :TB], BFv[:, U, g0_:g0_ + TB], r=[('BF', U, g0_)], w=['o2'])
                        yield
                        c.op('dve', lambda e: e.tensor_tensor(out=A_['y'][:, :TB], in0=A_['y'][:, :TB], in1=o1[:, :TB], op=ALU.add), r=[K('y'), 'o1'], w=[K('y')])
                        yield
                        c.op('dve', lambda e: e.tensor_tensor(out=A_['bon'][:, :TB], in0=A_['bon'][:, :TB], in1=o2[:, :TB], op=ALU.add), r=[K('bon'), 'o2'], w=[K('bon')])
                        yield
                        c.op('pe', lambda e: e.matmul(PS[5][:, :TB], lhsT=BLK[:], rhs=A_['y'][:, :TB], start=True, stop=True), r=['BLK', K('y')], w=[('ps', 5)])
                        yield
                        c.op('dve', lambda e: e.scalar_tensor_tensor(out=o1[:, :TB], in0=PS[5][:, :TB], scalar=-1.0 / 64.0, in1=A_['y'][:, :TB], op0=ALU.mult, op1=ALU.add), r=[('ps', 5), K('y')], w=['o1'])
                        yield
                        c.op('act', lambda e: e.activation(out=o2[:, :TB], in_=o1[:, :TB], func=AF.Square), r=['o1'], w=['o2'])
                        yield
                        c.op('pe', lambda e: e.matmul(PS[5][:, :TB], lhsT=BLK[:], rhs=o2[:, :TB], start=True, stop=True), r=['BLK', 'o2'], w=[('ps', 5)])
                        yield
                        c.op('act', lambda e: e.activation(out=o2[:, :TB], in_=PS[5][:, :TB], func=AF.Sqrt, scale=1.0 / 64.0, bias=self.cst[:, 2:3]), r=[('ps', 5), ('cst', 2)], w=['o2'])
                        yield
                        c.op('dve', lambda e: e.reciprocal(out=o2[:, :TB], in_=o2[:, :TB]), r=['o2'], w=['o2'])
                        yield
                        c.op('dve', lambda e: e.tensor_tensor(out=o1[:, :TB], in0=o1[:, :TB], in1=o2[:, :TB], op=ALU.mult), r=['o1', 'o2'], w=['o1'])
                        yield
                        c.op('dve', lambda e: e.tensor_scalar(out=o1[:, :TB], in0=o1[:, :TB], scalar1=LNW[:, U:U + 1], scalar2=LNB[:, U:U + 1], op0=ALU.mult, op1=ALU.add), r=['o1', 'l1_ln_w', 'l1_ln_b'], w=['o1'])
                        yield
                        c.op('dve', lambda e: e.tensor_tensor(out=o1[:, :TB], in0=o1[:, :TB], in1=A_['bon'][:, :TB], op=ALU.add), r=['o1', K('bon')], w=['o1'])
                        yield
                        c.op('pe', lambda e: e.matmul(PS[5][:, :TB], lhsT=G2[:, U * 128:(U + 1) * 128], rhs=hgs[bpar][:, :TB], start=True, stop=True), r=['G2', ('hg', bpar)], w=[('ps', 5)])
                        yield
                        c.op('dve', lambda e: e.tensor_tensor(out=yb[:, :TB], in0=o1[:, :TB], in1=PS[5][:, :TB], op=ALU.mult), r=['o1', ('ps', 5)], w=['yb'])
                        yield
                        c.dma('sp', Yv[:, 4 + U, g0_:g0_ + TB], yb[:, :TB], r=['yb'], w=[('Y', 4 + U, g0_)])
                        yield
                if it['bi'] == it['nb'] - 1 and U == 3:
                    if pidx is not None:
                        for Ui in range(4):
                            c.op('pe', lambda e: e.transpose(out=PS[5][:, 0:128], in_=P2[Ui][:], identity=self.ident[:]), r=[('P2', Ui), 'ident'], w=[('ps', 5)])
                            yield
                            c.op('act', lambda e: e.activation(out=Sout[:], in_=PS[5][:, 0:128], func=AF.Identity), r=[('ps', 5)], w=['Sout'])
                            yield
                            for hh in range(2):
                                lo = 64 * hh
                                c.dma('sp', O['new_rwkv'][pidx, d, 2 * Ui + hh], Sout[lo:lo + 64, lo:lo + 64], r=['Sout'], w=[('nrw', pidx, d, Ui, hh)])
                                yield

            for d in range(2):
                items = []
                for sq_ in self.seqs():
                    TB_ = min(TBM, sq_['L'])
                    for bi_ in range(sq_['L'] // TB_):
                        for U in range(4):
                            items.append(mk_item(d, sq_, bi_, U, len(items)))
                for _ in prep_gen(items[0]):
                    pass
                pend = None
                for k, it in enumerate(items):
                    if it['bi'] == 0 and it['U'] == 0 and pend is not None:
                        for _ in pend:
                            pass
                        pend = None
                    run_item(it, items[k + 1] if k + 1 < len(items) else None, pend)
                    pend = so_gen(it)
                for _ in pend:
                    pass
            c.barrier()


def host_consts(LS):
    ident = np.eye(128, dtype=np.float32)
    GRID_W = 64
    nf = 32
    t = np.arange(LS)
    row = (t // GRID_W).astype(np.float32)
    col = (t % GRID_W).astype(np.float32)
    freqs = (10000.0 ** (-np.arange(nf, dtype=np.float32) / nf)).astype(np.float32)
    cos = np.zeros((128, LS), np.float32)
    sin = np.zeros((128, LS), np.float32)
    rot = np.zeros((128, 128), np.float32)
    for a in range(2):
        pos = row if a == 0 else col
        ang = (pos[None, :] * freqs[:, None]).astype(np.float32)
        for b in range(2):
            p0 = a * 64 + b * 32
            cos[p0:p0 + 32] = np.cos(ang)
            sin[p0:p0 + 32] = np.sin(ang)
            for f in range(nf):
                m = p0 + f
                partner = a * 64 + (1 - b) * 32 + f
                rot[partner, m] = -1.0 if b == 0 else 1.0
    misc = np.zeros((128, 1024), np.float32)
    idx = np.arange(128)
    rel = idx[None, :] - idx[:, None]
    misc[:, 0:128] = np.abs(rel)
    misc[:, 128:256] = (rel >= 0)
    misc[:, 256:384] = (rel <= 0)
    misc[:, 384:512] = (rel > 0)
    misc[:, 512:640] = (rel < 0)
    misc[:, 640:768] = idx[None, :]
    misc[:, 768:896] = idx[None, :] + 1
    misc[:, 896] = idx
    misc[:, 897] = 127 - idx
    misc[:, 898] = 128.0
    iota = np.tile(np.arange(1, 513, dtype=np.float32)[None, :], (128, 1))
    s5m = np.zeros((128, 4, 128), np.float32)
    for p in range(128):
        for q in range(4):
            for col in range(128):
                if p // 16 == 2 * q + col // 64:
                    s5m[p, q, col] = 1.0
    cmf = np.ones((128, 512), np.float32)
    cmf[:, 0::128] = 0.0
    blk = np.zeros((128, 128), np.float32)
    blk[:64, :64] = 1.0
    blk[64:, 64:] = 1.0
    rwm = np.zeros((128, 8, 128), np.float32)
    pp = idx[:, None]
    ff = idx[None, :]
    up = pp < ff
    pats = [up & ((pp // 16) == (ff // 16))]
    for k in (16, 32, 64):
        pats.append(up & ((pp // (2 * k)) == (ff // (2 * k))) & ((pp // k) != (ff // k)))
    for i, pt in enumerate(pats):
        rwm[:, i, :] = pt
        rwm[:, 4 + i, :] = pt.T
    return dict(c_ident=ident, c_rope_cos=cos, c_rope_sin=sin, c_rope_rot=rot, c_misc=misc, c_iota=iota, c_s5mask=s5m.reshape(128, 512), c_cmf=cmf, c_blk=blk,
                c_rwm=rwm.reshape(128, 1024))


WEIGHT_KEYS = ['mod_w', 'mod_b', 'norm_g', 'ffn1_w13', 'ffn1_w2', 'ffn2_w13', 'ffn2_w2', 'final_norm',
               'l0_w_in', 'l0_w_out', 'l0_ret_decay', 'l0_conv_w', 'l0_conv_b', 'l0_lru_lam', 'l0_lru_wa', 'l0_lru_ba',
               'l0_lru_wx', 'l0_lru_bx', 'l1_w_in', 'l1_w_out', 'l1_s5_a_re', 'l1_s5_a_im', 'l1_s5_log_dt',
               'l1_s5_b_re', 'l1_s5_b_im', 'l1_s5_c_re', 'l1_s5_c_im', 'l1_s5_d', 'l1_glu_w', 'l1_glu_b',
               'l1_rw_mu', 'l1_rw_w0', 'l1_rw_w1', 'l1_rw_w2', 'l1_rw_a0', 'l1_rw_a1', 'l1_rw_a2', 'l1_rw_g1', 'l1_rw_g2',
               'l1_rw_kk', 'l1_rw_ka', 'l1_rw_rk', 'l1_ln_w', 'l1_ln_b']


def core_inputs(inp, b, pj, NP, consts):
    f = lambda a: np.ascontiguousarray(np.asarray(a, dtype=np.float32))
    xs = f(inp['x_sample'][b])
    xp = f(inp['x_prompt'][pj * NP:(pj + 1) * NP]).reshape(NP * LP, D)
    m = {'x_tok': np.concatenate([xs, xp], axis=0),
         'cond': np.stack([f(inp['c'][b]), f(inp['c_ctx'])], axis=0),
         'st_ret': f(inp['state_l0_ret'][b]), 'st_lru': f(inp['state_l0_lru'][b]),
         'st_s5': f(inp['state_l1_s5'][b]), 'st_rwkv': f(inp['state_l1_rwkv'][b])}
    for k in WEIGHT_KEYS:
        m[k] = f(inp[k])
    m.update(consts)
    return m


_PROG_CACHE = {}


def get_prog(cfg):
    key = tuple(sorted(cfg.items()))
    if key not in _PROG_CACHE:
        p = Prog(cfg)
        p.build()
        _PROG_CACHE[key] = p
    return _PROG_CACHE[key]


def kernel(**inputs):
    LS = 4096
    NP = 4
    cfg = dict(LS=LS, NP=NP)
    prog = get_prog(cfg)
    consts = host_consts(LS)
    in_maps = [core_inputs(inputs, cid % 4, cid, NP, consts) for cid in range(8)]
    res = run_bass_kernel_spmd(prog.nc, in_maps, core_ids=list(range(8)))
    R = res.results
    y_sample = np.stack([R[b]['y_tok'][:LS] for b in range(4)], axis=0)
    y_prompt = np.concatenate([R[cid]['y_tok'][LS:].reshape(NP, LP, D) for cid in range(8)], axis=0)
    new_ret = np.concatenate([R[cid]['new_ret'] for cid in range(8)], axis=0)
    new_lru = np.concatenate([R[cid]['new_lru'] for cid in range(8)], axis=0)
    new_s5 = np.concatenate([R[cid]['new_s5'] for cid in range(8)], axis=0)
    new_rwkv = np.concatenate([R[cid]['new_rwkv'] for cid in range(8)], axis=0)
    return (y_prompt.astype(np.float32), y_sample.astype(np.float32), new_ret.astype(np.float32),
            new_lru.astype(np.float32), new_s5.astype(np.float32), new_rwkv.astype(np.float32))
```

```python
import contextlib
import math
import numpy as np
import concourse.bass as bass
import concourse.mybir as mybir
from concourse.bass_utils import run_bass_kernel_spmd

F32 = mybir.dt.float32
BF16 = mybir.dt.bfloat16
I32 = mybir.dt.int32
ALU = mybir.AluOpType
AF = mybir.ActivationFunctionType

D = 1024
DFF = 2816
NJ = DFF // 128
NMOD = 9
LP = 256


class Ctx:
    def __init__(self, nc, es):
        self.nc = nc
        self.es = es
        self.eng = {'pe': nc.tensor, 'act': nc.scalar, 'dve': nc.vector, 'pool': nc.gpsimd, 'sp': nc.sync}
        self.sem = {}
        self.cnt = {}
        for n in ['pe', 'act', 'dve', 'pool']:
            self.sem[n] = es.enter_context(nc.semaphore('s_' + n))
            self.cnt[n] = 0
        self.slots = {}
        for q, n in (('sp', 8), ('pool', 6), ('act', 4)):
            lst = []
            for i in range(n):
                nm = 'd_%s%d' % (q, i)
                self.sem[nm] = es.enter_context(nc.semaphore(nm))
                self.cnt[nm] = 0
                lst.append(nm)
            self.slots[q] = lst
        self.slot_i = {'sp': 0, 'pool': 0, 'act': 0}
        self.waited = {e: {} for e in self.eng}
        self.lw = {}
        self.rd = {}
        self.nops = 0

    def _deps(self, r, w):
        d = {}

        def add(ev):
            if ev is None:
                return
            n, v = ev
            if d.get(n, 0) < v:
                d[n] = v
        for k in r:
            add(self.lw.get(k))
        for k in w:
            add(self.lw.get(k))
            for n, v in self.rd.get(k, {}).items():
                add((n, v))
        return d

    def _wait(self, e, d):
        for n, v in d.items():
            if e == 'pe' and n == 'pe':
                continue
            if self.waited[e].get(n, 0) < v:
                self.eng[e].wait_ge(self.sem[n], v)
                self.waited[e][n] = v

    def _rec(self, ev, r, w):
        for k in w:
            self.lw[k] = ev
            self.rd[k] = {}
        for k in r:
            dd = self.rd.setdefault(k, {})
            if dd.get(ev[0], 0) < ev[1]:
                dd[ev[0]] = ev[1]

    def op(self, e, fn, r=(), w=()):
        psr = [k for k in r if isinstance(k, tuple) and k[0] == 'ps' and k not in w]
        if psr:
            w = list(w) + psr
        self._wait(e, self._deps(r, w))
        inst = fn(self.eng[e])
        self.cnt[e] += 1
        inst.then_inc(self.sem[e], 1)
        self._rec((e, self.cnt[e]), r, w)
        self.nops += 1

    def dma(self, q, out, in_, r=(), w=(), **kw):
        d = self._deps(r, w)
        sl = self.slots[q][self.slot_i[q] % len(self.slots[q])]
        self.slot_i[q] += 1
        if self.cnt[sl] > 0:
            d[sl] = max(d.get(sl, 0), self.cnt[sl])
        self._wait(q, d)
        inst = self.eng[q].dma_start(out=out, in_=in_, **kw)
        self.cnt[sl] += 16
        inst.then_inc(self.sem[sl], 16)
        self._rec((sl, self.cnt[sl]), r, w)
        self.nops += 1

    def barrier(self, engines=('pe', 'act', 'dve', 'pool', 'sp')):
        d = {n: v for n, v in self.cnt.items() if v > 0}
        for e in engines:
            self._wait(e, d)
        self.lw = {}
        self.rd = {}

    def finish(self):
        d = {n: v for n, v in self.cnt.items() if v > 0}
        self._wait('sp', d)


def _col(ap, c):
    return ap[:, c:c + 1]


class Prog:
    def __init__(self, cfg):
        self.cfg = cfg
        self.LS = cfg['LS']
        self.NP = cfg['NP']
        self.NT = self.LS + self.NP * LP
        self.debug = cfg.get('debug', False)
        self.stages = cfg.get('stages', 'all')

    def build(self):
        nc = bass.Bass("TRN2", target_bir_lowering=False)
        self.nc = nc
        NT, LS, NP = self.NT, self.LS, self.NP

        def din(name, shape, dt=F32):
            return nc.dram_tensor(name, list(shape), dt, kind="ExternalInput").ap()

        def dout(name, shape, dt=F32):
            return nc.dram_tensor(name, list(shape), dt, kind="ExternalOutput").ap()

        def dscr(name, shape, dt=F32):
            kind = "ExternalOutput" if self.debug else "Internal"
            return nc.dram_tensor(name, list(shape), dt, kind=kind).ap()

        I = {}
        I['x_tok'] = din('x_tok', [NT, D])
        I['cond'] = din('cond', [2, D])
        I['st_ret'] = din('st_ret', [2, 4, 128, 128])
        I['st_lru'] = din('st_lru', [2, 512])
        I['st_s5'] = din('st_s5', [2, 2, 32, 64])
        I['st_rwkv'] = din('st_rwkv', [2, 8, 64, 64])
        I['mod_w'] = din('mod_w', [2, D, NMOD * D])
        I['mod_b'] = din('mod_b', [2, NMOD * D])
        I['norm_g'] = din('norm_g', [2, 3, D])
        for nm in ('ffn1', 'ffn2'):
            I[nm + '_w13'] = din(nm + '_w13', [2, D, 2 * DFF])
            I[nm + '_w2'] = din(nm + '_w2', [2, DFF, D])
        I['final_norm'] = din('final_norm', [D])
        I['l0_w_in'] = din('l0_w_in', [D, 3072])
        I['l0_w_out'] = din('l0_w_out', [D, D])
        I['l0_ret_decay'] = din('l0_ret_decay', [2, 4])
        I['l0_conv_w'] = din('l0_conv_w', [4, 512])
        I['l0_conv_b'] = din('l0_conv_b', [512])
        I['l0_lru_lam'] = din('l0_lru_lam', [2, 512])
        I['l0_lru_wa'] = din('l0_lru_wa', [2, 8, 64, 64])
        I['l0_lru_ba'] = din('l0_lru_ba', [2, 512])
        I['l0_lru_wx'] = din('l0_lru_wx', [2, 8, 64, 64])
        I['l0_lru_bx'] = din('l0_lru_bx', [2, 512])
        I['l1_w_in'] = din('l1_w_in', [D, 2560])
        I['l1_w_out'] = din('l1_w_out', [D, D])
        I['l1_s5_a_re'] = din('l1_s5_a_re', [2, 32, 64])
        I['l1_s5_a_im'] = din('l1_s5_a_im', [2, 32, 64])
        I['l1_s5_log_dt'] = din('l1_s5_log_dt', [2, 32])
        I['l1_s5_b_re'] = din('l1_s5_b_re', [2, 32, 64, 16])
        I['l1_s5_b_im'] = din('l1_s5_b_im', [2, 32, 64, 16])
        I['l1_s5_c_re'] = din('l1_s5_c_re', [2, 32, 16, 64])
        I['l1_s5_c_im'] = din('l1_s5_c_im', [2, 32, 16, 64])
        I['l1_s5_d'] = din('l1_s5_d', [512])
        I['l1_glu_w'] = din('l1_glu_w', [512, 512])
        I['l1_glu_b'] = din('l1_glu_b', [512])
        I['l1_rw_mu'] = din('l1_rw_mu', [6, 512])
        I['l1_rw_w0'] = din('l1_rw_w0', [2, 512])
        I['l1_rw_w1'] = din('l1_rw_w1', [2, 512, 64])
        I['l1_rw_w2'] = din('l1_rw_w2', [2, 64, 512])
        I['l1_rw_a0'] = din('l1_rw_a0', [2, 512])
        I['l1_rw_a1'] = din('l1_rw_a1', [2, 512, 64])
        I['l1_rw_a2'] = din('l1_rw_a2', [2, 64, 512])
        I['l1_rw_g1'] = din('l1_rw_g1', [512, 128])
        I['l1_rw_g2'] = din('l1_rw_g2', [128, 512])
        I['l1_rw_kk'] = din('l1_rw_kk', [512])
        I['l1_rw_ka'] = din('l1_rw_ka', [512])
        I['l1_rw_rk'] = din('l1_rw_rk', [512])
        I['l1_ln_w'] = din('l1_ln_w', [512])
        I['l1_ln_b'] = din('l1_ln_b', [512])
        I['c_ident'] = din('c_ident', [128, 128])
        I['c_rope_cos'] = din('c_rope_cos', [128, LS])
        I['c_rope_sin'] = din('c_rope_sin', [128, LS])
        I['c_rope_rot'] = din('c_rope_rot', [128, 128])
        I['c_misc'] = din('c_misc', [128, 1024])
        I['c_iota'] = din('c_iota', [128, 512])
        I['c_s5mask'] = din('c_s5mask', [128, 512])
        I['c_cmf'] = din('c_cmf', [128, 512])
        I['c_blk'] = din('c_blk', [128, 128])
        I['c_rwm'] = din('c_rwm', [128, 1024])
        self.I = I

        O = {}
        O['y_tok'] = dout('y_tok', [NT, D])
        O['new_ret'] = dout('new_ret', [NP, 2, 4, 128, 128])
        O['new_lru'] = dout('new_lru', [NP, 2, 512])
        O['new_s5'] = dout('new_s5', [NP, 2, 2, 32, 64])
        O['new_rwkv'] = dout('new_rwkv', [NP, 2, 8, 64, 64])
        self.O = O

        self.XS = dscr('XS', [D, NT])
        self.P = dscr('Pscr', [3072, NT])
        self.Y = dscr('Yscr', [D, NT], BF16)
        self.YF = dscr('YFscr', [512, NT])
        self.BFs = dscr('BFscr', [512, NT])

        with contextlib.ExitStack() as es:
            c = Ctx(nc, es)
            self.c = c
            self.es = es
            self.PS = [es.enter_context(nc.psum_tensor('ps%d' % i, [128, 512], F32)) for i in range(8)]
            self.ident = es.enter_context(nc.sbuf_tensor('ident', [128, 128], F32))
            self.ones_bf = es.enter_context(nc.sbuf_tensor('ones_bf', [128, 128], BF16))
            self.cst = es.enter_context(nc.sbuf_tensor('cst', [128, 8], F32))
            self.MODC = es.enter_context(nc.sbuf_tensor('MODC', [128, 2 * 3 * 3 * 2 + 2, 8], F32))
            c.dma('sp', self.ident[:], I['c_ident'][:, :], w=['ident'])
            c.op('dve', lambda e: e.memset(self.ones_bf[:], 1.0), w=['ones_bf'])
            for j, v in enumerate([1e-6, 1e-5, 64e-5, 0.0, 1.0]):
                c.op('dve', lambda e: e.memset(self.cst[:, j:j + 1], v), w=[('cst', j)])

            st = self.stages
            self.phase_transpose_in()
            self.phase_mod()
            for i in range(2):
                self.phase_ffn(i, 0)
                self.phase_inproj(i)
                if i == 0:
                    self.phase_mix_even()
                else:
                    self.phase_mix_odd()
                self.phase_outproj(i)
                self.phase_ffn(i, 2)
            self.phase_final()
            c.finish()
        return nc

    def sb(self, ph, name, shape, dt):
        self._uid = getattr(self, '_uid', 0) + 1
        return ph.enter_context(self.nc.sbuf_tensor('%s_u%d' % (name, self._uid), shape, dt))

    def modidx(self, i, s, kind, j):
        return ((i * 3 + s) * 3 + kind) * 2 + j

    def cond_of_tile(self, t0):
        return 0 if t0 < self.LS else 1

    def XSv(self):
        return self.XS.rearrange("(c p) t -> p c t", p=128)

    def phase_transpose_in(self):
        c, nc, I = self.c, self.nc, self.I
        with contextlib.ExitStack() as ph:
            xin = [self.sb(ph, 'p0_xin%d' % b, [128, 4, D], F32) for b in range(2)]
            xT = [self.sb(ph, 'p0_xT%d' % b, [128, 8, 512], F32) for b in range(2)]
            xv = I['x_tok'].rearrange("(n q p) d -> n p q d", p=128, q=4)
            ng = self.NT // 512
            for g in range(ng):
                b = g % 2
                c.dma('sp', xin[b][:], xv[g], w=[('xin', b)])
                for kc in range(8):
                    ps = self.PS[kc % 4]
                    for q in range(4):
                        c.op('pe', lambda e: e.transpose(out=ps[:, q * 128:(q + 1) * 128], in_=xin[b][:, q, kc * 128:(kc + 1) * 128], identity=self.ident[:]),
                             r=[('xin', b), 'ident'], w=[('ps', kc % 4)])
                    eng = 'act' if kc % 2 == 0 else 'dve'
                    if eng == 'act':
                        c.op('act', lambda e: e.activation(out=xT[b][:, kc, :], in_=ps[:], func=AF.Identity), r=[('ps', kc % 4)], w=[('xT', b, kc)])
                    else:
                        c.op('dve', lambda e: e.tensor_copy(out=xT[b][:, kc, :], in_=ps[:]), r=[('ps', kc % 4)], w=[('xT', b, kc)])
                c.dma('sp', self.XSv()[:, :, g * 512:(g + 1) * 512], xT[b][:], r=[('xT', b, kc) for kc in range(8)], w=[('XS', g * 512), ('XS', g * 512 + 256)])
            c.barrier()

    def phase_mod(self):
        c, nc, I = self.c, self.nc, self.I
        with contextlib.ExitStack() as ph:
            NB = 1152
            wb = [self.sb(ph, 'pm_w%d' % b, [128, 8, NB], F32) for b in range(2)]
            cT = self.sb(ph, 'pm_cT', [128, 8, 2], F32)
            sT = self.sb(ph, 'pm_sT', [128, 8, 2], F32)
            mb = self.sb(ph, 'pm_mb', [128, 2, 72], F32)
            ng = self.sb(ph, 'pm_ng', [128, 7, 8], F32)
            MT = self.sb(ph, 'pm_MT', [128, 2, 72, 2], F32)
            tmp = self.sb(ph, 'pm_tmp', [128, 8], F32)
            for j in range(2):
                c.dma('sp', cT[:, :, j], I['cond'][j].rearrange("(c p) -> p c", p=128), w=['cT'], allow_slow_non_contiguous=True)
            for i in range(2):
                c.dma('sp', mb[:, i, :], I['mod_b'][i].rearrange("(k p) -> p k", p=128), w=['mb'], allow_slow_non_contiguous=True)
                for s in range(3):
                    c.dma('sp', ng[:, i * 3 + s, :], I['norm_g'][i, s].rearrange("(c p) -> p c", p=128), w=['ng'], allow_slow_non_contiguous=True)
            c.dma('sp', ng[:, 6, :], I['final_norm'].rearrange("(c p) -> p c", p=128), w=['ng'], allow_slow_non_contiguous=True)
            c.op('act', lambda e: e.activation(out=sT[:], in_=cT[:], func=AF.Silu), r=['cT'], w=['sT'])
            ps = self.PS[0]
            psv = ps[:, 0:144].rearrange("p (f j) -> p f j", j=2)
            blk = 0
            for i in range(2):
                wv = I['mod_w'][i].rearrange("(c p) n -> p c n", p=128)
                for nb in range(8):
                    b = blk % 2
                    blk += 1
                    c.dma('sp', wb[b][:], wv[:, :, nb * NB:(nb + 1) * NB], w=[('wb', b)])
                    for fc in range(9):
                        f = nb * 9 + fc
                        for kc in range(8):
                            c.op('pe', lambda e: e.matmul(psv[:, f, :], lhsT=wb[b][:, kc, fc * 128:(fc + 1) * 128], rhs=sT[:, kc, :], start=(kc == 0), stop=(kc == 7)),
                                 r=[('wb', b), 'sT'], w=[('ps', 0)])
                c.op('dve', lambda e: e.tensor_tensor(out=MT[:, i, :, :], in0=psv[:, :, :], in1=mb[:, i, :].unsqueeze(2).broadcast_to([128, 72, 2]), op=ALU.add),
                     r=[('ps', 0), 'mb'], w=['MT'])
            MC = self.MODC
            for i in range(2):
                for s in range(3):
                    for j in range(2):
                        sh = MT[:, i, (3 * s) * 8:(3 * s + 1) * 8, j]
                        sc = MT[:, i, (3 * s + 1) * 8:(3 * s + 2) * 8, j]
                        ga = MT[:, i, (3 * s + 2) * 8:(3 * s + 3) * 8, j]
                        c.op('dve', lambda e: e.scalar_tensor_tensor(out=MC[:, self.modidx(i, s, 0, j), :], in0=sc, scalar=1.0, in1=ng[:, i * 3 + s, :], op0=ALU.add, op1=ALU.mult),
                             r=['MT', 'ng'], w=['MODC'])
                        c.op('dve', lambda e: e.tensor_copy(out=MC[:, self.modidx(i, s, 1, j), :], in_=sh), r=['MT'], w=['MODC'])
                        c.op('dve', lambda e: e.tensor_scalar(out=MC[:, self.modidx(i, s, 2, j), :], in0=ga, scalar1=(0.5 if s != 1 else 1.0), scalar2=None, op0=ALU.mult),
                             r=['MT'], w=['MODC'])
            c.op('dve', lambda e: e.tensor_copy(out=MC[:, 36, :], in_=ng[:, 6, :]), r=['ng'], w=['MODC'])
            c.op('dve', lambda e: e.memset(MC[:, 37, :], 0.0), w=['MODC'])
            c.barrier()

    def norm_mod(self, xt, xkey, TT, s1i, s2i, sq, rs, rstd, tmp, h, hkey):
        c = self.c
        MC = self.MODC
        for kc in range(8):
            c.op('act', lambda e: e.activation(out=sq[:, kc % 2, :TT], in_=xt[:, kc, :], func=AF.Square), r=[xkey], w=[('sq', kc % 2)])
            c.op('pe', lambda e: e.matmul(self.PS[0][:, :TT], lhsT=self.ones_bf[:], rhs=sq[:, kc % 2, :TT], start=(kc == 0), stop=(kc == 7)),
                 r=[('sq', kc % 2), 'ones_bf'], w=[('ps', 0)])
        c.op('act', lambda e: e.activation(out=rs[:, :TT], in_=self.PS[0][:, :TT], func=AF.Sqrt, scale=1.0 / D, bias=self.cst[:, 0:1]),
             r=[('ps', 0), ('cst', 0)], w=['rs'])
        rk = 'rs' if rstd is rs else 'rstd'
        c.op('dve', lambda e: e.reciprocal(out=rstd[:, :TT], in_=rs[:, :TT]), r=['rs'], w=[rk])
        for kc in range(8):
            t = tmp[kc % 2]
            c.op('dve', lambda e: e.tensor_tensor(out=t[:, :TT], in0=xt[:, kc, :], in1=rstd[:, :TT], op=ALU.mult), r=[xkey, rk], w=[('ntmp', kc % 2)])
            c.op('act', lambda e: e.activation(out=h[:, kc, :TT], in_=t[:, :TT], func=AF.Identity, scale=MC[:, s1i, kc:kc + 1], bias=MC[:, s2i, kc:kc + 1]),
                 r=[('ntmp', kc % 2), 'MODC'], w=[hkey])

    def phase_ffn(self, i, s):
        c, nc, I = self.c, self.nc, self.I
        nm = 'ffn1' if s == 0 else 'ffn2'
        TT = 512
        with contextlib.ExitStack() as ph:
            W13 = self.sb(ph, 'ff_w13', [128, 8, 2 * DFF], BF16)
            W2 = self.sb(ph, 'ff_w2', [128, NJ, D], BF16)
            xt = [self.sb(ph, 'ff_xt%d' % b, [128, 8, TT], F32) for b in range(2)]
            h = self.sb(ph, 'ff_h', [128, 8, TT], BF16)
            a = self.sb(ph, 'ff_a', [128, NJ, TT], BF16)
            sq = self.sb(ph, 'ff_sq', [128, 2, TT], BF16)
            rs = self.sb(ph, 'ff_rs', [128, TT], F32)
            rstd = rs
            tmp = [self.sb(ph, 'ff_tmp%d' % b, [128, TT], F32) for b in range(2)]
            sg = [self.sb(ph, 'ff_sg%d' % b, [128, TT], BF16) for b in range(2)]
            w13v = I[nm + '_w13'][i].rearrange("(c p) n -> p c n", p=128)
            w2v = I[nm + '_w2'][i].rearrange("(j p) n -> p j n", p=128)
            for q in range(4):
                c.dma('pool', W13[:, :, q * 1408:(q + 1) * 1408], w13v[:, :, q * 1408:(q + 1) * 1408], w=[('w13', q)])
            for q in range(2):
                c.dma('pool', W2[:, q * 11:(q + 1) * 11, :], w2v[:, q * 11:(q + 1) * 11, :], w=[('w2', q)])
            ntile = self.NT // TT
            XSv = self.XSv()

            def load(tt):
                c.dma('sp', xt[tt % 2][:], XSv[:, :, tt * TT:(tt + 1) * TT], r=[('XS', tt * TT)], w=[('xt', tt % 2)])

            def norm(tt):
                j = self.cond_of_tile(tt * TT)
                self.norm_mod(xt[tt % 2], ('xt', tt % 2), TT, self.modidx(i, s, 0, j), self.modidx(i, s, 1, j), sq, rs, rstd, tmp, h, 'h')

            load(0)
            norm(0)
            for tt in range(ntile):
                b = tt % 2
                j = self.cond_of_tile(tt * TT)
                if tt + 1 < ntile:
                    load(tt + 1)
                for jj in range(NJ):
                    gp = self.PS[1 + jj % 2]
                    up = self.PS[3 + jj % 2]
                    for kc in range(8):
                        c.op('pe', lambda e: e.matmul(gp[:, :TT], lhsT=W13[:, kc, jj * 128:(jj + 1) * 128], rhs=h[:, kc, :], start=(kc == 0), stop=(kc == 7)),
                             r=[('w13', jj // 11), 'h'], w=[('ps', 1 + jj % 2)])
                    for kc in range(8):
                        c.op('pe', lambda e: e.matmul(up[:, :TT], lhsT=W13[:, kc, DFF + jj * 128:DFF + (jj + 1) * 128], rhs=h[:, kc, :], start=(kc == 0), stop=(kc == 7)),
                             r=[('w13', 2 + jj // 11), 'h'], w=[('ps', 3 + jj % 2)])
                    c.op('act', lambda e: e.activation(out=sg[jj % 2][:], in_=gp[:, :TT], func=AF.Silu), r=[('ps', 1 + jj % 2)], w=[('sg', jj % 2)])
                    c.op('dve', lambda e: e.tensor_tensor(out=a[:, jj, :], in0=sg[jj % 2][:], in1=up[:, :TT], op=ALU.mult),
                         r=[('sg', jj % 2), ('ps', 3 + jj % 2)], w=[('a', jj)])
                if tt + 1 < ntile:
                    norm(tt + 1)
                gi = self.modidx(i, s, 2, j)
                for cc in range(8):
                    op_ = self.PS[5 + cc % 2]
                    for jj in range(NJ):
                        c.op('pe', lambda e: e.matmul(op_[:, :TT], lhsT=W2[:, jj, cc * 128:(cc + 1) * 128], rhs=a[:, jj, :], start=(jj == 0), stop=(jj == NJ - 1)),
                             r=[('w2', jj // 11), ('a', jj)], w=[('ps', 5 + cc % 2)])
                    c.op('dve', lambda e: e.scalar_tensor_tensor(out=xt[b][:, cc, :], in0=op_[:, :TT], scalar=self.MODC[:, gi, cc:cc + 1], in1=xt[b][:, cc, :], op0=ALU.mult, op1=ALU.add),
                         r=[('ps', 5 + cc % 2), 'MODC', ('xt', b)], w=[('xt', b)])
                c.dma('sp', XSv[:, :, tt * TT:(tt + 1) * TT], xt[b][:], r=[('xt', b)], w=[('XS', tt * TT)])
            c.barrier()

    def phase_inproj(self, i):
        c, nc, I = self.c, self.nc, self.I
        NIN = 3072 if i == 0 else 2560
        TT = 512
        nm_ = NIN // 128
        with contextlib.ExitStack() as ph:
            W = self.sb(ph, 'ip_w', [128, 8, NIN], BF16)
            xt = [self.sb(ph, 'ip_xt%d' % b, [128, 8, TT], F32) for b in range(2)]
            h = self.sb(ph, 'ip_h', [128, 8, TT], BF16)
            sq = self.sb(ph, 'ip_sq', [128, 8, TT], BF16)
            rs = self.sb(ph, 'ip_rs', [128, TT], F32)
            rstd = self.sb(ph, 'ip_rstd', [128, TT], F32)
            tmp = [self.sb(ph, 'ip_tmp%d' % b, [128, TT], F32) for b in range(2)]
            og = [self.sb(ph, 'ip_og%d' % b, [128, 4, TT], F32) for b in range(2)]
            wv = I['l%d_w_in' % i].rearrange("(c p) n -> p c n", p=128)
            for q in range(2):
                hh = NIN // 2
                c.dma('pool', W[:, :, q * hh:(q + 1) * hh], wv[:, :, q * hh:(q + 1) * hh], w=[('w', q)])
            ntile = self.NT // TT
            XSv = self.XSv()
            Pv = self.P.rearrange("(m p) t -> p m t", p=128)

            def load(tt):
                c.dma('sp', xt[tt % 2][:], XSv[:, :, tt * TT:(tt + 1) * TT], r=[('XS', tt * TT)], w=[('xt', tt % 2)])
            load(0)
            gcount = 0
            for tt in range(ntile):
                j = self.cond_of_tile(tt * TT)
                if tt + 1 < ntile:
                    load(tt + 1)
                self.norm_mod(xt[tt % 2], ('xt', tt % 2), TT, self.modidx(i, 1, 0, j), self.modidx(i, 1, 1, j), sq, rs, rstd, tmp, h, 'h')
                for m in range(nm_):
                    ps = self.PS[1 + m % 4]
                    g = gcount % 2
                    for kc in range(8):
                        c.op('pe', lambda e: e.matmul(ps[:, :TT], lhsT=W[:, kc, m * 128:(m + 1) * 128], rhs=h[:, kc, :], start=(kc == 0), stop=(kc == 7)),
                             r=[('w', (m * 128) // (NIN // 2)), 'h'], w=[('ps', 1 + m % 4)])
                    scale = (128.0 ** -0.5) if (i == 0 and m < 4) else 1.0
                    if m % 2 == 0:
                        c.op('act', lambda e: e.activation(out=og[g][:, m % 4, :], in_=ps[:, :TT], func=AF.Identity, scale=scale), r=[('ps', 1 + m % 4)], w=[('og', g, m % 4)])
                    else:
                        c.op('dve', lambda e: e.tensor_scalar(out=og[g][:, m % 4, :], in0=ps[:, :TT], scalar1=scale, scalar2=None, op0=ALU.mult), r=[('ps', 1 + m % 4)], w=[('og', g, m % 4)])
                    if m % 4 == 3:
                        m0 = m - 3
                        c.dma('sp', Pv[:, m0:m0 + 4, tt * TT:(tt + 1) * TT], og[g][:], r=[('og', g, q) for q in range(4)], w=[('P', tt * TT)])
                        gcount += 1
            c.barrier()

    def phase_outproj(self, i):
        c, nc, I = self.c, self.nc, self.I
        TT = 512
        with contextlib.ExitStack() as ph:
            W = self.sb(ph, 'op_w', [128, 8, D], BF16)
            xt = [self.sb(ph, 'op_xt%d' % b, [128, 8, TT], F32) for b in range(2)]
            yt = [self.sb(ph, 'op_yt%d' % b, [128, 8, TT], BF16) for b in range(2)]
            wv = I['l%d_w_out' % i].rearrange("(c p) n -> p c n", p=128)
            c.dma('pool', W[:], wv, w=['w'])
            ntile = self.NT // TT
            XSv = self.XSv()
            Yv = self.Y.rearrange("(c p) t -> p c t", p=128)

            def load(tt):
                c.dma('sp', xt[tt % 2][:], XSv[:, :, tt * TT:(tt + 1) * TT], r=[('XS', tt * TT)], w=[('xt', tt % 2)])
                c.dma('sp', yt[tt % 2][:], Yv[:, :, tt * TT:(tt + 1) * TT], r=[('Y', tt * TT)], w=[('yt', tt % 2)])
            load(0)
            for tt in range(ntile):
                b = tt % 2
                j = self.cond_of_tile(tt * TT)
                if tt + 1 < ntile:
                    load(tt + 1)
                gi = self.modidx(i, 1, 2, j)
                for cc in range(8):
                    ps = self.PS[1 + cc % 4]
                    for kc in range(8):
                        c.op('pe', lambda e: e.matmul(ps[:, :TT], lhsT=W[:, kc, cc * 128:(cc + 1) * 128], rhs=yt[b][:, kc, :], start=(kc == 0), stop=(kc == 7)),
                             r=['w', ('yt', b)], w=[('ps', 1 + cc % 4)])
                    c.op('dve', lambda e: e.scalar_tensor_tensor(out=xt[b][:, cc, :], in0=ps[:, :TT], scalar=self.MODC[:, gi, cc:cc + 1], in1=xt[b][:, cc, :], op0=ALU.mult, op1=ALU.add),
                         r=[('ps', 1 + cc % 4), 'MODC', ('xt', b)], w=[('xt', b)])
                c.dma('sp', XSv[:, :, tt * TT:(tt + 1) * TT], xt[b][:], r=[('xt', b)], w=[('XS', tt * TT)])
            c.barrier()

    def phase_final(self):
        c, nc, I, O = self.c, self.nc, self.I, self.O
        TT = 256
        with contextlib.ExitStack() as ph:
            xt = [self.sb(ph, 'fn_xt%d' % b, [128, 8, TT], F32) for b in range(2)]
            h = self.sb(ph, 'fn_h', [128, 8, TT], F32)
            sq = self.sb(ph, 'fn_sq', [128, 8, TT], BF16)
            rs = self.sb(ph, 'fn_rs', [128, TT], F32)
            rstd = self.sb(ph, 'fn_rstd', [128, TT], F32)
            tmp = [self.sb(ph, 'fn_tmp%d' % b, [128, TT], F32) for b in range(2)]
            yo = [self.sb(ph, 'fn_yo%d' % b, [128, 2, D], F32) for b in range(2)]
            ntile = self.NT // TT
            XSv = self.XSv()
            yv = O['y_tok'].rearrange("(n q p) d -> n p q d", p=128, q=2)

            def load(tt):
                c.dma('sp', xt[tt % 2][:], XSv[:, :, tt * TT:(tt + 1) * TT], r=[('XS', tt * TT)], w=[('xt', tt % 2)])
            load(0)
            for tt in range(ntile):
                b = tt % 2
                if tt + 1 < ntile:
                    load(tt + 1)
                self.norm_mod(xt[b], ('xt', b), TT, 36, 37, sq, rs, rstd, tmp, h, 'h')
                for q in range(2):
                    for kc in range(8):
                        ps = self.PS[1 + (kc // 4) % 2 + 2 * q]
                        c.op('pe', lambda e: e.transpose(out=ps[:, (kc % 4) * 128:(kc % 4 + 1) * 128], in_=h[:, kc, q * 128:(q + 1) * 128], identity=self.ident[:]),
                             r=['h', 'ident'], w=[('ps', 1 + (kc // 4) % 2 + 2 * q)])
                        if kc % 4 == 3:
                            half = kc // 4
                            if half == 0:
                                c.op('act', lambda e: e.activation(out=yo[b][:, q, 0:512], in_=ps[:], func=AF.Identity), r=[('ps', 1 + 2 * q)], w=[('yo', b, q, 0)])
                            else:
                                c.op('dve', lambda e: e.tensor_copy(out=yo[b][:, q, 512:1024], in_=ps[:]), r=[('ps', 2 + 2 * q)], w=[('yo', b, q, 1)])
                c.dma('sp', yv[tt], yo[b][:], r=[('yo', b, q, hh) for q in range(2) for hh in range(2)], w=[('yout', tt)])
            c.barrier()

    def seqs(self):
        lst = [dict(off=0, L=self.LS, latent=True, pidx=None)]
        for i in range(self.NP):
            lst.append(dict(off=self.LS + i * LP, L=LP, latent=False, pidx=i))
        return lst

    def phase_mix_even(self):
        import os
        which = os.environ.get('EVENPARTS', 'rl')
        if which != 'rl':
            self.mix_stub()
        if 'r' in which:
            self.mix_even_ret()
        if 'l' in which:
            self.mix_even_lru()

    def mix_even_ret(self):
        c, nc, I, O = self.c, self.nc, self.I, self.O
        LS = self.LS
        NCH = LS // 128
        Pv = self.P.rearrange("(m p) t -> p m t", p=128)
        Yv = self.Y.rearrange("(c p) t -> p c t", p=128)
        with contextlib.ExitStack() as ph:
            sb = lambda n, sh, dt: self.sb(ph, 're_' + n, sh, dt)
            misc = sb('misc', [128, 1024], F32)
            ROT = sb('rot', [128, 128], F32)
            COS = sb('cos', [128, LS], F32)
            SIN = sb('sin', [128, LS], F32)
            RD = sb('rd', [128, 8], F32)
            LG = sb('lg', [128, 8], F32)
            MT = sb('MT', [128, 4, 128], F32)
            QD = sb('QD', [128, 8, 128], F32)
            KD = sb('KD', [128, 8], F32)
            GC = sb('GC', [128, 8], F32)
            onesd = sb('onesd', [128, 128], F32)
            tA = sb('tA', [128, 128], F32)
            tB = sb('tB', [128, 128], F32)
            q_bf = sb('q_bf', [128, LS], BF16)
            k_bf = sb('k_bf', [128, LS], BF16)
            qf_ = sb('qf', [128, LS], BF16)
            qb_ = sb('qb', [128, LS], BF16)
            Vtok = sb('Vtok', [128, NCH, 128], BF16)
            Kf = sb('Kf', [128, NCH, 128], BF16)
            Kb = sb('Kb', [128, NCH, 128], BF16)
            o_a = sb('o_a', [128, LS], F32)
            o_b = sb('o_b', [128, LS], F32)
            qs = sb('qs', [128, 512], F32)
            ks = sb('ks', [128, 512], F32)
            vs = sb('vs', [128, 512], F32)
            gs = sb('gs', [128, 512], F32)
            qr = sb('qr', [128, 512], F32)
            kr = sb('kr', [128, 512], F32)
            t1 = sb('t1', [128, 512], F32)
            t2 = sb('t2', [128, 512], F32)
            t3 = sb('t3', [128, 512], F32)
            ya = sb('ya', [128, 512], BF16)
            sTm = [sb('sTm%d' % b, [128, 128], BF16) for b in range(2)]
            Sst = [sb('S%d' % d, [128, 128], F32) for d in range(2)]
            Sbf = [sb('Sbf%d' % d, [128, 128], BF16) for d in range(2)]
            PS = self.PS

            c.dma('sp', misc[:], I['c_misc'][:, :], w=['misc'])
            c.dma('sp', ROT[:], I['c_rope_rot'][:, :], w=['ROT'])
            c.dma('sp', COS[:], I['c_rope_cos'][:, :], w=['COS'])
            c.dma('sp', SIN[:], I['c_rope_sin'][:, :], w=['SIN'])
            c.dma('sp', RD[:], I['l0_ret_decay'].rearrange("d h -> (d h)").partition_broadcast(128), w=['RD'])
            c.op('dve', lambda e: e.memset(onesd[:], 1.0 / 128.0), w=['onesd'])
            c.op('act', lambda e: e.activation(out=LG[:], in_=RD[:], func=AF.Sigmoid), r=['RD'], w=['LG'])
            c.op('act', lambda e: e.activation(out=LG[:], in_=LG[:], func=AF.Ln), r=['LG'], w=['LG'])
            absrel = misc[:, 0:128]
            m_le = misc[:, 128:256]
            m_ge = misc[:, 256:384]
            iota0 = misc[:, 640:768]
            iota1 = misc[:, 768:896]
            for h in range(4):
                c.op('act', lambda e: e.activation(out=tA[:], in_=absrel, func=AF.Exp, scale=LG[:, h:h + 1]), r=['misc', 'LG'], w=['tA'])
                c.op('dve', lambda e: e.tensor_tensor(out=tA[:], in0=tA[:], in1=m_le, op=ALU.mult), r=['tA', 'misc'], w=['tA'])
                c.op('act', lambda e: e.activation(out=tB[:], in_=absrel, func=AF.Exp, scale=LG[:, 4 + h:5 + h]), r=['misc', 'LG'], w=['tB'])
                c.op('dve', lambda e: e.tensor_tensor(out=tB[:], in0=tB[:], in1=m_ge, op=ALU.mult), r=['tB', 'misc'], w=['tB'])
                c.op('dve', lambda e: e.tensor_tensor(out=MT[:, h, :], in0=tA[:], in1=tB[:], op=ALU.add), r=['tA', 'tB'], w=['MT'])
                c.op('act', lambda e: e.activation(out=QD[:, h, :], in_=iota1, func=AF.Exp, scale=LG[:, h:h + 1]), r=['misc', 'LG'], w=['QD'])
                c.op('dve', lambda e: e.tensor_scalar(out=tA[:], in0=iota0, scalar1=-1.0, scalar2=128.0, op0=ALU.mult, op1=ALU.add), r=['misc', 'MT'], w=['tA'])
                c.op('act', lambda e: e.activation(out=QD[:, 4 + h, :], in_=tA[:], func=AF.Exp, scale=LG[:, 4 + h:5 + h]), r=['tA', 'LG'], w=['QD'])
                c.op('act', lambda e: e.activation(out=KD[:, h:h + 1], in_=misc[:, 897:898], func=AF.Exp, scale=LG[:, h:h + 1]), r=['misc', 'LG'], w=['KD'])
                c.op('act', lambda e: e.activation(out=KD[:, 4 + h:5 + h], in_=misc[:, 896:897], func=AF.Exp, scale=LG[:, 4 + h:5 + h]), r=['misc', 'LG'], w=['KD'])
                for d in range(2):
                    c.op('act', lambda e: e.activation(out=GC[:, d * 4 + h:d * 4 + h + 1], in_=misc[:, 898:899], func=AF.Exp, scale=LG[:, d * 4 + h:d * 4 + h + 1]), r=['misc', 'LG'], w=['GC'])

            import os
            RETSTOP = int(os.environ.get('RETSTOP', '9'))
            for sq_ in (self.seqs() if RETSTOP > 0 else []):
                off, L, latent, pidx = sq_['off'], sq_['L'], sq_['latent'], sq_['pidx']
                TS = min(512, L)
                nt = L // TS
                nch = L // 128
                cpt = TS // 128
                for h in range(4):
                    for ti in range(nt):
                        t0 = ti * TS
                        g0 = off + t0
                        c.dma('sp', qs[:, :TS], Pv[:, h, g0:g0 + TS], w=['qs'])
                        c.dma('sp', ks[:, :TS], Pv[:, 4 + h, g0:g0 + TS], w=['ks'])
                        c.dma('sp', vs[:, :TS], Pv[:, 8 + h, g0:g0 + TS], w=['vs'])
                        if latent:
                            for (src, skey, dst, dkey) in ((qs, 'qs', qr, 'qr'), (ks, 'ks', kr, 'kr')):
                                c.op('pe', lambda e: e.matmul(PS[1][:, :TS], lhsT=ROT[:], rhs=src[:, :TS], start=True, stop=True), r=['ROT', skey], w=[('ps', 1)])
                                c.op('dve', lambda e: e.tensor_tensor(out=t1[:, :TS], in0=src[:, :TS], in1=COS[:, t0:t0 + TS], op=ALU.mult), r=[skey, 'COS'], w=['t1'])
                                c.op('dve', lambda e: e.tensor_tensor(out=t2[:, :TS], in0=PS[1][:, :TS], in1=SIN[:, t0:t0 + TS], op=ALU.mult), r=[('ps', 1), 'SIN'], w=['t2'])
                                c.op('dve', lambda e: e.tensor_tensor(out=dst[:, :TS], in0=t1[:, :TS], in1=t2[:, :TS], op=ALU.add), r=['t1', 't2'], w=[dkey])
                            qsrc, qk_, ksrc, kk_ = qr, 'qr', kr, 'kr'
                        else:
                            qsrc, qk_, ksrc, kk_ = qs, 'qs', ks, 'ks'
                        c.op('act', lambda e: e.activation(out=q_bf[:, t0:t0 + TS], in_=qsrc[:, :TS], func=AF.Identity), r=[qk_], w=['q_bf'])
                        c.op('act', lambda e: e.activation(out=k_bf[:, t0:t0 + TS], in_=ksrc[:, :TS], func=AF.Identity), r=[kk_], w=['k_bf'])
                        qv = qsrc[:, :TS].rearrange("p (c j) -> p c j", j=128)
                        c.op('dve', lambda e: e.tensor_tensor(out=qf_[:, t0:t0 + TS].rearrange("p (c j) -> p c j", j=128), in0=qv, in1=QD[:, h, :].unsqueeze(1).broadcast_to([128, cpt, 128]), op=ALU.mult),
                             r=[qk_, 'QD'], w=['qf'])
                        c.op('dve', lambda e: e.tensor_tensor(out=qb_[:, t0:t0 + TS].rearrange("p (c j) -> p c j", j=128), in0=qv, in1=QD[:, 4 + h, :].unsqueeze(1).broadcast_to([128, cpt, 128]), op=ALU.mult),
                             r=[qk_, 'QD'], w=['qb'])
                        for cc in range(cpt):
                            ch = ti * cpt + cc
                            c.op('pe', lambda e: e.transpose(out=PS[2][:, cc * 128:(cc + 1) * 128], in_=vs[:, cc * 128:(cc + 1) * 128], identity=self.ident[:]), r=['vs', 'ident'], w=[('ps', 2)])
                            c.op('pe', lambda e: e.transpose(out=PS[3][:, cc * 128:(cc + 1) * 128], in_=ksrc[:, cc * 128:(cc + 1) * 128], identity=self.ident[:]), r=[kk_, 'ident'], w=[('ps', 3)])
                        ch0 = ti * cpt
                        c.op('act', lambda e: e.activation(out=Vtok[:, ch0:ch0 + cpt, :], in_=PS[2][:, :TS].rearrange("p (c j) -> p c j", j=128), func=AF.Identity), r=[('ps', 2)], w=['Vtok'])
                        c.op('act', lambda e: e.activation(out=Kf[:, ch0:ch0 + cpt, :], in_=PS[3][:, :TS].rearrange("p (c j) -> p c j", j=128), func=AF.Identity, scale=KD[:, h:h + 1]), r=[('ps', 3), 'KD'], w=['Kf'])
                        c.op('dve', lambda e: e.tensor_scalar(out=Kb[:, ch0:ch0 + cpt, :], in0=PS[3][:, :TS].rearrange("p (c j) -> p c j", j=128), scalar1=KD[:, 4 + h:5 + h], scalar2=None, op0=ALU.mult), r=[('ps', 3), 'KD'], w=['Kb'])
                    if RETSTOP < 2:
                        continue
                    for d in range(2):
                        if latent:
                            c.dma('sp', Sst[d][:], I['st_ret'][d, h], w=[('S', d)])
                        else:
                            c.op('dve', lambda e: e.memset(Sst[d][:], 0.0), w=[('S', d)])
                        c.op('act', lambda e: e.activation(out=Sbf[d][:], in_=Sst[d][:], func=AF.Identity), r=[('S', d)], w=[('Sbf', d)])
                    for idx in range(nch):
                        cf = idx
                        cb = nch - 1 - idx
                        p2 = idx % 2
                        bS = 1 if p2 == 0 else 6
                        bO = 2 if p2 == 0 else 7
                        fsl = slice(cf * 128, cf * 128 + 128)
                        bsl = slice(cb * 128, cb * 128 + 128)
                        sl = slice(0, 128)
                        c.op('pe', lambda e: e.matmul(PS[bS][:, sl], lhsT=k_bf[:, fsl], rhs=q_bf[:, fsl], start=True, stop=True), r=['k_bf', 'q_bf'], w=[('ps', bS)])
                        c.op('dve', lambda e: e.tensor_tensor(out=sTm[p2][:], in0=PS[bS][:, sl], in1=MT[:, h, :], op=ALU.mult), r=[('ps', bS), 'MT'], w=[('sTm', p2)])
                        c.op('pe', lambda e: e.matmul(PS[bO][:, sl], lhsT=Vtok[:, cf, :], rhs=sTm[p2][:], start=True, stop=False), r=['Vtok', ('sTm', p2)], w=[('ps', bO)])
                        c.op('pe', lambda e: e.matmul(PS[bO][:, sl], lhsT=Sbf[0][:], rhs=qf_[:, fsl], start=False, stop=True), r=[('Sbf', 0), 'qf'], w=[('ps', bO)])
                        c.op('act', lambda e: e.activation(out=o_a[:, fsl], in_=PS[bO][:, sl], func=AF.Identity), r=[('ps', bO)], w=['o_a'])
                        c.op('pe', lambda e: e.matmul(PS[3][:, sl], lhsT=Kf[:, cf, :], rhs=Vtok[:, cf, :], start=True, stop=True), r=['Kf', 'Vtok'], w=[('ps', 3)])
                        c.op('dve', lambda e: e.scalar_tensor_tensor(out=Sst[0][:], in0=Sst[0][:], scalar=GC[:, h:h + 1], in1=PS[3][:, sl], op0=ALU.mult, op1=ALU.add), r=[('S', 0), 'GC', ('ps', 3)], w=[('S', 0)])
                        c.op('act', lambda e: e.activation(out=Sbf[0][:], in_=Sst[0][:], func=AF.Identity), r=[('S', 0)], w=[('Sbf', 0)])
                        c.op('pe', lambda e: e.matmul(PS[4][:, sl], lhsT=Sbf[1][:], rhs=qb_[:, bsl], start=True, stop=True), r=[('Sbf', 1), 'qb'], w=[('ps', 4)])
                        c.op('act', lambda e: e.activation(out=o_b[:, bsl], in_=PS[4][:, sl], func=AF.Identity), r=[('ps', 4)], w=['o_b'])
                        c.op('pe', lambda e: e.matmul(PS[5][:, sl], lhsT=Kb[:, cb, :], rhs=Vtok[:, cb, :], start=True, stop=True), r=['Kb', 'Vtok'], w=[('ps', 5)])
                        c.op('dve', lambda e: e.scalar_tensor_tensor(out=Sst[1][:], in0=Sst[1][:], scalar=GC[:, 4 + h:5 + h], in1=PS[5][:, sl], op0=ALU.mult, op1=ALU.add), r=[('S', 1), 'GC', ('ps', 5)], w=[('S', 1)])
                        c.op('act', lambda e: e.activation(out=Sbf[1][:], in_=Sst[1][:], func=AF.Identity), r=[('S', 1)], w=[('Sbf', 1)])
                    if pidx is not None:
                        for d in range(2):
                            c.dma('sp', O['new_ret'][pidx, d, h], Sst[d][:], r=[('S', d)], w=[('new_ret', pidx, d, h)])
                    if RETSTOP < 3:
                        continue
                    for ti in range(nt):
                        t0 = ti * TS
                        g0 = off + t0
                        tsl = slice(t0, t0 + TS)
                        c.dma('sp', gs[:, :TS], Pv[:, 12 + h, g0:g0 + TS], w=['gs'])
                        c.op('dve', lambda e: e.tensor_tensor(out=t1[:, :TS], in0=o_a[:, tsl], in1=o_b[:, tsl], op=ALU.add), r=['o_a', 'o_b'], w=['t1'])
                        c.op('pe', lambda e: e.matmul(PS[6][:, :TS], lhsT=onesd[:], rhs=t1[:, :TS], start=True, stop=True), r=['onesd', 't1'], w=[('ps', 6)])
                        c.op('dve', lambda e: e.tensor_tensor(out=t2[:, :TS], in0=t1[:, :TS], in1=PS[6][:, :TS], op=ALU.subtract), r=['t1', ('ps', 6)], w=['t2'])
                        c.op('act', lambda e: e.activation(out=t3[:, :TS], in_=t2[:, :TS], func=AF.Square), r=['t2'], w=['t3'])
                        c.op('pe', lambda e: e.matmul(PS[7][:, :TS], lhsT=onesd[:], rhs=t3[:, :TS], start=True, stop=True), r=['onesd', 't3'], w=[('ps', 7)])
                        c.op('act', lambda e: e.activation(out=t3[:, :TS], in_=PS[7][:, :TS], func=AF.Sqrt, bias=self.cst[:, 1:2]), r=[('ps', 7), ('cst', 1)], w=['t3'])
                        c.op('dve', lambda e: e.reciprocal(out=t3[:, :TS], in_=t3[:, :TS]), r=['t3'], w=['t3'])
                        c.op('dve', lambda e: e.tensor_tensor(out=t2[:, :TS], in0=t2[:, :TS], in1=t3[:, :TS], op=ALU.mult), r=['t2', 't3'], w=['t2'])
                        c.op('act', lambda e: e.activation(out=gs[:, :TS], in_=gs[:, :TS], func=AF.Silu), r=['gs'], w=['gs'])
                        c.op('dve', lambda e: e.tensor_tensor(out=ya[:, :TS], in0=t2[:, :TS], in1=gs[:, :TS], op=ALU.mult), r=['t2', 'gs'], w=['ya'])
                        c.dma('sp', Yv[:, h, g0:g0 + TS], ya[:, :TS], r=['ya'], w=[('Y', h, g0)])
            c.barrier()

    def mix_even_lru(self):
        c, nc, I, O = self.c, self.nc, self.I, self.O
        LS = self.LS
        Pv = self.P.rearrange("(m p) t -> p m t", p=128)
        Yv = self.Y.rearrange("(c p) t -> p c t", p=128)
        PS = self.PS
        with contextlib.ExitStack() as ph:
            sb = lambda n, sh, dt: self.sb(ph, 'lr_' + n, sh, dt)
            CW = sb('CW', [128, 4, 4], F32)
            CB = sb('CB', [128, 4], F32)
            LAM = sb('LAM', [128, 2, 4], F32)
            C8 = sb('C8', [128, 2, 4], F32)
            C16 = sb('C16', [128, 2, 4], F32)
            BA = sb('BA', [128, 2, 4], F32)
            BX = sb('BX', [128, 2, 4], F32)
            WA = sb('WA', [128, 8, 128], F32)
            WX = sb('WX', [128, 8, 128], F32)
            H0 = sb('H0', [128, 2, 4], F32)
            xpad = sb('xpad', [128, LS + 3], F32)
            xc = sb('xc', [128, LS], F32)
            a_ = sb('a', [128, LS], F32)
            b_ = sb('b', [128, LS], F32)
            hf = sb('hf', [128, LS], F32)
            hb = sb('hb', [128, LS], F32)
            gb = sb('gb', [128, 512], F32)
            r_ = sb('r', [128, 512], F32)
            i_ = sb('i', [128, 512], F32)
            a2 = sb('a2', [128, 512], F32)
            yb = sb('yb', [128, 512], BF16)
            for j in range(4):
                c.dma('sp', CW[:, :, j], I['l0_conv_w'][j].rearrange("(u p) -> p u", p=128), w=['CW'], allow_slow_non_contiguous=True)
            c.dma('sp', CB[:], I['l0_conv_b'].rearrange("(u p) -> p u", p=128), w=['CB'], allow_slow_non_contiguous=True)
            for d in range(2):
                c.dma('sp', LAM[:, d, :], I['l0_lru_lam'][d].rearrange("(u p) -> p u", p=128), w=['LAM'], allow_slow_non_contiguous=True)
                c.dma('sp', BA[:, d, :], I['l0_lru_ba'][d].rearrange("(u p) -> p u", p=128), w=['BA'], allow_slow_non_contiguous=True)
                c.dma('sp', BX[:, d, :], I['l0_lru_bx'][d].rearrange("(u p) -> p u", p=128), w=['BX'], allow_slow_non_contiguous=True)
                c.dma('sp', H0[:, d, :], I['st_lru'][d].rearrange("(u p) -> p u", p=128), w=['H0'], allow_slow_non_contiguous=True)
            c.op('dve', lambda e: e.memset(WA[:], 0.0), w=['WA'])
            c.op('dve', lambda e: e.memset(WX[:], 0.0), w=['WX'])
            for d in range(2):
                for U in range(4):
                    for g2 in range(2):
                        c.dma('sp', WA[g2 * 64:(g2 + 1) * 64, d * 4 + U, g2 * 64:(g2 + 1) * 64], I['l0_lru_wa'][d, 2 * U + g2], w=['WA'])
                        c.dma('sp', WX[g2 * 64:(g2 + 1) * 64, d * 4 + U, g2 * 64:(g2 + 1) * 64], I['l0_lru_wx'][d, 2 * U + g2], w=['WX'])
            c.op('act', lambda e: e.activation(out=C8[:], in_=LAM[:], func=AF.Sigmoid), r=['LAM'], w=['C8'])
            c.op('act', lambda e: e.activation(out=C8[:], in_=C8[:], func=AF.Ln), r=['C8'], w=['C8'])
            c.op('dve', lambda e: e.tensor_scalar(out=C16[:], in0=C8[:], scalar1=16.0, scalar2=None, op0=ALU.mult), r=['C8'], w=['C16'])
            c.op('dve', lambda e: e.tensor_scalar(out=C8[:], in0=C8[:], scalar1=8.0, scalar2=None, op0=ALU.mult), r=['C8', 'C16'], w=['C8'])
            for sq_ in self.seqs():
                off, L, latent, pidx = sq_['off'], sq_['L'], sq_['latent'], sq_['pidx']
                TS = min(512, L)
                nt = L // TS
                for U in range(4):
                    c.op('dve', lambda e: e.memset(xpad[:, 0:2], 0.0), w=['xpad'])
                    c.op('dve', lambda e: e.memset(xpad[:, L + 2:L + 3], 0.0), w=['xpad'])
                    c.dma('sp', xpad[:, 2:L + 2], Pv[:, 20 + U, off:off + L], w=['xpad'])
                    c.op('dve', lambda e: e.tensor_scalar(out=xc[:, :L], in0=xpad[:, 0:L], scalar1=CW[:, U, 0:1], scalar2=CB[:, U:U + 1], op0=ALU.mult, op1=ALU.add), r=['xpad', 'CW', 'CB'], w=['xc'])
                    for j in range(1, 4):
                        c.op('dve', lambda e: e.scalar_tensor_tensor(out=xc[:, :L], in0=xpad[:, j:j + L], scalar=CW[:, U, j:j + 1], in1=xc[:, :L], op0=ALU.mult, op1=ALU.add), r=['xpad', 'CW', 'xc'], w=['xc'])
                    for d in range(2):
                        for ti in range(nt):
                            tsl = slice(ti * TS, (ti + 1) * TS)
                            c.op('pe', lambda e: e.matmul(PS[1][:, :TS], lhsT=WA[:, d * 4 + U, :], rhs=xc[:, tsl], start=True, stop=True), r=['WA', 'xc'], w=[('ps', 1)])
                            c.op('pe', lambda e: e.matmul(PS[2][:, :TS], lhsT=WX[:, d * 4 + U, :], rhs=xc[:, tsl], start=True, stop=True), r=['WX', 'xc'], w=[('ps', 2)])
                            c.op('act', lambda e: e.activation(out=r_[:, :TS], in_=PS[1][:, :TS], func=AF.Sigmoid, bias=BA[:, d, U:U + 1]), r=[('ps', 1), 'BA'], w=['r'])
                            c.op('act', lambda e: e.activation(out=i_[:, :TS], in_=PS[2][:, :TS], func=AF.Sigmoid, bias=BX[:, d, U:U + 1]), r=[('ps', 2), 'BX'], w=['i'])
                            c.op('act', lambda e: e.activation(out=a_[:, tsl], in_=r_[:, :TS], func=AF.Exp, scale=C8[:, d, U:U + 1]), r=['r', 'C8'], w=['a'])
                            c.op('act', lambda e: e.activation(out=a2[:, :TS], in_=r_[:, :TS], func=AF.Exp, scale=C16[:, d, U:U + 1]), r=['r', 'C16'], w=['a2'])
                            c.op('dve', lambda e: e.tensor_scalar(out=a2[:, :TS], in0=a2[:, :TS], scalar1=-1.0, scalar2=1.0, op0=ALU.mult, op1=ALU.add), r=['a2'], w=['a2'])
                            c.op('act', lambda e: e.activation(out=a2[:, :TS], in_=a2[:, :TS], func=AF.Sqrt), r=['a2'], w=['a2'])
                            c.op('dve', lambda e: e.tensor_tensor(out=i_[:, :TS], in0=i_[:, :TS], in1=a2[:, :TS], op=ALU.mult), r=['i', 'a2'], w=['i'])
                            c.op('dve', lambda e: e.tensor_tensor(out=b_[:, tsl], in0=i_[:, :TS], in1=xc[:, tsl], op=ALU.mult), r=['i', 'xc'], w=['b'])
                        init = H0[:, d, U:U + 1] if latent else 0.0
                        if d == 0:
                            c.op('dve', lambda e: e.tensor_tensor_scan(out=hf[:, 0:L], data0=a_[:, 0:L], data1=b_[:, 0:L], initial=init, op0=ALU.mult, op1=ALU.add), r=['a', 'b', 'H0'], w=['hf'])
                        else:
                            c.op('dve', lambda e: e.tensor_tensor_scan(out=hb[:, 0:L][:, ::-1], data0=a_[:, 0:L][:, ::-1], data1=b_[:, 0:L][:, ::-1], initial=init, op0=ALU.mult, op1=ALU.add), r=['a', 'b', 'H0'], w=['hb'])
                    if pidx is not None:
                        c.dma('sp', O['new_lru'][pidx, 0, U * 128:(U + 1) * 128].rearrange("(p o) -> p o", o=1), hf[:, L - 1:L], r=['hf'], w=[('new_lru', pidx, 0, U)], allow_slow_non_contiguous=True)
                        c.dma('sp', O['new_lru'][pidx, 1, U * 128:(U + 1) * 128].rearrange("(p o) -> p o", o=1), hb[:, 0:1], r=['hb'], w=[('new_lru', pidx, 1, U)], allow_slow_non_contiguous=True)
                    for ti in range(nt):
                        tsl = slice(ti * TS, (ti + 1) * TS)
                        g0 = off + ti * TS
                        c.dma('sp', gb[:, :TS], Pv[:, 16 + U, g0:g0 + TS], w=['gb'])
                        c.op('act', lambda e: e.activation(out=gb[:, :TS], in_=gb[:, :TS], func=AF.Gelu_apprx_tanh), r=['gb'], w=['gb'])
                        c.op('dve', lambda e: e.tensor_tensor(out=r_[:, :TS], in0=hf[:, tsl], in1=hb[:, tsl], op=ALU.add), r=['hf', 'hb'], w=['r'])
                        c.op('dve', lambda e: e.tensor_tensor(out=yb[:, :TS], in0=r_[:, :TS], in1=gb[:, :TS], op=ALU.mult), r=['r', 'gb'], w=['yb'])
                        c.dma('sp', Yv[:, 4 + U, g0:g0 + TS], yb[:, :TS], r=['yb'], w=[('Y', 4 + U, g0)])
            c.barrier()

    def phase_mix_odd(self):
        import os
        which = os.environ.get('ODDPARTS', 'sr')
        if which != 'sr':
            self.mix_stub()
        if 's' in which:
            self.mix_odd_s5()
        if 'r' in which:
            self.mix_odd_rwkv()

    def sincos(self, sb_, turns, shape, out_s, out_c, keys_r, key_s, key_c, tmpf, tmpi, tkey='sc_tmpf'):
        c = self.c
        TWO_PI = 6.283184
        for (dst, dkey, shift) in ((out_s, key_s, 0.0), (out_c, key_c, 0.25)):
            c.op('dve', lambda e: e.tensor_scalar(out=tmpf, in0=turns, scalar1=shift, scalar2=None, op0=ALU.add), r=keys_r, w=[tkey])
            c.op('dve', lambda e: e.tensor_copy(out=tmpi, in_=tmpf), r=[tkey], w=['sc_tmpi'])
            c.op('dve', lambda e: e.tensor_copy(out=dst, in_=tmpi), r=['sc_tmpi'], w=[dkey])
            c.op('dve', lambda e: e.tensor_tensor(out=dst, in0=tmpf, in1=dst, op=ALU.subtract), r=[tkey, dkey], w=[dkey])
            c.op('act', lambda e: e.activation(out=dst, in_=dst, func=AF.Sin, scale=TWO_PI), r=[dkey], w=[dkey])

    def mix_odd_s5(self):
        c, nc, I, O = self.c, self.nc, self.I, self.O
        NT, NP = self.NT, self.NP
        Pv = self.P.rearrange("(m p) t -> p m t", p=128)
        Yv = self.Y.rearrange("(c p) t -> p c t", p=128)
        PS = self.PS
        CBM = 256
        with contextlib.ExitStack() as ph:
            sb = lambda n, sh, dt: self.sb(ph, 's5_' + n, sh, dt)
            ARE = sb('are', [128, 2, 16], F32)
            AIM = sb('aim', [128, 2, 16], F32)
            DT = sb('dt', [128, 2, 16], F32)
            MAG = sb('mag', [128, 2, 16], F32)
            THT = sb('tht', [128, 2, 16], F32)
            CS = sb('cs', [128, 2, 16], F32)
            SN = sb('sn', [128, 2, 16], F32)
            ABR = sb('abr', [128, 2, 16], F32)
            ABI = sb('abi', [128, 2, 16], F32)
            DEN = sb('den', [128, 2, 16], F32)
            FR = sb('fr', [128, 2, 16], F32)
            FI = sb('fi', [128, 2, 16], F32)
            p1 = sb('p1', [128, 2, 16], F32)
            p2 = sb('p2', [128, 2, 16], F32)
            pI = sb('pI', [128, 2, 16], I32)
            BRE = sb('bre', [128, 2, 16, 16], F32)
            BIM = sb('bim', [128, 2, 16, 16], F32)
            BBR = sb('bbr', [128, 2, 16, 16], F32)
            BBI = sb('bbi', [128, 2, 16, 16], F32)
            bt1 = sb('bt1', [128, 16, 16], F32)
            CRE = sb('cre', [128, 2, 4, 64], F32)
            CIM = sb('cim', [128, 2, 4, 64], F32)
            Bx = [sb('Bx%d' % q, [128, 128], F32) for q in range(4)]
            Cx = [sb('Cx%d' % q, [128, 128], F32) for q in range(4)]
            LB = sb('LB', [128, 4, 2, 128], BF16)
            LC = sb('LC', [128, 4, 2, 128], BF16)
            ST0 = sb('st0', [128, 2, 2, 16], F32)
            CMASK = sb('cmask', [128, 4, 128], F32)
            FIN = sb('fin', [128, max(NP, 2) * 64], F32)
            FINT = sb('fint', [128, 128], F32)
            IOTA = sb('iota', [128, 512], F32)
            COST = sb('cost', [128, 4, CBM], F32)
            SINT = sb('sint', [128, 4, CBM], F32)
            RHOT = sb('rhot', [128, 4, CBM], F32)
            ti_ = sb('ti', [128, CBM], I32)
            u_bf = sb('u_bf', [128, 4, NT], BF16)
            y_acc = sb('y_acc', [128, 4, NT], F32)
            w_ = {n: sb('w_' + n, [128, CBM], F32) for n in ['br', 'bi', 'hr', 'hi', 't1', 't2', 't3', 't4', 'or', 'oi', 'p3', 'p4']}
            hb_ = {n: sb('hb_' + n, [128, CBM], BF16) for n in ['r', 'i']}
            CAR = sb('car', [128, 4, 2], F32)
            tf, tfs, uf, sg = w_['t3'], w_['t4'], w_['or'], w_['oi']
            SD = sb('sd', [128, 4], F32)
            GB = sb('gb', [128, 4], F32)
            GW = sb('gw', [128, 4, 512], BF16)
            yc = sb('yc', [128, CBM], BF16)

            for d in range(2):
                c.dma('sp', ARE[:, d, :], I['l1_s5_a_re'][d].rearrange("(T g) n -> (g n) T", g=2), w=['ARE'], allow_slow_non_contiguous=True)
                c.dma('sp', AIM[:, d, :], I['l1_s5_a_im'][d].rearrange("(T g) n -> (g n) T", g=2), w=['AIM'], allow_slow_non_contiguous=True)
                ldt = I['l1_s5_log_dt'][d].rearrange("(T g) -> g T", g=2)
                for g2 in range(2):
                    c.dma('sp', DT[g2 * 64:(g2 + 1) * 64, d, :], ldt[g2].partition_broadcast(64), w=['DT'], allow_slow_non_contiguous=True)
                c.dma('sp', BRE[:, d, :, :], I['l1_s5_b_re'][d].rearrange("(T g) n s -> (g n) T s", g=2), w=['BRE'])
                c.dma('sp', BIM[:, d, :, :], I['l1_s5_b_im'][d].rearrange("(T g) n s -> (g n) T s", g=2), w=['BIM'])
                c.dma('sp', CRE[:, d, :, :], I['l1_s5_c_re'][d].rearrange("(U g) s n -> (g s) U n", g=8), w=['CRE'])
                c.dma('sp', CIM[:, d, :, :], I['l1_s5_c_im'][d].rearrange("(U g) s n -> (g s) U n", g=8), w=['CIM'])
                for ri in range(2):
                    c.dma('sp', ST0[:, d, ri, :], I['st_s5'][d, ri].rearrange("(T g) n -> (g n) T", g=2), w=['ST0'], allow_slow_non_contiguous=True)
            c.dma('sp', IOTA[:], I['c_iota'][:, :], w=['IOTA'])
            c.dma('sp', CMASK[:], I['c_s5mask'].rearrange("p (q n) -> p q n", q=4), w=['CMASK'])
            c.dma('sp', SD[:], I['l1_s5_d'].rearrange("(u p) -> p u", p=128), w=['SD'], allow_slow_non_contiguous=True)
            c.dma('sp', GB[:], I['l1_glu_b'].rearrange("(u p) -> p u", p=128), w=['GB'], allow_slow_non_contiguous=True)
            c.dma('pool', GW[:], I['l1_glu_w'].rearrange("(k p) n -> p k n", p=128), w=['GW'])
            for U in range(4):
                c.dma('pool', u_bf[:, U, :], Pv[:, U, :], w=['u_bf'])
            for q in range(4):
                c.op('dve', lambda e: e.memset(Bx[q][:], 0.0), w=[('Bx', q)])
                c.op('dve', lambda e: e.memset(Cx[q][:], 0.0), w=[('Cx', q)])
            c.op('dve', lambda e: e.memset(FIN[:], 0.0), w=['FIN'])
            c.op('act', lambda e: e.activation(out=DT[:], in_=DT[:], func=AF.Exp), r=['DT'], w=['DT'])
            c.op('dve', lambda e: e.tensor_tensor(out=p1[:], in0=ARE[:], in1=DT[:], op=ALU.mult), r=['ARE', 'DT'], w=['p1'])
            c.op('act', lambda e: e.activation(out=MAG[:], in_=p1[:], func=AF.Exp), r=['p1'], w=['MAG'])
            c.op('dve', lambda e: e.scalar_tensor_tensor(out=THT[:], in0=AIM[:], scalar=1.0 / (2.0 * math.pi), in1=DT[:], op0=ALU.mult, op1=ALU.mult), r=['AIM', 'DT'], w=['THT'])
            self.sincos(sb, THT[:], None, SN[:], CS[:], ['THT'], 'SN', 'CS', p2[:], pI[:])
            c.op('dve', lambda e: e.tensor_tensor(out=ABR[:], in0=MAG[:], in1=CS[:], op=ALU.mult), r=['MAG', 'CS'], w=['ABR'])
            c.op('dve', lambda e: e.tensor_tensor(out=ABI[:], in0=MAG[:], in1=SN[:], op=ALU.mult), r=['MAG', 'SN'], w=['ABI'])
            c.op('dve', lambda e: e.tensor_tensor(out=DEN[:], in0=ARE[:], in1=ARE[:], op=ALU.mult), r=['ARE'], w=['DEN'])
            c.op('dve', lambda e: e.tensor_tensor(out=p1[:], in0=AIM[:], in1=AIM[:], op=ALU.mult), r=['AIM', 'MAG'], w=['p1'])
            c.op('dve', lambda e: e.tensor_tensor(out=DEN[:], in0=DEN[:], in1=p1[:], op=ALU.add), r=['DEN', 'p1'], w=['DEN'])
            c.op('dve', lambda e: e.reciprocal(out=DEN[:], in_=DEN[:]), r=['DEN'], w=['DEN'])
            c.op('dve', lambda e: e.tensor_scalar(out=p1[:], in0=ABR[:], scalar1=-1.0, scalar2=None, op0=ALU.add), r=['ABR', 'DEN'], w=['p1'])
            c.op('dve', lambda e: e.tensor_tensor(out=FR[:], in0=p1[:], in1=ARE[:], op=ALU.mult), r=['p1', 'ARE'], w=['FR'])
            c.op('dve', lambda e: e.tensor_tensor(out=p2[:], in0=ABI[:], in1=AIM[:], op=ALU.mult), r=['ABI', 'AIM', 'SN', 'CS'], w=['p2'])
            c.op('dve', lambda e: e.tensor_tensor(out=FR[:], in0=FR[:], in1=p2[:], op=ALU.add), r=['FR', 'p2'], w=['FR'])
            c.op('dve', lambda e: e.tensor_tensor(out=FR[:], in0=FR[:], in1=DEN[:], op=ALU.mult), r=['FR', 'DEN'], w=['FR'])
            c.op('dve', lambda e: e.tensor_tensor(out=FI[:], in0=ABI[:], in1=ARE[:], op=ALU.mult), r=['ABI', 'ARE'], w=['FI'])
            c.op('dve', lambda e: e.tensor_tensor(out=p2[:], in0=p1[:], in1=AIM[:], op=ALU.mult), r=['p1', 'AIM', 'FR'], w=['p2'])
            c.op('dve', lambda e: e.tensor_tensor(out=FI[:], in0=FI[:], in1=p2[:], op=ALU.subtract), r=['FI', 'p2'], w=['FI'])
            c.op('dve', lambda e: e.tensor_tensor(out=FI[:], in0=FI[:], in1=DEN[:], op=ALU.mult), r=['FI', 'DEN'], w=['FI'])
            for d in range(2):
                frb = FR[:, d, :].unsqueeze(2).broadcast_to([128, 16, 16])
                fib = FI[:, d, :].unsqueeze(2).broadcast_to([128, 16, 16])
                c.op('dve', lambda e: e.tensor_tensor(out=BBR[:, d], in0=BRE[:, d], in1=frb, op=ALU.mult), r=['BRE', 'FR'], w=['BBR'])
                c.op('dve', lambda e: e.tensor_tensor(out=bt1[:], in0=BIM[:, d], in1=fib, op=ALU.mult), r=['BIM', 'FI'], w=['bt1'])
                c.op('dve', lambda e: e.tensor_tensor(out=BBR[:, d], in0=BBR[:, d], in1=bt1[:], op=ALU.subtract), r=['BBR', 'bt1'], w=['BBR'])
                c.op('dve', lambda e: e.tensor_tensor(out=BBI[:, d], in0=BIM[:, d], in1=frb, op=ALU.mult), r=['BIM', 'FR'], w=['BBI'])
                c.op('dve', lambda e: e.tensor_tensor(out=bt1[:], in0=BRE[:, d], in1=fib, op=ALU.mult), r=['BRE', 'FI', 'BBR'], w=['bt1'])
                c.op('dve', lambda e: e.tensor_tensor(out=BBI[:, d], in0=BBI[:, d], in1=bt1[:], op=ALU.add), r=['BBI', 'bt1'], w=['BBI'])

            seqs = self.seqs()
            first_contrib = {}
            ucnt = [0]
            w_sets = [w_, {n: sb('w2_' + n, [128, CBM], F32) for n in ['br', 'bi', 'hr', 'hi', 't1', 't2', 't3', 't4', 'or', 'oi', 'p3', 'p4']}]
            hb_sets = [hb_, {n: sb('hb2_' + n, [128, CBM], BF16) for n in ['r', 'i']}]
            pend_y = []

            def flush_y():
                while pend_y:
                    ykey, isfirst, U_, b0_, CB_ = pend_y.pop(0)
                    if isfirst:
                        c.op('act', lambda e: e.activation(out=y_acc[:, U_, b0_:b0_ + CB_], in_=PS[5][:, :CB_], func=AF.Identity), r=[('ps', 5)], w=[ykey])
                    else:
                        c.op('dve', lambda e: e.tensor_tensor(out=y_acc[:, U_, b0_:b0_ + CB_], in0=y_acc[:, U_, b0_:b0_ + CB_], in1=PS[5][:, :CB_], op=ALU.add), r=[('ps', 5), ykey], w=[ykey])
            for d in range(2):
                for U in range(4):
                    for q in range(4):
                        T = U * 4 + q
                        for ri, BB in ((0, BBR), (1, BBI)):
                            for g2 in range(2):
                                ps_ = slice(g2 * 64, (g2 + 1) * 64)
                                cs_ = slice((2 * q + g2) * 16, (2 * q + g2) * 16 + 16)
                                c.op('dve', lambda e: e.tensor_copy(out=Bx[q][ps_, cs_], in_=BB[ps_, d, T, :]), r=['BBR', 'BBI'], w=[('Bx', q)])
                            c.op('pe', lambda e: e.transpose(out=PS[1][:, 0:128], in_=Bx[q][:], identity=self.ident[:]), r=[('Bx', q), 'ident'], w=[('ps', 1)])
                            c.op('act', lambda e: e.activation(out=LB[:, q, ri, :], in_=PS[1][:, 0:128], func=AF.Identity), r=[('ps', 1)], w=['LB'])
                        for ri, CC, sgn in ((0, CRE, 1.0), (1, CIM, -1.0)):
                            c.op('dve', lambda e: e.scalar_tensor_tensor(out=Cx[q][:].rearrange("p (g n) -> p g n", g=2), in0=CC[:, d, U, :].unsqueeze(1).broadcast_to([128, 2, 64]), scalar=sgn,
                                                                         in1=CMASK[:, q, :].rearrange("p (g n) -> p g n", g=2), op0=ALU.mult, op1=ALU.mult), r=['CRE', 'CIM', 'CMASK'], w=[('Cx', q)])
                            c.op('pe', lambda e: e.transpose(out=PS[2][:, 0:128], in_=Cx[q][:], identity=self.ident[:]), r=[('Cx', q), 'ident'], w=[('ps', 2)])
                            c.op('act', lambda e: e.activation(out=LC[:, q, ri, :], in_=PS[2][:, 0:128], func=AF.Identity), r=[('ps', 2)], w=['LC'])
                        c.op('dve', lambda e: e.tensor_scalar(out=tf[:], in0=IOTA[:, :CBM], scalar1=THT[:, d, T:T + 1], scalar2=None, op0=ALU.mult), r=['IOTA', 'THT'], w=[('t3', 0)])
                        self.sincos(sb, tf[:], None, SINT[:, q, :], COST[:, q, :], [('t3', 0)], ('SINT', q), ('COST', q), tfs[:], ti_[:], tkey=('t4', 0))
                        c.op('dve', lambda e: e.tensor_scalar(out=RHOT[:, q, :], in0=IOTA[:, :CBM], scalar1=0.0, scalar2=MAG[:, d, T:T + 1], op0=ALU.mult, op1=ALU.add), r=['IOTA', 'MAG'], w=[('RHOT', q)])
                    units = []
                    for sq_ in seqs:
                        CB_ = min(CBM, sq_['L'])
                        nb_ = sq_['L'] // CB_
                        for bi_ in range(nb_):
                            for q in range(4):
                                units.append((sq_, bi_, q, ucnt[0] % 2))
                                ucnt[0] += 1

                    def unpack(u):
                        sq_, bi_, q, up = u
                        off, L, latent, pidx = sq_['off'], sq_['L'], sq_['latent'], sq_['pidx']
                        CB = min(CBM, L)
                        nb = L // CB
                        blk = bi_ if d == 0 else nb - 1 - bi_
                        b0 = off + blk * CB
                        T = U * 4 + q
                        pB0, pB1 = (3, 4) if up == 0 else (6, 7)
                        return off, L, latent, pidx, CB, nb, bi_, b0, q, T, up, pB0, pB1

                    def emit_B(u):
                        off, L, latent, pidx, CB, nb, bi_, b0, q, T, up, pB0, pB1 = unpack(u)
                        c.op('pe', lambda e: e.matmul(PS[pB0][:, :CB], lhsT=LB[:, q, 0, :], rhs=u_bf[:, U, b0:b0 + CB], start=True, stop=True), r=['LB', 'u_bf'], w=[('ps', pB0)])
                        c.op('pe', lambda e: e.matmul(PS[pB1][:, :CB], lhsT=LB[:, q, 1, :], rhs=u_bf[:, U, b0:b0 + CB], start=True, stop=True), r=['LB', 'u_bf'], w=[('ps', pB1)])

                    def emit_rest(u, nxt):
                        off, L, latent, pidx, CB, nb, bi_, b0, q, T, up, pB0, pB1 = unpack(u)
                        w_ = w_sets[up]
                        hb_ = hb_sets[up]
                        Dv = (lambda x: x[:, 0:CB]) if d == 0 else (lambda x: x[:, 0:CB][:, ::-1])
                        cosv = COST[:, q, :CB]
                        sinv = SINT[:, q, :CB]
                        c.op('dve', lambda e: e.tensor_tensor(out=w_['t1'][:, :CB], in0=Dv(PS[pB0]), in1=cosv, op=ALU.mult), r=[('ps', pB0), ('COST', q)], w=[('t1', up)])
                        c.op('dve', lambda e: e.tensor_tensor(out=w_['t2'][:, :CB], in0=Dv(PS[pB1]), in1=sinv, op=ALU.mult), r=[('ps', pB1), ('SINT', q)], w=[('t2', up)])
                        c.op('dve', lambda e: e.tensor_tensor(out=w_['br'][:, :CB], in0=w_['t1'][:, :CB], in1=w_['t2'][:, :CB], op=ALU.add), r=[('t1', up), ('t2', up)], w=[('br', up)])
                        c.op('dve', lambda e: e.tensor_tensor(out=w_['t3'][:, :CB], in0=Dv(PS[pB1]), in1=cosv, op=ALU.mult), r=[('ps', pB1), ('COST', q)], w=[('t3', up)])
                        c.op('dve', lambda e: e.tensor_tensor(out=w_['t4'][:, :CB], in0=Dv(PS[pB0]), in1=sinv, op=ALU.mult), r=[('ps', pB0), ('SINT', q)], w=[('t4', up)])
                        c.op('dve', lambda e: e.tensor_tensor(out=w_['bi'][:, :CB], in0=w_['t3'][:, :CB], in1=w_['t4'][:, :CB], op=ALU.subtract), r=[('t3', up), ('t4', up)], w=[('bi', up)])
                        flush_y()
                        if nxt is not None:
                            emit_B(nxt)
                        if bi_ == 0:
                            if latent:
                                ir, ii = ST0[:, d, 0, T:T + 1], ST0[:, d, 1, T:T + 1]
                            else:
                                ir, ii = 0.0, 0.0
                        else:
                            ir, ii = CAR[:, q, 0:1], CAR[:, q, 1:2]
                        c.op('dve', lambda e: e.tensor_tensor_scan(out=w_['hr'][:, :CB], data0=RHOT[:, q, :CB], data1=w_['br'][:, :CB], initial=ir, op0=ALU.mult, op1=ALU.add), r=[('RHOT', q), ('br', up), ('CAR', q), 'ST0'], w=[('hr', up)])
                        c.op('dve', lambda e: e.tensor_tensor_scan(out=w_['hi'][:, :CB], data0=RHOT[:, q, :CB], data1=w_['bi'][:, :CB], initial=ii, op0=ALU.mult, op1=ALU.add), r=[('RHOT', q), ('bi', up), ('CAR', q), 'ST0'], w=[('hi', up)])
                        c.op('dve', lambda e: e.tensor_tensor(out=w_['t1'][:, :CB], in0=w_['hr'][:, :CB], in1=cosv, op=ALU.mult), r=[('hr', up), ('COST', q)], w=[('t1', up)])
                        c.op('dve', lambda e: e.tensor_tensor(out=w_['t2'][:, :CB], in0=w_['hi'][:, :CB], in1=sinv, op=ALU.mult), r=[('hi', up), ('SINT', q)], w=[('t2', up)])
                        c.op('dve', lambda e: e.tensor_tensor(out=w_['or'][:, :CB], in0=w_['t1'][:, :CB], in1=w_['t2'][:, :CB], op=ALU.subtract), r=[('t1', up), ('t2', up)], w=[('or', up)])
                        c.op('pool', lambda e: e.tensor_tensor(out=w_['p3'][:, :CB], in0=w_['hi'][:, :CB], in1=cosv, op=ALU.mult), r=[('hi', up), ('COST', q)], w=[('p3', up)])
                        c.op('pool', lambda e: e.tensor_tensor(out=w_['p4'][:, :CB], in0=w_['hr'][:, :CB], in1=sinv, op=ALU.mult), r=[('hr', up), ('SINT', q)], w=[('p4', up)])
                        c.op('pool', lambda e: e.tensor_tensor(out=w_['oi'][:, :CB], in0=w_['p3'][:, :CB], in1=w_['p4'][:, :CB], op=ALU.add), r=[('p3', up), ('p4', up)], w=[('oi', up)])
                        c.op('act', lambda e: e.activation(out=CAR[:, q, 0:1], in_=w_['or'][:, CB - 1:CB], func=AF.Identity), r=[('or', up)], w=[('CAR', q)])
                        c.op('pool', lambda e: e.tensor_copy(out=CAR[:, q, 1:2], in_=w_['oi'][:, CB - 1:CB]), r=[('oi', up)], w=[('CAR', q)])
                        c.op('act', lambda e: e.activation(out=Dv(hb_['r']), in_=w_['or'][:, :CB], func=AF.Identity), r=[('or', up)], w=[('hb_r', up)])
                        c.op('act', lambda e: e.activation(out=Dv(hb_['i']), in_=w_['oi'][:, :CB], func=AF.Identity), r=[('oi', up)], w=[('hb_i', up)])
                        c.op('pe', lambda e: e.matmul(PS[5][:, :CB], lhsT=LC[:, q, 0, :], rhs=hb_['r'][:, :CB], start=(q == 0), stop=False), r=['LC', ('hb_r', up)], w=[('ps', 5)])
                        c.op('pe', lambda e: e.matmul(PS[5][:, :CB], lhsT=LC[:, q, 1, :], rhs=hb_['i'][:, :CB], start=False, stop=(q == 3)), r=['LC', ('hb_i', up)], w=[('ps', 5)])
                        if pidx is not None and bi_ == nb - 1:
                            for ri in range(2):
                                col = ((pidx * 2 + d) * 2 + ri) * 16 + T
                                c.op('dve', lambda e: e.tensor_copy(out=FIN[:, col:col + 1], in_=CAR[:, q, ri:ri + 1]), r=[('CAR', q)], w=['FIN'])
                        if q == 3:
                            ykey = ('y_acc', U, b0)
                            isfirst = ykey not in first_contrib
                            first_contrib[ykey] = True
                            pend_y.append((ykey, isfirst, U, b0, CB))


                    emit_B(units[0])
                    for ui, u in enumerate(units):
                        emit_rest(u, units[ui + 1] if ui + 1 < len(units) else None)
            flush_y()
            ns5 = O['new_s5'].rearrange("b d r (T g) n -> (b d r T) (g n)", g=2)
            for blk in range((NP * 64 + 127) // 128):
                ncol = min(128, NP * 64 - blk * 128)
                c.op('pe', lambda e: e.transpose(out=PS[1][:ncol, 0:128], in_=FIN[:, blk * 128:blk * 128 + ncol], identity=self.ident[:]), r=['FIN', 'ident'], w=[('ps', 1)])
                c.op('act', lambda e: e.activation(out=FINT[:ncol, :], in_=PS[1][:ncol, 0:128], func=AF.Identity), r=[('ps', 1)], w=['FINT'])
                c.dma('sp', ns5[blk * 128:blk * 128 + ncol, :], FINT[:ncol, :], r=['FINT'], w=[('ns5', blk)])
            z_bf = u_bf
            for tt in range(NT // CBM):
                t0 = tt * CBM
                tsl = slice(t0, t0 + CBM)
                for U in range(4):
                    c.dma('sp', uf[:], Pv[:, U, tsl], w=[('or', 0)])
                    c.op('dve', lambda e: e.scalar_tensor_tensor(out=y_acc[:, U, tsl], in0=uf[:], scalar=SD[:, U:U + 1], in1=y_acc[:, U, tsl], op0=ALU.mult, op1=ALU.add), r=[('or', 0), 'SD'] + [k for k in first_contrib if k[1] == U], w=[('z', U, tt)])
                    c.op('act', lambda e: e.activation(out=y_acc[:, U, tsl], in_=y_acc[:, U, tsl], func=AF.Gelu_apprx_tanh), r=[('z', U, tt)], w=[('z', U, tt)])
                    c.op('act', lambda e: e.activation(out=z_bf[:, U, tsl], in_=y_acc[:, U, tsl], func=AF.Identity), r=[('z', U, tt), 'u_bf'], w=[('zb', U, tt)])
                for fo in range(4):
                    for k in range(4):
                        c.op('pe', lambda e: e.matmul(PS[6][:, :CBM], lhsT=GW[:, k, fo * 128:(fo + 1) * 128], rhs=z_bf[:, k, tsl], start=(k == 0), stop=(k == 3)), r=['GW', ('zb', k, tt)], w=[('ps', 6)])
                    c.op('act', lambda e: e.activation(out=sg[:], in_=PS[6][:, :CBM], func=AF.Sigmoid, bias=GB[:, fo:fo + 1]), r=[('ps', 6), 'GB'], w=[('oi', 0)])
                    c.op('dve', lambda e: e.tensor_tensor(out=yc[:], in0=sg[:], in1=y_acc[:, fo, tsl], op=ALU.mult), r=[('oi', 0), ('z', fo, tt)], w=['yc'])
                    c.dma('sp', Yv[:, fo, tsl], yc[:], r=['yc'], w=[('Y', fo, tt)])
            c.barrier()

    def mix_odd_rwkv(self):
        c, nc, I, O = self.c, self.nc, self.I, self.O
        NT, NP = self.NT, self.NP
        Pv = self.P.rearrange("(m p) t -> p m t", p=128)
        Yv = self.Y.rearrange("(c p) t -> p c t", p=128)
        YFv = self.YF.rearrange("(c p) t -> p c t", p=128)
        BFv = self.BFs.rearrange("(c p) t -> p c t", p=128)
        PS = self.PS
        TBM = 256
        RW = F32
        EM05 = math.exp(-0.5)
        with contextlib.ExitStack() as ph:
            sb = lambda n, sh, dt: self.sb(ph, 'rw_' + n, sh, dt)
            misc = sb('misc', [128, 1024], F32)
            CMF = sb('cmf', [128, 512], F32)
            BLK = sb('blk', [128, 128], F32)
            MU = sb('mu', [128, 6, 4], F32)
            MUH = sb('muh', [128, 6, 4], F32)
            OMU = sb('omu', [128, 6, 4], F32)
            W0 = sb('w0', [128, 2, 4], F32)
            A0 = sb('a0', [128, 2, 4], F32)
            KKc = sb('kkc', [128, 4], F32)
            KAc = sb('kac', [128, 4], F32)
            OMKA = sb('omka', [128, 4], F32)
            RKc = sb('rkc', [128, 4], F32)
            LNW = sb('lnw', [128, 4], F32)
            LNB = sb('lnb', [128, 4], F32)
            W1 = sb('w1', [128, 2, 4, 64], BF16)
            A1 = sb('a1', [128, 2, 4, 64], BF16)
            W2 = sb('w2', [64, 2, 512], BF16)
            A2 = sb('a2', [64, 2, 512], BF16)
            G1 = sb('g1', [128, 4, 128], BF16)
            G2 = sb('g2', [128, 512], BF16)
            xdp = sb('xdp', [128, TBM + 2], F32)
            zp = {n: sb('zp_' + n, [128, TBM + 2], F32) for n in 'rkv'}
            cs = sb('cs', [128, TBM], F32)
            xw = sb('xw', [128, 4, TBM], BF16)
            xa = sb('xa', [128, 4, TBM], BF16)
            xg = sb('xg', [128, 4, TBM], BF16)
            hw = sb('hw', [64, TBM], BF16)
            ha = sb('ha', [64, TBM], BF16)
            hg = sb('hg', [128, TBM], BF16)
            names = ['rp', 'kp', 'vp', 'LW', 'a', 'kk', 'kd', 'akk', 'G', 'EG', 'EGN', 'EGm', 'AT', 'KAT', 'KT', 'RT', 'y', 't1', 't2', 'gg', 'bon']
            A_ = {n: sb('A_' + n, [128, TBM], F32) for n in names}
            yb = sb('yb', [128, TBM], BF16)
            NCN = 4
            Qb = [[sb('Q%d%d' % (n, g), [128, 128], RW) for g in range(2)] for n in range(NCN)]
            QTb = [[sb('QT%d%d' % (n, g), [128, 128], RW) for g in range(2)] for n in range(NCN)]
            Zb = [sb('Z%d' % n, [128, 128], RW) for n in range(NCN)]
            ZTb = [sb('ZT%d' % n, [128, 128], RW) for n in range(NCN)]
            Eu = [[sb('Eu%d%d' % (n, lv), [128, 128], RW) for lv in range(3)] for n in range(NCN)]
            El = [[sb('El%d%d' % (n, lv), [128, 128], RW) for lv in range(3)] for n in range(NCN)]
            Fb = [sb('F%d' % n, [128, 128], RW) for n in range(NCN)]
            Fpb = [sb('Fp%d' % n, [128, 128], RW) for n in range(NCN)]
            RWM = sb('rwm', [128, 8, 128], F32)
            A2T = [sb('A2T%d' % n, [128, 128], RW) for n in range(NCN)]
            B1T = [[sb('B1T%d_%d' % (pp, n), [128, 128], RW) for n in range(NCN)] for pp in range(2)]
            B2T = [[sb('B2T%d_%d' % (pp, n), [128, 128], RW) for n in range(NCN)] for pp in range(2)]
            Vx = [[sb('Vx%d_%d' % (pp, n), [128, 128], RW) for n in range(NCN)] for pp in range(2)]
            KAx = [sb('KAx%d' % n, [128, 128], RW) for n in range(NCN)]
            Ax = [[sb('Ax%d_%d' % (pp, n), [128, 128], RW) for n in range(NCN)] for pp in range(2)]
            Kx = [[sb('Kx%d_%d' % (pp, n), [128, 128], RW) for n in range(NCN)] for pp in range(2)]
            Ux = [sb('Ux%d' % hh, [128, 128], RW) for hh in range(2)]
            Xs = [sb('Xs%d' % n, [128, 64], RW) for n in range(NCN)]
            Uvs = [[sb('Uvs%d_%d' % (pp, n), [128, 64], RW) for n in range(NCN)] for pp in range(2)]
            WTs = [[sb('WTs%d_%d' % (pp, n), [128, 128], RW) for n in range(NCN)] for pp in range(2)]
            Ptmp = [sb('Ptmp%d' % hh, [128, 64], RW) for hh in range(2)]
            P2 = [sb('P2_%d' % U, [128, 128], RW) for U in range(4)]
            Sld = sb('Sld', [128, 128], F32)
            Sout = sb('Sout', [128, 128], F32)

            c.dma('sp', misc[:], I['c_misc'][:, :], w=['misc'])
            c.dma('sp', CMF[:], I['c_cmf'][:, :], w=['CMF'])
            c.dma('sp', BLK[:], I['c_blk'][:, :], w=['BLK'])
            c.dma('sp', RWM[:], I['c_rwm'].rearrange("p (q n) -> p q n", q=8), w=['RWM'])
            for i6 in range(6):
                c.dma('sp', MU[:, i6, :], I['l1_rw_mu'][i6].rearrange("(u p) -> p u", p=128), w=['MU'], allow_slow_non_contiguous=True)
            for d in range(2):
                c.dma('sp', W0[:, d, :], I['l1_rw_w0'][d].rearrange("(u p) -> p u", p=128), w=['W0'], allow_slow_non_contiguous=True)
                c.dma('sp', A0[:, d, :], I['l1_rw_a0'][d].rearrange("(u p) -> p u", p=128), w=['A0'], allow_slow_non_contiguous=True)
                c.dma('pool', W1[:, d, :, :], I['l1_rw_w1'][d].rearrange("(u p) r -> p u r", p=128), w=['W1'])
                c.dma('pool', A1[:, d, :, :], I['l1_rw_a1'][d].rearrange("(u p) r -> p u r", p=128), w=['A1'])
                c.dma('pool', W2[:, d, :], I['l1_rw_w2'][d], w=['W2'])
                c.dma('pool', A2[:, d, :], I['l1_rw_a2'][d], w=['A2'])
            c.dma('pool', G1[:], I['l1_rw_g1'].rearrange("(u p) r -> p u r", p=128), w=['G1'])
            c.dma('pool', G2[:], I['l1_rw_g2'][:, :], w=['G2'])
            for (t_, nm) in ((KKc, 'l1_rw_kk'), (KAc, 'l1_rw_ka'), (RKc, 'l1_rw_rk'), (LNW, 'l1_ln_w'), (LNB, 'l1_ln_b')):
                c.dma('sp', t_[:], I[nm].rearrange("(u p) -> p u", p=128), w=[nm], allow_slow_non_contiguous=True)
            c.op('dve', lambda e: e.tensor_scalar(out=MUH[:], in0=MU[:], scalar1=0.5, scalar2=None, op0=ALU.mult), r=['MU'], w=['MUH'])
            c.op('dve', lambda e: e.tensor_scalar(out=OMU[:], in0=MU[:], scalar1=-1.0, scalar2=1.0, op0=ALU.mult, op1=ALU.add), r=['MU'], w=['OMU'])
            c.op('dve', lambda e: e.tensor_scalar(out=OMKA[:], in0=KAc[:], scalar1=-1.0, scalar2=1.0, op0=ALU.mult, op1=ALU.add), r=['l1_rw_ka'], w=['OMKA'])
            for n in range(NCN):
                c.op('dve', lambda e: e.memset(KAx[n][:], 0.0), w=[('KAx', n)])
                for pp in range(2):
                    for t_, k_ in ((Vx, 'Vx'), (Ax, 'Ax'), (Kx, 'Kx')):
                        c.op('dve', lambda e: e.memset(t_[pp][n][:], 0.0), w=[(k_, pp, n)])
            for hh in range(2):
                c.op('dve', lambda e: e.memset(Ux[hh][:], 0.0), w=[('Ux', hh)])
            PK = ['MU', 'MUH', 'OMU', 'W0', 'A0', 'l1_rw_kk', 'l1_rw_ka', 'l1_rw_rk', 'l1_ln_w', 'l1_ln_b', 'OMKA', 'misc', 'CMF', 'BLK']
            MSf, MSb = misc[:, 384:512], misc[:, 512:640]
            MIf, MIb = misc[:, 128:256], misc[:, 256:384]

            DBN = ['AT', 'KAT', 'KT', 'RT', 'vp', 'EG', 'y', 'bon']
            A2_ = dict(A_)
            for n in DBN:
                A2_[n] = sb('B_' + n, [128, TBM], F32)
            A_sets = [A_, A2_]
            TBN = ['RT', 'EG', 'y', 'bon']
            A3_ = [dict((n, (A_[n] if t3 == 0 else (A2_[n] if t3 == 1 else sb('C_' + n, [128, TBM], F32)))) for n in TBN) for t3 in range(3)]
            o1 = sb('o1', [128, TBM], F32)
            o2 = sb('o2', [128, TBM], F32)
            hgs = [hg, sb('hg2', [128, TBM], BF16)]

            def mk_item(d, sq_, bi_, U, kidx):
                off, L = sq_['off'], sq_['L']
                TB = min(TBM, L)
                nb = L // TB
                blk = bi_ if d == 0 else nb - 1 - bi_
                return dict(d=d, sq=sq_, bi=bi_, U=U, TB=TB, nb=nb, ncc=TB // 128, b0=off + blk * TB, par=kidx % 2, bpar=(kidx // 4) % 2, p3=kidx % 3)

            def prep_gen(it):
                d, U, TB, b0, par, bpar = it['d'], it['U'], it['TB'], it['b0'], it['par'], it['bpar']
                off, L = it['sq']['off'], it['sq']['L']
                p3 = it['p3']
                A_ = dict(A_sets[par])
                A_.update(A3_[p3])
                K = lambda n: (n, 't', p3) if n in TBN else ((n, par) if n in DBN else n)
                lo_t = max(b0 - 1, off)
                hi_t = min(b0 + TB + 1, off + L)
                dlo = lo_t - (b0 - 1)
                dhi = dlo + (hi_t - lo_t)

                def load_pad(buf, key, m):
                    if dlo > 0:
                        c.op('dve', lambda e: e.memset(buf[:, 0:1], 0.0), w=[key])
                    if dhi < TB + 2:
                        c.op('dve', lambda e: e.memset(buf[:, TB + 1:TB + 2], 0.0), w=[key])
                    c.dma('sp', buf[:, dlo:dhi], Pv[:, m, lo_t:hi_t], w=[key])
                if U == 0:
                    Ucur = U
                    for U in range(4):
                        load_pad(xdp, 'xdp', 16 + U)
                        yield
                        c.op('dve', lambda e: e.tensor_tensor(out=cs[:, :TB], in0=xdp[:, 0:TB], in1=xdp[:, 2:TB + 2], op=ALU.add), r=['xdp'], w=['cs'])
                        yield
                        for (i6, dst, dk) in ((3, xw, 'xw'), (4, xa, 'xa'), (5, xg, 'xg')):
                            c.op('dve', lambda e: e.tensor_scalar(out=A_['t1'][:, :TB], in0=cs[:, :TB], scalar1=MUH[:, i6, U:U + 1], scalar2=None, op0=ALU.mult), r=['cs', 'MUH'], w=[K('t1')])
                            yield
                            c.op('dve', lambda e: e.scalar_tensor_tensor(out=dst[:, U, :TB], in0=xdp[:, 1:TB + 1], scalar=OMU[:, i6, U:U + 1], in1=A_['t1'][:, :TB], op0=ALU.mult, op1=ALU.add), r=['xdp', 'OMU', K('t1')], w=[dk])
                            yield
                    for U in range(4):
                        c.op('pe', lambda e: e.matmul(PS[0][:64, :TB], lhsT=W1[:, d, U, :], rhs=xw[:, U, :TB], start=(U == 0), stop=(U == 3)), r=['W1', 'xw'], w=[('ps', 0)])
                        yield
                    c.op('act', lambda e: e.activation(out=hw[:, :TB], in_=PS[0][:64, :TB], func=AF.Tanh), r=[('ps', 0)], w=['hw'])
                    yield
                    for U in range(4):
                        c.op('pe', lambda e: e.matmul(PS[0][:64, :TB], lhsT=A1[:, d, U, :], rhs=xa[:, U, :TB], start=(U == 0), stop=(U == 3)), r=['A1', 'xa'], w=[('ps', 0)])
                        yield
                    c.op('act', lambda e: e.activation(out=ha[:, :TB], in_=PS[0][:64, :TB], func=AF.Identity), r=[('ps', 0)], w=['ha'])
                    yield
                    if d == 1:
                        for U in range(4):
                            c.op('pe', lambda e: e.matmul(PS[0][:, :TB], lhsT=G1[:, U, :], rhs=xg[:, U, :TB], start=(U == 0), stop=(U == 3)), r=['G1', 'xg'], w=[('ps', 0)])
                            yield
                        c.op('act', lambda e: e.activation(out=hgs[bpar][:, :TB], in_=PS[0][:, :TB], func=AF.Sigmoid), r=[('ps', 0)], w=[('hg', bpar)])
                        yield
                    U = Ucur
                if True:
                    for (i6, n) in ((0, 'r'), (1, 'k'), (2, 'v')):
                        load_pad(zp[n], 'zp_' + n, 4 + 4 * i6 + U)
                        yield
                        c.op('dve', lambda e: e.tensor_tensor(out=cs[:, :TB], in0=zp[n][:, 0:TB], in1=zp[n][:, 2:TB + 2], op=ALU.add), r=['zp_' + n], w=['cs'])
                        yield
                        c.op('dve', lambda e: e.tensor_scalar(out=A_['t1'][:, :TB], in0=cs[:, :TB], scalar1=MUH[:, i6, U:U + 1], scalar2=None, op0=ALU.mult), r=['cs', 'MUH'], w=[K('t1')])
                        yield
                        c.op('dve', lambda e: e.scalar_tensor_tensor(out=A_[n + 'p'][:, :TB], in0=zp[n][:, 1:TB + 1], scalar=OMU[:, i6, U:U + 1], in1=A_['t1'][:, :TB], op0=ALU.mult, op1=ALU.add), r=['zp_' + n, 'OMU', K('t1')], w=[K(n + 'p')])
                        yield
                    rp, kp, vp = A_['rp'], A_['kp'], A_['vp']
                    c.op('pe', lambda e: e.matmul(PS[0][:, :TB], lhsT=W2[:, d, U * 128:(U + 1) * 128], rhs=hw[:, :TB], start=True, stop=True), r=['W2', 'hw'], w=[('ps', 0)])
                    yield
                    c.op('act', lambda e: e.activation(out=A_['LW'][:, :TB], in_=PS[0][:, :TB], func=AF.Sigmoid, bias=W0[:, d, U:U + 1]), r=[('ps', 0), 'W0'], w=[K('LW')])
                    yield
                    c.op('dve', lambda e: e.tensor_scalar(out=A_['LW'][:, :TB], in0=A_['LW'][:, :TB], scalar1=-EM05, scalar2=None, op0=ALU.mult), r=[K('LW')], w=[K('LW')])
                    yield
                    c.op('pe', lambda e: e.matmul(PS[0][:, :TB], lhsT=A2[:, d, U * 128:(U + 1) * 128], rhs=ha[:, :TB], start=True, stop=True), r=['A2', 'ha'], w=[('ps', 0)])
                    yield
                    c.op('act', lambda e: e.activation(out=A_['a'][:, :TB], in_=PS[0][:, :TB], func=AF.Sigmoid, bias=A0[:, d, U:U + 1]), r=[('ps', 0), 'A0'], w=[K('a')])
                    yield
                    c.op('dve', lambda e: e.tensor_scalar(out=A_['kk'][:, :TB], in0=kp[:, :TB], scalar1=KKc[:, U:U + 1], scalar2=None, op0=ALU.mult), r=[K('kp'), 'l1_rw_kk'], w=[K('kk')])
                    yield
                    c.op('act', lambda e: e.activation(out=A_['t1'][:, :TB], in_=A_['kk'][:, :TB], func=AF.Square), r=[K('kk')], w=[K('t1')])
                    yield
                    c.op('pe', lambda e: e.matmul(PS[0][:, :TB], lhsT=BLK[:], rhs=A_['t1'][:, :TB], start=True, stop=True), r=['BLK', K('t1')], w=[('ps', 0)])
                    yield
                    c.op('act', lambda e: e.activation(out=A_['t2'][:, :TB], in_=PS[0][:, :TB], func=AF.Sqrt), r=[('ps', 0)], w=[K('t2')])
                    yield
                    c.op('dve', lambda e: e.tensor_scalar(out=A_['t2'][:, :TB], in0=A_['t2'][:, :TB], scalar1=1e-12, scalar2=None, op0=ALU.max), r=[K('t2')], w=[K('t2')])
                    yield
                    c.op('dve', lambda e: e.reciprocal(out=A_['t2'][:, :TB], in_=A_['t2'][:, :TB]), r=[K('t2')], w=[K('t2')])
                    yield
                    c.op('dve', lambda e: e.tensor_tensor(out=A_['kk'][:, :TB], in0=A_['kk'][:, :TB], in1=A_['t2'][:, :TB], op=ALU.mult), r=[K('kk'), K('t2')], w=[K('kk')])
                    yield
                    c.op('dve', lambda e: e.tensor_scalar(out=A_['t1'][:, :TB], in0=A_['a'][:, :TB], scalar1=KAc[:, U:U + 1], scalar2=OMKA[:, U:U + 1], op0=ALU.mult, op1=ALU.add), r=[K('a'), 'l1_rw_ka', 'OMKA'], w=[K('t1')])
                    yield
                    c.op('dve', lambda e: e.tensor_tensor(out=A_['kd'][:, :TB], in0=kp[:, :TB], in1=A_['t1'][:, :TB], op=ALU.mult), r=[K('kp'), K('t1')], w=[K('kd')])
                    yield
                    c.op('dve', lambda e: e.tensor_tensor(out=A_['akk'][:, :TB], in0=A_['a'][:, :TB], in1=A_['kk'][:, :TB], op=ALU.mult), r=[K('a'), K('kk')], w=[K('akk')])
                    yield
                    c.op('dve', lambda e: e.scalar_tensor_tensor(out=A_['t1'][:, :TB], in0=rp[:, :TB], scalar=RKc[:, U:U + 1], in1=A_['kd'][:, :TB], op0=ALU.mult, op1=ALU.mult), r=[K('rp'), 'l1_rw_rk', K('kd')], w=[K('t1')])
                    yield
                    c.op('pe', lambda e: e.matmul(PS[0][:, :TB], lhsT=BLK[:], rhs=A_['t1'][:, :TB], start=True, stop=True), r=['BLK', K('t1')], w=[('ps', 0)])
                    yield
                    c.op('dve', lambda e: e.tensor_tensor(out=A_['bon'][:, :TB], in0=PS[0][:, :TB], in1=vp[:, :TB], op=ALU.mult), r=[('ps', 0), K('vp')], w=[K('bon')])
                    yield
                    Dv = (lambda x: x[:, 0:TB]) if d == 0 else (lambda x: x[:, 0:TB][:, ::-1])
                    c.op('dve', lambda e: e.tensor_tensor_scan(out=Dv(A_['G']), data0=CMF[:, :TB], data1=Dv(A_['LW']), initial=0.0, op0=ALU.mult, op1=ALU.add), r=['CMF', K('LW')], w=[K('G')])
                    yield
                    c.op('act', lambda e: e.activation(out=A_['EG'][:, :TB], in_=A_['G'][:, :TB], func=AF.Exp), r=[K('G')], w=[K('EG')])
                    yield
                    c.op('act', lambda e: e.activation(out=A_['EGN'][:, :TB], in_=A_['G'][:, :TB], func=AF.Exp, scale=-1.0), r=[K('G')], w=[K('EGN')])
                    yield
                    c.op('dve', lambda e: e.tensor_tensor(out=A_['t1'][:, :TB], in0=A_['G'][:, :TB], in1=A_['LW'][:, :TB], op=ALU.subtract), r=[K('G'), K('LW')], w=[K('t1')])
                    yield
                    c.op('act', lambda e: e.activation(out=A_['EGm'][:, :TB], in_=A_['t1'][:, :TB], func=AF.Exp), r=[K('t1')], w=[K('EGm')])
                    yield
                    c.op('dve', lambda e: e.tensor_tensor(out=A_['AT'][:, :TB], in0=A_['akk'][:, :TB], in1=A_['EGN'][:, :TB], op=ALU.mult), r=[K('akk'), K('EGN')], w=[K('AT')])
                    yield
                    c.op('dve', lambda e: e.tensor_tensor(out=A_['KAT'][:, :TB], in0=A_['kk'][:, :TB], in1=A_['EGm'][:, :TB], op=ALU.mult), r=[K('kk'), K('EGm')], w=[K('KAT')])
                    yield
                    c.op('dve', lambda e: e.tensor_tensor(out=A_['KT'][:, :TB], in0=A_['kd'][:, :TB], in1=A_['EGN'][:, :TB], op=ALU.mult), r=[K('kd'), K('EGN')], w=[K('KT')])
                    yield
                    c.op('dve', lambda e: e.tensor_tensor(out=A_['RT'][:, :TB], in0=rp[:, :TB], in1=A_['EG'][:, :TB], op=ALU.mult), r=[K('rp'), K('EG')], w=[K('RT')])
                    yield
                    AT, KAT, KT, RT = A_['AT'], A_['KAT'], A_['KT'], A_['RT']

            def run_item(it, nxt, pend):
                d, U, TB, b0, par, bpar, ncc = it['d'], it['U'], it['TB'], it['b0'], it['par'], it['bpar'], it['ncc']
                sq_ = it['sq']
                off, L, latent, pidx = sq_['off'], sq_['L'], sq_['latent'], sq_['pidx']
                p3 = it['p3']
                A_ = dict(A_sets[par])
                A_.update(A3_[p3])
                K = lambda n: (n, 't', p3) if n in TBN else ((n, par) if n in DBN else n)
                mS = MSf if d == 0 else MSb
                mI = MIf if d == 0 else MIb
                mo_S = 0 if d == 0 else 4
                mo_T = 4 if d == 0 else 0
                AT, KAT, KT, RT = A_['AT'], A_['KAT'], A_['KT'], A_['RT']
                if it['bi'] == 0 and U == 0:
                    for Ui in range(4):
                        if latent:
                            c.op('dve', lambda e: e.memset(Sld[:], 0.0), w=['Sld'])
                            for hh in range(2):
                                lo = 64 * hh
                                c.dma('sp', Sld[lo:lo + 64, lo:lo + 64], I['st_rwkv'][d, 2 * Ui + hh], w=['Sld'])
                            c.op('pe', lambda e: e.transpose(out=PS[0][:, 0:128], in_=Sld[:], identity=self.ident[:]), r=['Sld', 'ident'], w=[('ps', 0)])
                            c.op('act', lambda e: e.activation(out=P2[Ui][:], in_=PS[0][:, 0:128], func=AF.Identity), r=[('ps', 0)], w=[('P2', Ui)])
                        else:
                            c.op('dve', lambda e: e.memset(P2[Ui][:], 0.0), w=[('P2', Ui)])
                if True:
                    def X(arr, hh, csl):
                        return arr[64 * hh:64 * hh + 64, csl]

                    def chain(n, cc, hh):
                        csl = slice(cc * 128, (cc + 1) * 128)
                        lo = 64 * hh
                        bnk = 1 + n % 7
                        pk = ('ps', bnk)
                        c.op('pe', lambda e: e.matmul(PS[bnk][:, 0:128], lhsT=X(AT, hh, csl), rhs=X(KAT, hh, csl), start=True, stop=True), r=[K('AT'), K('KAT')], w=[pk])
                        c.op('dve', lambda e: e.scalar_tensor_tensor(out=Qb[n][0][:], in0=PS[bnk][:, 0:128], scalar=-1.0, in1=RWM[:, mo_S, :], op0=ALU.mult, op1=ALU.mult), r=[pk, 'RWM'], w=[('Q', n, 0)])
                        yield
                        c.op('pe', lambda e: e.matmul(PS[bnk][:, 0:128], lhsT=X(KAT, hh, csl), rhs=X(AT, hh, csl), start=True, stop=True), r=[K('AT'), K('KAT')], w=[pk])
                        c.op('dve', lambda e: e.scalar_tensor_tensor(out=QTb[n][0][:], in0=PS[bnk][:, 0:128], scalar=-1.0, in1=RWM[:, mo_T, :], op0=ALU.mult, op1=ALU.mult), r=[pk, 'RWM'], w=[('QT', n, 0)])
                        for lv in range(3):
                            c.op('dve', lambda e: e.tensor_tensor(out=El[n][lv][:], in0=PS[bnk][:, 0:128], in1=RWM[:, mo_T + 1 + lv, :], op=ALU.mult), r=[pk, 'RWM'], w=[('El', n, lv)])
                        yield
                        for (nm, la, ra, msk, dst) in (('A2T', KT, KAT, mS, A2T[n]), ('B1T', AT, RT, mI, B1T[par][n]), ('B2T', KT, RT, mI, B2T[par][n])):
                            c.op('pe', lambda e: e.matmul(PS[bnk][:, 0:128], lhsT=X(la, hh, csl), rhs=X(ra, hh, csl), start=True, stop=True), r=[K('AT'), K('KAT'), K('KT'), K('RT')], w=[pk])
                            c.op('dve', lambda e: e.tensor_tensor(out=dst[:], in0=PS[bnk][:, 0:128], in1=msk, op=ALU.mult), r=[pk, 'misc'], w=[((nm, par, n) if nm != 'A2T' else (nm, n))])
                            yield
                        c.op('dve', lambda e: e.tensor_tensor(out=Zb[n][:], in0=Qb[n][0][:], in1=self.ident[:], op=ALU.add), r=[('Q', n, 0), 'ident'], w=[('Z', n)])
                        for (src, skey, dstl, dk) in ((A_['vp'], 'vp', Vx[par], ('Vx', par)), (KAT, 'KAT', KAx, ('KAx',)), (AT, 'AT', Ax[par], ('Ax', par)), (KT, 'KT', Kx[par], ('Kx', par))):
                            c.op('pe', lambda e: e.transpose(out=PS[bnk][:, 0:64], in_=src[lo:lo + 64, csl], identity=self.ident[lo:lo + 64, lo:lo + 64]), r=[K(skey), 'ident'], w=[pk])
                            c.op('act', lambda e: e.activation(out=dstl[n][:, lo:lo + 64], in_=PS[bnk][:, 0:64], func=AF.Identity), r=[pk], w=[dk + (n,)])
                            yield
                        for i in range(1, 4):
                            g0, g1 = (i - 1) % 2, i % 2
                            if i < 3:
                                c.op('pe', lambda e: e.matmul(PS[bnk][:, 0:128], lhsT=QTb[n][g0][:], rhs=Qb[n][g0][:], start=True, stop=True), r=[('QT', n, g0), ('Q', n, g0)], w=[pk])
                                c.op('act', lambda e: e.activation(out=Qb[n][g1][:], in_=PS[bnk][:, 0:128], func=AF.Identity), r=[pk], w=[('Q', n, g1)])
                                yield
                            c.op('pe', lambda e: e.matmul(PS[bnk][:, 0:128], lhsT=Qb[n][g0][:], rhs=QTb[n][g0][:], start=True, stop=True), r=[('QT', n, g0), ('Q', n, g0)], w=[pk])
                            c.op('act', lambda e: e.activation(out=QTb[n][g1][:], in_=PS[bnk][:, 0:128], func=AF.Identity), r=[pk], w=[('QT', n, g1)])
                            yield
                            c.op('pe', lambda e: e.matmul(PS[bnk][:, 0:128], lhsT=QTb[n][g1][:], rhs=Zb[n][:], start=True, stop=True), r=[('QT', n, g1), ('Z', n)], w=[pk])
                            c.op('dve', lambda e: e.tensor_tensor(out=Zb[n][:], in0=Zb[n][:], in1=PS[bnk][:, 0:128], op=ALU.add), r=[pk, ('Z', n)], w=[('Z', n)])
                            yield
                        for lv in range(3):
                            c.op('pe', lambda e: e.transpose(out=PS[bnk][:, 0:128], in_=Zb[n][:], identity=self.ident[:]), r=[('Z', n), 'ident'], w=[pk])
                            c.op('act', lambda e: e.activation(out=ZTb[n][:], in_=PS[bnk][:, 0:128], func=AF.Identity), r=[pk], w=[('ZT', n)])
                            yield
                            c.op('pe', lambda e: e.matmul(PS[bnk][:, 0:128], lhsT=El[n][lv][:], rhs=Zb[n][:], start=True, stop=True), r=[('El', n, lv), ('Z', n)], w=[pk])
                            c.op('act', lambda e: e.activation(out=Fb[n][:], in_=PS[bnk][:, 0:128], func=AF.Identity), r=[pk], w=[('F', n)])
                            yield
                            c.op('pe', lambda e: e.matmul(PS[bnk][:, 0:128], lhsT=ZTb[n][:], rhs=Fb[n][:], start=True, stop=True), r=[('ZT', n), ('F', n)], w=[pk])
                            c.op('dve', lambda e: e.tensor_tensor(out=Zb[n][:], in0=Zb[n][:], in1=PS[bnk][:, 0:128], op=ALU.subtract), r=[pk, ('Z', n)], w=[('Z', n)])
                            yield
                        c.op('pe', lambda e: e.matmul(PS[bnk][:, 0:64], lhsT=A2T[n][:], rhs=Vx[par][n][:, lo:lo + 64], start=True, stop=True), r=[('A2T', n), ('Vx', par, n)], w=[pk])
                        c.op('act', lambda e: e.activation(out=Xs[n][:], in_=PS[bnk][:, 0:64], func=AF.Identity), r=[pk], w=[('Xs', n)])
                        yield
                        c.op('pe', lambda e: e.matmul(PS[bnk][:, 0:64], lhsT=Zb[n][:], rhs=Xs[n][:], start=True, stop=True), r=[('Z', n), ('Xs', n)], w=[pk])
                        c.op('act', lambda e: e.activation(out=Uvs[par][n][:], in_=PS[bnk][:, 0:64], func=AF.Identity), r=[pk], w=[('Uvs', par, n)])
                        yield
                        c.op('pe', lambda e: e.matmul(PS[bnk][:, 0:128], lhsT=KAx[n][:], rhs=Zb[n][:], start=True, stop=True), r=[('KAx', n), ('Z', n)], w=[pk])
                        c.op('act', lambda e: e.activation(out=WTs[par][n][lo:lo + 64, :], in_=PS[bnk][lo:lo + 64, 0:128], func=AF.Identity), r=[pk], w=[('WTs', par, n)])
                        yield
                chains = []
                for ci in range(ncc):
                    cc = ci if d == 0 else ncc - 1 - ci
                    for hh in range(2):
                        chains.append(chain(ci * 2 + hh, cc, hh))
                if nxt is not None:
                    chains.append(prep_gen(nxt))
                if pend is not None:
                    chains.append(pend)
                while chains:
                    for g_ in list(chains):
                        try:
                            next(g_)
                        except StopIteration:
                            chains.remove(g_)
            def so_gen(it):
                d, U, TB, b0, par, bpar, ncc = it['d'], it['U'], it['TB'], it['b0'], it['par'], it['bpar'], it['ncc']
                sq_ = it['sq']
                off, L, latent, pidx = sq_['off'], sq_['L'], sq_['latent'], sq_['pidx']
                p3 = it['p3']
                A_ = dict(A_sets[par])
                A_.update(A3_[p3])
                K = lambda n: (n, 't', p3) if n in TBN else ((n, par) if n in DBN else n)
                mS = MSf if d == 0 else MSb
                mI = MIf if d == 0 else MIb
                mo_S = 0 if d == 0 else 4
                mo_T = 4 if d == 0 else 0
                AT, KAT, KT, RT = A_['AT'], A_['KAT'], A_['KT'], A_['RT']
                if True:
                    for ci in range(ncc):
                        cc = ci if d == 0 else ncc - 1 - ci
                        csl = slice(cc * 128, (cc + 1) * 128)
                        glast = (cc * 128 + 127) if d == 0 else cc * 128
                        for hh in range(2):
                            n = ci * 2 + hh
                            lo = 64 * hh
                            bnk = 5 + hh
                            c.op('pe', lambda e: e.matmul(PS[bnk][:, 0:64], lhsT=WTs[par][n][lo:lo + 64, :], rhs=P2[U][lo:lo + 64, lo:lo + 64], start=True, stop=True), r=[('WTs', par, n), ('P2', U)], w=[('ps', bnk)])
                            yield
                            c.op('dve', lambda e: e.scalar_tensor_tensor(out=Ux[hh][:, lo:lo + 64], in0=PS[bnk][:, 0:64], scalar=-1.0, in1=Uvs[par][n][:], op0=ALU.mult, op1=ALU.subtract), r=[('ps', bnk), ('Uvs', par, n)], w=[('Ux', hh)])
                            yield
                        for hh in range(2):
                            n = ci * 2 + hh
                            lo = 64 * hh
                            c.op('pe', lambda e: e.matmul(PS[7][:, 0:128], lhsT=P2[U][lo:lo + 64, :], rhs=RT[lo:lo + 64, csl], start=(hh == 0), stop=False), r=[('P2', U), K('RT')], w=[('ps', 7)])
                            yield
                            c.op('pe', lambda e: e.matmul(PS[7][:, 0:128], lhsT=Ux[hh][:], rhs=B1T[par][n][:], start=False, stop=False), r=[('Ux', hh), ('B1T', par, n)], w=[('ps', 7)])
                            yield
                            c.op('pe', lambda e: e.matmul(PS[7][:, 0:128], lhsT=Vx[par][n][:], rhs=B2T[par][n][:], start=False, stop=(hh == 1)), r=[('Vx', par, n), ('B2T', par, n)], w=[('ps', 7)])
                            yield
                        c.op('act', lambda e: e.activation(out=A_['y'][:, csl], in_=PS[7][:, 0:128], func=AF.Identity), r=[('ps', 7)], w=[K('y')])
                        yield
                        for hh in range(2):
                            n = ci * 2 + hh
                            lo = 64 * hh
                            bnk = 5 + hh
                            c.op('pe', lambda e: e.matmul(PS[bnk][:, 0:64], lhsT=Ax[par][n][:], rhs=Ux[hh][:, lo:lo + 64], start=True, stop=False), r=[('Ax', par, n), ('Ux', hh)], w=[('ps', bnk)])
                            yield
                            c.op('pe', lambda e: e.matmul(PS[bnk][:, 0:64], lhsT=Kx[par][n][:], rhs=Vx[par][n][:, lo:lo + 64], start=False, stop=True), r=[('Kx', par, n), ('Vx', par, n)], w=[('ps', bnk)])
                            yield
                            c.op('dve', lambda e: e.tensor_tensor(out=Ptmp[hh][lo:lo + 64, :], in0=P2[U][lo:lo + 64, lo:lo + 64], in1=PS[bnk][lo:lo + 64, 0:64], op=ALU.add), r=[('ps', bnk), ('P2', U)], w=[('Ptmp', hh)])
                            yield
                            c.op('act', lambda e: e.activation(out=P2[U][lo:lo + 64, lo:lo + 64], in_=Ptmp[hh][lo:lo + 64, :], func=AF.Identity, scale=A_['EG'][lo:lo + 64, glast:glast + 1]), r=[('Ptmp', hh), K('EG')], w=[('P2', U)])
                            yield
                if True:
                    g0_ = b0
                    if d == 0:
                        c.dma('sp', YFv[:, U, g0_:g0_ + TB], A_['y'][:, :TB], r=[K('y')], w=[('YF', U, g0_)])
                        yield
                        c.dma('sp', BFv[:, U, g0_:g0_ + TB], A_['bon'][:, :TB], r=[K('bon')], w=[('BF', U, g0_)])
                        yield
                    else:
                        c.dma('sp', o1[:, :TB], YFv[:, U, g0_:g0_ + TB], r=[('YF', U, g0_)], w=['o1'])
                        yield
                        c.dma('sp', o2[:, :TB], BFv[:, U, g0_:g0_ + TB], r=[('BF', U, g0_)], w=['o2'])
                        yield
                        c.op('dve', lambda e: e.tensor_tensor(out=A_['y'][:, :TB], in0=A_['y'][:, :TB], in1=o1[:, :TB], op=ALU.add), r=[K('y'), 'o1'], w=[K('y')])
                        yield
                        c.op('dve', lambda e: e.tensor_tensor(out=A_['bon'][:, :TB], in0=A_['bon'][:, :TB], in1=o2[:, :TB], op=ALU.add), r=[K('bon'), 'o2'], w=[K('bon')])
                        yield
                        c.op('pe', lambda e: e.matmul(PS[5][:, :TB], lhsT=BLK[:], rhs=A_['y'][:, :TB], start=True, stop=True), r=['BLK', K('y')], w=[('ps', 5)])
                        yield
                        c.op('dve', lambda e: e.scalar_tensor_tensor(out=o1[:, :TB], in0=PS[5][:, :TB], scalar=-1.0 / 64.0, in1=A_['y'][:, :TB], op0=ALU.mult, op1=ALU.add), r=[('ps', 5), K('y')], w=['o1'])
                        yield
                        c.op('act', lambda e: e.activation(out=o2[:, :TB], in_=o1[:, :TB], func=AF.Square), r=['o1'], w=['o2'])
                        yield
                        c.op('pe', lambda e: e.matmul(PS[5][:, :TB], lhsT=BLK[:], rhs=o2[:, :TB], start=True, stop=True), r=['BLK', 'o2'], w=[('ps', 5)])
                        yield
                        c.op('act', lambda e: e.activation(out=o2[:, :TB], in_=PS[5][:, :TB], func=AF.Sqrt, scale=1.0 / 64.0, bias=self.cst[:, 2:3]), r=[('ps', 5), ('cst', 2)], w=['o2'])
                        yield
                        c.op('dve', lambda e: e.reciprocal(out=o2[:, :TB], in_=o2[:, :TB]), r=['o2'], w=['o2'])
                        yield
                        c.op('dve', lambda e: e.tensor_tensor(out=o1[:, :TB], in0=o1[:, :TB], in1=o2[:, :TB], op=ALU.mult), r=['o1', 'o2'], w=['o1'])
                        yield
                        c.op('dve', lambda e: e.tensor_scalar(out=o1[:, :TB], in0=o1[:, :TB], scalar1=LNW[:, U:U + 1], scalar2=LNB[:, U:U + 1], op0=ALU.mult, op1=ALU.add), r=['o1', 'l1_ln_w', 'l1_ln_b'], w=['o1'])
                        yield
                        c.op('dve', lambda e: e.tensor_tensor(out=o1[:, :TB], in0=o1[:, :TB], in1=A_['bon'][:, :TB], op=ALU.add), r=['o1', K('bon')], w=['o1'])
                        yield
                        c.op('pe', lambda e: e.matmul(PS[5][:, :TB], lhsT=G2[:, U * 128:(U + 1) * 128], rhs=hgs[bpar][:, :TB], start=True, stop=True), r=['G2', ('hg', bpar)], w=[('ps', 5)])
                        yield
                        c.op('dve', lambda e: e.tensor_tensor(out=yb[:, :TB], in0=o1[:, :TB], in1=PS[5][:, :TB], op=ALU.mult), r=['o1', ('ps', 5)], w=['yb'])
                        yield
                        c.dma('sp', Yv[:, 4 + U, g0_:g0_ + TB], yb[:, :TB], r=['yb'], w=[('Y', 4 + U, g0_)])
                        yield
                if it['bi'] == it['nb'] - 1 and U == 3:
                    if pidx is not None:
                        for Ui in range(4):
                            c.op('pe', lambda e: e.transpose(out=PS[5][:, 0:128], in_=P2[Ui][:], identity=self.ident[:]), r=[('P2', Ui), 'ident'], w=[('ps', 5)])
                            yield
                            c.op('act', lambda e: e.activation(out=Sout[:], in_=PS[5][:, 0:128], func=AF.Identity), r=[('ps', 5)], w=['Sout'])
                            yield
                            for hh in range(2):
                                lo = 64 * hh
                                c.dma('sp', O['new_rwkv'][pidx, d, 2 * Ui + hh], Sout[lo:lo + 64, lo:lo + 64], r=['Sout'], w=[('nrw', pidx, d, Ui, hh)])
                                yield

            for d in range(2):
                items = []
                for sq_ in self.seqs():
                    TB_ = min(TBM, sq_['L'])
                    for bi_ in range(sq_['L'] // TB_):
                        for U in range(4):
                            items.append(mk_item(d, sq_, bi_, U, len(items)))
                for _ in prep_gen(items[0]):
                    pass
                pend = None
                for k, it in enumerate(items):
                    if it['bi'] == 0 and it['U'] == 0 and pend is not None:
                        for _ in pend:
                            pass
                        pend = None
                    run_item(it, items[k + 1] if k + 1 < len(items) else None, pend)
                    pend = so_gen(it)
                for _ in pend:
                    pass
            c.barrier()


def host_consts(LS):
    ident = np.eye(128, dtype=np.float32)
    GRID_W = 64
    nf = 32
    t = np.arange(LS)
    row = (t // GRID_W).astype(np.float32)
    col = (t % GRID_W).astype(np.float32)
    freqs = (10000.0 ** (-np.arange(nf, dtype=np.float32) / nf)).astype(np.float32)
    cos = np.zeros((128, LS), np.float32)
    sin = np.zeros((128, LS), np.float32)
    rot = np.zeros((128, 128), np.float32)
    for a in range(2):
        pos = row if a == 0 else col
        ang = (pos[None, :] * freqs[:, None]).astype(np.float32)
        for b in range(2):
            p0 = a * 64 + b * 32
            cos[p0:p0 + 32] = np.cos(ang)
            sin[p0:p0 + 32] = np.sin(ang)
            for f in range(nf):
                m = p0 + f
                partner = a * 64 + (1 - b) * 32 + f
                rot[partner, m] = -1.0 if b == 0 else 1.0
    misc = np.zeros((128, 1024), np.float32)
    idx = np.arange(128)
    rel = idx[None, :] - idx[:, None]
    misc[:, 0:128] = np.abs(rel)
    misc[:, 128:256] = (rel >= 0)
    misc[:, 256:384] = (rel <= 0)
    misc[:, 384:512] = (rel > 0)
    misc[:, 512:640] = (rel < 0)
    misc[:, 640:768] = idx[None, :]
    misc[:, 768:896] = idx[None, :] + 1
    misc[:, 896] = idx
    misc[:, 897] = 127 - idx
    misc[:, 898] = 128.0
    iota = np.tile(np.arange(1, 513, dtype=np.float32)[None, :], (128, 1))
    s5m = np.zeros((128, 4, 128), np.float32)
    for p in range(128):
        for q in range(4):
            for col in range(128):
                if p // 16 == 2 * q + col // 64:
                    s5m[p, q, col] = 1.0
    cmf = np.ones((128, 512), np.float32)
    cmf[:, 0::128] = 0.0
    blk = np.zeros((128, 128), np.float32)
    blk[:64, :64] = 1.0
    blk[64:, 64:] = 1.0
    rwm = np.zeros((128, 8, 128), np.float32)
    pp = idx[:, None]
    ff = idx[None, :]
    up = pp < ff
    pats = [up & ((pp // 16) == (ff // 16))]
    for k in (16, 32, 64):
        pats.append(up & ((pp // (2 * k)) == (ff // (2 * k))) & ((pp // k) != (ff // k)))
    for i, pt in enumerate(pats):
        rwm[:, i, :] = pt
        rwm[:, 4 + i, :] = pt.T
    return dict(c_ident=ident, c_rope_cos=cos, c_rope_sin=sin, c_rope_rot=rot, c_misc=misc, c_iota=iota, c_s5mask=s5m.reshape(128, 512), c_cmf=cmf, c_blk=blk,
                c_rwm=rwm.reshape(128, 1024))


WEIGHT_KEYS = ['mod_w', 'mod_b', 'norm_g', 'ffn1_w13', 'ffn1_w2', 'ffn2_w13', 'ffn2_w2', 'final_norm',
               'l0_w_in', 'l0_w_out', 'l0_ret_decay', 'l0_conv_w', 'l0_conv_b', 'l0_lru_lam', 'l0_lru_wa', 'l0_lru_ba',
               'l0_lru_wx', 'l0_lru_bx', 'l1_w_in', 'l1_w_out', 'l1_s5_a_re', 'l1_s5_a_im', 'l1_s5_log_dt',
               'l1_s5_b_re', 'l1_s5_b_im', 'l1_s5_c_re', 'l1_s5_c_im', 'l1_s5_d', 'l1_glu_w', 'l1_glu_b',
               'l1_rw_mu', 'l1_rw_w0', 'l1_rw_w1', 'l1_rw_w2', 'l1_rw_a0', 'l1_rw_a1', 'l1_rw_a2', 'l1_rw_g1', 'l1_rw_g2',
               'l1_rw_kk', 'l1_rw_ka', 'l1_rw_rk', 'l1_ln_w', 'l1_ln_b']


def core_inputs(inp, b, pj, NP, consts):
    f = lambda a: np.ascontiguousarray(np.asarray(a, dtype=np.float32))
    xs = f(inp['x_sample'][b])
    xp = f(inp['x_prompt'][pj * NP:(pj + 1) * NP]).reshape(NP * LP, D)
    m = {'x_tok': np.concatenate([xs, xp], axis=0),
         'cond': np.stack([f(inp['c'][b]), f(inp['c_ctx'])], axis=0),
         'st_ret': f(inp['state_l0_ret'][b]), 'st_lru': f(inp['state_l0_lru'][b]),
         'st_s5': f(inp['state_l1_s5'][b]), 'st_rwkv': f(inp['state_l1_rwkv'][b])}
    for k in WEIGHT_KEYS:
        m[k] = f(inp[k])
    m.update(consts)
    return m


_PROG_CACHE = {}


def get_prog(cfg):
    key = tuple(sorted(cfg.items()))
    if key not in _PROG_CACHE:
        p = Prog(cfg)
        p.build()
        _PROG_CACHE[key] = p
    return _PROG_CACHE[key]


def kernel(**inputs):
    LS = 4096
    NP = 4
    cfg = dict(LS=LS, NP=NP)
    prog = get_prog(cfg)
    consts = host_consts(LS)
    in_maps = [core_inputs(inputs, cid % 4, cid, NP, consts) for cid in range(8)]
    res = run_bass_kernel_spmd(prog.nc, in_maps, core_ids=list(range(8)))
    R = res.results
    y_sample = np.stack([R[b]['y_tok'][:LS] for b in range(4)], axis=0)
    y_prompt = np.concatenate([R[cid]['y_tok'][LS:].reshape(NP, LP, D) for cid in range(8)], axis=0)
    new_ret = np.concatenate([R[cid]['new_ret'] for cid in range(8)], axis=0)
    new_lru = np.concatenate([R[cid]['new_lru'] for cid in range(8)], axis=0)
    new_s5 = np.concatenate([R[cid]['new_s5'] for cid in range(8)], axis=0)
    new_rwkv = np.concatenate([R[cid]['new_rwkv'] for cid in range(8)], axis=0)
    return (y_prompt.astype(np.float32), y_sample.astype(np.float32), new_ret.astype(np.float32),
            new_lru.astype(np.float32), new_s5.astype(np.float32), new_rwkv.astype(np.float32))
```
